# Optimizing a Trainium2 kernel written in Bass

```python
import math
import jax, jax.numpy as jnp
from jax import lax
import numpy as np

D_MODEL = 2048
BATCH = 8
SEQ = 2048
DEPTH = 4

N_MIXERS = 3
N_A = (DEPTH + 2) // 3
N_B = (DEPTH + 1) // 3
N_C = DEPTH // 3

MIX_WIDTH = 3 * D_MODEL // 4
MEM_LEN = 256
MEM_HEADS = 4
MEM_HEAD_DIM = D_MODEL // 16
MEM_WIDTH = MEM_HEADS * MEM_HEAD_DIM
OUT_WIDTH = MIX_WIDTH + MEM_WIDTH
NORM_EPS = 1e-6

SWA_HEAD_DIM = 64
SWA_Q_HEADS = MIX_WIDTH // SWA_HEAD_DIM
SWA_KV_HEADS = 4
SWA_GROUP = SWA_Q_HEADS // SWA_KV_HEADS
SWA_WINDOW = 128
SWA_BLOCK = 128
A_WIDTHS = (SWA_Q_HEADS * SWA_HEAD_DIM, SWA_KV_HEADS * SWA_HEAD_DIM, SWA_KV_HEADS * SWA_HEAD_DIM, MEM_WIDTH)

RWKV_HEAD_DIM = 64
RWKV_HEADS = MIX_WIDTH // RWKV_HEAD_DIM
RWKV_DECAY_RANK = 96
RWKV_ICLR_RANK = 96
RWKV_GATE_RANK = 256
RWKV_GN_EPS = 64e-5
B_SHIFT_WIDTHS = (MIX_WIDTH, MIX_WIDTH, MIX_WIDTH, RWKV_DECAY_RANK, RWKV_ICLR_RANK, RWKV_GATE_RANK)
B_SHIFT = sum(B_SHIFT_WIDTHS)

GDN_HEAD_DIM = 128
GDN_V_HEADS = MIX_WIDTH // GDN_HEAD_DIM
GDN_QK_HEADS = GDN_V_HEADS // 2
GDN_CONV = 4
GDN_CHUNK = 64
GDN_QK_WIDTH = GDN_QK_HEADS * GDN_HEAD_DIM
GDN_CONV_WIDTH = 2 * GDN_QK_WIDTH + MIX_WIDTH
C_WIDTHS = (GDN_QK_WIDTH, GDN_QK_WIDTH, MIX_WIDTH, MIX_WIDTH, GDN_V_HEADS, GDN_V_HEADS, MEM_WIDTH)

D_FF = 5632
FFN_CONV = 3

kernel_name = "hybrid_swa_rwkv7_gdn_memxattn_convffn"


def split_cols(p, widths):
    return jnp.split(p, [int(i) for i in np.cumsum(widths)[:-1]], axis=-1)


def rmsnorm(x, g):
    xf = x.astype(jnp.float32)
    y = xf * lax.rsqrt(jnp.mean(xf * xf, axis=-1, keepdims=True) + NORM_EPS)
    return (y * g.astype(jnp.float32)).astype(x.dtype)


def l2norm(x):
    x = x.astype(jnp.float32)
    return x * lax.rsqrt(jnp.sum(x * x, axis=-1, keepdims=True) + 1e-6)


def token_shift(x):
    return jnp.pad(x, ((0, 0), (1, 0), (0, 0)))[:, :-1]


def causal_dwconv(x, w):
    k_w, s = w.shape[0], x.shape[1]
    xp = jnp.pad(x, ((0, 0), (k_w - 1, 0), (0, 0)))
    out = xp[:, :s] * w[0]
    for j in range(1, k_w):
        out = out + xp[:, j:j + s] * w[j]
    return out


def alibi_slopes(n):
    return jnp.exp2(-8.0 * (jnp.arange(n, dtype=jnp.float32) + 1.0) / n)


def swa_sink_attention(q, k, v, sinks):
    b, s, _ = q.shape
    t, nb, dh = SWA_BLOCK, s // SWA_BLOCK, SWA_HEAD_DIM
    qb = q.reshape(b, nb, t, SWA_KV_HEADS, SWA_GROUP, dh)

    def banded(z):
        zb = z.reshape(b, nb, t, SWA_KV_HEADS, dh)
        prev = jnp.pad(zb, ((0, 0), (1, 0), (0, 0), (0, 0), (0, 0)))[:, :-1]
        return jnp.concatenate([prev, zb], axis=2)

    kb, vb = banded(k), banded(v)
    scores = jnp.einsum('bntkgd,bnjkd->bnkgtj', qb, kb).astype(jnp.float32) * (dh ** -0.5)
    blk = jnp.arange(nb)[:, None]
    qpos = blk * t + jnp.arange(t)[None, :]
    kpos = (blk - 1) * t + jnp.arange(2 * t)[None, :]
    dist = qpos[:, :, None] - kpos[:, None, :]
    valid = (dist >= 0) & (dist < SWA_WINDOW) & (kpos[:, None, :] >= 0)
    slopes = alibi_slopes(SWA_Q_HEADS).reshape(SWA_KV_HEADS, SWA_GROUP)
    bias = -slopes[None, :, :, None, None] * dist[:, None, None].astype(jnp.float32)
    scores = jnp.where(valid[:, None, None], scores + bias, -jnp.inf)
    sink = jnp.broadcast_to(sinks.astype(jnp.float32).reshape(1, 1, SWA_KV_HEADS, SWA_GROUP, 1, 1),
                            scores.shape[:-1] + (1,))
    probs = jax.nn.softmax(jnp.concatenate([scores, sink], axis=-1), axis=-1)[..., :-1]
    out = jnp.einsum('bnkgtj,bnjkd->bntkgd', probs.astype(v.dtype), vb)
    return out.reshape(b, s, SWA_Q_HEADS * dh)


def memory_attention(q, mem_kv):
    b, s, _ = q.shape
    qh = q.reshape(b, s, MEM_HEADS, MEM_HEAD_DIM)
    k, v = jnp.split(mem_kv, 2, axis=-1)
    kh = k.reshape(b, -1, MEM_HEADS, MEM_HEAD_DIM)
    vh = v.reshape(b, -1, MEM_HEADS, MEM_HEAD_DIM)
    scores = jnp.einsum('bshd,bmhd->bhsm', qh, kh).astype(jnp.float32) * (MEM_HEAD_DIM ** -0.5)
    probs = jax.nn.softmax(scores, axis=-1).astype(vh.dtype)
    return jnp.einsum('bhsm,bmhd->bshd', probs, vh).reshape(b, s, MEM_WIDTH)


def rwkv7_scan(r, w, k, v, kk, a):
    b, s, h, n = r.shape

    def step(state, inp):
        r_t, w_t, k_t, v_t, kk_t, a_t = inp
        sa = jnp.einsum('bhvk,bhk->bhv', state, -kk_t)
        state = (state * w_t[:, :, None, :] + sa[..., None] * (kk_t * a_t)[:, :, None, :]
                 + v_t[..., None] * k_t[:, :, None, :])
        return state, jnp.einsum('bhvk,bhk->bhv', state, r_t)

    xs = tuple(jnp.moveaxis(z, 1, 0) for z in (r, w, k, v, kk, a))
    _, ys = lax.scan(step, jnp.zeros((b, h, n, n), jnp.float32), xs)
    return jnp.moveaxis(ys, 0, 1)


def rwkv7_time_mix(p, mu, w0, w_decay_up, a0, w_iclr_up, w_gate_up, k_k, k_a, r_k, gn_g, gn_b):
    b, s, _ = p.shape
    f32 = jnp.float32
    h, n = RWKV_HEADS, RWKV_HEAD_DIM
    p = p + (token_shift(p) - p) * mu
    r, k, v, wd, ad, gd = split_cols(p, B_SHIFT_WIDTHS)
    w_log = -jax.nn.softplus(-(w0 + jnp.tanh(wd) @ w_decay_up).astype(f32)) - 0.5
    decay = jnp.exp(-jnp.exp(w_log))
    a = jax.nn.sigmoid((a0 + ad @ w_iclr_up).astype(f32))
    g = (jax.nn.sigmoid(gd) @ w_gate_up).astype(f32)
    k = k.astype(f32)
    heads = lambda z: z.reshape(b, s, h, n)
    kk = l2norm(heads(k * k_k))
    k = k * (1.0 + (a - 1.0) * k_a)
    rh, kh, vh = heads(r.astype(f32)), heads(k), heads(v.astype(f32))
    y = rwkv7_scan(rh, heads(decay), kh, vh, kk, heads(a))
    mean = jnp.mean(y, axis=-1, keepdims=True)
    var = jnp.mean(jnp.square(y - mean), axis=-1, keepdims=True)
    y = ((y - mean) * lax.rsqrt(var + RWKV_GN_EPS)).reshape(b, s, MIX_WIDTH) * gn_g + gn_b
    bonus = jnp.sum(rh * kh * r_k, axis=-1, keepdims=True) * vh
    y = y + bonus.reshape(b, s, MIX_WIDTH)
    return (y * g).astype(p.dtype)


def chunk_gated_delta_rule(q, k, v, g, beta):
    b, s, h, dk = q.shape
    dv = v.shape[-1]
    c = GDN_CHUNK
    nc = s // c
    chunks = lambda z: jnp.moveaxis(z.reshape(b, nc, c, h, -1), 3, 1)
    q = chunks(q * (dk ** -0.5))
    k = chunks(k)
    v = chunks(v)
    beta = chunks(beta[..., None])
    gc = jnp.cumsum(chunks(g[..., None])[..., 0], axis=-1)
    idx = jnp.arange(c)
    causal = idx[:, None] >= idx[None, :]
    strict = idx[:, None] > idx[None, :]
    decay = jnp.exp(jnp.where(causal, gc[..., :, None] - gc[..., None, :], -jnp.inf))
    kb = k * beta
    lmat = jnp.where(strict, jnp.einsum('bhnid,bhnjd->bhnij', kb, k) * decay, 0.0)
    eye = jnp.eye(c, dtype=lmat.dtype)
    tmat = lax.linalg.triangular_solve(lmat + eye, jnp.broadcast_to(eye, lmat.shape),
                                       left_side=True, lower=True, unit_diagonal=True)
    u = tmat @ (v * beta)
    w = tmat @ (kb * jnp.exp(gc)[..., None])
    a_qk = jnp.where(causal, jnp.einsum('bhnid,bhnjd->bhnij', q, k) * decay, 0.0)
    q_dec = q * jnp.exp(gc)[..., None]
    g_last = gc[..., -1]
    k_dec = k * jnp.exp(g_last[..., None] - gc)[..., None]

    def step(state, inp):
        u_c, w_c, qd_c, a_c, kd_c, gl_c = inp
        v_new = u_c - w_c @ state
        out = qd_c @ state + a_c @ v_new
        state = state * jnp.exp(gl_c)[..., None, None] + jnp.swapaxes(kd_c, -1, -2) @ v_new
        return state, out

    xs = tuple(jnp.moveaxis(z, 2, 0) for z in (u, w, q_dec, a_qk, k_dec, g_last))
    _, out = lax.scan(step, jnp.zeros((b, h, dk, dv), jnp.float32), xs)
    return jnp.transpose(out, (1, 0, 3, 2, 4)).reshape(b, s, h, dv)


def gated_deltanet(p, conv_w, a_log, dt_bias, norm_g):
    b, s, _ = p.shape
    f32 = jnp.float32
    qkv, z, bt, at = split_cols(p, (GDN_CONV_WIDTH, MIX_WIDTH, GDN_V_HEADS, GDN_V_HEADS))
    qkv = jax.nn.silu(causal_dwconv(qkv, conv_w))
    q, k, v = split_cols(qkv, (GDN_QK_WIDTH, GDN_QK_WIDTH, MIX_WIDTH))
    rep = GDN_V_HEADS // GDN_QK_HEADS
    q = jnp.repeat(l2norm(q.reshape(b, s, GDN_QK_HEADS, GDN_HEAD_DIM)), rep, axis=2)
    k = jnp.repeat(l2norm(k.reshape(b, s, GDN_QK_HEADS, GDN_HEAD_DIM)), rep, axis=2)
    v = v.reshape(b, s, GDN_V_HEADS, GDN_HEAD_DIM).astype(f32)
    beta = jax.nn.sigmoid(bt.astype(f32))
    g = -jnp.exp(a_log.astype(f32)) * jax.nn.softplus(at.astype(f32) + dt_bias.astype(f32))
    o = chunk_gated_delta_rule(q, k, v, g, beta)
    o = o * lax.rsqrt(jnp.mean(o * o, axis=-1, keepdims=True) + NORM_EPS) * norm_g.astype(f32)
    o = o.reshape(b, s, MIX_WIDTH) * jax.nn.silu(z.astype(f32))
    return o.astype(p.dtype)


def setup_inputs(seed: int = 0) -> dict:
    key = jax.random.key(seed)
    ks = iter(jax.random.split(key, 48))
    nrm = lambda shape, scale: jax.random.normal(next(ks), shape, jnp.float32) * scale
    gain = lambda shape: 1.0 + nrm(shape, 0.02)
    unif = lambda shape, lo, hi: jax.random.uniform(next(ks), shape, jnp.float32, lo, hi)
    d = D_MODEL
    a_in, b_in, c_in = sum(A_WIDTHS), B_SHIFT + MEM_WIDTH, sum(C_WIDTHS)
    dt = jnp.exp(unif((N_C, GDN_V_HEADS), math.log(1e-3), math.log(1e-1)))
    return {
        "x": nrm((BATCH, SEQ, d), 1.0),
        "mem": nrm((BATCH, MEM_LEN, d), 1.0),
        "attn_norm": gain((DEPTH, d)),
        "mem_norm": gain((DEPTH, d)),
        "w_mem_kv": nrm((DEPTH, d, 2 * MEM_WIDTH), d ** -0.5),
        "w_out": nrm((DEPTH, OUT_WIDTH, d), OUT_WIDTH ** -0.5),
        "ffn_norm": gain((DEPTH, d)),
        "w_ffn_up": nrm((DEPTH, d, 2 * D_FF), d ** -0.5),
        "ffn_conv": nrm((DEPTH, FFN_CONV, 2 * D_FF), FFN_CONV ** -0.5),
        "w_ffn_down": nrm((DEPTH, D_FF, d), D_FF ** -0.5),
        "final_norm": gain((d,)),
        "a_w_in": nrm((N_A, d, a_in), d ** -0.5),
        "a_sinks": nrm((N_A, SWA_Q_HEADS), 0.5),
        "b_w_in": nrm((N_B, d, b_in), d ** -0.5),
        "b_mu": unif((N_B, B_SHIFT), 0.0, 1.0),
        "b_w0": unif((N_B, MIX_WIDTH), -6.0, -1.0),
        "b_w_decay_up": nrm((N_B, RWKV_DECAY_RANK, MIX_WIDTH), 0.1 * RWKV_DECAY_RANK ** -0.5),
        "b_a0": nrm((N_B, MIX_WIDTH), 0.1),
        "b_w_iclr_up": nrm((N_B, RWKV_ICLR_RANK, MIX_WIDTH), RWKV_ICLR_RANK ** -0.5),
        "b_w_gate_up": nrm((N_B, RWKV_GATE_RANK, MIX_WIDTH), RWKV_GATE_RANK ** -0.5),
        "b_k_k": 0.85 + nrm((N_B, MIX_WIDTH), 0.02),
        "b_k_a": gain((N_B, MIX_WIDTH)),
        "b_r_k": nrm((N_B, RWKV_HEADS, RWKV_HEAD_DIM), 0.1),
        "b_gn_g": gain((N_B, MIX_WIDTH)),
        "b_gn_b": nrm((N_B, MIX_WIDTH), 0.02),
        "c_w_in": nrm((N_C, d, c_in), d ** -0.5),
        "c_conv": nrm((N_C, GDN_CONV, GDN_CONV_WIDTH), GDN_CONV ** -0.5),
        "c_a_log": jnp.log(unif((N_C, GDN_V_HEADS), 1.0, 16.0)),
        "c_dt_bias": dt + jnp.log(-jnp.expm1(-dt)),
        "c_norm_g": gain((N_C, GDN_HEAD_DIM)),
    }


def reference(x, mem, attn_norm, mem_norm, w_mem_kv, w_out, ffn_norm, w_ffn_up, ffn_conv, w_ffn_down,
              final_norm, a_w_in, a_sinks, b_w_in, b_mu, b_w0, b_w_decay_up, b_a0, b_w_iclr_up,
              b_w_gate_up, b_k_k, b_k_a, b_r_k, b_gn_g, b_gn_b, c_w_in, c_conv, c_a_log, c_dt_bias,
              c_norm_g):
    for i in range(DEPTH):
        kind, j = i % N_MIXERS, i // N_MIXERS
        h = rmsnorm(x, attn_norm[i])
        mem_kv = rmsnorm(mem, mem_norm[i]) @ w_mem_kv[i]
        if kind == 0:
            q, k, v, q_mem = split_cols(h @ a_w_in[j], A_WIDTHS)
            y = swa_sink_attention(q, k, v, a_sinks[j])
        elif kind == 1:
            p = h @ b_w_in[j]
            p_mix, q_mem = p[..., :B_SHIFT], p[..., B_SHIFT:]
            y = rwkv7_time_mix(p_mix, b_mu[j], b_w0[j], b_w_decay_up[j], b_a0[j], b_w_iclr_up[j],
                               b_w_gate_up[j], b_k_k[j], b_k_a[j], b_r_k[j], b_gn_g[j], b_gn_b[j])
        else:
            p = h @ c_w_in[j]
            p_mix, q_mem = p[..., :-MEM_WIDTH], p[..., -MEM_WIDTH:]
            y = gated_deltanet(p_mix, c_conv[j], c_a_log[j], c_dt_bias[j], c_norm_g[j])
        y_mem = memory_attention(q_mem, mem_kv)
        x = x + jnp.concatenate([y, y_mem], axis=-1) @ w_out[i]
        hf = rmsnorm(x, ffn_norm[i])
        u = causal_dwconv(hf @ w_ffn_up[i], ffn_conv[i])
        u_gate, u_val = jnp.split(u, 2, axis=-1)
        x = x + (jax.nn.silu(u_gate) * u_val) @ w_ffn_down[i]
    return rmsnorm(x, final_norm)
```

```python
import numpy as np
import concourse.bass as bass
import concourse.mybir as mybir
from concourse.bass_utils import run_bass_kernel_spmd

F32 = mybir.dt.float32
BF16 = mybir.dt.bfloat16
I32 = mybir.dt.int32
AF = mybir.ActivationFunctionType
ALU = mybir.AluOpType
AX = mybir.AxisListType


class Buf:
    __slots__ = ("w", "rs")

    def __init__(self):
        self.w = None
        self.rs = {}


class Ref:
    __slots__ = ("ap", "buf")

    def __init__(self, ap, buf):
        self.ap = ap
        self.buf = buf


class TB:
    def __init__(self, t, buf=None):
        self.t = t
        self.buf = buf or Buf()

    def __getitem__(self, idx):
        return Ref(self.t[idx], self.buf)

    def view(self, ap):
        return Ref(ap, self.buf)

    def part(self):
        return TB(self.t, Buf())


class Eng:
    def __init__(self, name, obj, sem):
        self.name = name
        self.obj = obj
        self.sem = sem
        self.count = 0
        self.waited = {}
        self.dma_sems = []
        self.dma_uses = []
        self.rr = 0
        self.nins = 0


WRITE_KW = ("out", "accum_out")


class K:
    def __init__(self, nc, stack, ndma=8):
        self.nc = nc
        self.stack = stack
        self.eng = {}
        for name, obj in (("pe", nc.tensor), ("act", nc.scalar), ("dve", nc.vector),
                          ("pool", nc.gpsimd), ("sp", nc.sync)):
            sem = stack.enter_context(nc.semaphore("s_" + name))
            self.eng[name] = Eng(name, obj, sem)
        for q in ("sp", "act", "pool"):
            E = self.eng[q]
            for i in range(ndma):
                E.dma_sems.append(stack.enter_context(nc.semaphore("d_%s%d" % (q, i))))
                E.dma_uses.append(0)
        self.uid = 0

    def sb(self, shape, dtype, name=None):
        self.uid += 1
        t = self.stack.enter_context(self.nc.sbuf_tensor(name or ("sb%d" % self.uid), list(shape), dtype))
        return TB(t)

    def ps(self, shape, dtype, name=None):
        self.uid += 1
        t = self.stack.enter_context(self.nc.psum_tensor(name or ("ps%d" % self.uid), list(shape), dtype))
        return TB(t)

    def dram(self, name, shape, dtype, kind="Internal"):
        t = self.nc.dram_tensor(name, list(shape), dtype, kind=kind)
        return TB(t.ap())

    def _wait(self, E, evs):
        for sem, val, owner in evs:
            if owner == "pe" and E.name == "pe":
                continue
            key = id(sem)
            if E.waited.get(key, 0) >= val:
                continue
            E.obj.wait_ge(sem, val)
            E.waited[key] = val
            E.nins += 1

    def _deps(self, reads, writes):
        evs = []
        for b in reads:
            if b.w is not None:
                evs.append(b.w)
        for b in writes:
            if b.w is not None:
                evs.append(b.w)
            evs.extend(b.rs.values())
        return evs

    def _record(self, ev, reads, writes):
        key = id(ev[0])
        for b in reads:
            old = b.rs.get(key)
            if old is None or old[1] < ev[1]:
                b.rs[key] = ev
        for b in writes:
            b.w = ev
            b.rs = {}

    def op(self, en, meth, *args, sig=True, R=(), W=(), **kw):
        E = self.eng[en]
        reads = [r.buf if isinstance(r, Ref) else r for r in R]
        writes = [w.buf if isinstance(w, Ref) else w for w in W]
        a2 = []
        for a in args:
            if isinstance(a, Ref):
                reads.append(a.buf)
                a = a.ap
            a2.append(a)
        k2 = {}
        for n, v in kw.items():
            if isinstance(v, Ref):
                (writes if n in WRITE_KW else reads).append(v.buf)
                v = v.ap
            k2[n] = v
        self._wait(E, self._deps(reads, writes))
        ins = getattr(E.obj, meth)(*a2, **k2)
        E.nins += 1
        if sig:
            E.count += 1
            ins.then_inc(E.sem, 1)
            ev = (E.sem, E.count, en)
        else:
            ev = (E.sem, E.count + 1, en)
        self._record(ev, reads, writes)
        return ins

    def dma(self, q, out, in_, **kw):
        E = self.eng[q]
        self._wait(E, self._deps([in_.buf], [out.buf]))
        k = E.rr
        sem = E.dma_sems[k]
        if E.dma_uses[k] > 0:
            self._wait(E, [(sem, 16 * E.dma_uses[k], "dma")])
        E.obj.dma_start(out=out.ap, in_=in_.ap, **kw).then_inc(sem, 16)
        E.nins += 1
        E.dma_uses[k] += 1
        ev = (sem, 16 * E.dma_uses[k], "dma")
        self._record(ev, [in_.buf], [out.buf])
        E.rr = (k + 1) % len(E.dma_sems)

    def finish(self):
        for q in ("sp", "act", "pool"):
            E = self.eng[q]
            for sem, uses in zip(E.dma_sems, E.dma_uses):
                if uses:
                    self._wait(E, [(sem, 16 * uses, "dma")])

    def mm(self, out, lhsT, rhs, start=True, stop=True, sig=None, **kw):
        if sig is None:
            sig = stop
        return self.op("pe", "matmul", out=out, lhsT=lhsT, rhs=rhs, start=start, stop=stop, sig=sig, **kw)

    def tr(self, out, in_, ident, sig=True):
        return self.op("pe", "transpose", out=out, in_=in_, identity=ident, sig=sig)

    def act(self, out, in_, func, **kw):
        return self.op("act", "activation", out=out, in_=in_, func=func, **kw)


from contextlib import ExitStack, contextmanager

S = 2048
D = 2048
NT = 16
NCH = 16
DFF = 5632
NFT = 44
EPS = 1e-6
NEG = -30000.0
WRITE_KW = ("out", "accum_out", "ap")


class Net:
    pass


def k_barrier(k):
    evs = []
    for n, E in k.eng.items():
        if E.count:
            evs.append((E.sem, E.count, n))
        for sem, uses in zip(E.dma_sems, E.dma_uses):
            if uses:
                evs.append((sem, 16 * uses, "dma"))
    for n, E in k.eng.items():
        k._wait(E, evs)


class StopBuild(Exception):
    pass


CFG = {}
PH = {"n": 0, "max": 10 ** 9}


@contextmanager
def phase(k):
    if PH["n"] >= PH["max"]:
        raise StopBuild()
    PH["n"] += 1
    k_barrier(k)
    saved = k.stack
    with ExitStack() as st:
        k.stack = st
        yield
        k_barrier(k)
    k.stack = saved


def psbf(ps):
    return TB(ps.t[:].bitcast(BF16), ps.buf)


def setup_consts(k):
    C = Net()
    C.onesf = k.sb([128, 128], F32)
    k.op("pool", "memset", ap=C.onesf[:], constant=1.0)
    C.onesb = k.sb([128, 128], BF16)
    k.op("dve", "tensor_copy", out=C.onesb[:], in_=C.onesf[:])

    def mask(cm, step, cmp):
        m = k.sb([128, 128], F32)
        k.op("pool", "affine_select", out=m[:], in_=C.onesf[:], pattern=[[step, 128]],
             compare_op=cmp, fill=0.0, base=0, channel_multiplier=cm)
        return m
    C.mask = mask
    C.identf = mask(1, -1, ALU.is_equal)
    C.identb = k.sb([128, 128], BF16)
    k.op("dve", "tensor_copy", out=C.identb[:], in_=C.identf[:])
    C.ps = [k.ps([128, 512], F32) for _ in range(8)]
    C.psi = 0
    return C


def nextps(C):
    p = C.ps[C.psi % 8]
    C.psi += 1
    return p


def bcast_rows(ap1d, n):
    return ap1d.partition_broadcast(128)


class NormBufs:
    def __init__(self, k, with_x=True):
        self.xt = [k.sb([128, D], F32) for _ in range(2)] if with_x else None
        self.junk = k.sb([128, D], BF16)
        self.ss = [k.sb([128, 1], F32) for _ in range(2)]
        self.rstd = [k.sb([128, 1], F32) for _ in range(2)]
        self.sd = [k.sb([128, 1], F32) for _ in range(2)]
        self.xn = [k.sb([128, D], BF16) for _ in range(2)]
        self.hts = [k.sb([128, NCH, 128], BF16) for _ in range(2)]


def rstd_from_ss(k, nb, b):
    k.op("dve", "tensor_scalar", out=nb.sd[b][:], in0=nb.ss[b][:], scalar1=1.0 / D, scalar2=EPS,
         op0=ALU.mult, op1=ALU.add)
    k.act(out=nb.sd[b][:], in_=nb.sd[b][:], func=AF.Sqrt)
    k.op("dve", "reciprocal", out=nb.rstd[b][:], in_=nb.sd[b][:])


def norm_tile(k, C, nb, xt, g_rep, out_tb, t, b):
    k.op("dve", "memset", ap=nb.ss[b][:], constant=0.0)
    k.act(out=nb.junk[:], in_=xt[:], func=AF.Square, accum_out=nb.ss[b][:])
    rstd_from_ss(k, nb, b)
    k.op("dve", "scalar_tensor_tensor", out=nb.xn[b][:], in0=xt[:], scalar=nb.rstd[b][:, 0:1],
         in1=g_rep[:], op0=ALU.mult, op1=ALU.mult)
    for half in range(2):
        ps = psbf(nextps(C))
        for c8 in range(8):
            c = half * 8 + c8
            k.tr(ps[:, c8 * 128:(c8 + 1) * 128], nb.xn[b][:, c * 128:(c + 1) * 128], C.identb[:], sig=(c8 == 7))
        src = ps.view(ps.t[:, :].rearrange("p (c n) -> p c n", c=8))
        if half == 0:
            k.op("act", "copy", out=nb.hts[b][:, 0:8, :], in_=src)
        else:
            k.op("dve", "tensor_copy", out=nb.hts[b][:, 8:16, :], in_=src)
    dst = out_tb.t.rearrange("(c p) n -> p c n", p=128)[:, :, t * 128:(t + 1) * 128]
    k.dma("sp", out_tb.view(dst), nb.hts[b][:])


def load_grep(k, gvec_ref):
    g_rep = k.sb([128, D], F32)
    k.dma("sp", g_rep[:], Ref(bcast_rows(gvec_ref.ap, D), gvec_ref.buf))
    return g_rep


def phase_norm(k, C, x_tb, gvec_ref, out_tb, ntok):
    with phase(k):
        g_rep = load_grep(k, gvec_ref)
        nb = NormBufs(k)
        for t in range(ntok // 128):
            b = t % 2
            k.dma("sp", nb.xt[b][:], x_tb[t * 128:(t + 1) * 128, :])
            norm_tile(k, C, nb, nb.xt[b], g_rep, out_tb, t, b)


def phase_final_norm(k, C, x_tb, gvec_ref, out_tb):
    with phase(k):
        g_rep = load_grep(k, gvec_ref)
        nb = NormBufs(k)
        ot = [k.sb([128, D], F32) for _ in range(2)]
        for t in range(NT):
            b = t % 2
            k.dma("sp", nb.xt[b][:], x_tb[t * 128:(t + 1) * 128, :])
            k.op("dve", "memset", ap=nb.ss[b][:], constant=0.0)
            k.act(out=nb.junk[:], in_=nb.xt[b][:], func=AF.Square, accum_out=nb.ss[b][:])
            rstd_from_ss(k, nb, b)
            k.op("dve", "scalar_tensor_tensor", out=ot[b][:], in0=nb.xt[b][:], scalar=nb.rstd[b][:, 0:1],
                 in1=g_rep[:], op0=ALU.mult, op1=ALU.mult)
            k.dma("sp", out_tb[t * 128:(t + 1) * 128, :], ot[b][:])


def load_hT(k, hT_tb, ntok):
    h = k.sb([128, NCH, ntok], BF16)
    v = hT_tb.t.rearrange("(c p) n -> p c n", p=128)
    for c0 in range(0, NCH, 4):
        k.dma("sp", h[:, c0:c0 + 4, :], hT_tb.view(v[:, c0:c0 + 4, :]))
    return h


class Stager:
    def __init__(self, k, nbuf=3, elems=2048, engines=("pool",)):
        self.k = k
        self.bufs = [k.sb([128, elems], F32) for _ in range(nbuf)]
        self.elems = elems
        self.i = 0
        self.engines = engines
        self.e = 0

    def load(self, dst, src, shape):
        k = self.k
        a, b = shape
        assert a * b <= self.elems
        st = self.bufs[self.i % len(self.bufs)]
        self.i += 1
        sv = st.view(st.t[:, 0:a * b].rearrange("p (a b) -> p a b", a=a))
        k.dma("sp", sv, src)
        eng = self.engines[self.e % len(self.engines)]
        self.e += 1
        if eng == "act":
            k.op("act", "copy", out=dst, in_=sv)
        else:
            k.op(eng, "tensor_copy", out=dst, in_=sv)


def load_w(k, wt, n, wref, stager):
    v = wref.ap.rearrange("(c p) n -> p c n", p=128)
    for n0 in range(0, n, 128):
        w = min(128, n - n0)
        stager.load(wt[:, :, n0:n0 + w], Ref(v[:, :, n0:n0 + w], wref.buf), (NCH, w))


class ProjCtx:
    def __init__(self, k, C, h_sb, ntok, nwt=3, wmax=128):
        self.k, self.C, self.h, self.ntok = k, C, h_sb, ntok
        self.wts = [k.sb([128, NCH, wmax], BF16) for _ in range(nwt)]
        self.i = 0
        self.stager = Stager(k, nbuf=2)

    def F(self, wref, evac, n=128):
        k = self.k
        wt = self.wts[self.i % len(self.wts)]
        self.i += 1
        load_w(k, wt, n, wref, self.stager)
        for tb in range(self.ntok // 512 if self.ntok >= 512 else 1):
            w = min(512, self.ntok)
            ps = nextps(self.C)
            for c in range(NCH):
                k.mm(ps[0:n, 0:w], lhsT=wt[:, c, 0:n], rhs=self.h[:, c, tb * 512:tb * 512 + w],
                     start=(c == 0), stop=(c == NCH - 1))
            evac(tb, ps)

    def T(self, wref, n, evac):
        k = self.k
        wt = self.wts[self.i % len(self.wts)]
        self.i += 1
        load_w(k, wt, n, wref, self.stager)
        for tt in range(self.ntok // 128):
            ps = nextps(self.C)
            for c in range(NCH):
                k.mm(ps[:, 0:n], lhsT=self.h[:, c, tt * 128:(tt + 1) * 128], rhs=wt[:, c, 0:n],
                     start=(c == 0), stop=(c == NCH - 1))
            evac(tt, ps)


def alt_copy(k, i, out, in_):
    if i % 2 == 0:
        k.op("act", "copy", out=out, in_=in_)
    else:
        k.op("dve", "tensor_copy", out=out, in_=in_)


def proj_F_to_dram(k, P, wref, dst_tb, row0, stg, cnt, n=128):
    st = stg[cnt[0] % len(stg)]
    cnt[0] += 1

    def evac(tb, ps):
        w = min(512, P.ntok)
        alt_copy(k, tb, st[0:n, tb * 512:tb * 512 + w], ps[0:n, 0:w])
    P.F(wref, evac, n)
    k.dma("sp", dst_tb[row0:row0 + n, :], st[0:n, 0:P.ntok])


def phase_mem_kv(k, C, N, li):
    with phase(k):
        h = load_hT(k, N.memhT, 256)
        P = ProjCtx(k, C, h, 256, nwt=3, wmax=512)
        stg = [k.sb([128, 512], BF16) for _ in range(2)]
        cnt = [0]
        W = N.w_mem_kv
        for j in range(4):
            proj_F_to_dram(k, P, Ref(W.t[li, :, j * 128:(j + 1) * 128], W.buf), N.memkT, j * 128, stg, cnt)
        st2 = [k.sb([128, 512], BF16) for _ in range(2)]

        def evac(tt, ps):
            s = st2[tt % 2]
            alt_copy(k, tt, s[:, :], ps[:, 0:512])
            k.dma("sp", N.memv[tt * 128:(tt + 1) * 128, :], s[:, :])
        P.T(Ref(W.t[li, :, 512:1024], W.buf), 512, evac)


def mem_attention(k, C, N):
    kT = k.sb([128, 4, 256], BF16)
    k.dma("sp", kT[:], N.memkT.view(N.memkT.t.rearrange("(h p) m -> p h m", p=128)))
    mv = k.sb([128, 2, 512], BF16)
    k.dma("sp", mv[:], N.memv.view(N.memv.t.rearrange("(t p) n -> p t n", p=128)))
    qm = [k.sb([128, 4, 512], BF16) for _ in range(2)]
    pt = [k.sb([128, 2, 512], BF16) for _ in range(2)]
    rec = [k.sb([128, 512], F32) for _ in range(2)]
    ym = [k.sb([128, 4, 512], BF16) for _ in range(2)]
    sc = 1.0 / np.sqrt(128.0)
    u = 0
    for tb in range(4):
        q = qm[tb % 2]
        k.dma("sp", q[:], N.qmT.view(N.qmT.t.rearrange("(h p) s -> p h s", p=128)[:, :, tb * 512:(tb + 1) * 512]))
        y = ym[tb % 2]
        for hm in range(4):
            p = pt[u % 2]
            r = rec[u % 2]
            u += 1
            for mt in range(2):
                ps = nextps(C)
                k.mm(ps[:, :], lhsT=kT[:, hm, mt * 128:(mt + 1) * 128], rhs=q[:, hm, :])
                k.act(out=p[:, mt, :], in_=ps[:, :], func=AF.Exp, scale=float(sc))
            pso = nextps(C)
            psd = nextps(C)
            for mt in range(2):
                k.mm(pso[:, :], lhsT=mv[:, mt, hm * 128:(hm + 1) * 128], rhs=p[:, mt, :], start=(mt == 0), stop=(mt == 1))
            for mt in range(2):
                k.mm(psd[:, :], lhsT=C.onesb[:], rhs=p[:, mt, :], start=(mt == 0), stop=(mt == 1))
            k.op("dve", "reciprocal", out=r[:], in_=psd[:, :])
            k.op("dve", "tensor_tensor", out=y[:, hm, :], in0=pso[:, :], in1=r[:], op=ALU.mult)
        dst = N.yT.t[1536:2048, :].rearrange("(h p) s -> p h s", p=128)[:, :, tb * 512:(tb + 1) * 512]
        k.dma("sp", N.yT.view(dst), y[:])


def alibi_slope(h):
    return float(2.0 ** (-8.0 * (h + 1.0) / 24.0))


def phase_proj_a(k, C, N, j):
    with phase(k):
        h = load_hT(k, N.hT, S)
        P = ProjCtx(k, C, h, S, nwt=3, wmax=512)
        stg = [k.sb([128, S], BF16) for _ in range(2)]
        cnt = [0]
        W = N.a_w_in
        for c in range(12):
            proj_F_to_dram(k, P, Ref(W.t[j, :, c * 128:(c + 1) * 128], W.buf), N.qT, c * 128, stg, cnt)
        for c in range(4):
            proj_F_to_dram(k, P, Ref(W.t[j, :, 2048 + c * 128:2048 + (c + 1) * 128], W.buf), N.qmT, c * 128, stg, cnt)
        for c in range(2):
            st = stg[cnt[0] % 2]
            cnt[0] += 1

            def evac(tb, ps, st=st):
                alt_copy(k, tb, st[:, tb * 512:(tb + 1) * 512], ps[:, :])
            P.F(Ref(W.t[j, :, 1536 + c * 128:1536 + (c + 1) * 128], W.buf), evac)
            for gg in range(2):
                g = 2 * c + gg
                for dup in range(2):
                    k.dma("sp", N.kT2[g * 128 + dup * 64:g * 128 + dup * 64 + 64, :], st[gg * 64:(gg + 1) * 64, :])
        st2 = [k.sb([128, 4, 128], BF16) for _ in range(2)]

        def evacv(tt, ps):
            s = st2[tt % 2]
            src = ps.view(ps.t[:, 0:256].rearrange("p (g d) -> p g d", g=4))
            k.op("act", "copy", out=s[:, :, 0:64], in_=src)
            k.op("dve", "tensor_copy", out=s[:, :, 64:128], in_=src)
            k.dma("sp", N.v2.view(N.v2.t[tt * 128:(tt + 1) * 128, :].rearrange("p (g d) -> p g d", g=4)), s[:])
        P.T(Ref(W.t[j, :, 1792:2048], W.buf), 256, evacv)


def phase_attn_a(k, C, N, j):
    with phase(k):
        dist = k.sb([128, 128], F32)
        k.op("pool", "iota", dist[:], pattern=[[1, 128]], base=0, channel_multiplier=-1,
             allow_small_or_imprecise_dtypes=True, W=[dist[:]])
        mbc = k.sb([128, 24, 128], F32)
        mbp = k.sb([128, 24, 128], F32)
        for h in range(24):
            sl = alibi_slope(h)
            k.op("dve", "tensor_scalar", out=mbc[:, h, :], in0=dist[:], scalar1=-sl, scalar2=None, op0=ALU.mult)
            k.op("dve", "tensor_scalar", out=mbp[:, h, :], in0=dist[:], scalar1=-sl, scalar2=-128.0 * sl,
                 op0=ALU.mult, op1=ALU.add)
        k.op("pool", "affine_select", out=mbc[:], in_=mbc[:], pattern=[[0, 24], [1, 128]],
             compare_op=ALU.is_ge, fill=NEG, base=0, channel_multiplier=-1)
        k.op("pool", "affine_select", out=mbp[:], in_=mbp[:], pattern=[[0, 24], [-1, 128]],
             compare_op=ALU.is_gt, fill=NEG, base=0, channel_multiplier=1)
        sk = k.sb([128, 24], F32)
        k.dma("sp", sk[:], Ref(bcast_rows(N.a_sinks.t[j, :], 24), N.a_sinks.buf))
        sinkexp = k.sb([128, 24], F32)
        k.act(out=sinkexp[:], in_=sk[:], func=AF.Exp)
        kTz = k.sb([128, 8, S], BF16)
        k.op("pool", "memset", ap=kTz[:], constant=0.0)
        for g in range(4):
            for hf in range(2):
                k.dma("sp", kTz[hf * 64:hf * 64 + 64, 2 * g + hf, :],
                      N.kT2[g * 128 + hf * 64:g * 128 + hf * 64 + 64, :])
        v2 = k.sb([128, NT, 512], BF16)
        vv = N.v2.t.rearrange("(t p) n -> p t n", p=128)
        for t0 in range(0, NT, 4):
            k.dma("sp", v2[:, t0:t0 + 4, :], N.v2.view(vv[:, t0:t0 + 4, :]))
        qs = [k.sb([128, 12, 512], BF16) for _ in range(2)]
        ys = [k.sb([128, 12, 512], BF16) for _ in range(2)]
        scb = [k.sb([128, 2, 384], F32) for _ in range(2)]
        ptb = [k.sb([128, 2, 384], BF16) for _ in range(2)]
        rcb = [k.sb([128, 3, 128], F32) for _ in range(2)]
        qv = N.qT.t.rearrange("(c p) s -> p c s", p=128)
        yv = N.yT.t[0:1536, :].rearrange("(c p) s -> p c s", p=128)
        u = 0
        for n4 in range(CFG.get("attn_n4", 4)):
            q = qs[n4 % 2]
            y = ys[n4 % 2]
            k.dma("sp", q[:], N.qT.view(qv[:, :, n4 * 512:(n4 + 1) * 512]))
            for nn in range(4):
                n = n4 * 4 + nn
                qc = slice(nn * 128, (nn + 1) * 128)
                kbs = [n] if n == 0 else [n - 1, n]
                for g in range(4):
                    for h3 in range(2):
                        sc_, pt_, rc_ = scb[u % 2], ptb[u % 2], rcb[u % 2]
                        u += 1
                        heads = [g * 6 + h3 * 3 + i for i in range(3)]
                        pss = []
                        for bi, kb in enumerate(kbs):
                            ps = nextps(C)
                            pss.append(ps)
                            for i, hh in enumerate(heads):
                                c, hf = hh // 2, hh % 2
                                k.mm(ps[:, i * 128:(i + 1) * 128], lhsT=kTz[:, 2 * g + hf, kb * 128:(kb + 1) * 128],
                                     rhs=q[:, c, qc], start=True, stop=True, sig=(i == 2))
                        for bi, kb in enumerate(kbs):
                            mb = mbc if kb == n else mbp
                            k.op("dve", "scalar_tensor_tensor",
                                 out=sc_.view(sc_.t[:, bi, :].rearrange("p (h q) -> p h q", h=3)),
                                 in0=pss[bi].view(pss[bi].t[:, 0:384].rearrange("p (h q) -> p h q", h=3)),
                                 scalar=0.125, in1=mb[:, heads[0]:heads[0] + 3, :], op0=ALU.mult, op1=ALU.add)
                            k.act(out=pt_[:, bi, :], in_=sc_[:, bi, :], func=AF.Exp)
                        if CFG.get("attn_stage", 3) < 2:
                            continue
                        pso = nextps(C)
                        psd = nextps(C)
                        nk = len(kbs)
                        for bi, kb in enumerate(kbs):
                            k.mm(pso[:, 0:384], lhsT=v2[:, kb, g * 128:(g + 1) * 128], rhs=pt_[:, bi, :],
                                 start=(bi == 0), stop=(bi == nk - 1))
                        for bi, kb in enumerate(kbs):
                            k.mm(psd[:, 0:384], lhsT=C.onesb[:], rhs=pt_[:, bi, :],
                                 start=(bi == 0), stop=(bi == nk - 1))
                        if CFG.get("attn_stage", 3) < 3:
                            continue
                        for i, hh in enumerate(heads):
                            k.op("dve", "tensor_scalar", out=rc_[:, i, :], in0=psd[:, i * 128:(i + 1) * 128],
                                 scalar1=sinkexp[:, hh:hh + 1], scalar2=None, op0=ALU.add)
                        k.op("dve", "reciprocal", out=rc_[:], in_=rc_[:])
                        for i, hh in enumerate(heads):
                            c, hf = hh // 2, hh % 2
                            rows = slice(hf * 64, hf * 64 + 64)
                            k.op("dve", "tensor_tensor", out=y[rows, c, qc], in0=pso[rows, i * 128:(i + 1) * 128],
                                 in1=rc_[rows, i, :], op=ALU.mult)
            k.dma("sp", N.yT.view(yv[:, :, n4 * 512:(n4 + 1) * 512]), y[:])
        if CFG.get("memattn", True):
            mem_attention(k, C, N)


def phase_outproj(k, C, N, li, x_in, x_out):
    with phase(k):
        g_rep = load_grep(k, Ref(N.ffn_norm.t[li, :], N.ffn_norm.buf))
        nb = NormBufs(k)
        wo = k.sb([128, NCH, D], BF16)
        wv = N.w_out.t[li].rearrange("(c p) n -> p c n", p=128)
        stg = Stager(k, nbuf=3, elems=2048, engines=("pool", "act", "pool", "dve"))
        for c0 in range(NCH):
            stg.load(wo[:, c0:c0 + 1, :], Ref(wv[:, c0:c0 + 1, :], N.w_out.buf), (1, D))
        yts = [k.sb([128, NCH, 128], BF16) for _ in range(2)]
        yv = N.yT.t.rearrange("(c p) s -> p c s", p=128)
        for t in range(NT):
            b = t % 2
            yt = yts[b]
            xt = nb.xt[b]
            k.dma("sp", yt[:], N.yT.view(yv[:, :, t * 128:(t + 1) * 128]))
            k.dma("sp", xt[:], x_in[t * 128:(t + 1) * 128, :])
            for nbk in range(4):
                ps = nextps(C)
                for c in range(NCH):
                    k.mm(ps[:, :], lhsT=yt[:, c, :], rhs=wo[:, c, nbk * 512:(nbk + 1) * 512],
                         start=(c == 0), stop=(c == NCH - 1))
                k.op("dve", "tensor_tensor", out=xt[:, nbk * 512:(nbk + 1) * 512],
                     in0=xt[:, nbk * 512:(nbk + 1) * 512], in1=ps[:, :], op=ALU.add)
            k.dma("sp", x_out[t * 128:(t + 1) * 128, :], xt[:])
            norm_tile(k, C, nb, xt, g_rep, N.hT, t, b)


def phase_ffn_up(k, C, N, li):
    with phase(k):
        h = load_hT(k, N.hT, S)
        cwj = k.sb([88, 3, 128], F32)
        k.dma("sp", cwj[:], Ref(N.ffn_conv.t[li].rearrange("k (j p) -> j k p", p=128), N.ffn_conv.buf))
        cw = k.sb([128, 3, 88], F32)
        for kk in range(3):
            ps = nextps(C)
            k.op("pe", "transpose", out=ps[:, 0:88], in_=cwj[:, kk, :], identity=C.identf[0:88, 0:88])
            k.op("dve", "tensor_copy", out=cw[:, kk, :], in_=ps[:, 0:88])
        wg = [k.sb([128, NCH, 128], BF16) for _ in range(3)]
        wvv = [k.sb([128, NCH, 128], BF16) for _ in range(3)]
        ug = [k.sb([128, 2 + S], F32) for _ in range(2)]
        uv = [k.sb([128, 2 + S], F32) for _ in range(2)]
        for t_ in ug + uv:
            k.op("pool", "memset", ap=t_[:, 0:2], constant=0.0)
        cg = [k.sb([128, 1024], F32) for _ in range(2)]
        cv = [k.sb([128, 1024], F32) for _ in range(2)]
        sg = [k.sb([128, 1024], F32) for _ in range(2)]
        ao = [k.sb([128, 1024], BF16) for _ in range(2)]
        stg = Stager(k, nbuf=4)
        W = N.w_ffn_up
        u = 0
        for j in range(NFT):
            a, b_ = wg[j % 3], wvv[j % 3]
            load_w(k, a, 128, Ref(W.t[li, :, j * 128:(j + 1) * 128], W.buf), stg)
            load_w(k, b_, 128, Ref(W.t[li, :, DFF + j * 128:DFF + (j + 1) * 128], W.buf), stg)
            ugj, uvj = ug[j % 2], uv[j % 2]
            for half in range(2):
                o = half * 1024
                for which, wt, ub in ((0, a, ugj), (1, b_, uvj)):
                    for tb in range(2):
                        ps = nextps(C)
                        for c in range(NCH):
                            k.mm(ps[:, :], lhsT=wt[:, c, :], rhs=h[:, c, o + tb * 512:o + (tb + 1) * 512],
                                 start=(c == 0), stop=(c == NCH - 1))
                        k.op("act", "copy", out=ub[:, 2 + o + tb * 512:2 + o + (tb + 1) * 512], in_=ps[:, :])
                cgu, cvu, sgu, aou = cg[u % 2], cv[u % 2], sg[u % 2], ao[u % 2]
                u += 1
                for eng, ub, co, jj in (("dve", ugj, cgu, j), ("dve", uvj, cvu, NFT + j)):
                    k.op(eng, "tensor_scalar", out=co[:], in0=ub[:, 2 + o:2 + o + 1024],
                         scalar1=cw[:, 2, jj:jj + 1], scalar2=None, op0=ALU.mult)
                    k.op(eng, "scalar_tensor_tensor", out=co[:], in0=ub[:, 1 + o:1 + o + 1024],
                         scalar=cw[:, 1, jj:jj + 1], in1=co[:], op0=ALU.mult, op1=ALU.add)
                    k.op(eng, "scalar_tensor_tensor", out=co[:], in0=ub[:, o:o + 1024],
                         scalar=cw[:, 0, jj:jj + 1], in1=co[:], op0=ALU.mult, op1=ALU.add)
                k.act(out=sgu[:], in_=cgu[:], func=AF.Silu)
                k.op("pool", "tensor_tensor", out=aou[:], in0=sgu[:], in1=cvu[:], op=ALU.mult)
                k.dma("sp", N.aT[j * 128:(j + 1) * 128, o:o + 1024], aou[:])


def phase_ffn_down(k, C, N, li, x_tb):
    with phase(k):
        wd = k.sb([128, NFT, 1024], BF16)
        ats = [k.sb([128, NFT, 256], BF16) for _ in range(2)]
        xts = [k.sb([128, 2, 1024], F32) for _ in range(2)]
        stg = Stager(k, nbuf=3, elems=2048, engines=("pool", "act", "dve"))
        av = N.aT.t.rearrange("(c p) s -> p c s", p=128)
        W = N.w_ffn_down
        u = 0
        for nh in range(2):
            wv = W.t[li, :, nh * 1024:(nh + 1) * 1024].rearrange("(c p) n -> p c n", p=128)
            for c0 in range(0, NFT, 2):
                stg.load(wd[:, c0:c0 + 2, :], Ref(wv[:, c0:c0 + 2, :], W.buf), (2, 1024))
            for t2 in range(NT // 2):
                at, xt = ats[u % 2], xts[u % 2]
                u += 1
                for c0 in range(0, NFT, 11):
                    k.dma("sp", at[:, c0:c0 + 11, :], N.aT.view(av[:, c0:c0 + 11, t2 * 256:(t2 + 1) * 256]))
                xv = x_tb.t[t2 * 256:(t2 + 1) * 256, nh * 1024:(nh + 1) * 1024].rearrange("(t p) n -> p t n", p=128)
                k.dma("sp", xt[:], x_tb.view(xv))
                for ts in range(2):
                    for nbk in range(2):
                        ps = nextps(C)
                        for c in range(NFT):
                            k.mm(ps[:, :], lhsT=at[:, c, ts * 128:(ts + 1) * 128],
                                 rhs=wd[:, c, nbk * 512:(nbk + 1) * 512], start=(c == 0), stop=(c == NFT - 1))
                        k.op("dve", "tensor_tensor", out=xt[:, ts, nbk * 512:(nbk + 1) * 512],
                             in0=xt[:, ts, nbk * 512:(nbk + 1) * 512], in1=ps[:, :], op=ALU.add)
                k.dma("sp", x_tb.view(xv), xt[:])


B_SCRATCH = [
    ("brT", [1536, S], BF16), ("bkT", [1536, S], BF16), ("bkkT", [1536, S], BF16), ("bbT", [1536, S], BF16),
    ("bvT", [1536, S], BF16), ("blwT", [1536, S], F32), ("bbonT", [1536, S], BF16), ("bgT", [1536, S], BF16),
]
DECAY_C = 0.6065306597126334


def colvec(k, ref1d, n=1536):
    nc_ = n // 128
    rows = k.sb([nc_, 128], F32)
    k.dma("sp", rows[:], Ref(ref1d.ap.rearrange("(c p) -> c p", p=128), ref1d.buf))
    t = k.sb([128, nc_], F32)
    ps = nextps(CREF[0])
    k.op("pe", "transpose", out=ps[:, 0:nc_], in_=rows[:], identity=CREF[0].identf[0:nc_, 0:nc_])
    k.op("dve", "tensor_copy", out=t[:], in_=ps[:, 0:nc_])
    return t


CREF = [None]


def phase_proj_b(k, C, N, j):
    CREF[0] = C
    with phase(k):
        h = load_hT(k, N.hT, S)
        P = ProjCtx(k, C, h, S, nwt=3, wmax=128)
        W = N.b_w_in
        V = lambda name: Ref(getattr(N, name).t[j, :], getattr(N, name).buf)
        w0c, a0c, kkc, kac, gngc = colvec(k, V("b_w0")), colvec(k, V("b_a0")), colvec(k, V("b_k_k")), \
            colvec(k, V("b_k_a")), None
        rkc = colvec(k, Ref(N.b_r_k.t[j].rearrange("h d -> (h d)"), N.b_r_k.buf))
        omka = k.sb([128, 12], F32)
        k.op("dve", "tensor_scalar", out=omka[:], in0=kac[:], scalar1=-1.0, scalar2=1.0, op0=ALU.mult, op1=ALU.add)
        blk = k.sb([128, 128], BF16)
        k.op("pool", "memset", ap=blk[:], constant=0.0)
        k.op("pool", "memset", ap=blk[0:64, 0:64], constant=1.0)
        k.op("pool", "memset", ap=blk[64:128, 64:128], constant=1.0)
        wdec = k.sb([96, 1536], BF16)
        wicl = k.sb([96, 1536], BF16)
        wgt = k.sb([128, 2, 1536], BF16)
        aT = k.sb([128, S], F32)
        t1 = k.sb([128, S], F32)
        rn = k.sb([128, S], F32)
        t2 = rn
        k.dma("sp", aT[0:96, 0:1536], Ref(N.b_w_decay_up.t[j], N.b_w_decay_up.buf))
        k.op("pool", "tensor_copy", out=wdec[:], in_=aT[0:96, 0:1536])
        k.dma("sp", t1[0:96, 0:1536], Ref(N.b_w_iclr_up.t[j], N.b_w_iclr_up.buf))
        k.op("pool", "tensor_copy", out=wicl[:], in_=t1[0:96, 0:1536])
        for kc in range(2):
            k.dma("sp", rn[:, 0:1536], Ref(N.b_w_gate_up.t[j, kc * 128:(kc + 1) * 128, :], N.b_w_gate_up.buf))
            k.op("pool", "tensor_copy", out=wgt[:, kc, :], in_=rn[:, 0:1536])
        ub = [k.sb([128, 1 + S], F32) for _ in range(1)]
        for t_ in ub:
            k.op("pool", "memset", ap=t_[:, 0:1], constant=0.0)
        mus = [k.sb([128, 2], F32) for _ in range(2)]
        mx = [k.sb([128, S], F32) for _ in range(2)]
        cnt = [0]

        def mixed(c0, n):
            i = cnt[0]
            cnt[0] += 1
            u, mu, m = ub[0], mus[i % 2], mx[i % 2]

            def evac(tb, ps):
                alt_copy(k, tb, u[0:n, 1 + tb * 512:1 + (tb + 1) * 512], ps[0:n, :])
            P.F(Ref(W.t[j, :, c0:c0 + n], W.buf), evac, n)
            k.dma("sp", mu[0:n, 0:1], Ref(N.b_mu.t[j, c0:c0 + n].rearrange("(p o) -> p o", o=1), N.b_mu.buf))
            k.op("dve", "tensor_scalar", out=mu[0:n, 1:2], in0=mu[0:n, 0:1], scalar1=-1.0, scalar2=1.0,
                 op0=ALU.mult, op1=ALU.add)
            k.op("dve", "tensor_scalar", out=m[0:n, :], in0=u[0:n, 0:S], scalar1=mu[0:n, 0:1], scalar2=None,
                 op0=ALU.mult)
            k.op("dve", "scalar_tensor_tensor", out=m[0:n, :], in0=u[0:n, 1:1 + S], scalar=mu[0:n, 1:2],
                 in1=m[0:n, :], op0=ALU.mult, op1=ALU.add)
            return m
        twT = k.sb([96, S], BF16)
        adT = k.sb([96, S], BF16)
        sgT = k.sb([128, 2, S], BF16)
        m = mixed(4608, 96)
        k.act(out=twT[:], in_=m[0:96, :], func=AF.Tanh)
        m = mixed(4704, 96)
        k.op("dve", "tensor_copy", out=adT[:], in_=m[0:96, :])
        for i in range(2):
            m = mixed(4800 + i * 128, 128)
            k.act(out=sgT[:, i, :], in_=m[:], func=AF.Sigmoid)
        sq = k.sb([128, S], BF16)
        o16 = [k.sb([128, S], BF16) for _ in range(3)]
        o32 = [k.sb([128, S], F32) for _ in range(1)]
        vb = k.sb([128, S], BF16)
        rb = k.sb([128, S], BF16)
        oc = [0]

        def out16():
            oc[0] += 1
            return o16[oc[0] % 3]
        for c in range(12):
            cs = slice(c * 128, (c + 1) * 128)
            lw = o32[0]
            go = out16()
            for tb in range(4):
                ts_ = slice(tb * 512, (tb + 1) * 512)
                ps = nextps(C)
                k.mm(ps[:, :], lhsT=wicl[:, cs], rhs=adT[:, ts_])
                k.act(out=aT[:, ts_], in_=ps[:, :], func=AF.Sigmoid, bias=a0c[:, c:c + 1])
                ps = nextps(C)
                k.mm(ps[:, :], lhsT=wdec[:, cs], rhs=twT[:, ts_])
                k.act(out=lw[:, ts_], in_=ps[:, :], func=AF.Sigmoid, bias=w0c[:, c:c + 1])
                ps = nextps(C)
                for kc in range(2):
                    k.mm(ps[:, :], lhsT=wgt[:, kc, cs], rhs=sgT[:, kc, ts_], start=(kc == 0), stop=(kc == 1))
                k.op("dve", "tensor_copy", out=go[:, ts_], in_=ps[:, :])
            k.op("dve", "tensor_scalar", out=lw[:], in0=lw[:], scalar1=-DECAY_C, scalar2=None, op0=ALU.mult)
            k.dma("sp", N.blwT[cs, :], lw[:])
            k.dma("sp", N.bgT[cs, :], go[:])
            m = mixed(3072 + c * 128, 128)
            k.op("pool", "tensor_copy", out=vb[:], in_=m[:])
            k.dma("sp", N.bvT[cs, :], vb[:])
            m = mixed(c * 128, 128)
            k.op("pool", "tensor_copy", out=rb[:], in_=m[:])
            k.dma("sp", N.brT[cs, :], rb[:])
            m = mixed(1536 + c * 128, 128)
            k.op("dve", "tensor_scalar", out=t1[:], in0=m[:], scalar1=kkc[:, c:c + 1], scalar2=None, op0=ALU.mult)
            k.act(out=sq[:], in_=t1[:], func=AF.Square)
            for tb in range(4):
                ts_ = slice(tb * 512, (tb + 1) * 512)
                ps = nextps(C)
                k.mm(ps[:, :], lhsT=blk[:], rhs=sq[:, ts_])
                k.op("dve", "tensor_scalar", out=rn[:, ts_], in0=ps[:, :], scalar1=1e-6, scalar2=None, op0=ALU.add)
            k.act(out=rn[:], in_=rn[:], func=AF.Sqrt)
            k.op("dve", "reciprocal", out=rn[:], in_=rn[:])
            kko = out16()
            k.op("dve", "tensor_tensor", out=t1[:], in0=t1[:], in1=rn[:], op=ALU.mult)
            k.op("pool", "tensor_copy", out=kko[:], in_=t1[:])
            k.dma("sp", N.bkkT[cs, :], kko[:])
            bo = out16()
            k.op("dve", "tensor_tensor", out=bo[:], in0=t1[:], in1=aT[:], op=ALU.mult)
            k.dma("sp", N.bbT[cs, :], bo[:])
            k.op("dve", "tensor_scalar", out=t2[:], in0=aT[:], scalar1=kac[:, c:c + 1], scalar2=omka[:, c:c + 1],
                 op0=ALU.mult, op1=ALU.add)
            k.op("dve", "tensor_tensor", out=t2[:], in0=t2[:], in1=m[:], op=ALU.mult)
            ko = out16()
            k.op("pool", "tensor_copy", out=ko[:], in_=t2[:])
            k.dma("sp", N.bkT[cs, :], ko[:])
            k.op("dve", "scalar_tensor_tensor", out=sq[:], in0=t2[:], scalar=rkc[:, c:c + 1], in1=rb[:],
                 op0=ALU.mult, op1=ALU.mult)
            bon = out16()
            for tb in range(4):
                ts_ = slice(tb * 512, (tb + 1) * 512)
                ps = nextps(C)
                k.mm(ps[:, :], lhsT=blk[:], rhs=sq[:, ts_])
                k.op("dve", "tensor_tensor", out=bon[:, ts_], in0=ps[:, :], in1=vb[:, ts_], op=ALU.mult)
            k.dma("sp", N.bbonT[cs, :], bon[:])
        stg = o16[0:2]
        cn2 = [0]
        for c in range(4):
            proj_F_to_dram(k, P, Ref(W.t[j, :, 5056 + c * 128:5056 + (c + 1) * 128], W.buf), N.qmT, c * 128, stg, cn2)


class RwTmp:
    def __init__(self, k):
        f = lambda dt, n=128: k.sb([128, n], dt)
        self.KiP = f(BF16)
        self.PiP = f(BF16)
        self.AbT = f(F32)
        self.Ab = f(F32)
        self.AkT = f(BF16)
        self.ArT = f(BF16, 256)
        self.PTb = f(BF16)
        self.Zb = f(BF16, 64)
        self.Un = f(BF16, 64)
        self.y = f(F32, 64)
        self.junk = f(F32, 64)
        self.s1 = k.sb([128, 1], F32)
        self.s2 = k.sb([128, 1], F32)
        self.mean = k.sb([128, 1], F32)
        self.var = k.sb([128, 1], F32)
        self.rs = k.sb([128, 1], F32)
        self.tmpH = f(F32, 64)
        self.ws = TriWS(k)


class RwPair:
    def __init__(self, k):
        f = lambda dt, n=128: k.sb([128, n], dt)
        self.g = f(F32)
        self.gx = f(F32)
        self.Ei, self.En, self.Ex, self.Ed = f(F32), f(F32), f(F32), f(F32)
        self.Rd, self.Ki, self.Pi, self.KKd, self.Kdc, self.Pdc = (f(BF16) for _ in range(6))
        self.Kdt, self.Pdt, self.Vt = f(BF16), f(BF16), f(BF16)
        self.yn = f(BF16)
        self.yf = f(F32)
        self.yo = f(BF16)


def phase_scan_b(k, C, N, j):
    CREF[0] = C
    with phase(k):
        strictT = C.mask(-1, 1, ALU.is_gt)
        inclT = C.mask(-1, 1, ALU.is_ge)
        msk2i = k.sb([128, 2, 128], F32)
        for i in range(2):
            k.op("dve", "tensor_copy", out=msk2i[:, i, :], in_=inclT[:])
        hm = k.sb([128, 2], F32)
        k.op("pool", "memset", ap=hm[:], constant=0.0)
        k.op("pool", "memset", ap=hm[0:64, 0:1], constant=1.0)
        k.op("pool", "memset", ap=hm[64:128, 1:2], constant=1.0)
        V = lambda name: Ref(getattr(N, name).t[j, :], getattr(N, name).buf)
        gng, gnb = colvec(k, V("b_gn_g")), colvec(k, V("b_gn_b"))
        names = ("brT", "bkT", "bkkT", "bbT", "bvT", "bbonT", "bgT")
        inb = [{nm: k.sb([128, S], BF16) for nm in names} for _ in range(2)]
        lwb = [k.sb([128, S], F32) for _ in range(2)]
        prs = [RwPair(k) for _ in range(2)]
        tms = [RwTmp(k) for _ in range(2)]
        Hf = [k.sb([128, 64], F32) for _ in range(2)]
        Hb = [k.sb([128, 64], BF16) for _ in range(2)]
        un = 0
        pu = 0
        for c in range(12):
            cs = slice(c * 128, (c + 1) * 128)
            I = inb[c % 2]
            lw = lwb[c % 2]
            for nm in names:
                k.dma("sp", I[nm][:], getattr(N, nm)[cs, :])
            k.dma("sp", lw[:], N.blwT[cs, :])
            for i in range(2):
                k.op("pool", "memset", ap=Hf[i][:], constant=0.0)
                k.op("pool", "memset", ap=Hb[i][:], constant=0.0)
            for n in range(NT):
                tc = slice(n * 128, (n + 1) * 128)
                Pp = prs[pu % 2]
                pu += 1
                k.op("dve", "tensor_tensor_scan", out=Pp.g[:], data0=C.onesf[:], data1=lw[:, tc], initial=0.0,
                     op0=ALU.mult, op1=ALU.add)
                k.op("dve", "tensor_tensor", out=Pp.gx[:], in0=Pp.g[:], in1=lw[:, tc], op=ALU.subtract)
                k.act(out=Pp.Ei[:], in_=Pp.g[:], func=AF.Exp)
                k.act(out=Pp.En[:], in_=Pp.g[:], func=AF.Exp, scale=-1.0)
                k.act(out=Pp.Ex[:], in_=Pp.gx[:], func=AF.Exp)
                k.act(out=Pp.Ed[:], in_=Pp.g[:], func=AF.Exp, scale=-1.0, bias=Pp.g[:, 127:128])
                k.op("dve", "tensor_tensor", out=Pp.Rd[:], in0=I["brT"][:, tc], in1=Pp.Ei[:], op=ALU.mult)
                k.op("dve", "tensor_tensor", out=Pp.Ki[:], in0=I["bkT"][:, tc], in1=Pp.En[:], op=ALU.mult)
                k.op("dve", "tensor_tensor", out=Pp.Pi[:], in0=I["bbT"][:, tc], in1=Pp.En[:], op=ALU.mult)
                k.op("dve", "tensor_tensor", out=Pp.KKd[:], in0=I["bkkT"][:, tc], in1=Pp.Ex[:], op=ALU.mult)
                k.op("pool", "tensor_tensor", out=Pp.Kdc[:], in0=I["bkT"][:, tc], in1=Pp.Ed[:], op=ALU.mult)
                k.op("pool", "tensor_tensor", out=Pp.Pdc[:], in0=I["bbT"][:, tc], in1=Pp.Ed[:], op=ALU.mult)
                pst = psbf(nextps(C))
                k.tr(pst[:, 0:128], Pp.Kdc[:], C.identb[:], sig=False)
                k.tr(pst[:, 128:256], Pp.Pdc[:], C.identb[:], sig=False)
                k.tr(pst[:, 256:384], I["bvT"][:, tc], C.identb[:])
                k.op("act", "copy", out=Pp.Kdt[:], in_=pst[:, 0:128])
                k.op("act", "copy", out=Pp.Pdt[:], in_=pst[:, 128:256])
                k.op("act", "copy", out=Pp.Vt[:], in_=pst[:, 256:384])
                for i in range(2):
                    T = tms[un % 2]
                    un += 1
                    vs = slice(i * 64, (i + 1) * 64)
                    k.op("dve", "tensor_scalar", out=T.KiP[:], in0=Pp.Ki[:], scalar1=hm[:, i:i + 1], scalar2=None,
                         op0=ALU.mult)
                    k.op("dve", "tensor_scalar", out=T.PiP[:], in0=Pp.Pi[:], scalar1=hm[:, i:i + 1], scalar2=None,
                         op0=ALU.mult)
                    psA = nextps(C)
                    k.mm(psA[:, 0:128], lhsT=T.PiP[:], rhs=Pp.KKd[:], sig=False)
                    k.mm(psA[:, 128:256], lhsT=T.KiP[:], rhs=Pp.KKd[:], sig=False)
                    k.mm(psA[:, 256:384], lhsT=T.KiP[:], rhs=Pp.Rd[:], sig=False)
                    k.mm(psA[:, 384:512], lhsT=T.PiP[:], rhs=Pp.Rd[:])
                    k.op("dve", "tensor_tensor", out=T.AbT[:], in0=psA[:, 0:128], in1=strictT[:], op=ALU.mult)
                    k.op("dve", "tensor_tensor", out=T.AkT[:], in0=psA[:, 128:256], in1=strictT[:], op=ALU.mult)
                    k.op("dve", "tensor_tensor", out=T.ArT.view(T.ArT.t[:, :].rearrange("p (a b) -> p a b", a=2)),
                         in0=psA.view(psA.t[:, 256:512].rearrange("p (a b) -> p a b", a=2)), in1=msk2i[:],
                         op=ALU.mult)
                    psl = nextps(C)
                    k.op("pe", "transpose", out=psl[:, 0:128], in_=T.AbT[:], identity=C.identf[:])
                    k.op("act", "copy", out=T.Ab[:], in_=psl[:, 0:128])
                    PT = tri_inv_T(k, C, T.Ab, T.AbT, T.ws)
                    k.op("act", "copy", out=T.PTb[:], in_=PT[:])
                    psz = nextps(C)
                    k.mm(psz[:, 0:64], lhsT=Pp.KKd[:], rhs=Hb[i][:], start=True, stop=False)
                    k.mm(psz[:, 0:64], lhsT=T.AkT[:], rhs=Pp.Vt[:, vs], start=False, stop=True)
                    k.op("act", "copy", out=T.Zb[:], in_=psz[:, 0:64])
                    psu = nextps(C)
                    k.mm(psu[:, 0:64], lhsT=T.PTb[:], rhs=T.Zb[:])
                    k.op("dve", "tensor_scalar", out=T.Un[:], in0=psu[:, 0:64], scalar1=-1.0, scalar2=None,
                         op0=ALU.mult)
                    psy = nextps(C)
                    k.mm(psy[:, 0:64], lhsT=Pp.Rd[:], rhs=Hb[i][:], start=True, stop=False)
                    k.mm(psy[:, 0:64], lhsT=T.ArT[:, 0:128], rhs=Pp.Vt[:, vs], start=False, stop=False)
                    k.mm(psy[:, 0:64], lhsT=T.ArT[:, 128:256], rhs=T.Un[:], start=False, stop=True)
                    psh = nextps(C)
                    k.mm(psh[:, 0:64], lhsT=Pp.Kdt[:], rhs=Pp.Vt[:, vs], start=True, stop=False)
                    k.mm(psh[:, 0:64], lhsT=Pp.Pdt[:], rhs=T.Un[:], start=False, stop=True)
                    k.op("dve", "tensor_scalar", out=T.tmpH[:], in0=Hf[i][:], scalar1=Pp.Ei[:, 127:128], scalar2=None,
                         op0=ALU.mult)
                    k.op("dve", "scalar_tensor_tensor", out=Hf[i][:], in0=psh[:, 0:64], scalar=hm[:, i:i + 1],
                         in1=T.tmpH[:], op0=ALU.mult, op1=ALU.add)
                    k.op("act", "copy", out=Hb[i][:], in_=Hf[i][:])
                    k.op("pool", "memset", ap=T.s1[:], constant=0.0)
                    k.op("pool", "memset", ap=T.s2[:], constant=0.0)
                    k.act(out=T.y[:], in_=psy[:, 0:64], func=AF.Identity, accum_out=T.s1[:])
                    k.act(out=T.junk[:], in_=T.y[:], func=AF.Square, accum_out=T.s2[:])
                    k.op("dve", "tensor_scalar", out=T.mean[:], in0=T.s1[:], scalar1=1.0 / 64.0, scalar2=None,
                         op0=ALU.mult)
                    k.op("dve", "tensor_tensor", out=T.var[:], in0=T.mean[:], in1=T.mean[:], op=ALU.mult)
                    k.op("dve", "scalar_tensor_tensor", out=T.var[:], in0=T.s2[:], scalar=1.0 / 64.0, in1=T.var[:],
                         op0=ALU.mult, op1=ALU.subtract)
                    k.op("dve", "tensor_scalar", out=T.var[:], in0=T.var[:], scalar1=64e-5, scalar2=None, op0=ALU.add)
                    k.act(out=T.var[:], in_=T.var[:], func=AF.Sqrt)
                    k.op("dve", "reciprocal", out=T.rs[:], in_=T.var[:])
                    k.op("dve", "tensor_scalar", out=Pp.yn[:, vs], in0=T.y[:], scalar1=T.mean[:, 0:1],
                         scalar2=T.rs[:, 0:1], op0=ALU.subtract, op1=ALU.mult)
                pst = psbf(nextps(C))
                k.tr(pst[:, 0:128], Pp.yn[:], C.identb[:])
                k.op("dve", "tensor_scalar", out=Pp.yf[:], in0=pst[:, 0:128], scalar1=gng[:, c:c + 1],
                     scalar2=gnb[:, c:c + 1], op0=ALU.mult, op1=ALU.add)
                k.op("pool", "tensor_tensor", out=Pp.yf[:], in0=Pp.yf[:], in1=I["bbonT"][:, tc], op=ALU.add)
                k.op("pool", "tensor_tensor", out=Pp.yo[:], in0=Pp.yf[:], in1=I["bgT"][:, tc], op=ALU.mult)
                k.dma("sp", N.yT[cs, tc], Pp.yo[:])
        mem_attention(k, C, N)


def phase_mixer_b(k, C, N, j):
    phase_proj_b(k, C, N, j)
    phase_scan_b(k, C, N, j)


class TriWS:
    def __init__(self, k):
        self.x = [k.sb([128, 128], F32) for _ in range(2)]
        self.xt = [k.sb([128, 128], F32) for _ in range(2)]
        self.pt = [k.sb([128, 128], F32) for _ in range(2)]


def tri_inv_T(k, C, L, LT, ws):
    X, XT = L, LT
    PT = ws.pt[0]
    k.op("dve", "tensor_tensor", out=PT[:], in0=C.identf[:], in1=LT[:], op=ALU.subtract)
    for lvl in range(6):
        ps = nextps(C)
        k.mm(ps[:, 0:128], lhsT=XT[:], rhs=X[:])
        if lvl < 5:
            k.mm(ps[:, 128:256], lhsT=X[:], rhs=XT[:])
        X2 = ws.x[lvl % 2]
        k.op("act", "copy", out=X2[:], in_=ps[:, 0:128])
        X2T = ws.xt[lvl % 2]
        if lvl < 5:
            k.op("act", "copy", out=X2T[:], in_=ps[:, 128:256])
        ps3 = nextps(C)
        k.mm(ps3[:, 0:128], lhsT=X2[:], rhs=PT[:])
        PTn = ws.pt[(lvl + 1) % 2]
        k.op("dve", "tensor_tensor", out=PTn[:], in0=PT[:], in1=ps3[:, 0:128], op=ALU.add)
        X, XT, PT = X2, X2T, PTn
    return PT


EXTRA_SCRATCH = [
    ("gq", [768, S], BF16), ("gk", [768, S], BF16), ("gv", [1536, S], BF16), ("gz", [S, 1536], BF16),
    ("gbg", [S, 24], F32),
]


def phase_proj_c(k, C, N, j):
    with phase(k):
        h = load_hT(k, N.hT, S)
        P = ProjCtx(k, C, h, S, nwt=3, wmax=512)
        W = N.c_w_in
        cwj = k.sb([24, 4, 128], F32)
        k.dma("sp", cwj[:], Ref(N.c_conv.t[j].rearrange("k (t p) -> t k p", p=128), N.c_conv.buf))
        cw = k.sb([128, 4, 24], F32)
        for kk in range(4):
            ps = nextps(C)
            k.op("pe", "transpose", out=ps[:, 0:24], in_=cwj[:, kk, :], identity=C.identf[0:24, 0:24])
            k.op("dve", "tensor_copy", out=cw[:, kk, :], in_=ps[:, 0:24])
        ub = [k.sb([128, 3 + S], F32) for _ in range(2)]
        for t_ in ub:
            k.op("pool", "memset", ap=t_[:, 0:3], constant=0.0)
        cv = [k.sb([128, S], F32) for _ in range(2)]
        sq = k.sb([128, S], BF16)
        rn = k.sb([128, S], F32)
        ob = [k.sb([128, S], BF16) for _ in range(2)]
        for t in range(24):
            u = ub[t % 2]

            def evac(tb, ps, u=u):
                alt_copy(k, tb, u[:, 3 + tb * 512:3 + (tb + 1) * 512], ps[:, :])
            P.F(Ref(W.t[j, :, t * 128:(t + 1) * 128], W.buf), evac)
            c = cv[t % 2]
            k.op("dve", "tensor_scalar", out=c[:], in0=u[:, 3:3 + S], scalar1=cw[:, 3, t:t + 1], scalar2=None,
                 op0=ALU.mult)
            for kk in range(3):
                k.op("dve", "scalar_tensor_tensor", out=c[:], in0=u[:, kk:kk + S], scalar=cw[:, kk, t:t + 1],
                     in1=c[:], op0=ALU.mult, op1=ALU.add)
            k.act(out=c[:], in_=c[:], func=AF.Silu)
            o = ob[t % 2]
            if t < 12:
                k.act(out=sq[:], in_=c[:], func=AF.Square)
                for tb in range(4):
                    ps = nextps(C)
                    k.mm(ps[:, :], lhsT=C.onesb[:], rhs=sq[:, tb * 512:(tb + 1) * 512])
                    k.op("dve", "tensor_scalar", out=rn[:, tb * 512:(tb + 1) * 512], in0=ps[:, :], scalar1=1e-6,
                         scalar2=None, op0=ALU.add)
                k.act(out=rn[:], in_=rn[:], func=AF.Sqrt)
                k.op("dve", "reciprocal", out=rn[:], in_=rn[:])
                k.op("dve", "tensor_tensor", out=o[:], in0=c[:], in1=rn[:], op=ALU.mult)
            else:
                k.op("pool", "tensor_copy", out=o[:], in_=c[:])
            if t < 6:
                dst = N.gq[t * 128:(t + 1) * 128, :]
            elif t < 12:
                dst = N.gk[(t - 6) * 128:(t - 5) * 128, :]
            else:
                dst = N.gv[(t - 12) * 128:(t - 11) * 128, :]
            k.dma("sp", dst, o[:])
        zs = [k.sb([128, 512], BF16) for _ in range(2)]
        for zb in range(3):
            def evz(tt, ps, zb=zb):
                s_ = zs[tt % 2]
                k.act(out=s_[:], in_=ps[:, :], func=AF.Silu)
                k.dma("sp", N.gz[tt * 128:(tt + 1) * 128, zb * 512:(zb + 1) * 512], s_[:])
            P.T(Ref(W.t[j, :, 3072 + zb * 512:3072 + (zb + 1) * 512], W.buf), 512, evz)
        al = k.sb([128, 12], F32)
        k.dma("sp", al[:], Ref(bcast_rows(N.c_a_log.t[j, :], 12), N.c_a_log.buf))
        dtb = k.sb([128, 12], F32)
        k.dma("sp", dtb[:], Ref(bcast_rows(N.c_dt_bias.t[j, :], 12), N.c_dt_bias.buf))
        nea = k.sb([128, 12], F32)
        k.act(out=nea[:], in_=al[:], func=AF.Exp)
        k.op("dve", "tensor_scalar", out=nea[:], in0=nea[:], scalar1=-1.0, scalar2=None, op0=ALU.mult)
        bg = [k.sb([128, 24], F32) for _ in range(2)]

        def evbg(tt, ps):
            s_ = bg[tt % 2]
            k.act(out=s_[:, 0:12], in_=ps[:, 0:12], func=AF.Sigmoid)
            k.op("dve", "tensor_tensor", out=s_[:, 12:24], in0=ps[:, 12:24], in1=dtb[:], op=ALU.add)
            k.act(out=s_[:, 12:24], in_=s_[:, 12:24], func=AF.Exp)
            k.act(out=s_[:, 12:24], in_=s_[:, 12:24], func=AF.Ln, bias=C.onesf[:, 0:1])
            k.op("dve", "tensor_tensor", out=s_[:, 12:24], in0=s_[:, 12:24], in1=nea[:], op=ALU.mult)
            k.dma("sp", N.gbg[tt * 128:(tt + 1) * 128, :], s_[:])
        P.T(Ref(W.t[j, :, 4608:4632], W.buf), 24, evbg)
        stg = [k.sb([128, S], BF16) for _ in range(2)]
        cnt = [0]
        for c in range(4):
            proj_F_to_dram(k, P, Ref(W.t[j, :, 4632 + c * 128:4632 + (c + 1) * 128], W.buf), N.qmT, c * 128, stg, cnt)


class GdnTmp:
    def __init__(self, k):
        f = lambda dt: k.sb([128, 128], dt)
        self.vtok = k.sb([128, 256], BF16)
        self.kdec = f(BF16)
        self.gbc = f(F32)
        self.tmp = f(F32)
        self.DT = f(F32)
        self.L2T = f(F32)
        self.L2 = f(F32)
        self.AT = f(BF16)
        self.PTb = f(BF16)
        self.u = f(F32)
        self.wtok = f(BF16)
        self.wT = f(BF16)
        self.vnew = f(BF16)
        self.ob = f(F32)
        self.o = f(F32)
        self.junk = f(F32)
        self.ss = k.sb([128, 1], F32)
        self.sd = k.sb([128, 1], F32)
        self.rs = k.sb([128, 1], F32)
        self.y = f(F32)
        self.y2 = f(BF16)
        self.yT = f(BF16)
        self.zt = f(BF16)
        self.ws = TriWS(k)


def phase_scan_c(k, C, N, j):
    with phase(k):
        M1 = C.mask(-1, 1, ALU.is_ge)
        strictT = C.mask(-1, 1, ALU.is_gt)
        bgall = k.sb([128, NT, 24], F32)
        k.dma("sp", bgall[:], N.gbg.view(N.gbg.t.rearrange("(t p) c -> p t c", p=128)))
        gc = k.sb([128, NT, 12], F32)
        gl = k.sb([128, NT, 12], F32)
        for n in range(NT):
            ps = nextps(C)
            k.mm(ps[:, 0:12], lhsT=M1[:], rhs=bgall[:, n, 12:24])
            k.mm(ps[:, 16:28], lhsT=C.onesf[:], rhs=bgall[:, n, 12:24])
            k.op("act", "copy", out=gc[:, n, :], in_=ps[:, 0:12])
            k.op("act", "copy", out=gl[:, n, :], in_=ps[:, 16:28])
        egc = k.sb([128, NT, 12], F32)
        egl = k.sb([128, NT, 12], F32)
        edec = k.sb([128, NT, 12], F32)
        qsc = k.sb([128, NT, 12], F32)
        k.act(out=egc[:], in_=gc[:], func=AF.Exp)
        k.act(out=egl[:], in_=gl[:], func=AF.Exp)
        k.op("dve", "tensor_tensor", out=edec[:], in0=gl[:], in1=gc[:], op=ALU.subtract)
        k.act(out=edec[:], in_=edec[:], func=AF.Exp)
        k.op("dve", "tensor_scalar", out=qsc[:], in0=egc[:], scalar1=float(128.0 ** -0.5), scalar2=None, op0=ALU.mult)
        normg = k.sb([128, 128], F32)
        k.dma("sp", normg[:], Ref(bcast_rows(N.c_norm_g.t[j, :], 128), N.c_norm_g.buf))
        kTs = [k.sb([128, S], BF16) for _ in range(2)]
        qTs = [k.sb([128, S], BF16) for _ in range(2)]
        vTs = [k.sb([128, S], BF16) for _ in range(4)]
        KKs = [k.sb([128, 128], F32) for _ in range(2)]
        QKs = [k.sb([128, 128], F32) for _ in range(2)]
        ktoks = [k.sb([128, 128], BF16) for _ in range(2)]
        tmps = [GdnTmp(k) for _ in range(2)]
        H = [k.sb([128, 128], F32) for _ in range(2)]
        Hb = [k.sb([128, 128], BF16) for _ in range(2)]
        un = 0
        sh = 0
        for hq in range(6):
            kT, qT = kTs[hq % 2], qTs[hq % 2]
            k.dma("sp", kT[:], N.gk[hq * 128:(hq + 1) * 128, :])
            k.dma("sp", qT[:], N.gq[hq * 128:(hq + 1) * 128, :])
            vT2 = []
            for i in range(2):
                hv = 2 * hq + i
                vT = vTs[hv % 4]
                k.dma("sp", vT[:], N.gv[hv * 128:(hv + 1) * 128, :])
                vT2.append(vT)
                k.op("pool", "memset", ap=H[i][:], constant=0.0)
                k.op("pool", "memset", ap=Hb[i][:], constant=0.0)
            for n in range(NT):
                tc = slice(n * 128, (n + 1) * 128)
                KK, QK, ktok = KKs[sh % 2], QKs[sh % 2], ktoks[sh % 2]
                sh += 1
                ps = nextps(C)
                k.mm(ps[:, 0:128], lhsT=kT[:, tc], rhs=kT[:, tc])
                k.mm(ps[:, 128:256], lhsT=kT[:, tc], rhs=qT[:, tc])
                k.op("dve", "tensor_tensor", out=KK[:], in0=ps[:, 0:128], in1=strictT[:], op=ALU.mult)
                k.op("dve", "scalar_tensor_tensor", out=QK[:], in0=ps[:, 128:256], scalar=float(128.0 ** -0.5),
                     in1=M1[:], op0=ALU.mult, op1=ALU.mult)
                pst = psbf(nextps(C))
                k.tr(pst[:, 0:128], kT[:, tc], C.identb[:])
                k.op("act", "copy", out=ktok[:], in_=pst[:, 0:128])
                for i in range(2):
                    hv = 2 * hq + i
                    T = tmps[un % 2]
                    un += 1
                    bcol = bgall[:, n, hv:hv + 1]
                    gcol = bgall[:, n, 12 + hv:13 + hv]
                    pst = psbf(nextps(C))
                    k.tr(pst[:, 0:128], vT2[i][:, tc], C.identb[:])
                    k.op("act", "copy", out=T.vtok[:, 0:128], in_=pst[:, 0:128])
                    k.op("dve", "tensor_scalar", out=T.vtok[:, 128:256], in0=ktok[:], scalar1=egc[:, n, hv:hv + 1],
                         scalar2=None, op0=ALU.mult)
                    k.act(out=T.kdec[:], in_=ktok[:], func=AF.Copy, scale=edec[:, n, hv:hv + 1])
                    k.op("dve", "tensor_scalar", out=T.gbc[:], in0=C.onesf[:], scalar1=gcol, scalar2=None, op0=ALU.mult)
                    psg = nextps(C)
                    k.mm(psg[:, 0:128], lhsT=T.gbc[:], rhs=M1[:])
                    k.op("dve", "tensor_scalar", out=T.tmp[:], in0=psg[:, 0:128], scalar1=gc[:, n, hv:hv + 1],
                         scalar2=0.0, op0=ALU.subtract, op1=ALU.min)
                    k.act(out=T.DT[:], in_=T.tmp[:], func=AF.Exp)
                    k.op("dve", "scalar_tensor_tensor", out=T.L2T[:], in0=KK[:], scalar=bcol, in1=T.DT[:],
                         op0=ALU.mult, op1=ALU.mult)
                    k.op("dve", "tensor_tensor", out=T.AT[:], in0=QK[:], in1=T.DT[:], op=ALU.mult)
                    psl = nextps(C)
                    k.op("pe", "transpose", out=psl[:, 0:128], in_=T.L2T[:], identity=C.identf[:])
                    k.op("act", "copy", out=T.L2[:], in_=psl[:, 0:128])
                    PT = tri_inv_T(k, C, T.L2, T.L2T, T.ws)
                    k.op("act", "copy", out=T.PTb[:], in_=PT[:])
                    psu = nextps(C)
                    k.mm(psu[:, 0:256], lhsT=T.PTb[:], rhs=T.vtok[:])
                    k.op("dve", "tensor_scalar", out=T.u[:], in0=psu[:, 0:128], scalar1=bcol, scalar2=None, op0=ALU.mult)
                    k.op("dve", "tensor_scalar", out=T.wtok[:], in0=psu[:, 128:256], scalar1=bcol, scalar2=None,
                         op0=ALU.mult)
                    pst = psbf(nextps(C))
                    k.tr(pst[:, 0:128], T.wtok[:], C.identb[:])
                    k.op("act", "copy", out=T.wT[:], in_=pst[:, 0:128])
                    ps1 = nextps(C)
                    k.mm(ps1[:, 0:128], lhsT=T.wT[:], rhs=Hb[i][:])
                    k.op("dve", "tensor_tensor", out=T.vnew[:], in0=T.u[:], in1=ps1[:, 0:128], op=ALU.subtract)
                    pso = nextps(C)
                    k.mm(pso[:, 0:128], lhsT=qT[:, tc], rhs=Hb[i][:])
                    k.mm(pso[:, 128:256], lhsT=T.AT[:], rhs=T.vnew[:])
                    k.op("act", "copy", out=T.ob[:], in_=pso[:, 128:256])
                    k.op("dve", "scalar_tensor_tensor", out=T.o[:], in0=pso[:, 0:128], scalar=qsc[:, n, hv:hv + 1],
                         in1=T.ob[:], op0=ALU.mult, op1=ALU.add)
                    psh = nextps(C)
                    k.mm(psh[:, 0:128], lhsT=T.kdec[:], rhs=T.vnew[:])
                    k.op("dve", "scalar_tensor_tensor", out=H[i][:], in0=H[i][:], scalar=egl[:, n, hv:hv + 1],
                         in1=psh[:, 0:128], op0=ALU.mult, op1=ALU.add)
                    k.op("act", "copy", out=Hb[i][:], in_=H[i][:])
                    k.op("pool", "memset", ap=T.ss[:], constant=0.0)
                    k.act(out=T.junk[:], in_=T.o[:], func=AF.Square, accum_out=T.ss[:])
                    k.op("dve", "tensor_scalar", out=T.sd[:], in0=T.ss[:], scalar1=1.0 / 128.0, scalar2=EPS,
                         op0=ALU.mult, op1=ALU.add)
                    k.act(out=T.sd[:], in_=T.sd[:], func=AF.Sqrt)
                    k.op("dve", "reciprocal", out=T.rs[:], in_=T.sd[:])
                    k.op("dve", "scalar_tensor_tensor", out=T.y[:], in0=T.o[:], scalar=T.rs[:, 0:1], in1=normg[:],
                         op0=ALU.mult, op1=ALU.mult)
                    k.dma("sp", T.zt[:], N.gz[n * 128:(n + 1) * 128, hv * 128:(hv + 1) * 128])
                    k.op("pool", "tensor_tensor", out=T.y2[:], in0=T.y[:], in1=T.zt[:], op=ALU.mult)
                    pst = psbf(nextps(C))
                    k.tr(pst[:, 0:128], T.y2[:], C.identb[:])
                    k.op("act", "copy", out=T.yT[:], in_=pst[:, 0:128])
                    k.dma("sp", N.yT[hv * 128:(hv + 1) * 128, tc], T.yT[:])
        mem_attention(k, C, N)


def phase_mixer_c(k, C, N, j):
    phase_proj_c(k, C, N, j)
    phase_scan_c(k, C, N, j)


WEIGHT_SHAPES = [
    ("attn_norm", [4, 2048]), ("mem_norm", [4, 2048]), ("w_mem_kv", [4, 2048, 1024]), ("w_out", [4, 2048, 2048]),
    ("ffn_norm", [4, 2048]), ("w_ffn_up", [4, 2048, 11264]), ("ffn_conv", [4, 3, 11264]),
    ("w_ffn_down", [4, 5632, 2048]), ("final_norm", [2048]), ("a_w_in", [2, 2048, 2560]), ("a_sinks", [2, 24]),
    ("b_w_in", [1, 2048, 5568]), ("b_mu", [1, 5056]), ("b_w0", [1, 1536]), ("b_w_decay_up", [1, 96, 1536]),
    ("b_a0", [1, 1536]), ("b_w_iclr_up", [1, 96, 1536]), ("b_w_gate_up", [1, 256, 1536]), ("b_k_k", [1, 1536]),
    ("b_k_a", [1, 1536]), ("b_r_k", [1, 24, 64]), ("b_gn_g", [1, 1536]), ("b_gn_b", [1, 1536]),
    ("c_w_in", [1, 2048, 5144]), ("c_conv", [1, 4, 3072]), ("c_a_log", [1, 12]), ("c_dt_bias", [1, 12]),
    ("c_norm_g", [1, 128]),
]

SCRATCH = [
    ("xs", [S, D], F32), ("hT", [D, S], BF16), ("memhT", [D, 256], BF16), ("memkT", [512, 256], BF16),
    ("memv", [256, 512], BF16), ("qT", [1536, S], BF16), ("kT2", [512, S], BF16), ("v2", [S, 512], BF16),
    ("qmT", [512, S], BF16), ("yT", [D, S], BF16), ("aT", [DFF, S], BF16),
]


def fresh_patch():
    TB.f = lambda self: TB(self.t)


def emit_layer(k, C, N, li, cfg):
    kind, j = li % 3, li // 3
    only = cfg.get("only")

    def ph(name, fn, *a):
        if only is None or name in only:
            fn(*a)
    x_in = N.x if li == list(cfg.get('layers', range(4)))[0] else N.xs
    ph("norm1", phase_norm, k, C, x_in, Ref(N.attn_norm.t[li, :], N.attn_norm.buf), N.hT, S)
    ph("normm", phase_norm, k, C, N.mem, Ref(N.mem_norm.t[li, :], N.mem_norm.buf), N.memhT, 256)
    ph("memkv", phase_mem_kv, k, C, N, li)
    if kind == 0:
        ph("proj", phase_proj_a, k, C, N, j)
        ph("mix", phase_attn_a, k, C, N, j)
    elif kind == 1:
        ph("mix", phase_mixer_b, k, C, N, j)
    else:
        ph("mix", phase_mixer_c, k, C, N, j)
    ph("outproj", phase_outproj, k, C, N, li, x_in, N.xs)
    ph("ffnup", phase_ffn_up, k, C, N, li)
    ph("ffndown", phase_ffn_down, k, C, N, li, N.xs)
    return True


def build(cfg):
    nc = bass.Bass("TRN2", target_bir_lowering=False)
    dump = cfg.get("dump", ())
    with ExitStack() as st:
        k = K(nc, st)
        N = Net()
        N.x = k.dram("x", [S, D], F32, kind="ExternalInput")
        N.mem = k.dram("mem", [256, D], F32, kind="ExternalInput")
        used = cfg.get("weights")
        for name, shape in WEIGHT_SHAPES:
            if used is None or name in used:
                setattr(N, name, k.dram(name, shape, F32, kind="ExternalInput"))
        N.out = k.dram("out", [S, D], F32, kind="ExternalOutput")
        for name, shape, dt in SCRATCH + EXTRA_SCRATCH + B_SCRATCH:
            setattr(N, name, k.dram(name, shape, dt, kind=("ExternalOutput" if name in dump else "Internal")))
        C = setup_consts(k)
        layers = cfg.get("layers", range(4))
        CFG.clear()
        CFG.update(cfg)
        ok = True
        PH["n"] = 0
        PH["max"] = cfg.get("max_phases", 10 ** 9)
        try:
            for li in layers:
                ok = emit_layer(k, C, N, li, cfg)
                if not ok:
                    break
            if ok and cfg.get("final", True):
                phase_final_norm(k, C, N.xs, Ref(N.final_norm.t[:], N.final_norm.buf), N.out)
        except StopBuild:
            pass
        k_barrier(k)
        k.finish()
        k.stats = {n: (e.nins, e.count) for n, e in k.eng.items()}
        print("instr stats", k.stats)
    return nc


_CACHE = {}


def run(inputs, cfg, cores=8):
    key = repr(sorted((a, repr(b)) for a, b in cfg.items()))
    if key not in _CACHE:
        _CACHE[key] = build(cfg)
    nc = _CACHE[key]
    used = cfg.get("weights")
    wts = {n: np.ascontiguousarray(inputs[n], dtype=np.float32) for n, _ in WEIGHT_SHAPES
           if used is None or n in used}
    in_maps = []
    for b in range(cores):
        m = dict(wts)
        m["x"] = np.ascontiguousarray(inputs["x"][b], dtype=np.float32)
        m["mem"] = np.ascontiguousarray(inputs["mem"][b], dtype=np.float32)
        in_maps.append(m)
    return run_bass_kernel_spmd(nc, in_maps, core_ids=list(range(cores)))


def kernel(**inputs):
    res = run(inputs, {"layers": (0, 1, 2, 3)}, cores=8)
    return np.stack([np.asarray(r["out"], dtype=np.float32) for r in res.results], axis=0)
```

```python
import numpy as np
import concourse.bass as bass
import concourse.mybir as mybir
from concourse.bass_utils import run_bass_kernel_spmd

F32 = mybir.dt.float32
BF16 = mybir.dt.bfloat16
I32 = mybir.dt.int32
AF = mybir.ActivationFunctionType
ALU = mybir.AluOpType
AX = mybir.AxisListType


class Buf:
    __slots__ = ("w", "rs")

    def __init__(self):
        self.w = None
        self.rs = {}


class Ref:
    __slots__ = ("ap", "buf")

    def __init__(self, ap, buf):
        self.ap = ap
        self.buf = buf


class TB:
    def __init__(self, t, buf=None):
        self.t = t
        self.buf = buf or Buf()

    def __getitem__(self, idx):
        return Ref(self.t[idx], self.buf)

    def view(self, ap):
        return Ref(ap, self.buf)

    def part(self):
        return TB(self.t, Buf())


class Eng:
    def __init__(self, name, obj, sem):
        self.name = name
        self.obj = obj
        self.sem = sem
        self.count = 0
        self.waited = {}
        self.dma_sems = []
        self.dma_uses = []
        self.rr = 0
        self.nins = 0


WRITE_KW = ("out", "accum_out")


class K:
    def __init__(self, nc, stack, ndma=8):
        self.nc = nc
        self.stack = stack
        self.eng = {}
        for name, obj in (("pe", nc.tensor), ("act", nc.scalar), ("dve", nc.vector),
                          ("pool", nc.gpsimd), ("sp", nc.sync)):
            sem = stack.enter_context(nc.semaphore("s_" + name))
            self.eng[name] = Eng(name, obj, sem)
        for q in ("sp", "act", "pool"):
            E = self.eng[q]
            for i in range(ndma):
                E.dma_sems.append(stack.enter_context(nc.semaphore("d_%s%d" % (q, i))))
                E.dma_uses.append(0)
        self.uid = 0

    def sb(self, shape, dtype, name=None):
        self.uid += 1
        t = self.stack.enter_context(self.nc.sbuf_tensor(name or ("sb%d" % self.uid), list(shape), dtype))
        return TB(t)

    def ps(self, shape, dtype, name=None):
        self.uid += 1
        t = self.stack.enter_context(self.nc.psum_tensor(name or ("ps%d" % self.uid), list(shape), dtype))
        return TB(t)

    def dram(self, name, shape, dtype, kind="Internal"):
        t = self.nc.dram_tensor(name, list(shape), dtype, kind=kind)
        return TB(t.ap())

    def _wait(self, E, evs):
        for sem, val, owner in evs:
            if owner == "pe" and E.name == "pe":
                continue
            key = id(sem)
            if E.waited.get(key, 0) >= val:
                continue
            E.obj.wait_ge(sem, val)
            E.waited[key] = val
            E.nins += 1

    def _deps(self, reads, writes):
        evs = []
        for b in reads:
            if b.w is not None:
                evs.append(b.w)
        for b in writes:
            if b.w is not None:
                evs.append(b.w)
            evs.extend(b.rs.values())
        return evs

    def _record(self, ev, reads, writes):
        key = id(ev[0])
        for b in reads:
            old = b.rs.get(key)
            if old is None or old[1] < ev[1]:
                b.rs[key] = ev
        for b in writes:
            b.w = ev
            b.rs = {}

    def op(self, en, meth, *args, sig=True, R=(), W=(), **kw):
        E = self.eng[en]
        reads = [r.buf if isinstance(r, Ref) else r for r in R]
        writes = [w.buf if isinstance(w, Ref) else w for w in W]
        a2 = []
        for a in args:
            if isinstance(a, Ref):
                reads.append(a.buf)
                a = a.ap
            a2.append(a)
        k2 = {}
        for n, v in kw.items():
            if isinstance(v, Ref):
                (writes if n in WRITE_KW else reads).append(v.buf)
                v = v.ap
            k2[n] = v
        self._wait(E, self._deps(reads, writes))
        ins = getattr(E.obj, meth)(*a2, **k2)
        E.nins += 1
        if sig:
            E.count += 1
            ins.then_inc(E.sem, 1)
            ev = (E.sem, E.count, en)
        else:
            ev = (E.sem, E.count + 1, en)
        self._record(ev, reads, writes)
        return ins

    def dma(self, q, out, in_, **kw):
        E = self.eng[q]
        self._wait(E, self._deps([in_.buf], [out.buf]))
        k = E.rr
        sem = E.dma_sems[k]
        if E.dma_uses[k] > 0:
            self._wait(E, [(sem, 16 * E.dma_uses[k], "dma")])
        E.obj.dma_start(out=out.ap, in_=in_.ap, **kw).then_inc(sem, 16)
        E.nins += 1
        E.dma_uses[k] += 1
        ev = (sem, 16 * E.dma_uses[k], "dma")
        self._record(ev, [in_.buf], [out.buf])
        E.rr = (k + 1) % len(E.dma_sems)

    def finish(self):
        for q in ("sp", "act", "pool"):
            E = self.eng[q]
            for sem, uses in zip(E.dma_sems, E.dma_uses):
                if uses:
                    self._wait(E, [(sem, 16 * uses, "dma")])

    def mm(self, out, lhsT, rhs, start=True, stop=True, sig=None, **kw):
        if sig is None:
            sig = stop
        return self.op("pe", "matmul", out=out, lhsT=lhsT, rhs=rhs, start=start, stop=stop, sig=sig, **kw)

    def tr(self, out, in_, ident, sig=True):
        return self.op("pe", "transpose", out=out, in_=in_, identity=ident, sig=sig)

    def act(self, out, in_, func, **kw):
        return self.op("act", "activation", out=out, in_=in_, func=func, **kw)


from contextlib import ExitStack, contextmanager

S = 2048
D = 2048
NT = 16
NCH = 16
DFF = 5632
NFT = 44
EPS = 1e-6
NEG = -30000.0
WRITE_KW = ("out", "accum_out", "ap")


class Net:
    pass


def k_barrier(k):
    evs = []
    for n, E in k.eng.items():
        if E.count:
            evs.append((E.sem, E.count, n))
        for sem, uses in zip(E.dma_sems, E.dma_uses):
            if uses:
                evs.append((sem, 16 * uses, "dma"))
    for n, E in k.eng.items():
        k._wait(E, evs)


class StopBuild(Exception):
    pass


CFG = {}
PH = {"n": 0, "max": 10 ** 9}


@contextmanager
def phase(k):
    if PH["n"] >= PH["max"]:
        raise StopBuild()
    PH["n"] += 1
    k_barrier(k)
    saved = k.stack
    with ExitStack() as st:
        k.stack = st
        yield
        k_barrier(k)
    k.stack = saved


def psbf(ps):
    return TB(ps.t[:].bitcast(BF16), ps.buf)


def setup_consts(k):
    C = Net()
    C.onesf = k.sb([128, 128], F32)
    k.op("pool", "memset", ap=C.onesf[:], constant=1.0)
    C.onesb = k.sb([128, 128], BF16)
    k.op("dve", "tensor_copy", out=C.onesb[:], in_=C.onesf[:])

    def mask(cm, step, cmp):
        m = k.sb([128, 128], F32)
        k.op("pool", "affine_select", out=m[:], in_=C.onesf[:], pattern=[[step, 128]],
             compare_op=cmp, fill=0.0, base=0, channel_multiplier=cm)
        return m
    C.mask = mask
    C.identf = mask(1, -1, ALU.is_equal)
    C.identb = k.sb([128, 128], BF16)
    k.op("dve", "tensor_copy", out=C.identb[:], in_=C.identf[:])
    C.ps = [k.ps([128, 512], F32) for _ in range(8)]
    C.psi = 0
    return C


def nextps(C):
    p = C.ps[C.psi % 8]
    C.psi += 1
    return p


def bcast_rows(ap1d, n):
    return ap1d.partition_broadcast(128)


class NormBufs:
    def __init__(self, k, with_x=True):
        self.xt = [k.sb([128, D], F32) for _ in range(2)] if with_x else None
        self.junk = k.sb([128, D], BF16)
        self.ss = [k.sb([128, 1], F32) for _ in range(2)]
        self.rstd = [k.sb([128, 1], F32) for _ in range(2)]
        self.sd = [k.sb([128, 1], F32) for _ in range(2)]
        self.xn = [k.sb([128, D], BF16) for _ in range(2)]
        self.hts = [k.sb([128, NCH, 128], BF16) for _ in range(2)]


def rstd_from_ss(k, nb, b):
    k.op("dve", "tensor_scalar", out=nb.sd[b][:], in0=nb.ss[b][:], scalar1=1.0 / D, scalar2=EPS,
         op0=ALU.mult, op1=ALU.add)
    k.act(out=nb.sd[b][:], in_=nb.sd[b][:], func=AF.Sqrt)
    k.op("dve", "reciprocal", out=nb.rstd[b][:], in_=nb.sd[b][:])


def norm_tile(k, C, nb, xt, g_rep, out_tb, t, b):
    k.op("dve", "memset", ap=nb.ss[b][:], constant=0.0)
    k.act(out=nb.junk[:], in_=xt[:], func=AF.Square, accum_out=nb.ss[b][:])
    rstd_from_ss(k, nb, b)
    k.op("dve", "scalar_tensor_tensor", out=nb.xn[b][:], in0=xt[:], scalar=nb.rstd[b][:, 0:1],
         in1=g_rep[:], op0=ALU.mult, op1=ALU.mult)
    for half in range(2):
        ps = psbf(nextps(C))
        for c8 in range(8):
            c = half * 8 + c8
            k.tr(ps[:, c8 * 128:(c8 + 1) * 128], nb.xn[b][:, c * 128:(c + 1) * 128], C.identb[:], sig=(c8 == 7))
        src = ps.view(ps.t[:, :].rearrange("p (c n) -> p c n", c=8))
        if half == 0:
            k.op("act", "copy", out=nb.hts[b][:, 0:8, :], in_=src)
        else:
            k.op("dve", "tensor_copy", out=nb.hts[b][:, 8:16, :], in_=src)
    dst = out_tb.t.rearrange("(c p) n -> p c n", p=128)[:, :, t * 128:(t + 1) * 128]
    k.dma("sp", TB(out_tb.t).view(dst), nb.hts[b][:])


def load_grep(k, gvec_ref):
    g_rep = k.sb([128, D], F32)
    k.dma("sp", g_rep[:], Ref(bcast_rows(gvec_ref.ap, D), gvec_ref.buf))
    return g_rep


def phase_norm(k, C, x_tb, gvec_ref, out_tb, ntok):
    with phase(k):
        g_rep = load_grep(k, gvec_ref)
        nb = NormBufs(k)
        nt_ = ntok // 128
        k.dma("sp", nb.xt[0][:], x_tb[0:128, :])
        for t in range(nt_):
            b = t % 2
            if t + 1 < nt_:
                k.dma("sp", nb.xt[1 - b][:], x_tb[(t + 1) * 128:(t + 2) * 128, :])
            norm_tile(k, C, nb, nb.xt[b], g_rep, out_tb, t, b)


def phase_final_norm(k, C, x_tb, gvec_ref, out_tb):
    with phase(k):
        g_rep = load_grep(k, gvec_ref)
        nb = NormBufs(k)
        ot = [k.sb([128, D], F32) for _ in range(2)]
        k.dma("sp", nb.xt[0][:], x_tb[0:128, :])
        for t in range(NT):
            b = t % 2
            if t + 1 < NT:
                k.dma("sp", nb.xt[1 - b][:], x_tb[(t + 1) * 128:(t + 2) * 128, :])
            k.op("dve", "memset", ap=nb.ss[b][:], constant=0.0)
            k.act(out=nb.junk[:], in_=nb.xt[b][:], func=AF.Square, accum_out=nb.ss[b][:])
            rstd_from_ss(k, nb, b)
            k.op("dve", "scalar_tensor_tensor", out=ot[b][:], in0=nb.xt[b][:], scalar=nb.rstd[b][:, 0:1],
                 in1=g_rep[:], op0=ALU.mult, op1=ALU.mult)
            k.dma("sp", out_tb[t * 128:(t + 1) * 128, :], ot[b][:])


def load_hT(k, hT_tb, ntok):
    h = k.sb([128, NCH, ntok], BF16)
    v = hT_tb.t.rearrange("(c p) n -> p c n", p=128)
    for c0 in range(0, NCH, 4):
        k.dma("sp", h[:, c0:c0 + 4, :], hT_tb.view(v[:, c0:c0 + 4, :]))
    return h


class Stager:
    def __init__(self, k, nbuf=3, elems=2048, engines=("pool",)):
        self.k = k
        self.bufs = [k.sb([128, elems], F32) for _ in range(nbuf)]
        self.elems = elems
        self.i = 0
        self.engines = engines
        self.e = 0

    def load(self, dst, src, shape):
        k = self.k
        a, b = shape
        assert a * b <= self.elems
        st = self.bufs[self.i % len(self.bufs)]
        self.i += 1
        sv = st.view(st.t[:, 0:a * b].rearrange("p (a b) -> p a b", a=a))
        k.dma("sp", sv, src)
        eng = self.engines[self.e % len(self.engines)]
        self.e += 1
        if eng == "act":
            k.op("act", "copy", out=dst, in_=sv)
        else:
            k.op(eng, "tensor_copy", out=dst, in_=sv)


def load_w(k, wt, n, wref, stager):
    v = wref.ap.rearrange("(c p) n -> p c n", p=128)
    for n0 in range(0, n, 128):
        w = min(128, n - n0)
        stager.load(wt[:, :, n0:n0 + w], Ref(v[:, :, n0:n0 + w], wref.buf), (NCH, w))


class ProjCtx:
    def __init__(self, k, C, h_sb, ntok, nwt=3, wmax=128):
        self.k, self.C, self.h, self.ntok = k, C, h_sb, ntok
        self.wts = [k.sb([128, NCH, wmax], BF16) for _ in range(nwt)]
        self.i = 0
        self.stager = Stager(k, nbuf=2)
        self.q = []
        self.qi = 0
        self.loaded = {}

    def plan(self, lst):
        self.q = list(lst)
        self.qi = 0
        self.loaded = {}

    def _load(self, wref, n):
        wt = self.wts[self.i % len(self.wts)]
        self.i += 1
        load_w(self.k, wt, n, wref, self.stager)
        return wt

    def _take(self, wref, n):
        key = repr(wref.ap)
        if self.qi < len(self.q) and repr(self.q[self.qi][0].ap) == key:
            if self.qi not in self.loaded:
                self.loaded[self.qi] = self._load(wref, n)
            wt = self.loaded.pop(self.qi)
            self.qi += 1
            if self.qi < len(self.q):
                nr, nn = self.q[self.qi]
                self.loaded[self.qi] = self._load(nr, nn)
            return wt
        return self._load(wref, n)

    def F(self, wref, evac, n=128):
        k = self.k
        wt = self._take(wref, n)
        for tb in range(self.ntok // 512 if self.ntok >= 512 else 1):
            w = min(512, self.ntok)
            ps = nextps(self.C)
            for c in range(NCH):
                k.mm(ps[0:n, 0:w], lhsT=wt[:, c, 0:n], rhs=self.h[:, c, tb * 512:tb * 512 + w],
                     start=(c == 0), stop=(c == NCH - 1))
            evac(tb, ps)

    def T(self, wref, n, evac):
        k = self.k
        wt = self._take(wref, n)
        for tt in range(self.ntok // 128):
            ps = nextps(self.C)
            for c in range(NCH):
                k.mm(ps[:, 0:n], lhsT=self.h[:, c, tt * 128:(tt + 1) * 128], rhs=wt[:, c, 0:n],
                     start=(c == 0), stop=(c == NCH - 1))
            evac(tt, ps)


def alt_copy(k, i, out, in_):
    if i % 2 == 0:
        k.op("act", "copy", out=out, in_=in_)
    else:
        k.op("dve", "tensor_copy", out=out, in_=in_)


def proj_F_to_dram(k, P, wref, dst_tb, row0, stg, cnt, n=128):
    st = stg[cnt[0] % len(stg)]
    cnt[0] += 1

    def evac(tb, ps):
        w = min(512, P.ntok)
        alt_copy(k, tb, st[0:n, tb * 512:tb * 512 + w], ps[0:n, 0:w])
    P.F(wref, evac, n)
    k.dma("sp", dst_tb[row0:row0 + n, :], st[0:n, 0:P.ntok])


def phase_mem_kv(k, C, N, li):
    with phase(k):
        h = load_hT(k, N.memhT, 256)
        P = ProjCtx(k, C, h, 256, nwt=3, wmax=512)
        stg = [k.sb([128, 512], BF16) for _ in range(2)]
        cnt = [0]
        W = N.w_mem_kv
        P.plan([(Ref(W.t[li, :, j * 128:(j + 1) * 128], W.buf), 128) for j in range(4)]
               + [(Ref(W.t[li, :, 512:1024], W.buf), 512)])
        for j in range(4):
            proj_F_to_dram(k, P, Ref(W.t[li, :, j * 128:(j + 1) * 128], W.buf), N.memkT, j * 128, stg, cnt)
        st2 = [k.sb([128, 512], BF16) for _ in range(2)]

        def evac(tt, ps):
            s = st2[tt % 2]
            alt_copy(k, tt, s[:, :], ps[:, 0:512])
            k.dma("sp", N.memv[tt * 128:(tt + 1) * 128, :], s[:, :])
        P.T(Ref(W.t[li, :, 512:1024], W.buf), 512, evac)


def mem_attention(k, C, N):
    kT = k.sb([128, 4, 256], BF16)
    k.dma("sp", kT[:], N.memkT.view(N.memkT.t.rearrange("(h p) m -> p h m", p=128)))
    mv = k.sb([128, 2, 512], BF16)
    k.dma("sp", mv[:], N.memv.view(N.memv.t.rearrange("(t p) n -> p t n", p=128)))
    qm = [k.sb([128, 4, 512], BF16) for _ in range(2)]
    pt = [k.sb([128, 2, 512], BF16) for _ in range(2)]
    rec = [k.sb([128, 512], F32) for _ in range(2)]
    ym = [k.sb([128, 4, 512], BF16) for _ in range(2)]
    sc = 1.0 / np.sqrt(128.0)
    u = 0
    for tb in range(4):
        q = qm[tb % 2]
        k.dma("sp", q[:], N.qmT.view(N.qmT.t.rearrange("(h p) s -> p h s", p=128)[:, :, tb * 512:(tb + 1) * 512]))
        y = ym[tb % 2]
        for hm in range(4):
            p = pt[u % 2]
            r = rec[u % 2]
            u += 1
            for mt in range(2):
                ps = nextps(C)
                k.mm(ps[:, :], lhsT=kT[:, hm, mt * 128:(mt + 1) * 128], rhs=q[:, hm, :])
                k.act(out=p[:, mt, :], in_=ps[:, :], func=AF.Exp, scale=float(sc))
            pso = nextps(C)
            psd = nextps(C)
            for mt in range(2):
                k.mm(pso[:, :], lhsT=mv[:, mt, hm * 128:(hm + 1) * 128], rhs=p[:, mt, :], start=(mt == 0), stop=(mt == 1))
            for mt in range(2):
                k.mm(psd[:, :], lhsT=C.onesb[:], rhs=p[:, mt, :], start=(mt == 0), stop=(mt == 1))
            k.op("dve", "reciprocal", out=r[:], in_=psd[:, :])
            k.op("dve", "tensor_tensor", out=y[:, hm, :], in0=pso[:, :], in1=r[:], op=ALU.mult)
        dst = N.yT.t[1536:2048, :].rearrange("(h p) s -> p h s", p=128)[:, :, tb * 512:(tb + 1) * 512]
        k.dma("sp", N.yT.view(dst), y[:])


def alibi_slope(h):
    return float(2.0 ** (-8.0 * (h + 1.0) / 24.0))


def phase_proj_a(k, C, N, j):
    with phase(k):
        h = load_hT(k, N.hT, S)
        P = ProjCtx(k, C, h, S, nwt=3, wmax=512)
        stg = [k.sb([128, S], BF16) for _ in range(2)]
        cnt = [0]
        W = N.a_w_in
        P.plan([(Ref(W.t[j, :, c * 128:(c + 1) * 128], W.buf), 128) for c in range(12)]
               + [(Ref(W.t[j, :, 2048 + c * 128:2048 + (c + 1) * 128], W.buf), 128) for c in range(4)]
               + [(Ref(W.t[j, :, 1536 + c * 128:1536 + (c + 1) * 128], W.buf), 128) for c in range(2)]
               + [(Ref(W.t[j, :, 1792:2048], W.buf), 256)])
        for c in range(12):
            proj_F_to_dram(k, P, Ref(W.t[j, :, c * 128:(c + 1) * 128], W.buf), N.qT, c * 128, stg, cnt)
        for c in range(4):
            proj_F_to_dram(k, P, Ref(W.t[j, :, 2048 + c * 128:2048 + (c + 1) * 128], W.buf), N.qmT, c * 128, stg, cnt)
        for c in range(2):
            st = stg[cnt[0] % 2]
            cnt[0] += 1

            def evac(tb, ps, st=st):
                alt_copy(k, tb, st[:, tb * 512:(tb + 1) * 512], ps[:, :])
            P.F(Ref(W.t[j, :, 1536 + c * 128:1536 + (c + 1) * 128], W.buf), evac)
            for gg in range(2):
                g = 2 * c + gg
                for dup in range(2):
                    k.dma("sp", N.kT2[g * 128 + dup * 64:g * 128 + dup * 64 + 64, :], st[gg * 64:(gg + 1) * 64, :])
        st2 = [k.sb([128, 4, 128], BF16) for _ in range(2)]

        def evacv(tt, ps):
            s = st2[tt % 2]
            src = ps.view(ps.t[:, 0:256].rearrange("p (g d) -> p g d", g=4))
            k.op("act", "copy", out=s[:, :, 0:64], in_=src)
            k.op("dve", "tensor_copy", out=s[:, :, 64:128], in_=src)
            k.dma("sp", N.v2.view(N.v2.t[tt * 128:(tt + 1) * 128, :].rearrange("p (g d) -> p g d", g=4)), s[:])
        P.T(Ref(W.t[j, :, 1792:2048], W.buf), 256, evacv)


def phase_attn_a(k, C, N, j):
    with phase(k):
        dist = k.sb([128, 128], F32)
        k.op("pool", "iota", dist[:], pattern=[[1, 128]], base=0, channel_multiplier=-1,
             allow_small_or_imprecise_dtypes=True, W=[dist[:]])
        mbc = k.sb([128, 24, 128], F32)
        mbp = k.sb([128, 24, 128], F32)
        for h in range(24):
            sl = alibi_slope(h)
            k.op("dve", "tensor_scalar", out=mbc[:, h, :], in0=dist[:], scalar1=-sl, scalar2=None, op0=ALU.mult)
            k.op("dve", "tensor_scalar", out=mbp[:, h, :], in0=dist[:], scalar1=-sl, scalar2=-128.0 * sl,
                 op0=ALU.mult, op1=ALU.add)
        k.op("pool", "affine_select", out=mbc[:], in_=mbc[:], pattern=[[0, 24], [1, 128]],
             compare_op=ALU.is_ge, fill=NEG, base=0, channel_multiplier=-1)
        k.op("pool", "affine_select", out=mbp[:], in_=mbp[:], pattern=[[0, 24], [-1, 128]],
             compare_op=ALU.is_gt, fill=NEG, base=0, channel_multiplier=1)
        sk = k.sb([128, 24], F32)
        k.dma("sp", sk[:], Ref(bcast_rows(N.a_sinks.t[j, :], 24), N.a_sinks.buf))
        sinkexp = k.sb([128, 24], F32)
        k.act(out=sinkexp[:], in_=sk[:], func=AF.Exp)
        kTz = k.sb([128, 8, S], BF16)
        k.op("pool", "memset", ap=kTz[:], constant=0.0)
        for g in range(4):
            for hf in range(2):
                k.dma("sp", kTz[hf * 64:hf * 64 + 64, 2 * g + hf, :],
                      N.kT2[g * 128 + hf * 64:g * 128 + hf * 64 + 64, :])
        v2 = k.sb([128, NT, 512], BF16)
        vv = N.v2.t.rearrange("(t p) n -> p t n", p=128)
        for t0 in range(0, NT, 4):
            k.dma("sp", v2[:, t0:t0 + 4, :], N.v2.view(vv[:, t0:t0 + 4, :]))
        qs = [k.sb([128, 12, 512], BF16) for _ in range(2)]
        ys = [k.sb([128, 12, 512], BF16) for _ in range(2)]
        scb = [k.sb([128, 2, 384], F32) for _ in range(2)]
        ptb = [k.sb([128, 2, 384], BF16) for _ in range(2)]
        rcb = [k.sb([128, 3, 128], F32) for _ in range(2)]
        qv = N.qT.t.rearrange("(c p) s -> p c s", p=128)
        yv = N.yT.t[0:1536, :].rearrange("(c p) s -> p c s", p=128)
        u = 0
        for n4 in range(CFG.get("attn_n4", 4)):
            q = qs[n4 % 2]
            y = ys[n4 % 2]
            k.dma("sp", q[:], N.qT.view(qv[:, :, n4 * 512:(n4 + 1) * 512]))
            for nn in range(4):
                n = n4 * 4 + nn
                qc = slice(nn * 128, (nn + 1) * 128)
                kbs = [n] if n == 0 else [n - 1, n]
                for g in range(4):
                    for h3 in range(2):
                        sc_, pt_, rc_ = scb[u % 2], ptb[u % 2], rcb[u % 2]
                        u += 1
                        heads = [g * 6 + h3 * 3 + i for i in range(3)]
                        pss = []
                        for bi, kb in enumerate(kbs):
                            ps = nextps(C)
                            pss.append(ps)
                            for i, hh in enumerate(heads):
                                c, hf = hh // 2, hh % 2
                                k.mm(ps[:, i * 128:(i + 1) * 128], lhsT=kTz[:, 2 * g + hf, kb * 128:(kb + 1) * 128],
                                     rhs=q[:, c, qc], start=True, stop=True, sig=(i == 2))
                        for bi, kb in enumerate(kbs):
                            mb = mbc if kb == n else mbp
                            k.op("dve", "scalar_tensor_tensor",
                                 out=sc_.view(sc_.t[:, bi, :].rearrange("p (h q) -> p h q", h=3)),
                                 in0=pss[bi].view(pss[bi].t[:, 0:384].rearrange("p (h q) -> p h q", h=3)),
                                 scalar=0.125, in1=mb[:, heads[0]:heads[0] + 3, :], op0=ALU.mult, op1=ALU.add)
                            k.act(out=pt_[:, bi, :], in_=sc_[:, bi, :], func=AF.Exp)
                        if CFG.get("attn_stage", 3) < 2:
                            continue
                        pso = nextps(C)
                        psd = nextps(C)
                        nk = len(kbs)
                        for bi, kb in enumerate(kbs):
                            k.mm(pso[:, 0:384], lhsT=v2[:, kb, g * 128:(g + 1) * 128], rhs=pt_[:, bi, :],
                                 start=(bi == 0), stop=(bi == nk - 1))
                        for bi, kb in enumerate(kbs):
                            k.mm(psd[:, 0:384], lhsT=C.onesb[:], rhs=pt_[:, bi, :],
                                 start=(bi == 0), stop=(bi == nk - 1))
                        if CFG.get("attn_stage", 3) < 3:
                            continue
                        for i, hh in enumerate(heads):
                            k.op("dve", "tensor_scalar", out=rc_[:, i, :], in0=psd[:, i * 128:(i + 1) * 128],
                                 scalar1=sinkexp[:, hh:hh + 1], scalar2=None, op0=ALU.add)
                        k.op("dve", "reciprocal", out=rc_[:], in_=rc_[:])
                        for i, hh in enumerate(heads):
                            c, hf = hh // 2, hh % 2
                            rows = slice(hf * 64, hf * 64 + 64)
                            k.op("dve", "tensor_tensor", out=y[rows, c, qc], in0=pso[rows, i * 128:(i + 1) * 128],
                                 in1=rc_[rows, i, :], op=ALU.mult)
            k.dma("sp", N.yT.view(yv[:, :, n4 * 512:(n4 + 1) * 512]), y[:])
        if CFG.get("memattn", True):
            mem_attention(k, C, N)


def phase_outproj(k, C, N, li, x_in, x_out):
    with phase(k):
        g_rep = load_grep(k, Ref(N.ffn_norm.t[li, :], N.ffn_norm.buf))
        nb = NormBufs(k)
        wo = k.sb([128, NCH, D], BF16)
        wv = N.w_out.t[li].rearrange("(c p) n -> p c n", p=128)
        stg = Stager(k, nbuf=3, elems=2048, engines=("pool", "act", "pool", "dve"))
        for c0 in range(NCH):
            stg.load(wo[:, c0:c0 + 1, :], Ref(wv[:, c0:c0 + 1, :], N.w_out.buf), (1, D))
        yts = [k.sb([128, NCH, 128], BF16) for _ in range(2)]
        yv = N.yT.t.rearrange("(c p) s -> p c s", p=128)
        def ld(t):
            k.dma("sp", yts[t % 2][:], N.yT.view(yv[:, :, t * 128:(t + 1) * 128]))
            k.dma("sp", nb.xt[t % 2][:], TB(x_in.t)[t * 128:(t + 1) * 128, :])
        ld(0)
        for t in range(NT):
            b = t % 2
            yt = yts[b]
            xt = nb.xt[b]
            if t + 1 < NT:
                ld(t + 1)
            for nbk in range(4):
                ps = nextps(C)
                for c in range(NCH):
                    k.mm(ps[:, :], lhsT=yt[:, c, :], rhs=wo[:, c, nbk * 512:(nbk + 1) * 512],
                         start=(c == 0), stop=(c == NCH - 1))
                k.op("dve", "tensor_tensor", out=xt[:, nbk * 512:(nbk + 1) * 512],
                     in0=xt[:, nbk * 512:(nbk + 1) * 512], in1=ps[:, :], op=ALU.add)
            k.dma("sp", TB(x_out.t)[t * 128:(t + 1) * 128, :], xt[:])
            norm_tile(k, C, nb, xt, g_rep, N.hT, t, b)


def phase_ffn_up(k, C, N, li):
    with phase(k):
        h = load_hT(k, N.hT, S)
        cwj = k.sb([88, 3, 128], F32)
        k.dma("sp", cwj[:], Ref(N.ffn_conv.t[li].rearrange("k (j p) -> j k p", p=128), N.ffn_conv.buf))
        cw = k.sb([128, 3, 88], F32)
        for kk in range(3):
            ps = nextps(C)
            k.op("pe", "transpose", out=ps[:, 0:88], in_=cwj[:, kk, :], identity=C.identf[0:88, 0:88])
            k.op("dve", "tensor_copy", out=cw[:, kk, :], in_=ps[:, 0:88])
        wg = [k.sb([128, NCH, 128], BF16) for _ in range(3)]
        wvv = [k.sb([128, NCH, 128], BF16) for _ in range(3)]
        ug = [k.sb([128, 2 + S], F32) for _ in range(2)]
        uv = [k.sb([128, 2 + S], F32) for _ in range(2)]
        for t_ in ug + uv:
            k.op("pool", "memset", ap=t_[:, 0:2], constant=0.0)
        cg = [k.sb([128, 1024], F32) for _ in range(2)]
        cv = [k.sb([128, 1024], F32) for _ in range(2)]
        sg = [k.sb([128, 1024], F32) for _ in range(2)]
        ao = [k.sb([128, 1024], BF16) for _ in range(2)]
        stg = Stager(k, nbuf=4)
        W = N.w_ffn_up
        u = 0
        def ldw(j):
            load_w(k, wg[j % 3], 128, Ref(W.t[li, :, j * 128:(j + 1) * 128], W.buf), stg)
            load_w(k, wvv[j % 3], 128, Ref(W.t[li, :, DFF + j * 128:DFF + (j + 1) * 128], W.buf), stg)
        ldw(0)
        for j in range(NFT):
            a, b_ = wg[j % 3], wvv[j % 3]
            if j + 1 < NFT:
                ldw(j + 1)
            ugj, uvj = ug[j % 2], uv[j % 2]
            for half in range(2):
                o = half * 1024
                for which, wt, ub in ((0, a, ugj), (1, b_, uvj)):
                    for tb in range(2):
                        ps = nextps(C)
                        for c in range(NCH):
                            k.mm(ps[:, :], lhsT=wt[:, c, :], rhs=h[:, c, o + tb * 512:o + (tb + 1) * 512],
                                 start=(c == 0), stop=(c == NCH - 1))
                        k.op("act", "copy", out=ub[:, 2 + o + tb * 512:2 + o + (tb + 1) * 512], in_=ps[:, :])
                cgu, cvu, sgu, aou = cg[u % 2], cv[u % 2], sg[u % 2], ao[u % 2]
                u += 1
                for eng, ub, co, jj in (("dve", ugj, cgu, j), ("dve", uvj, cvu, NFT + j)):
                    k.op(eng, "tensor_scalar", out=co[:], in0=ub[:, 2 + o:2 + o + 1024],
                         scalar1=cw[:, 2, jj:jj + 1], scalar2=None, op0=ALU.mult)
                    k.op(eng, "scalar_tensor_tensor", out=co[:], in0=ub[:, 1 + o:1 + o + 1024],
                         scalar=cw[:, 1, jj:jj + 1], in1=co[:], op0=ALU.mult, op1=ALU.add)
                    k.op(eng, "scalar_tensor_tensor", out=co[:], in0=ub[:, o:o + 1024],
                         scalar=cw[:, 0, jj:jj + 1], in1=co[:], op0=ALU.mult, op1=ALU.add)
                k.act(out=sgu[:], in_=cgu[:], func=AF.Silu)
                k.op("pool", "tensor_tensor", out=aou[:], in0=sgu[:], in1=cvu[:], op=ALU.mult)
                k.dma("sp", N.aT[j * 128:(j + 1) * 128, o:o + 1024], aou[:])


def phase_ffn_down(k, C, N, li, x_tb):
    with phase(k):
        wd = k.sb([128, NFT, 1024], BF16)
        ats = [k.sb([128, NFT, 256], BF16) for _ in range(2)]
        xts = [k.sb([128, 2, 1024], F32) for _ in range(2)]
        stg = Stager(k, nbuf=3, elems=2048, engines=("pool", "act", "dve"))
        av = N.aT.t.rearrange("(c p) s -> p c s", p=128)
        W = N.w_ffn_down
        u = 0
        for nh in range(2):
            wv = W.t[li, :, nh * 1024:(nh + 1) * 1024].rearrange("(c p) n -> p c n", p=128)
            for c0 in range(0, NFT, 2):
                stg.load(wd[:, c0:c0 + 2, :], Ref(wv[:, c0:c0 + 2, :], W.buf), (2, 1024))
            def ld(uu, nh_, t2_):
                at_, xt_ = ats[uu % 2], xts[uu % 2]
                for c0 in range(0, NFT, 11):
                    k.dma("sp", at_[:, c0:c0 + 11, :], N.aT.view(av[:, c0:c0 + 11, t2_ * 256:(t2_ + 1) * 256]))
                xv_ = x_tb.t[t2_ * 256:(t2_ + 1) * 256, nh_ * 1024:(nh_ + 1) * 1024].rearrange(
                    "(t p) n -> p t n", p=128)
                k.dma("sp", xt_[:], TB(x_tb.t).view(xv_))
            if nh == 0:
                ld(0, 0, 0)
            for t2 in range(NT // 2):
                at, xt = ats[u % 2], xts[u % 2]
                u += 1
                if t2 + 1 < NT // 2:
                    ld(u, nh, t2 + 1)
                elif nh == 0:
                    ld(u, 1, 0)
                xv = x_tb.t[t2 * 256:(t2 + 1) * 256, nh * 1024:(nh + 1) * 1024].rearrange("(t p) n -> p t n", p=128)
                for ts in range(2):
                    for nbk in range(2):
                        ps = nextps(C)
                        for c in range(NFT):
                            k.mm(ps[:, :], lhsT=at[:, c, ts * 128:(ts + 1) * 128],
                                 rhs=wd[:, c, nbk * 512:(nbk + 1) * 512], start=(c == 0), stop=(c == NFT - 1))
                        k.op("dve", "tensor_tensor", out=xt[:, ts, nbk * 512:(nbk + 1) * 512],
                             in0=xt[:, ts, nbk * 512:(nbk + 1) * 512], in1=ps[:, :], op=ALU.add)
                k.dma("sp", TB(x_tb.t).view(xv), xt[:])


B_SCRATCH = [
    ("brT", [1536, S], BF16), ("bkT", [1536, S], BF16), ("bkkT", [1536, S], BF16), ("bbT", [1536, S], BF16),
    ("bvT", [1536, S], BF16), ("blwT", [1536, S], F32), ("bbonT", [1536, S], BF16), ("bgT", [1536, S], BF16),
]
DECAY_C = 0.6065306597126334


def colvec(k, ref1d, n=1536):
    nc_ = n // 128
    rows = k.sb([nc_, 128], F32)
    k.dma("sp", rows[:], Ref(ref1d.ap.rearrange("(c p) -> c p", p=128), ref1d.buf))
    t = k.sb([128, nc_], F32)
    ps = nextps(CREF[0])
    k.op("pe", "transpose", out=ps[:, 0:nc_], in_=rows[:], identity=CREF[0].identf[0:nc_, 0:nc_])
    k.op("dve", "tensor_copy", out=t[:], in_=ps[:, 0:nc_])
    return t


CREF = [None]


def phase_proj_b(k, C, N, j):
    CREF[0] = C
    with phase(k):
        h = load_hT(k, N.hT, S)
        P = ProjCtx(k, C, h, S, nwt=3, wmax=128)
        W = N.b_w_in
        pl = [(4608, 96), (4704, 96), (4800, 128), (4928, 128)]
        for c in range(12):
            pl += [(3072 + c * 128, 128), (c * 128, 128), (1536 + c * 128, 128)]
        pl += [(5056 + c * 128, 128) for c in range(4)]
        P.plan([(Ref(W.t[j, :, c0:c0 + n], W.buf), n) for c0, n in pl])
        V = lambda name: Ref(getattr(N, name).t[j, :], getattr(N, name).buf)
        w0c, a0c, kkc, kac, gngc = colvec(k, V("b_w0")), colvec(k, V("b_a0")), colvec(k, V("b_k_k")), \
            colvec(k, V("b_k_a")), None
        rkc = colvec(k, Ref(N.b_r_k.t[j].rearrange("h d -> (h d)"), N.b_r_k.buf))
        omka = k.sb([128, 12], F32)
        k.op("dve", "tensor_scalar", out=omka[:], in0=kac[:], scalar1=-1.0, scalar2=1.0, op0=ALU.mult, op1=ALU.add)
        blk = k.sb([128, 128], BF16)
        k.op("pool", "memset", ap=blk[:], constant=0.0)
        k.op("pool", "memset", ap=blk[0:64, 0:64], constant=1.0)
        k.op("pool", "memset", ap=blk[64:128, 64:128], constant=1.0)
        wdec = k.sb([96, 1536], BF16)
        wicl = k.sb([96, 1536], BF16)
        wgt = k.sb([128, 2, 1536], BF16)
        aT = k.sb([128, S], F32)
        t1 = k.sb([128, S], F32)
        rn = k.sb([128, S], F32)
        t2 = rn
        k.dma("sp", aT[0:96, 0:1536], Ref(N.b_w_decay_up.t[j], N.b_w_decay_up.buf))
        k.op("pool", "tensor_copy", out=wdec[:], in_=aT[0:96, 0:1536])
        k.dma("sp", t1[0:96, 0:1536], Ref(N.b_w_iclr_up.t[j], N.b_w_iclr_up.buf))
        k.op("pool", "tensor_copy", out=wicl[:], in_=t1[0:96, 0:1536])
        for kc in range(2):
            k.dma("sp", rn[:, 0:1536], Ref(N.b_w_gate_up.t[j, kc * 128:(kc + 1) * 128, :], N.b_w_gate_up.buf))
            k.op("pool", "tensor_copy", out=wgt[:, kc, :], in_=rn[:, 0:1536])
        ub = [k.sb([128, 1 + S], F32) for _ in range(1)]
        for t_ in ub:
            k.op("pool", "memset", ap=t_[:, 0:1], constant=0.0)
        mus = [k.sb([128, 2], F32) for _ in range(2)]
        mx = [k.sb([128, S], F32) for _ in range(2)]
        cnt = [0]

        def mixed(c0, n):
            i = cnt[0]
            cnt[0] += 1
            u, mu, m = ub[0], mus[i % 2], mx[i % 2]

            def evac(tb, ps):
                alt_copy(k, tb, u[0:n, 1 + tb * 512:1 + (tb + 1) * 512], ps[0:n, :])
            P.F(Ref(W.t[j, :, c0:c0 + n], W.buf), evac, n)
            k.dma("sp", mu[0:n, 0:1], Ref(N.b_mu.t[j, c0:c0 + n].rearrange("(p o) -> p o", o=1), N.b_mu.buf))
            k.op("dve", "tensor_scalar", out=mu[0:n, 1:2], in0=mu[0:n, 0:1], scalar1=-1.0, scalar2=1.0,
                 op0=ALU.mult, op1=ALU.add)
            k.op("dve", "tensor_scalar", out=m[0:n, :], in0=u[0:n, 0:S], scalar1=mu[0:n, 0:1], scalar2=None,
                 op0=ALU.mult)
            k.op("dve", "scalar_tensor_tensor", out=m[0:n, :], in0=u[0:n, 1:1 + S], scalar=mu[0:n, 1:2],
                 in1=m[0:n, :], op0=ALU.mult, op1=ALU.add)
            return m
        twT = k.sb([96, S], BF16)
        adT = k.sb([96, S], BF16)
        sgT = k.sb([128, 2, S], BF16)
        m = mixed(4608, 96)
        k.act(out=twT[:], in_=m[0:96, :], func=AF.Tanh)
        m = mixed(4704, 96)
        k.op("dve", "tensor_copy", out=adT[:], in_=m[0:96, :])
        for i in range(2):
            m = mixed(4800 + i * 128, 128)
            k.act(out=sgT[:, i, :], in_=m[:], func=AF.Sigmoid)
        sq = k.sb([128, S], BF16)
        o16 = [k.sb([128, S], BF16) for _ in range(3)]
        o32 = [k.sb([128, S], F32) for _ in range(1)]
        vb = k.sb([128, S], BF16)
        rb = k.sb([128, S], BF16)
        oc = [0]

        def out16():
            oc[0] += 1
            return o16[oc[0] % 3]
        for c in range(12):
            cs = slice(c * 128, (c + 1) * 128)
            lw = o32[0]
            go = out16()
            for tb in range(4):
                ts_ = slice(tb * 512, (tb + 1) * 512)
                ps = nextps(C)
                k.mm(ps[:, :], lhsT=wicl[:, cs], rhs=adT[:, ts_])
                k.act(out=aT[:, ts_], in_=ps[:, :], func=AF.Sigmoid, bias=a0c[:, c:c + 1])
                ps = nextps(C)
                k.mm(ps[:, :], lhsT=wdec[:, cs], rhs=twT[:, ts_])
                k.act(out=lw[:, ts_], in_=ps[:, :], func=AF.Sigmoid, bias=w0c[:, c:c + 1])
                ps = nextps(C)
                for kc in range(2):
                    k.mm(ps[:, :], lhsT=wgt[:, kc, cs], rhs=sgT[:, kc, ts_], start=(kc == 0), stop=(kc == 1))
                k.op("dve", "tensor_copy", out=go[:, ts_], in_=ps[:, :])
            k.op("dve", "tensor_scalar", out=lw[:], in0=lw[:], scalar1=-DECAY_C, scalar2=None, op0=ALU.mult)
            k.dma("sp", N.blwT[cs, :], lw[:])
            k.dma("sp", N.bgT[cs, :], go[:])
            m = mixed(3072 + c * 128, 128)
            k.op("pool", "tensor_copy", out=vb[:], in_=m[:])
            k.dma("sp", N.bvT[cs, :], vb[:])
            m = mixed(c * 128, 128)
            k.op("pool", "tensor_copy", out=rb[:], in_=m[:])
            k.dma("sp", N.brT[cs, :], rb[:])
            m = mixed(1536 + c * 128, 128)
            k.op("dve", "tensor_scalar", out=t1[:], in0=m[:], scalar1=kkc[:, c:c + 1], scalar2=None, op0=ALU.mult)
            k.act(out=sq[:], in_=t1[:], func=AF.Square)
            for tb in range(4):
                ts_ = slice(tb * 512, (tb + 1) * 512)
                ps = nextps(C)
                k.mm(ps[:, :], lhsT=blk[:], rhs=sq[:, ts_])
                k.op("dve", "tensor_scalar", out=rn[:, ts_], in0=ps[:, :], scalar1=1e-6, scalar2=None, op0=ALU.add)
            k.act(out=rn[:], in_=rn[:], func=AF.Sqrt)
            k.op("dve", "reciprocal", out=rn[:], in_=rn[:])
            kko = out16()
            k.op("dve", "tensor_tensor", out=t1[:], in0=t1[:], in1=rn[:], op=ALU.mult)
            k.op("pool", "tensor_copy", out=kko[:], in_=t1[:])
            k.dma("sp", N.bkkT[cs, :], kko[:])
            bo = out16()
            k.op("dve", "tensor_tensor", out=bo[:], in0=t1[:], in1=aT[:], op=ALU.mult)
            k.dma("sp", N.bbT[cs, :], bo[:])
            k.op("dve", "tensor_scalar", out=t2[:], in0=aT[:], scalar1=kac[:, c:c + 1], scalar2=omka[:, c:c + 1],
                 op0=ALU.mult, op1=ALU.add)
            k.op("dve", "tensor_tensor", out=t2[:], in0=t2[:], in1=m[:], op=ALU.mult)
            ko = out16()
            k.op("pool", "tensor_copy", out=ko[:], in_=t2[:])
            k.dma("sp", N.bkT[cs, :], ko[:])
            k.op("dve", "scalar_tensor_tensor", out=sq[:], in0=t2[:], scalar=rkc[:, c:c + 1], in1=rb[:],
                 op0=ALU.mult, op1=ALU.mult)
            bon = out16()
            for tb in range(4):
                ts_ = slice(tb * 512, (tb + 1) * 512)
                ps = nextps(C)
                k.mm(ps[:, :], lhsT=blk[:], rhs=sq[:, ts_])
                k.op("dve", "tensor_tensor", out=bon[:, ts_], in0=ps[:, :], in1=vb[:, ts_], op=ALU.mult)
            k.dma("sp", N.bbonT[cs, :], bon[:])
        stg = o16[0:2]
        cn2 = [0]
        for c in range(4):
            proj_F_to_dram(k, P, Ref(W.t[j, :, 5056 + c * 128:5056 + (c + 1) * 128], W.buf), N.qmT, c * 128, stg, cn2)


class RwTmp:
    def __init__(self, k):
        f = lambda dt, n=128: k.sb([128, n], dt)
        self.KiP = f(BF16)
        self.PiP = f(BF16)
        self.AbT = f(F32)
        self.Ab = f(F32)
        self.AkT = f(BF16)
        self.ArT = f(BF16, 256)
        self.PTb = f(BF16)
        self.Zb = f(BF16, 64)
        self.Un = f(BF16, 64)
        self.y = f(F32, 64)
        self.junk = f(F32, 64)
        self.s1 = k.sb([128, 1], F32)
        self.s2 = k.sb([128, 1], F32)
        self.mean = k.sb([128, 1], F32)
        self.var = k.sb([128, 1], F32)
        self.rs = k.sb([128, 1], F32)
        self.tmpH = f(F32, 64)
        self.ws = TriWS(k)


class RwPair:
    def __init__(self, k):
        f = lambda dt, n=128: k.sb([128, n], dt)
        self.g = f(F32)
        self.gx = f(F32)
        self.Ei, self.En, self.Ex, self.Ed = f(F32), f(F32), f(F32), f(F32)
        self.Rd, self.Ki, self.Pi, self.KKd, self.Kdc, self.Pdc = (f(BF16) for _ in range(6))
        self.Kdt, self.Pdt, self.Vt = f(BF16), f(BF16), f(BF16)
        self.yn = f(BF16)
        self.yf = f(F32)
        self.yo = f(BF16)


def phase_scan_b(k, C, N, j):
    CREF[0] = C
    with phase(k):
        strictT = C.mask(-1, 1, ALU.is_gt)
        inclT = C.mask(-1, 1, ALU.is_ge)
        msk2i = k.sb([128, 2, 128], F32)
        for i in range(2):
            k.op("dve", "tensor_copy", out=msk2i[:, i, :], in_=inclT[:])
        hm = k.sb([128, 2], F32)
        k.op("pool", "memset", ap=hm[:], constant=0.0)
        k.op("pool", "memset", ap=hm[0:64, 0:1], constant=1.0)
        k.op("pool", "memset", ap=hm[64:128, 1:2], constant=1.0)
        V = lambda name: Ref(getattr(N, name).t[j, :], getattr(N, name).buf)
        gng, gnb = colvec(k, V("b_gn_g")), colvec(k, V("b_gn_b"))
        names = ("brT", "bkT", "bkkT", "bbT", "bvT", "bbonT", "bgT")
        inb = [{nm: k.sb([128, S], BF16) for nm in names} for _ in range(2)]
        lwb = [k.sb([128, S], F32) for _ in range(2)]
        prs = [RwPair(k) for _ in range(2)]
        tms = [RwTmp(k) for _ in range(2)]
        Hf = [k.sb([128, 64], F32) for _ in range(2)]
        Hb = [k.sb([128, 64], BF16) for _ in range(2)]
        un = 0
        pu = 0
        for c in range(12):
            cs = slice(c * 128, (c + 1) * 128)
            I = inb[c % 2]
            lw = lwb[c % 2]
            for nm in names:
                k.dma("sp", I[nm][:], getattr(N, nm)[cs, :])
            k.dma("sp", lw[:], N.blwT[cs, :])
            for i in range(2):
                k.op("pool", "memset", ap=Hf[i][:], constant=0.0)
                k.op("pool", "memset", ap=Hb[i][:], constant=0.0)
            for n in range(NT):
                tc = slice(n * 128, (n + 1) * 128)
                Pp = prs[pu % 2]
                pu += 1
                k.op("dve", "tensor_tensor_scan", out=Pp.g[:], data0=C.onesf[:], data1=lw[:, tc], initial=0.0,
                     op0=ALU.mult, op1=ALU.add)
                k.op("dve", "tensor_tensor", out=Pp.gx[:], in0=Pp.g[:], in1=lw[:, tc], op=ALU.subtract)
                k.act(out=Pp.Ei[:], in_=Pp.g[:], func=AF.Exp)
                k.act(out=Pp.En[:], in_=Pp.g[:], func=AF.Exp, scale=-1.0)
                k.act(out=Pp.Ex[:], in_=Pp.gx[:], func=AF.Exp)
                k.act(out=Pp.Ed[:], in_=Pp.g[:], func=AF.Exp, scale=-1.0, bias=Pp.g[:, 127:128])
                k.op("dve", "tensor_tensor", out=Pp.Rd[:], in0=I["brT"][:, tc], in1=Pp.Ei[:], op=ALU.mult)
                k.op("dve", "tensor_tensor", out=Pp.Ki[:], in0=I["bkT"][:, tc], in1=Pp.En[:], op=ALU.mult)
                k.op("dve", "tensor_tensor", out=Pp.Pi[:], in0=I["bbT"][:, tc], in1=Pp.En[:], op=ALU.mult)
                k.op("dve", "tensor_tensor", out=Pp.KKd[:], in0=I["bkkT"][:, tc], in1=Pp.Ex[:], op=ALU.mult)
                k.op("pool", "tensor_tensor", out=Pp.Kdc[:], in0=I["bkT"][:, tc], in1=Pp.Ed[:], op=ALU.mult)
                k.op("pool", "tensor_tensor", out=Pp.Pdc[:], in0=I["bbT"][:, tc], in1=Pp.Ed[:], op=ALU.mult)
                pst = psbf(nextps(C))
                k.tr(pst[:, 0:128], Pp.Kdc[:], C.identb[:], sig=False)
                k.tr(pst[:, 128:256], Pp.Pdc[:], C.identb[:], sig=False)
                k.tr(pst[:, 256:384], I["bvT"][:, tc], C.identb[:])
                k.op("act", "copy", out=Pp.Kdt[:], in_=pst[:, 0:128])
                k.op("act", "copy", out=Pp.Pdt[:], in_=pst[:, 128:256])
                k.op("act", "copy", out=Pp.Vt[:], in_=pst[:, 256:384])
                def prep(i):
                    T = tms[i]
                    k.op("dve", "tensor_scalar", out=T.KiP[:], in0=Pp.Ki[:], scalar1=hm[:, i:i + 1], scalar2=None,
                         op0=ALU.mult)
                    k.op("dve", "tensor_scalar", out=T.PiP[:], in0=Pp.Pi[:], scalar1=hm[:, i:i + 1], scalar2=None,
                         op0=ALU.mult)
                    psA = nextps(C)
                    k.mm(psA[:, 0:128], lhsT=T.PiP[:], rhs=Pp.KKd[:], sig=False)
                    k.mm(psA[:, 128:256], lhsT=T.KiP[:], rhs=Pp.KKd[:], sig=False)
                    k.mm(psA[:, 256:384], lhsT=T.KiP[:], rhs=Pp.Rd[:], sig=False)
                    k.mm(psA[:, 384:512], lhsT=T.PiP[:], rhs=Pp.Rd[:])
                    k.op("dve", "tensor_tensor", out=T.AbT[:], in0=psA[:, 0:128], in1=strictT[:], op=ALU.mult)
                    k.op("dve", "tensor_tensor", out=T.AkT[:], in0=psA[:, 128:256], in1=strictT[:], op=ALU.mult)
                    k.op("dve", "tensor_tensor", out=T.ArT.view(T.ArT.t[:, :].rearrange("p (a b) -> p a b", a=2)),
                         in0=psA.view(psA.t[:, 256:512].rearrange("p (a b) -> p a b", a=2)), in1=msk2i[:],
                         op=ALU.mult)
                    yield
                    psl = nextps(C)
                    k.op("pe", "transpose", out=psl[:, 0:128], in_=T.AbT[:], identity=C.identf[:])
                    k.op("act", "copy", out=T.Ab[:], in_=psl[:, 0:128])
                    yield
                    res = [None]
                    for _ in tri_inv_gen(k, C, T.Ab, T.AbT, T.ws, res):
                        yield
                    k.op("act", "copy", out=T.PTb[:], in_=res[0][:])

                def seq(i):
                    T = tms[i]
                    vs = slice(i * 64, (i + 1) * 64)
                    psz = nextps(C)
                    k.mm(psz[:, 0:64], lhsT=Pp.KKd[:], rhs=Hb[i][:], start=True, stop=False)
                    k.mm(psz[:, 0:64], lhsT=T.AkT[:], rhs=Pp.Vt[:, vs], start=False, stop=True)
                    k.op("act", "copy", out=T.Zb[:], in_=psz[:, 0:64])
                    psu = nextps(C)
                    k.mm(psu[:, 0:64], lhsT=T.PTb[:], rhs=T.Zb[:])
                    k.op("dve", "tensor_scalar", out=T.Un[:], in0=psu[:, 0:64], scalar1=-1.0, scalar2=None,
                         op0=ALU.mult)
                    psy = nextps(C)
                    k.mm(psy[:, 0:64], lhsT=Pp.Rd[:], rhs=Hb[i][:], start=True, stop=False)
                    k.mm(psy[:, 0:64], lhsT=T.ArT[:, 0:128], rhs=Pp.Vt[:, vs], start=False, stop=False)
                    k.mm(psy[:, 0:64], lhsT=T.ArT[:, 128:256], rhs=T.Un[:], start=False, stop=True)
                    psh = nextps(C)
                    k.mm(psh[:, 0:64], lhsT=Pp.Kdt[:], rhs=Pp.Vt[:, vs], start=True, stop=False)
                    k.mm(psh[:, 0:64], lhsT=Pp.Pdt[:], rhs=T.Un[:], start=False, stop=True)
                    k.op("dve", "tensor_scalar", out=T.tmpH[:], in0=Hf[i][:], scalar1=Pp.Ei[:, 127:128], scalar2=None,
                         op0=ALU.mult)
                    k.op("dve", "scalar_tensor_tensor", out=Hf[i][:], in0=psh[:, 0:64], scalar=hm[:, i:i + 1],
                         in1=T.tmpH[:], op0=ALU.mult, op1=ALU.add)
                    k.op("act", "copy", out=Hb[i][:], in_=Hf[i][:])
                    k.op("pool", "memset", ap=T.s1[:], constant=0.0)
                    k.op("pool", "memset", ap=T.s2[:], constant=0.0)
                    k.act(out=T.y[:], in_=psy[:, 0:64], func=AF.Identity, accum_out=T.s1[:])
                    k.act(out=T.junk[:], in_=T.y[:], func=AF.Square, accum_out=T.s2[:])
                    k.op("dve", "tensor_scalar", out=T.mean[:], in0=T.s1[:], scalar1=1.0 / 64.0, scalar2=None,
                         op0=ALU.mult)
                    k.op("dve", "tensor_tensor", out=T.var[:], in0=T.mean[:], in1=T.mean[:], op=ALU.mult)
                    k.op("dve", "scalar_tensor_tensor", out=T.var[:], in0=T.s2[:], scalar=1.0 / 64.0, in1=T.var[:],
                         op0=ALU.mult, op1=ALU.subtract)
                    k.op("dve", "tensor_scalar", out=T.var[:], in0=T.var[:], scalar1=64e-5, scalar2=None, op0=ALU.add)
                    k.act(out=T.var[:], in_=T.var[:], func=AF.Sqrt)
                    k.op("dve", "reciprocal", out=T.rs[:], in_=T.var[:])
                    k.op("dve", "tensor_scalar", out=Pp.yn[:, vs], in0=T.y[:], scalar1=T.mean[:, 0:1],
                         scalar2=T.rs[:, 0:1], op0=ALU.subtract, op1=ALU.mult)
                interleave([prep(0), prep(1)])
                seq(0)
                seq(1)
                pst = psbf(nextps(C))
                k.tr(pst[:, 0:128], Pp.yn[:], C.identb[:])
                k.op("dve", "tensor_scalar", out=Pp.yf[:], in0=pst[:, 0:128], scalar1=gng[:, c:c + 1],
                     scalar2=gnb[:, c:c + 1], op0=ALU.mult, op1=ALU.add)
                k.op("pool", "tensor_tensor", out=Pp.yf[:], in0=Pp.yf[:], in1=I["bbonT"][:, tc], op=ALU.add)
                k.op("pool", "tensor_tensor", out=Pp.yo[:], in0=Pp.yf[:], in1=I["bgT"][:, tc], op=ALU.mult)
                k.dma("sp", TB(N.yT.t)[cs, tc], Pp.yo[:])
        mem_attention(k, C, N)


def phase_mixer_b(k, C, N, j):
    phase_proj_b(k, C, N, j)
    phase_scan_b(k, C, N, j)


class TriWS:
    def __init__(self, k):
        self.x = [k.sb([128, 128], F32) for _ in range(2)]
        self.xt = [k.sb([128, 128], F32) for _ in range(2)]
        self.pt = [k.sb([128, 128], F32) for _ in range(2)]


def tri_inv_gen(k, C, L, LT, ws, res):
    X, XT = L, LT
    PT = ws.pt[0]
    k.op("dve", "tensor_tensor", out=PT[:], in0=C.identf[:], in1=LT[:], op=ALU.subtract)
    for lvl in range(6):
        ps = nextps(C)
        k.mm(ps[:, 0:128], lhsT=XT[:], rhs=X[:])
        if lvl < 5:
            k.mm(ps[:, 128:256], lhsT=X[:], rhs=XT[:])
        X2 = ws.x[lvl % 2]
        k.op("act", "copy", out=X2[:], in_=ps[:, 0:128])
        X2T = ws.xt[lvl % 2]
        if lvl < 5:
            k.op("act", "copy", out=X2T[:], in_=ps[:, 128:256])
        yield
        ps3 = nextps(C)
        k.mm(ps3[:, 0:128], lhsT=X2[:], rhs=PT[:])
        PTn = ws.pt[(lvl + 1) % 2]
        k.op("dve", "tensor_tensor", out=PTn[:], in0=PT[:], in1=ps3[:, 0:128], op=ALU.add)
        X, XT, PT = X2, X2T, PTn
        yield
    res[0] = PT


def interleave(gens):
    alive = list(gens)
    while alive:
        for g in list(alive):
            try:
                next(g)
            except StopIteration:
                alive.remove(g)


EXTRA_SCRATCH = [
    ("gq", [768, S], BF16), ("gk", [768, S], BF16), ("gv", [1536, S], BF16), ("gz", [S, 1536], BF16),
    ("gbg", [S, 24], F32),
]


def phase_proj_c(k, C, N, j):
    with phase(k):
        h = load_hT(k, N.hT, S)
        P = ProjCtx(k, C, h, S, nwt=3, wmax=512)
        W = N.c_w_in
        P.plan([(Ref(W.t[j, :, t * 128:(t + 1) * 128], W.buf), 128) for t in range(24)]
               + [(Ref(W.t[j, :, 3072 + zb * 512:3072 + (zb + 1) * 512], W.buf), 512) for zb in range(3)]
               + [(Ref(W.t[j, :, 4608:4632], W.buf), 24)]
               + [(Ref(W.t[j, :, 4632 + c * 128:4632 + (c + 1) * 128], W.buf), 128) for c in range(4)])
        cwj = k.sb([24, 4, 128], F32)
        k.dma("sp", cwj[:], Ref(N.c_conv.t[j].rearrange("k (t p) -> t k p", p=128), N.c_conv.buf))
        cw = k.sb([128, 4, 24], F32)
        for kk in range(4):
            ps = nextps(C)
            k.op("pe", "transpose", out=ps[:, 0:24], in_=cwj[:, kk, :], identity=C.identf[0:24, 0:24])
            k.op("dve", "tensor_copy", out=cw[:, kk, :], in_=ps[:, 0:24])
        ub = [k.sb([128, 3 + S], F32) for _ in range(2)]
        for t_ in ub:
            k.op("pool", "memset", ap=t_[:, 0:3], constant=0.0)
        cv = [k.sb([128, S], F32) for _ in range(2)]
        sq = k.sb([128, S], BF16)
        rn = k.sb([128, S], F32)
        ob = [k.sb([128, S], BF16) for _ in range(2)]
        for t in range(24):
            u = ub[t % 2]

            def evac(tb, ps, u=u):
                alt_copy(k, tb, u[:, 3 + tb * 512:3 + (tb + 1) * 512], ps[:, :])
            P.F(Ref(W.t[j, :, t * 128:(t + 1) * 128], W.buf), evac)
            c = cv[t % 2]
            k.op("dve", "tensor_scalar", out=c[:], in0=u[:, 3:3 + S], scalar1=cw[:, 3, t:t + 1], scalar2=None,
                 op0=ALU.mult)
            for kk in range(3):
                k.op("dve", "scalar_tensor_tensor", out=c[:], in0=u[:, kk:kk + S], scalar=cw[:, kk, t:t + 1],
                     in1=c[:], op0=ALU.mult, op1=ALU.add)
            k.act(out=c[:], in_=c[:], func=AF.Silu)
            o = ob[t % 2]
            if t < 12:
                k.act(out=sq[:], in_=c[:], func=AF.Square)
                for tb in range(4):
                    ps = nextps(C)
                    k.mm(ps[:, :], lhsT=C.onesb[:], rhs=sq[:, tb * 512:(tb + 1) * 512])
                    k.op("dve", "tensor_scalar", out=rn[:, tb * 512:(tb + 1) * 512], in0=ps[:, :], scalar1=1e-6,
                         scalar2=None, op0=ALU.add)
                k.act(out=rn[:], in_=rn[:], func=AF.Sqrt)
                k.op("dve", "reciprocal", out=rn[:], in_=rn[:])
                k.op("dve", "tensor_tensor", out=o[:], in0=c[:], in1=rn[:], op=ALU.mult)
            else:
                k.op("pool", "tensor_copy", out=o[:], in_=c[:])
            if t < 6:
                dst = N.gq[t * 128:(t + 1) * 128, :]
            elif t < 12:
                dst = N.gk[(t - 6) * 128:(t - 5) * 128, :]
            else:
                dst = N.gv[(t - 12) * 128:(t - 11) * 128, :]
            k.dma("sp", dst, o[:])
        zs = [k.sb([128, 512], BF16) for _ in range(2)]
        for zb in range(3):
            def evz(tt, ps, zb=zb):
                s_ = zs[tt % 2]
                k.act(out=s_[:], in_=ps[:, :], func=AF.Silu)
                k.dma("sp", N.gz[tt * 128:(tt + 1) * 128, zb * 512:(zb + 1) * 512], s_[:])
            P.T(Ref(W.t[j, :, 3072 + zb * 512:3072 + (zb + 1) * 512], W.buf), 512, evz)
        al = k.sb([128, 12], F32)
        k.dma("sp", al[:], Ref(bcast_rows(N.c_a_log.t[j, :], 12), N.c_a_log.buf))
        dtb = k.sb([128, 12], F32)
        k.dma("sp", dtb[:], Ref(bcast_rows(N.c_dt_bias.t[j, :], 12), N.c_dt_bias.buf))
        nea = k.sb([128, 12], F32)
        k.act(out=nea[:], in_=al[:], func=AF.Exp)
        k.op("dve", "tensor_scalar", out=nea[:], in0=nea[:], scalar1=-1.0, scalar2=None, op0=ALU.mult)
        bg = [k.sb([128, 24], F32) for _ in range(2)]

        def evbg(tt, ps):
            s_ = bg[tt % 2]
            k.act(out=s_[:, 0:12], in_=ps[:, 0:12], func=AF.Sigmoid)
            k.op("dve", "tensor_tensor", out=s_[:, 12:24], in0=ps[:, 12:24], in1=dtb[:], op=ALU.add)
            k.act(out=s_[:, 12:24], in_=s_[:, 12:24], func=AF.Exp)
            k.act(out=s_[:, 12:24], in_=s_[:, 12:24], func=AF.Ln, bias=C.onesf[:, 0:1])
            k.op("dve", "tensor_tensor", out=s_[:, 12:24], in0=s_[:, 12:24], in1=nea[:], op=ALU.mult)
            k.dma("sp", N.gbg[tt * 128:(tt + 1) * 128, :], s_[:])
        P.T(Ref(W.t[j, :, 4608:4632], W.buf), 24, evbg)
        stg = [k.sb([128, S], BF16) for _ in range(2)]
        cnt = [0]
        for c in range(4):
            proj_F_to_dram(k, P, Ref(W.t[j, :, 4632 + c * 128:4632 + (c + 1) * 128], W.buf), N.qmT, c * 128, stg, cnt)


class GdnTmp:
    def __init__(self, k):
        f = lambda dt: k.sb([128, 128], dt)
        self.vtok = k.sb([128, 256], BF16)
        self.kdec = f(BF16)
        self.gbc = f(F32)
        self.tmp = f(F32)
        self.DT = f(F32)
        self.L2T = f(F32)
        self.L2 = f(F32)
        self.AT = f(BF16)
        self.PTb = f(BF16)
        self.u = f(F32)
        self.wtok = f(BF16)
        self.wT = f(BF16)
        self.vnew = f(BF16)
        self.ob = f(F32)
        self.o = f(F32)
        self.junk = f(F32)
        self.ss = k.sb([128, 1], F32)
        self.sd = k.sb([128, 1], F32)
        self.rs = k.sb([128, 1], F32)
        self.y = f(F32)
        self.y2 = f(BF16)
        self.yT = f(BF16)
        self.zt = f(BF16)
        self.ws = TriWS(k)


def phase_scan_c(k, C, N, j):
    with phase(k):
        M1 = C.mask(-1, 1, ALU.is_ge)
        strictT = C.mask(-1, 1, ALU.is_gt)
        bgall = k.sb([128, NT, 24], F32)
        k.dma("sp", bgall[:], N.gbg.view(N.gbg.t.rearrange("(t p) c -> p t c", p=128)))
        gc = k.sb([128, NT, 12], F32)
        gl = k.sb([128, NT, 12], F32)
        for n in range(NT):
            ps = nextps(C)
            k.mm(ps[:, 0:12], lhsT=M1[:], rhs=bgall[:, n, 12:24])
            k.mm(ps[:, 16:28], lhsT=C.onesf[:], rhs=bgall[:, n, 12:24])
            k.op("act", "copy", out=gc[:, n, :], in_=ps[:, 0:12])
            k.op("act", "copy", out=gl[:, n, :], in_=ps[:, 16:28])
        egc = k.sb([128, NT, 12], F32)
        egl = k.sb([128, NT, 12], F32)
        edec = k.sb([128, NT, 12], F32)
        qsc = k.sb([128, NT, 12], F32)
        k.act(out=egc[:], in_=gc[:], func=AF.Exp)
        k.act(out=egl[:], in_=gl[:], func=AF.Exp)
        k.op("dve", "tensor_tensor", out=edec[:], in0=gl[:], in1=gc[:], op=ALU.subtract)
        k.act(out=edec[:], in_=edec[:], func=AF.Exp)
        k.op("dve", "tensor_scalar", out=qsc[:], in0=egc[:], scalar1=float(128.0 ** -0.5), scalar2=None, op0=ALU.mult)
        normg = k.sb([128, 128], F32)
        k.dma("sp", normg[:], Ref(bcast_rows(N.c_norm_g.t[j, :], 128), N.c_norm_g.buf))
        kTs = [k.sb([128, S], BF16) for _ in range(2)]
        qTs = [k.sb([128, S], BF16) for _ in range(2)]
        vTs = [k.sb([128, S], BF16) for _ in range(4)]
        KKs = [k.sb([128, 128], F32) for _ in range(2)]
        QKs = [k.sb([128, 128], F32) for _ in range(2)]
        ktoks = [k.sb([128, 128], BF16) for _ in range(2)]
        tmps = [GdnTmp(k) for _ in range(2)]
        H = [k.sb([128, 128], F32) for _ in range(2)]
        Hb = [k.sb([128, 128], BF16) for _ in range(2)]
        un = 0
        sh = 0
        for hq in range(6):
            kT, qT = kTs[hq % 2], qTs[hq % 2]
            k.dma("sp", kT[:], N.gk[hq * 128:(hq + 1) * 128, :])
            k.dma("sp", qT[:], N.gq[hq * 128:(hq + 1) * 128, :])
            vT2 = []
            for i in range(2):
                hv = 2 * hq + i
                vT = vTs[hv % 4]
                k.dma("sp", vT[:], N.gv[hv * 128:(hv + 1) * 128, :])
                vT2.append(vT)
                k.op("pool", "memset", ap=H[i][:], constant=0.0)
                k.op("pool", "memset", ap=Hb[i][:], constant=0.0)
            for n in range(NT):
                tc = slice(n * 128, (n + 1) * 128)
                KK, QK, ktok = KKs[sh % 2], QKs[sh % 2], ktoks[sh % 2]
                sh += 1
                ps = nextps(C)
                k.mm(ps[:, 0:128], lhsT=kT[:, tc], rhs=kT[:, tc])
                k.mm(ps[:, 128:256], lhsT=kT[:, tc], rhs=qT[:, tc])
                k.op("dve", "tensor_tensor", out=KK[:], in0=ps[:, 0:128], in1=strictT[:], op=ALU.mult)
                k.op("dve", "scalar_tensor_tensor", out=QK[:], in0=ps[:, 128:256], scalar=float(128.0 ** -0.5),
                     in1=M1[:], op0=ALU.mult, op1=ALU.mult)
                pst = psbf(nextps(C))
                k.tr(pst[:, 0:128], kT[:, tc], C.identb[:])
                k.op("act", "copy", out=ktok[:], in_=pst[:, 0:128])
                def prep(i):
                    hv = 2 * hq + i
                    T = tmps[i]
                    bcol = bgall[:, n, hv:hv + 1]
                    gcol = bgall[:, n, 12 + hv:13 + hv]
                    pst = psbf(nextps(C))
                    k.tr(pst[:, 0:128], vT2[i][:, tc], C.identb[:])
                    k.op("act", "copy", out=T.vtok[:, 0:128], in_=pst[:, 0:128])
                    k.op("dve", "tensor_scalar", out=T.vtok[:, 128:256], in0=ktok[:], scalar1=egc[:, n, hv:hv + 1],
                         scalar2=None, op0=ALU.mult)
                    k.act(out=T.kdec[:], in_=ktok[:], func=AF.Copy, scale=edec[:, n, hv:hv + 1])
                    k.op("dve", "tensor_scalar", out=T.gbc[:], in0=C.onesf[:], scalar1=gcol, scalar2=None, op0=ALU.mult)
                    psg = nextps(C)
                    k.mm(psg[:, 0:128], lhsT=T.gbc[:], rhs=M1[:])
                    k.op("dve", "tensor_scalar", out=T.tmp[:], in0=psg[:, 0:128], scalar1=gc[:, n, hv:hv + 1],
                         scalar2=0.0, op0=ALU.subtract, op1=ALU.min)
                    k.act(out=T.DT[:], in_=T.tmp[:], func=AF.Exp)
                    k.op("dve", "scalar_tensor_tensor", out=T.L2T[:], in0=KK[:], scalar=bcol, in1=T.DT[:],
                         op0=ALU.mult, op1=ALU.mult)
                    k.op("dve", "tensor_tensor", out=T.AT[:], in0=QK[:], in1=T.DT[:], op=ALU.mult)
                    yield
                    psl = nextps(C)
                    k.op("pe", "transpose", out=psl[:, 0:128], in_=T.L2T[:], identity=C.identf[:])
                    k.op("act", "copy", out=T.L2[:], in_=psl[:, 0:128])
                    yield
                    res = [None]
                    for _ in tri_inv_gen(k, C, T.L2, T.L2T, T.ws, res):
                        yield
                    PT = res[0]
                    k.op("act", "copy", out=T.PTb[:], in_=PT[:])
                    psu = nextps(C)
                    k.mm(psu[:, 0:256], lhsT=T.PTb[:], rhs=T.vtok[:])
                    k.op("dve", "tensor_scalar", out=T.u[:], in0=psu[:, 0:128], scalar1=bcol, scalar2=None, op0=ALU.mult)
                    k.op("dve", "tensor_scalar", out=T.wtok[:], in0=psu[:, 128:256], scalar1=bcol, scalar2=None,
                         op0=ALU.mult)
                    pst = psbf(nextps(C))
                    k.tr(pst[:, 0:128], T.wtok[:], C.identb[:])
                    k.op("act", "copy", out=T.wT[:], in_=pst[:, 0:128])

                def seq(i):
                    hv = 2 * hq + i
                    T = tmps[i]
                    ps1 = nextps(C)
                    k.mm(ps1[:, 0:128], lhsT=T.wT[:], rhs=Hb[i][:])
                    k.op("dve", "tensor_tensor", out=T.vnew[:], in0=T.u[:], in1=ps1[:, 0:128], op=ALU.subtract)
                    pso = nextps(C)
                    k.mm(pso[:, 0:128], lhsT=qT[:, tc], rhs=Hb[i][:])
                    k.mm(pso[:, 128:256], lhsT=T.AT[:], rhs=T.vnew[:])
                    k.op("act", "copy", out=T.ob[:], in_=pso[:, 128:256])
                    k.op("dve", "scalar_tensor_tensor", out=T.o[:], in0=pso[:, 0:128], scalar=qsc[:, n, hv:hv + 1],
                         in1=T.ob[:], op0=ALU.mult, op1=ALU.add)
                    psh = nextps(C)
                    k.mm(psh[:, 0:128], lhsT=T.kdec[:], rhs=T.vnew[:])
                    k.op("dve", "scalar_tensor_tensor", out=H[i][:], in0=H[i][:], scalar=egl[:, n, hv:hv + 1],
                         in1=psh[:, 0:128], op0=ALU.mult, op1=ALU.add)
                    k.op("act", "copy", out=Hb[i][:], in_=H[i][:])
                    k.op("pool", "memset", ap=T.ss[:], constant=0.0)
                    k.act(out=T.junk[:], in_=T.o[:], func=AF.Square, accum_out=T.ss[:])
                    k.op("dve", "tensor_scalar", out=T.sd[:], in0=T.ss[:], scalar1=1.0 / 128.0, scalar2=EPS,
                         op0=ALU.mult, op1=ALU.add)
                    k.act(out=T.sd[:], in_=T.sd[:], func=AF.Sqrt)
                    k.op("dve", "reciprocal", out=T.rs[:], in_=T.sd[:])
                    k.op("dve", "scalar_tensor_tensor", out=T.y[:], in0=T.o[:], scalar=T.rs[:, 0:1], in1=normg[:],
                         op0=ALU.mult, op1=ALU.mult)
                    k.dma("sp", T.zt[:], N.gz[n * 128:(n + 1) * 128, hv * 128:(hv + 1) * 128])
                    k.op("pool", "tensor_tensor", out=T.y2[:], in0=T.y[:], in1=T.zt[:], op=ALU.mult)
                    pst = psbf(nextps(C))
                    k.tr(pst[:, 0:128], T.y2[:], C.identb[:])
                    k.op("act", "copy", out=T.yT[:], in_=pst[:, 0:128])
                    k.dma("sp", TB(N.yT.t)[hv * 128:(hv + 1) * 128, tc], T.yT[:])
                interleave([prep(0), prep(1)])
                seq(0)
                seq(1)
        mem_attention(k, C, N)


def phase_mixer_c(k, C, N, j):
    phase_proj_c(k, C, N, j)
    phase_scan_c(k, C, N, j)


WEIGHT_SHAPES = [
    ("attn_norm", [4, 2048]), ("mem_norm", [4, 2048]), ("w_mem_kv", [4, 2048, 1024]), ("w_out", [4, 2048, 2048]),
    ("ffn_norm", [4, 2048]), ("w_ffn_up", [4, 2048, 11264]), ("ffn_conv", [4, 3, 11264]),
    ("w_ffn_down", [4, 5632, 2048]), ("final_norm", [2048]), ("a_w_in", [2, 2048, 2560]), ("a_sinks", [2, 24]),
    ("b_w_in", [1, 2048, 5568]), ("b_mu", [1, 5056]), ("b_w0", [1, 1536]), ("b_w_decay_up", [1, 96, 1536]),
    ("b_a0", [1, 1536]), ("b_w_iclr_up", [1, 96, 1536]), ("b_w_gate_up", [1, 256, 1536]), ("b_k_k", [1, 1536]),
    ("b_k_a", [1, 1536]), ("b_r_k", [1, 24, 64]), ("b_gn_g", [1, 1536]), ("b_gn_b", [1, 1536]),
    ("c_w_in", [1, 2048, 5144]), ("c_conv", [1, 4, 3072]), ("c_a_log", [1, 12]), ("c_dt_bias", [1, 12]),
    ("c_norm_g", [1, 128]),
]

SCRATCH = [
    ("xs", [S, D], F32), ("hT", [D, S], BF16), ("memhT", [D, 256], BF16), ("memkT", [512, 256], BF16),
    ("memv", [256, 512], BF16), ("qT", [1536, S], BF16), ("kT2", [512, S], BF16), ("v2", [S, 512], BF16),
    ("qmT", [512, S], BF16), ("yT", [D, S], BF16), ("aT", [DFF, S], BF16),
]


def fresh_patch():
    TB.f = lambda self: TB(self.t)


def emit_layer(k, C, N, li, cfg):
    kind, j = li % 3, li // 3
    only = cfg.get("only")

    def ph(name, fn, *a):
        if only is None or name in only:
            fn(*a)
    x_in = N.x if li == list(cfg.get('layers', range(4)))[0] else N.xs
    ph("norm1", phase_norm, k, C, x_in, Ref(N.attn_norm.t[li, :], N.attn_norm.buf), N.hT, S)
    ph("normm", phase_norm, k, C, N.mem, Ref(N.mem_norm.t[li, :], N.mem_norm.buf), N.memhT, 256)
    ph("memkv", phase_mem_kv, k, C, N, li)
    if kind == 0:
        ph("proj", phase_proj_a, k, C, N, j)
        ph("mix", phase_attn_a, k, C, N, j)
    elif kind == 1:
        ph("mix", phase_mixer_b, k, C, N, j)
    else:
        ph("mix", phase_mixer_c, k, C, N, j)
    ph("outproj", phase_outproj, k, C, N, li, x_in, N.xs)
    ph("ffnup", phase_ffn_up, k, C, N, li)
    ph("ffndown", phase_ffn_down, k, C, N, li, N.xs)
    return True


def build(cfg):
    nc = bass.Bass("TRN2", target_bir_lowering=False)
    dump = cfg.get("dump", ())
    with ExitStack() as st:
        k = K(nc, st)
        N = Net()
        N.x = k.dram("x", [S, D], F32, kind="ExternalInput")
        N.mem = k.dram("mem", [256, D], F32, kind="ExternalInput")
        used = cfg.get("weights")
        for name, shape in WEIGHT_SHAPES:
            if used is None or name in used:
                setattr(N, name, k.dram(name, shape, F32, kind="ExternalInput"))
        N.out = k.dram("out", [S, D], F32, kind="ExternalOutput")
        for name, shape, dt in SCRATCH + EXTRA_SCRATCH + B_SCRATCH:
            setattr(N, name, k.dram(name, shape, dt, kind=("ExternalOutput" if name in dump else "Internal")))
        C = setup_consts(k)
        layers = cfg.get("layers", range(4))
        CFG.clear()
        CFG.update(cfg)
        ok = True
        PH["n"] = 0
        PH["max"] = cfg.get("max_phases", 10 ** 9)
        try:
            for li in layers:
                ok = emit_layer(k, C, N, li, cfg)
                if not ok:
                    break
            if ok and cfg.get("final", True):
                phase_final_norm(k, C, N.xs, Ref(N.final_norm.t[:], N.final_norm.buf), N.out)
        except StopBuild:
            pass
        k_barrier(k)
        k.finish()
        k.stats = {n: (e.nins, e.count) for n, e in k.eng.items()}
        print("instr stats", k.stats)
    return nc


_CACHE = {}


def run(inputs, cfg, cores=8):
    key = repr(sorted((a, repr(b)) for a, b in cfg.items()))
    if key not in _CACHE:
        _CACHE[key] = build(cfg)
    nc = _CACHE[key]
    used = cfg.get("weights")
    wts = {n: np.ascontiguousarray(inputs[n], dtype=np.float32) for n, _ in WEIGHT_SHAPES
           if used is None or n in used}
    in_maps = []
    for b in range(cores):
        m = dict(wts)
        m["x"] = np.ascontiguousarray(inputs["x"][b], dtype=np.float32)
        m["mem"] = np.ascontiguousarray(inputs["mem"][b], dtype=np.float32)
        in_maps.append(m)
    return run_bass_kernel_spmd(nc, in_maps, core_ids=list(range(cores)))


def kernel(**inputs):
    res = run(inputs, {"layers": (0, 1, 2, 3)}, cores=8)
    return np.stack([np.asarray(r["out"], dtype=np.float32) for r in res.results], axis=0)
```

```python
import numpy as np
import concourse.bass as bass
import concourse.mybir as mybir
from concourse.bass_utils import run_bass_kernel_spmd

F32 = mybir.dt.float32
BF16 = mybir.dt.bfloat16
I32 = mybir.dt.int32
AF = mybir.ActivationFunctionType
ALU = mybir.AluOpType
AX = mybir.AxisListType


class Buf:
    __slots__ = ("w", "rs")

    def __init__(self):
        self.w = None
        self.rs = {}


class Ref:
    __slots__ = ("ap", "buf")

    def __init__(self, ap, buf):
        self.ap = ap
        self.buf = buf


class TB:
    def __init__(self, t, buf=None):
        self.t = t
        self.buf = buf or Buf()

    def __getitem__(self, idx):
        return Ref(self.t[idx], self.buf)

    def view(self, ap):
        return Ref(ap, self.buf)

    def part(self):
        return TB(self.t, Buf())


class Eng:
    def __init__(self, name, obj, sem):
        self.name = name
        self.obj = obj
        self.sem = sem
        self.count = 0
        self.waited = {}
        self.dma_sems = []
        self.dma_uses = []
        self.rr = 0
        self.nins = 0


WRITE_KW = ("out", "accum_out")


class K:
    def __init__(self, nc, stack, ndma=8):
        self.nc = nc
        self.stack = stack
        self.eng = {}
        for name, obj in (("pe", nc.tensor), ("act", nc.scalar), ("dve", nc.vector),
                          ("pool", nc.gpsimd), ("sp", nc.sync)):
            sem = stack.enter_context(nc.semaphore("s_" + name))
            self.eng[name] = Eng(name, obj, sem)
        for q in ("sp", "act", "pool"):
            E = self.eng[q]
            for i in range(ndma):
                E.dma_sems.append(stack.enter_context(nc.semaphore("d_%s%d" % (q, i))))
                E.dma_uses.append(0)
        self.uid = 0

    def sb(self, shape, dtype, name=None):
        self.uid += 1
        t = self.stack.enter_context(self.nc.sbuf_tensor(name or ("sb%d" % self.uid), list(shape), dtype))
        return TB(t)

    def ps(self, shape, dtype, name=None):
        self.uid += 1
        t = self.stack.enter_context(self.nc.psum_tensor(name or ("ps%d" % self.uid), list(shape), dtype))
        return TB(t)

    def dram(self, name, shape, dtype, kind="Internal"):
        t = self.nc.dram_tensor(name, list(shape), dtype, kind=kind)
        return TB(t.ap())

    def _wait(self, E, evs):
        for sem, val, owner in evs:
            if owner == "pe" and E.name == "pe":
                continue
            key = id(sem)
            if E.waited.get(key, 0) >= val:
                continue
            E.obj.wait_ge(sem, val)
            E.waited[key] = val
            E.nins += 1

    def _deps(self, reads, writes):
        evs = []
        for b in reads:
            if b.w is not None:
                evs.append(b.w)
        for b in writes:
            if b.w is not None:
                evs.append(b.w)
            evs.extend(b.rs.values())
        return evs

    def _record(self, ev, reads, writes):
        key = id(ev[0])
        for b in reads:
            old = b.rs.get(key)
            if old is None or old[1] < ev[1]:
                b.rs[key] = ev
        for b in writes:
            b.w = ev
            b.rs = {}

    def op(self, en, meth, *args, sig=True, R=(), W=(), **kw):
        E = self.eng[en]
        reads = [r.buf if isinstance(r, Ref) else r for r in R]
        writes = [w.buf if isinstance(w, Ref) else w for w in W]
        a2 = []
        for a in args:
            if isinstance(a, Ref):
                reads.append(a.buf)
                a = a.ap
            a2.append(a)
        k2 = {}
        for n, v in kw.items():
            if isinstance(v, Ref):
                (writes if n in WRITE_KW else reads).append(v.buf)
                v = v.ap
            k2[n] = v
        self._wait(E, self._deps(reads, writes))
        ins = getattr(E.obj, meth)(*a2, **k2)
        E.nins += 1
        if sig:
            E.count += 1
            ins.then_inc(E.sem, 1)
            ev = (E.sem, E.count, en)
        else:
            ev = (E.sem, E.count + 1, en)
        self._record(ev, reads, writes)
        return ins

    def dma(self, q, out, in_, **kw):
        E = self.eng[q]
        self._wait(E, self._deps([in_.buf], [out.buf]))
        k = E.rr
        sem = E.dma_sems[k]
        if E.dma_uses[k] > 0:
            self._wait(E, [(sem, 16 * E.dma_uses[k], "dma")])
        E.obj.dma_start(out=out.ap, in_=in_.ap, **kw).then_inc(sem, 16)
        E.nins += 1
        E.dma_uses[k] += 1
        ev = (sem, 16 * E.dma_uses[k], "dma")
        self._record(ev, [in_.buf], [out.buf])
        E.rr = (k + 1) % len(E.dma_sems)

    def finish(self):
        for q in ("sp", "act", "pool"):
            E = self.eng[q]
            for sem, uses in zip(E.dma_sems, E.dma_uses):
                if uses:
                    self._wait(E, [(sem, 16 * uses, "dma")])

    def mm(self, out, lhsT, rhs, start=True, stop=True, sig=None, **kw):
        if sig is None:
            sig = stop
        return self.op("pe", "matmul", out=out, lhsT=lhsT, rhs=rhs, start=start, stop=stop, sig=sig, **kw)

    def tr(self, out, in_, ident, sig=True):
        return self.op("pe", "transpose", out=out, in_=in_, identity=ident, sig=sig)

    def act(self, out, in_, func, **kw):
        return self.op("act", "activation", out=out, in_=in_, func=func, **kw)


from contextlib import ExitStack, contextmanager

S = 2048
D = 2048
NT = 16
NCH = 16
DFF = 5632
NFT = 44
EPS = 1e-6
NEG = -30000.0
WRITE_KW = ("out", "accum_out", "ap")


class Net:
    pass


def k_barrier(k):
    evs = []
    for n, E in k.eng.items():
        if E.count:
            evs.append((E.sem, E.count, n))
        for sem, uses in zip(E.dma_sems, E.dma_uses):
            if uses:
                evs.append((sem, 16 * uses, "dma"))
    for n, E in k.eng.items():
        k._wait(E, evs)


class StopBuild(Exception):
    pass


CFG = {}
PH = {"n": 0, "max": 10 ** 9}


@contextmanager
def phase(k):
    if PH["n"] >= PH["max"]:
        raise StopBuild()
    PH["n"] += 1
    k_barrier(k)
    saved = k.stack
    with ExitStack() as st:
        k.stack = st
        yield
        k_barrier(k)
    k.stack = saved


def psbf(ps):
    return TB(ps.t[:].bitcast(BF16), ps.buf)


def setup_consts(k):
    C = Net()
    C.onesf = k.sb([128, 128], F32)
    k.op("pool", "memset", ap=C.onesf[:], constant=1.0)
    C.onesb = k.sb([128, 128], BF16)
    k.op("dve", "tensor_copy", out=C.onesb[:], in_=C.onesf[:])

    def mask(cm, step, cmp):
        m = k.sb([128, 128], F32)
        k.op("pool", "affine_select", out=m[:], in_=C.onesf[:], pattern=[[step, 128]],
             compare_op=cmp, fill=0.0, base=0, channel_multiplier=cm)
        return m
    C.mask = mask
    C.identf = mask(1, -1, ALU.is_equal)
    C.identb = k.sb([128, 128], BF16)
    k.op("dve", "tensor_copy", out=C.identb[:], in_=C.identf[:])
    C.ps = [k.ps([128, 512], F32) for _ in range(8)]
    C.psi = 0
    return C


def nextps(C):
    p = C.ps[C.psi % 8]
    C.psi += 1
    return p


def bcast_rows(ap1d, n):
    return ap1d.partition_broadcast(128)


class NormBufs:
    def __init__(self, k, with_x=True):
        self.xt = [k.sb([128, D], F32) for _ in range(2)] if with_x else None
        self.junk = k.sb([128, D], BF16)
        self.ss = [k.sb([128, 1], F32) for _ in range(2)]
        self.rstd = [k.sb([128, 1], F32) for _ in range(2)]
        self.sd = [k.sb([128, 1], F32) for _ in range(2)]
        self.xn = [k.sb([128, D], BF16) for _ in range(2)]
        self.hts = [k.sb([128, NCH, 128], BF16) for _ in range(2)]


def rstd_from_ss(k, nb, b):
    k.op("dve", "tensor_scalar", out=nb.sd[b][:], in0=nb.ss[b][:], scalar1=1.0 / D, scalar2=EPS,
         op0=ALU.mult, op1=ALU.add)
    k.act(out=nb.sd[b][:], in_=nb.sd[b][:], func=AF.Sqrt)
    k.op("dve", "reciprocal", out=nb.rstd[b][:], in_=nb.sd[b][:])


def norm_tile(k, C, nb, xt, g_rep, out_tb, t, b):
    k.op("dve", "memset", ap=nb.ss[b][:], constant=0.0)
    k.act(out=nb.junk[:], in_=xt[:], func=AF.Square, accum_out=nb.ss[b][:])
    rstd_from_ss(k, nb, b)
    k.op("dve", "scalar_tensor_tensor", out=nb.xn[b][:], in0=xt[:], scalar=nb.rstd[b][:, 0:1],
         in1=g_rep[:], op0=ALU.mult, op1=ALU.mult)
    for half in range(2):
        ps = psbf(nextps(C))
        for c8 in range(8):
            c = half * 8 + c8
            k.tr(ps[:, c8 * 128:(c8 + 1) * 128], nb.xn[b][:, c * 128:(c + 1) * 128], C.identb[:], sig=(c8 == 7))
        src = ps.view(ps.t[:, :].rearrange("p (c n) -> p c n", c=8))
        if half == 0:
            k.op("act", "copy", out=nb.hts[b][:, 0:8, :], in_=src)
        else:
            k.op("dve", "tensor_copy", out=nb.hts[b][:, 8:16, :], in_=src)
    dst = out_tb.t.rearrange("(c p) n -> p c n", p=128)[:, :, t * 128:(t + 1) * 128]
    k.dma("sp", TB(out_tb.t).view(dst), nb.hts[b][:])


def load_grep(k, gvec_ref):
    g_rep = k.sb([128, D], F32)
    k.dma("sp", g_rep[:], Ref(bcast_rows(gvec_ref.ap, D), gvec_ref.buf))
    return g_rep


def phase_norm(k, C, x_tb, gvec_ref, out_tb, ntok):
    with phase(k):
        g_rep = load_grep(k, gvec_ref)
        nb = NormBufs(k)
        nt_ = ntok // 128
        k.dma("sp", nb.xt[0][:], x_tb[0:128, :])
        for t in range(nt_):
            b = t % 2
            if t + 1 < nt_:
                k.dma("sp", nb.xt[1 - b][:], x_tb[(t + 1) * 128:(t + 2) * 128, :])
            norm_tile(k, C, nb, nb.xt[b], g_rep, out_tb, t, b)


def phase_final_norm(k, C, x_tb, gvec_ref, out_tb):
    with phase(k):
        g_rep = load_grep(k, gvec_ref)
        nb = NormBufs(k)
        ot = [k.sb([128, D], F32) for _ in range(2)]
        k.dma("sp", nb.xt[0][:], x_tb[0:128, :])
        for t in range(NT):
            b = t % 2
            if t + 1 < NT:
                k.dma("sp", nb.xt[1 - b][:], x_tb[(t + 1) * 128:(t + 2) * 128, :])
            k.op("dve", "memset", ap=nb.ss[b][:], constant=0.0)
            k.act(out=nb.junk[:], in_=nb.xt[b][:], func=AF.Square, accum_out=nb.ss[b][:])
            rstd_from_ss(k, nb, b)
            k.op("dve", "scalar_tensor_tensor", out=ot[b][:], in0=nb.xt[b][:], scalar=nb.rstd[b][:, 0:1],
                 in1=g_rep[:], op0=ALU.mult, op1=ALU.mult)
            k.dma("sp", out_tb[t * 128:(t + 1) * 128, :], ot[b][:])


def load_hT(k, hT_tb, ntok):
    h = k.sb([128, NCH, ntok], BF16)
    v = hT_tb.t.rearrange("(c p) n -> p c n", p=128)
    for c0 in range(0, NCH, 4):
        k.dma("sp", h[:, c0:c0 + 4, :], hT_tb.view(v[:, c0:c0 + 4, :]))
    return h


class Stager:
    def __init__(self, k, nbuf=3, elems=2048, engines=("pool",)):
        self.k = k
        self.bufs = [k.sb([128, elems], F32) for _ in range(nbuf)]
        self.elems = elems
        self.i = 0
        self.engines = engines
        self.e = 0

    def load(self, dst, src, shape):
        k = self.k
        a, b = shape
        assert a * b <= self.elems
        st = self.bufs[self.i % len(self.bufs)]
        self.i += 1
        sv = st.view(st.t[:, 0:a * b].rearrange("p (a b) -> p a b", a=a))
        k.dma("sp", sv, src)
        eng = self.engines[self.e % len(self.engines)]
        self.e += 1
        if eng == "act":
            k.op("act", "copy", out=dst, in_=sv)
        else:
            k.op(eng, "tensor_copy", out=dst, in_=sv)


def load_w(k, wt, n, wref, stager):
    v = wref.ap.rearrange("(c p) n -> p c n", p=128)
    for n0 in range(0, n, 128):
        w = min(128, n - n0)
        stager.load(wt[:, :, n0:n0 + w], Ref(v[:, :, n0:n0 + w], wref.buf), (NCH, w))


class ProjCtx:
    def __init__(self, k, C, h_sb, ntok, nwt=3, wmax=128):
        self.k, self.C, self.h, self.ntok = k, C, h_sb, ntok
        self.wts = [k.sb([128, NCH, wmax], BF16) for _ in range(nwt)]
        self.i = 0
        self.stager = Stager(k, nbuf=2)
        self.q = []
        self.qi = 0
        self.loaded = {}

    def plan(self, lst):
        self.q = list(lst)
        self.qi = 0
        self.loaded = {}

    def _load(self, wref, n):
        wt = self.wts[self.i % len(self.wts)]
        self.i += 1
        load_w(self.k, wt, n, wref, self.stager)
        return wt

    def _take(self, wref, n):
        key = repr(wref.ap)
        if self.qi < len(self.q) and repr(self.q[self.qi][0].ap) == key:
            if self.qi not in self.loaded:
                self.loaded[self.qi] = self._load(wref, n)
            wt = self.loaded.pop(self.qi)
            self.qi += 1
            if self.qi < len(self.q):
                nr, nn = self.q[self.qi]
                self.loaded[self.qi] = self._load(nr, nn)
            return wt
        return self._load(wref, n)

    def F(self, wref, evac, n=128):
        k = self.k
        wt = self._take(wref, n)
        for tb in range(self.ntok // 512 if self.ntok >= 512 else 1):
            w = min(512, self.ntok)
            ps = nextps(self.C)
            for c in range(NCH):
                k.mm(ps[0:n, 0:w], lhsT=wt[:, c, 0:n], rhs=self.h[:, c, tb * 512:tb * 512 + w],
                     start=(c == 0), stop=(c == NCH - 1))
            evac(tb, ps)

    def T(self, wref, n, evac):
        k = self.k
        wt = self._take(wref, n)
        for tt in range(self.ntok // 128):
            ps = nextps(self.C)
            for c in range(NCH):
                k.mm(ps[:, 0:n], lhsT=self.h[:, c, tt * 128:(tt + 1) * 128], rhs=wt[:, c, 0:n],
                     start=(c == 0), stop=(c == NCH - 1))
            evac(tt, ps)


def alt_copy(k, i, out, in_):
    if i % 2 == 0:
        k.op("act", "copy", out=out, in_=in_)
    else:
        k.op("dve", "tensor_copy", out=out, in_=in_)


def proj_F_to_dram(k, P, wref, dst_tb, row0, stg, cnt, n=128):
    st = stg[cnt[0] % len(stg)]
    cnt[0] += 1

    def evac(tb, ps):
        w = min(512, P.ntok)
        alt_copy(k, tb, st[0:n, tb * 512:tb * 512 + w], ps[0:n, 0:w])
    P.F(wref, evac, n)
    k.dma("sp", dst_tb[row0:row0 + n, :], st[0:n, 0:P.ntok])


def phase_mem_kv(k, C, N, li):
    with phase(k):
        h = load_hT(k, N.memhT, 256)
        P = ProjCtx(k, C, h, 256, nwt=3, wmax=512)
        stg = [k.sb([128, 512], BF16) for _ in range(2)]
        cnt = [0]
        W = N.w_mem_kv
        P.plan([(Ref(W.t[li, :, j * 128:(j + 1) * 128], W.buf), 128) for j in range(4)]
               + [(Ref(W.t[li, :, 512:1024], W.buf), 512)])
        for j in range(4):
            proj_F_to_dram(k, P, Ref(W.t[li, :, j * 128:(j + 1) * 128], W.buf), N.memkT, j * 128, stg, cnt)
        st2 = [k.sb([128, 512], BF16) for _ in range(2)]

        def evac(tt, ps):
            s = st2[tt % 2]
            alt_copy(k, tt, s[:, :], ps[:, 0:512])
            k.dma("sp", N.memv[tt * 128:(tt + 1) * 128, :], s[:, :])
        P.T(Ref(W.t[li, :, 512:1024], W.buf), 512, evac)


def mem_attention(k, C, N):
    kT = k.sb([128, 4, 256], BF16)
    k.dma("sp", kT[:], N.memkT.view(N.memkT.t.rearrange("(h p) m -> p h m", p=128)))
    mv = k.sb([128, 2, 512], BF16)
    k.dma("sp", mv[:], N.memv.view(N.memv.t.rearrange("(t p) n -> p t n", p=128)))
    qm = [k.sb([128, 4, 512], BF16) for _ in range(2)]
    pt = [k.sb([128, 2, 512], BF16) for _ in range(2)]
    rec = [k.sb([128, 512], F32) for _ in range(2)]
    ym = [k.sb([128, 4, 512], BF16) for _ in range(2)]
    sc = 1.0 / np.sqrt(128.0)
    u = 0
    for tb in range(4):
        q = qm[tb % 2]
        k.dma("sp", q[:], N.qmT.view(N.qmT.t.rearrange("(h p) s -> p h s", p=128)[:, :, tb * 512:(tb + 1) * 512]))
        y = ym[tb % 2]
        for hm in range(4):
            p = pt[u % 2]
            r = rec[u % 2]
            u += 1
            for mt in range(2):
                ps = nextps(C)
                k.mm(ps[:, :], lhsT=kT[:, hm, mt * 128:(mt + 1) * 128], rhs=q[:, hm, :])
                k.act(out=p[:, mt, :], in_=ps[:, :], func=AF.Exp, scale=float(sc))
            pso = nextps(C)
            psd = nextps(C)
            for mt in range(2):
                k.mm(pso[:, :], lhsT=mv[:, mt, hm * 128:(hm + 1) * 128], rhs=p[:, mt, :], start=(mt == 0), stop=(mt == 1))
            for mt in range(2):
                k.mm(psd[:, :], lhsT=C.onesb[:], rhs=p[:, mt, :], start=(mt == 0), stop=(mt == 1))
            k.op("dve", "reciprocal", out=r[:], in_=psd[:, :])
            k.op("dve", "tensor_tensor", out=y[:, hm, :], in0=pso[:, :], in1=r[:], op=ALU.mult)
        dst = N.yT.t[1536:2048, :].rearrange("(h p) s -> p h s", p=128)[:, :, tb * 512:(tb + 1) * 512]
        k.dma("sp", N.yT.view(dst), y[:])


def alibi_slope(h):
    return float(2.0 ** (-8.0 * (h + 1.0) / 24.0))


def phase_proj_a(k, C, N, j):
    with phase(k):
        h = load_hT(k, N.hT, S)
        P = ProjCtx(k, C, h, S, nwt=3, wmax=512)
        stg = [k.sb([128, S], BF16) for _ in range(2)]
        cnt = [0]
        W = N.a_w_in
        P.plan([(Ref(W.t[j, :, c * 128:(c + 1) * 128], W.buf), 128) for c in range(12)]
               + [(Ref(W.t[j, :, 2048 + c * 128:2048 + (c + 1) * 128], W.buf), 128) for c in range(4)]
               + [(Ref(W.t[j, :, 1536 + c * 128:1536 + (c + 1) * 128], W.buf), 128) for c in range(2)]
               + [(Ref(W.t[j, :, 1792:2048], W.buf), 256)])
        for c in range(12):
            proj_F_to_dram(k, P, Ref(W.t[j, :, c * 128:(c + 1) * 128], W.buf), N.qT, c * 128, stg, cnt)
        for c in range(4):
            proj_F_to_dram(k, P, Ref(W.t[j, :, 2048 + c * 128:2048 + (c + 1) * 128], W.buf), N.qmT, c * 128, stg, cnt)
        for c in range(2):
            st = stg[cnt[0] % 2]
            cnt[0] += 1

            def evac(tb, ps, st=st):
                alt_copy(k, tb, st[:, tb * 512:(tb + 1) * 512], ps[:, :])
            P.F(Ref(W.t[j, :, 1536 + c * 128:1536 + (c + 1) * 128], W.buf), evac)
            for gg in range(2):
                g = 2 * c + gg
                for dup in range(2):
                    k.dma("sp", N.kT2[g * 128 + dup * 64:g * 128 + dup * 64 + 64, :], st[gg * 64:(gg + 1) * 64, :])
        st2 = [k.sb([128, 4, 128], BF16) for _ in range(2)]

        def evacv(tt, ps):
            s = st2[tt % 2]
            src = ps.view(ps.t[:, 0:256].rearrange("p (g d) -> p g d", g=4))
            k.op("act", "copy", out=s[:, :, 0:64], in_=src)
            k.op("dve", "tensor_copy", out=s[:, :, 64:128], in_=src)
            k.dma("sp", N.v2.view(N.v2.t[tt * 128:(tt + 1) * 128, :].rearrange("p (g d) -> p g d", g=4)), s[:])
        P.T(Ref(W.t[j, :, 1792:2048], W.buf), 256, evacv)


def phase_attn_a(k, C, N, j):
    with phase(k):
        dist = k.sb([128, 128], F32)
        k.op("pool", "iota", dist[:], pattern=[[1, 128]], base=0, channel_multiplier=-1,
             allow_small_or_imprecise_dtypes=True, W=[dist[:]])
        mbc = k.sb([128, 24, 128], F32)
        mbp = k.sb([128, 24, 128], F32)
        for h in range(24):
            sl = alibi_slope(h)
            k.op("dve", "tensor_scalar", out=mbc[:, h, :], in0=dist[:], scalar1=-sl, scalar2=None, op0=ALU.mult)
            k.op("dve", "tensor_scalar", out=mbp[:, h, :], in0=dist[:], scalar1=-sl, scalar2=-128.0 * sl,
                 op0=ALU.mult, op1=ALU.add)
        k.op("pool", "affine_select", out=mbc[:], in_=mbc[:], pattern=[[0, 24], [1, 128]],
             compare_op=ALU.is_ge, fill=NEG, base=0, channel_multiplier=-1)
        k.op("pool", "affine_select", out=mbp[:], in_=mbp[:], pattern=[[0, 24], [-1, 128]],
             compare_op=ALU.is_gt, fill=NEG, base=0, channel_multiplier=1)
        sk = k.sb([128, 24], F32)
        k.dma("sp", sk[:], Ref(bcast_rows(N.a_sinks.t[j, :], 24), N.a_sinks.buf))
        sinkexp = k.sb([128, 24], F32)
        k.act(out=sinkexp[:], in_=sk[:], func=AF.Exp)
        kTz = k.sb([128, 8, S], BF16)
        k.op("pool", "memset", ap=kTz[:], constant=0.0)
        for g in range(4):
            for hf in range(2):
                k.dma("sp", kTz[hf * 64:hf * 64 + 64, 2 * g + hf, :],
                      N.kT2[g * 128 + hf * 64:g * 128 + hf * 64 + 64, :])
        v2 = k.sb([128, NT, 512], BF16)
        vv = N.v2.t.rearrange("(t p) n -> p t n", p=128)
        for t0 in range(0, NT, 4):
            k.dma("sp", v2[:, t0:t0 + 4, :], N.v2.view(vv[:, t0:t0 + 4, :]))
        qs = [k.sb([128, 12, 512], BF16) for _ in range(2)]
        ys = [k.sb([128, 12, 512], BF16) for _ in range(2)]
        scb = [k.sb([128, 2, 384], F32) for _ in range(2)]
        ptb = [k.sb([128, 2, 384], BF16) for _ in range(2)]
        rcb = [k.sb([128, 3, 128], F32) for _ in range(2)]
        qv = N.qT.t.rearrange("(c p) s -> p c s", p=128)
        yv = N.yT.t[0:1536, :].rearrange("(c p) s -> p c s", p=128)
        u = 0
        for n4 in range(CFG.get("attn_n4", 4)):
            q = qs[n4 % 2]
            y = ys[n4 % 2]
            k.dma("sp", q[:], N.qT.view(qv[:, :, n4 * 512:(n4 + 1) * 512]))
            for nn in range(4):
                n = n4 * 4 + nn
                qc = slice(nn * 128, (nn + 1) * 128)
                kbs = [n] if n == 0 else [n - 1, n]
                for g in range(4):
                    for h3 in range(2):
                        sc_, pt_, rc_ = scb[u % 2], ptb[u % 2], rcb[u % 2]
                        u += 1
                        heads = [g * 6 + h3 * 3 + i for i in range(3)]
                        pss = []
                        for bi, kb in enumerate(kbs):
                            ps = nextps(C)
                            pss.append(ps)
                            for i, hh in enumerate(heads):
                                c, hf = hh // 2, hh % 2
                                k.mm(ps[:, i * 128:(i + 1) * 128], lhsT=kTz[:, 2 * g + hf, kb * 128:(kb + 1) * 128],
                                     rhs=q[:, c, qc], start=True, stop=True, sig=(i == 2))
                        for bi, kb in enumerate(kbs):
                            mb = mbc if kb == n else mbp
                            k.op("dve", "scalar_tensor_tensor",
                                 out=sc_.view(sc_.t[:, bi, :].rearrange("p (h q) -> p h q", h=3)),
                                 in0=pss[bi].view(pss[bi].t[:, 0:384].rearrange("p (h q) -> p h q", h=3)),
                                 scalar=0.125, in1=mb[:, heads[0]:heads[0] + 3, :], op0=ALU.mult, op1=ALU.add)
                            k.act(out=pt_[:, bi, :], in_=sc_[:, bi, :], func=AF.Exp)
                        if CFG.get("attn_stage", 3) < 2:
                            continue
                        pso = nextps(C)
                        psd = nextps(C)
                        nk = len(kbs)
                        for bi, kb in enumerate(kbs):
                            k.mm(pso[:, 0:384], lhsT=v2[:, kb, g * 128:(g + 1) * 128], rhs=pt_[:, bi, :],
                                 start=(bi == 0), stop=(bi == nk - 1))
                        for bi, kb in enumerate(kbs):
                            k.mm(psd[:, 0:384], lhsT=C.onesb[:], rhs=pt_[:, bi, :],
                                 start=(bi == 0), stop=(bi == nk - 1))
                        if CFG.get("attn_stage", 3) < 3:
                            continue
                        for i, hh in enumerate(heads):
                            k.op("dve", "tensor_scalar", out=rc_[:, i, :], in0=psd[:, i * 128:(i + 1) * 128],
                                 scalar1=sinkexp[:, hh:hh + 1], scalar2=None, op0=ALU.add)
                        k.op("dve", "reciprocal", out=rc_[:], in_=rc_[:])
                        for i, hh in enumerate(heads):
                            c, hf = hh // 2, hh % 2
                            rows = slice(hf * 64, hf * 64 + 64)
                            k.op("dve", "tensor_tensor", out=y[rows, c, qc], in0=pso[rows, i * 128:(i + 1) * 128],
                                 in1=rc_[rows, i, :], op=ALU.mult)
            k.dma("sp", N.yT.view(yv[:, :, n4 * 512:(n4 + 1) * 512]), y[:])
        if CFG.get("memattn", True):
            mem_attention(k, C, N)


def phase_outproj(k, C, N, li, x_in, x_out):
    with phase(k):
        g_rep = load_grep(k, Ref(N.ffn_norm.t[li, :], N.ffn_norm.buf))
        nb = NormBufs(k)
        wo = k.sb([128, NCH, D], BF16)
        wv = N.w_out.t[li].rearrange("(c p) n -> p c n", p=128)
        stg = Stager(k, nbuf=3, elems=2048, engines=("pool", "act", "pool", "dve"))
        for c0 in range(NCH):
            stg.load(wo[:, c0:c0 + 1, :], Ref(wv[:, c0:c0 + 1, :], N.w_out.buf), (1, D))
        yts = [k.sb([128, NCH, 128], BF16) for _ in range(2)]
        yv = N.yT.t.rearrange("(c p) s -> p c s", p=128)
        def ld(t):
            k.dma("sp", yts[t % 2][:], N.yT.view(yv[:, :, t * 128:(t + 1) * 128]))
            k.dma("sp", nb.xt[t % 2][:], TB(x_in.t)[t * 128:(t + 1) * 128, :])
        ld(0)
        for t in range(NT):
            b = t % 2
            yt = yts[b]
            xt = nb.xt[b]
            if t + 1 < NT:
                ld(t + 1)
            for nbk in range(4):
                ps = nextps(C)
                for c in range(NCH):
                    k.mm(ps[:, :], lhsT=yt[:, c, :], rhs=wo[:, c, nbk * 512:(nbk + 1) * 512],
                         start=(c == 0), stop=(c == NCH - 1))
                k.op("dve", "tensor_tensor", out=xt[:, nbk * 512:(nbk + 1) * 512],
                     in0=xt[:, nbk * 512:(nbk + 1) * 512], in1=ps[:, :], op=ALU.add)
            k.dma("sp", TB(x_out.t)[t * 128:(t + 1) * 128, :], xt[:])
            norm_tile(k, C, nb, xt, g_rep, N.hT, t, b)


def phase_ffn_up(k, C, N, li):
    with phase(k):
        h = load_hT(k, N.hT, S)
        cwj = k.sb([88, 3, 128], F32)
        k.dma("sp", cwj[:], Ref(N.ffn_conv.t[li].rearrange("k (j p) -> j k p", p=128), N.ffn_conv.buf))
        cw = k.sb([128, 3, 88], F32)
        for kk in range(3):
            ps = nextps(C)
            k.op("pe", "transpose", out=ps[:, 0:88], in_=cwj[:, kk, :], identity=C.identf[0:88, 0:88])
            k.op("dve", "tensor_copy", out=cw[:, kk, :], in_=ps[:, 0:88])
        wg = [k.sb([128, NCH, 128], BF16) for _ in range(3)]
        wvv = [k.sb([128, NCH, 128], BF16) for _ in range(3)]
        ug = [k.sb([128, 2 + S], F32) for _ in range(2)]
        uv = [k.sb([128, 2 + S], F32) for _ in range(2)]
        for t_ in ug + uv:
            k.op("pool", "memset", ap=t_[:, 0:2], constant=0.0)
        cg = [k.sb([128, 1024], F32) for _ in range(2)]
        cv = [k.sb([128, 1024], F32) for _ in range(2)]
        sg = [k.sb([128, 1024], F32) for _ in range(2)]
        ao = [k.sb([128, 1024], BF16) for _ in range(2)]
        stg = Stager(k, nbuf=4)
        W = N.w_ffn_up
        u = 0
        def ldw(j):
            load_w(k, wg[j % 3], 128, Ref(W.t[li, :, j * 128:(j + 1) * 128], W.buf), stg)
            load_w(k, wvv[j % 3], 128, Ref(W.t[li, :, DFF + j * 128:DFF + (j + 1) * 128], W.buf), stg)
        ldw(0)
        for j in range(NFT):
            a, b_ = wg[j % 3], wvv[j % 3]
            if j + 1 < NFT:
                ldw(j + 1)
            ugj, uvj = ug[j % 2], uv[j % 2]
            for half in range(2):
                o = half * 1024
                for which, wt, ub in ((0, a, ugj), (1, b_, uvj)):
                    for tb in range(2):
                        ps = nextps(C)
                        for c in range(NCH):
                            k.mm(ps[:, :], lhsT=wt[:, c, :], rhs=h[:, c, o + tb * 512:o + (tb + 1) * 512],
                                 start=(c == 0), stop=(c == NCH - 1))
                        k.op("act", "copy", out=ub[:, 2 + o + tb * 512:2 + o + (tb + 1) * 512], in_=ps[:, :])
                cgu, cvu, sgu, aou = cg[u % 2], cv[u % 2], sg[u % 2], ao[u % 2]
                u += 1
                for eng, ub, co, jj in (("dve", ugj, cgu, j), ("dve", uvj, cvu, NFT + j)):
                    k.op(eng, "tensor_scalar", out=co[:], in0=ub[:, 2 + o:2 + o + 1024],
                         scalar1=cw[:, 2, jj:jj + 1], scalar2=None, op0=ALU.mult)
                    k.op(eng, "scalar_tensor_tensor", out=co[:], in0=ub[:, 1 + o:1 + o + 1024],
                         scalar=cw[:, 1, jj:jj + 1], in1=co[:], op0=ALU.mult, op1=ALU.add)
                    k.op(eng, "scalar_tensor_tensor", out=co[:], in0=ub[:, o:o + 1024],
                         scalar=cw[:, 0, jj:jj + 1], in1=co[:], op0=ALU.mult, op1=ALU.add)
                k.act(out=sgu[:], in_=cgu[:], func=AF.Silu)
                k.op("pool", "tensor_tensor", out=aou[:], in0=sgu[:], in1=cvu[:], op=ALU.mult)
                k.dma("sp", N.aT[j * 128:(j + 1) * 128, o:o + 1024], aou[:])


def phase_ffn_down(k, C, N, li, x_tb):
    with phase(k):
        wd = k.sb([128, NFT, 1024], BF16)
        ats = [k.sb([128, NFT, 256], BF16) for _ in range(2)]
        xts = [k.sb([128, 2, 1024], F32) for _ in range(2)]
        stg = Stager(k, nbuf=3, elems=2048, engines=("pool", "act", "dve"))
        av = N.aT.t.rearrange("(c p) s -> p c s", p=128)
        W = N.w_ffn_down
        u = 0
        for nh in range(2):
            wv = W.t[li, :, nh * 1024:(nh + 1) * 1024].rearrange("(c p) n -> p c n", p=128)
            for c0 in range(0, NFT, 2):
                stg.load(wd[:, c0:c0 + 2, :], Ref(wv[:, c0:c0 + 2, :], W.buf), (2, 1024))
            def ld(uu, nh_, t2_):
                at_, xt_ = ats[uu % 2], xts[uu % 2]
                for c0 in range(0, NFT, 11):
                    k.dma("sp", at_[:, c0:c0 + 11, :], N.aT.view(av[:, c0:c0 + 11, t2_ * 256:(t2_ + 1) * 256]))
                xv_ = x_tb.t[t2_ * 256:(t2_ + 1) * 256, nh_ * 1024:(nh_ + 1) * 1024].rearrange(
                    "(t p) n -> p t n", p=128)
                k.dma("sp", xt_[:], TB(x_tb.t).view(xv_))
            if nh == 0:
                ld(0, 0, 0)
            for t2 in range(NT // 2):
                at, xt = ats[u % 2], xts[u % 2]
                u += 1
                if t2 + 1 < NT // 2:
                    ld(u, nh, t2 + 1)
                elif nh == 0:
                    ld(u, 1, 0)
                xv = x_tb.t[t2 * 256:(t2 + 1) * 256, nh * 1024:(nh + 1) * 1024].rearrange("(t p) n -> p t n", p=128)
                for ts in range(2):
                    for nbk in range(2):
                        ps = nextps(C)
                        for c in range(NFT):
                            k.mm(ps[:, :], lhsT=at[:, c, ts * 128:(ts + 1) * 128],
                                 rhs=wd[:, c, nbk * 512:(nbk + 1) * 512], start=(c == 0), stop=(c == NFT - 1))
                        k.op("dve", "tensor_tensor", out=xt[:, ts, nbk * 512:(nbk + 1) * 512],
                             in0=xt[:, ts, nbk * 512:(nbk + 1) * 512], in1=ps[:, :], op=ALU.add)
                k.dma("sp", TB(x_tb.t).view(xv), xt[:])


B_SCRATCH = [
    ("brT", [1536, S], BF16), ("bkT", [1536, S], BF16), ("bkkT", [1536, S], BF16), ("bbT", [1536, S], BF16),
    ("bvT", [1536, S], BF16), ("blwT", [1536, S], F32), ("bbonT", [1536, S], BF16), ("bgT", [1536, S], BF16),
]
DECAY_C = 0.6065306597126334


def colvec(k, ref1d, n=1536):
    nc_ = n // 128
    rows = k.sb([nc_, 128], F32)
    k.dma("sp", rows[:], Ref(ref1d.ap.rearrange("(c p) -> c p", p=128), ref1d.buf))
    t = k.sb([128, nc_], F32)
    ps = nextps(CREF[0])
    k.op("pe", "transpose", out=ps[:, 0:nc_], in_=rows[:], identity=CREF[0].identf[0:nc_, 0:nc_])
    k.op("dve", "tensor_copy", out=t[:], in_=ps[:, 0:nc_])
    return t


CREF = [None]


def phase_proj_b(k, C, N, j):
    CREF[0] = C
    with phase(k):
        h = load_hT(k, N.hT, S)
        P = ProjCtx(k, C, h, S, nwt=3, wmax=128)
        W = N.b_w_in
        pl = [(4608, 96), (4704, 96), (4800, 128), (4928, 128)]
        for c in range(12):
            pl += [(3072 + c * 128, 128), (c * 128, 128), (1536 + c * 128, 128)]
        pl += [(5056 + c * 128, 128) for c in range(4)]
        P.plan([(Ref(W.t[j, :, c0:c0 + n], W.buf), n) for c0, n in pl])
        V = lambda name: Ref(getattr(N, name).t[j, :], getattr(N, name).buf)
        w0c, a0c, kkc, kac, gngc = colvec(k, V("b_w0")), colvec(k, V("b_a0")), colvec(k, V("b_k_k")), \
            colvec(k, V("b_k_a")), None
        rkc = colvec(k, Ref(N.b_r_k.t[j].rearrange("h d -> (h d)"), N.b_r_k.buf))
        omka = k.sb([128, 12], F32)
        k.op("dve", "tensor_scalar", out=omka[:], in0=kac[:], scalar1=-1.0, scalar2=1.0, op0=ALU.mult, op1=ALU.add)
        blk = k.sb([128, 128], BF16)
        k.op("pool", "memset", ap=blk[:], constant=0.0)
        k.op("pool", "memset", ap=blk[0:64, 0:64], constant=1.0)
        k.op("pool", "memset", ap=blk[64:128, 64:128], constant=1.0)
        wdec = k.sb([96, 1536], BF16)
        wicl = k.sb([96, 1536], BF16)
        wgt = k.sb([128, 2, 1536], BF16)
        aT = k.sb([128, S], F32)
        t1 = k.sb([128, S], F32)
        rn = k.sb([128, S], F32)
        t2 = rn
        k.dma("sp", aT[0:96, 0:1536], Ref(N.b_w_decay_up.t[j], N.b_w_decay_up.buf))
        k.op("pool", "tensor_copy", out=wdec[:], in_=aT[0:96, 0:1536])
        k.dma("sp", t1[0:96, 0:1536], Ref(N.b_w_iclr_up.t[j], N.b_w_iclr_up.buf))
        k.op("pool", "tensor_copy", out=wicl[:], in_=t1[0:96, 0:1536])
        for kc in range(2):
            k.dma("sp", rn[:, 0:1536], Ref(N.b_w_gate_up.t[j, kc * 128:(kc + 1) * 128, :], N.b_w_gate_up.buf))
            k.op("pool", "tensor_copy", out=wgt[:, kc, :], in_=rn[:, 0:1536])
        ub = [k.sb([128, 1 + S], F32) for _ in range(1)]
        for t_ in ub:
            k.op("pool", "memset", ap=t_[:, 0:1], constant=0.0)
        mus = [k.sb([128, 2], F32) for _ in range(2)]
        mx = [k.sb([128, S], F32) for _ in range(2)]
        cnt = [0]

        def mixed(c0, n):
            i = cnt[0]
            cnt[0] += 1
            u, mu, m = ub[0], mus[i % 2], mx[i % 2]

            def evac(tb, ps):
                alt_copy(k, tb, u[0:n, 1 + tb * 512:1 + (tb + 1) * 512], ps[0:n, :])
            P.F(Ref(W.t[j, :, c0:c0 + n], W.buf), evac, n)
            k.dma("sp", mu[0:n, 0:1], Ref(N.b_mu.t[j, c0:c0 + n].rearrange("(p o) -> p o", o=1), N.b_mu.buf))
            k.op("dve", "tensor_scalar", out=mu[0:n, 1:2], in0=mu[0:n, 0:1], scalar1=-1.0, scalar2=1.0,
                 op0=ALU.mult, op1=ALU.add)
            k.op("dve", "tensor_scalar", out=m[0:n, :], in0=u[0:n, 0:S], scalar1=mu[0:n, 0:1], scalar2=None,
                 op0=ALU.mult)
            k.op("dve", "scalar_tensor_tensor", out=m[0:n, :], in0=u[0:n, 1:1 + S], scalar=mu[0:n, 1:2],
                 in1=m[0:n, :], op0=ALU.mult, op1=ALU.add)
            return m
        twT = k.sb([96, S], BF16)
        adT = k.sb([96, S], BF16)
        sgT = k.sb([128, 2, S], BF16)
        m = mixed(4608, 96)
        k.act(out=twT[:], in_=m[0:96, :], func=AF.Tanh)
        m = mixed(4704, 96)
        k.op("dve", "tensor_copy", out=adT[:], in_=m[0:96, :])
        for i in range(2):
            m = mixed(4800 + i * 128, 128)
            k.act(out=sgT[:, i, :], in_=m[:], func=AF.Sigmoid)
        sq = k.sb([128, S], BF16)
        o16 = [k.sb([128, S], BF16) for _ in range(3)]
        o32 = [k.sb([128, S], F32) for _ in range(1)]
        vb = k.sb([128, S], BF16)
        rb = k.sb([128, S], BF16)
        oc = [0]

        def out16():
            oc[0] += 1
            return o16[oc[0] % 3]
        for c in range(12):
            cs = slice(c * 128, (c + 1) * 128)
            lw = o32[0]
            go = out16()
            for tb in range(4):
                ts_ = slice(tb * 512, (tb + 1) * 512)
                ps = nextps(C)
                k.mm(ps[:, :], lhsT=wicl[:, cs], rhs=adT[:, ts_])
                k.act(out=aT[:, ts_], in_=ps[:, :], func=AF.Sigmoid, bias=a0c[:, c:c + 1])
                ps = nextps(C)
                k.mm(ps[:, :], lhsT=wdec[:, cs], rhs=twT[:, ts_])
                k.act(out=lw[:, ts_], in_=ps[:, :], func=AF.Sigmoid, bias=w0c[:, c:c + 1])
                ps = nextps(C)
                for kc in range(2):
                    k.mm(ps[:, :], lhsT=wgt[:, kc, cs], rhs=sgT[:, kc, ts_], start=(kc == 0), stop=(kc == 1))
                k.op("dve", "tensor_copy", out=go[:, ts_], in_=ps[:, :])
            k.op("dve", "tensor_scalar", out=lw[:], in0=lw[:], scalar1=-DECAY_C, scalar2=None, op0=ALU.mult)
            k.dma("sp", N.blwT[cs, :], lw[:])
            k.dma("sp", N.bgT[cs, :], go[:])
            m = mixed(3072 + c * 128, 128)
            k.op("pool", "tensor_copy", out=vb[:], in_=m[:])
            k.dma("sp", N.bvT[cs, :], vb[:])
            m = mixed(c * 128, 128)
            k.op("pool", "tensor_copy", out=rb[:], in_=m[:])
            k.dma("sp", N.brT[cs, :], rb[:])
            m = mixed(1536 + c * 128, 128)
            k.op("dve", "tensor_scalar", out=t1[:], in0=m[:], scalar1=kkc[:, c:c + 1], scalar2=None, op0=ALU.mult)
            k.act(out=sq[:], in_=t1[:], func=AF.Square)
            for tb in range(4):
                ts_ = slice(tb * 512, (tb + 1) * 512)
                ps = nextps(C)
                k.mm(ps[:, :], lhsT=blk[:], rhs=sq[:, ts_])
                k.op("dve", "tensor_scalar", out=rn[:, ts_], in0=ps[:, :], scalar1=1e-6, scalar2=None, op0=ALU.add)
            k.act(out=rn[:], in_=rn[:], func=AF.Sqrt)
            k.op("dve", "reciprocal", out=rn[:], in_=rn[:])
            kko = out16()
            k.op("dve", "tensor_tensor", out=t1[:], in0=t1[:], in1=rn[:], op=ALU.mult)
            k.op("pool", "tensor_copy", out=kko[:], in_=t1[:])
            k.dma("sp", N.bkkT[cs, :], kko[:])
            bo = out16()
            k.op("dve", "tensor_tensor", out=bo[:], in0=t1[:], in1=aT[:], op=ALU.mult)
            k.dma("sp", N.bbT[cs, :], bo[:])
            k.op("dve", "tensor_scalar", out=t2[:], in0=aT[:], scalar1=kac[:, c:c + 1], scalar2=omka[:, c:c + 1],
                 op0=ALU.mult, op1=ALU.add)
            k.op("dve", "tensor_tensor", out=t2[:], in0=t2[:], in1=m[:], op=ALU.mult)
            ko = out16()
            k.op("pool", "tensor_copy", out=ko[:], in_=t2[:])
            k.dma("sp", N.bkT[cs, :], ko[:])
            k.op("dve", "scalar_tensor_tensor", out=sq[:], in0=t2[:], scalar=rkc[:, c:c + 1], in1=rb[:],
                 op0=ALU.mult, op1=ALU.mult)
            bon = out16()
            for tb in range(4):
                ts_ = slice(tb * 512, (tb + 1) * 512)
                ps = nextps(C)
                k.mm(ps[:, :], lhsT=blk[:], rhs=sq[:, ts_])
                k.op("dve", "tensor_tensor", out=bon[:, ts_], in0=ps[:, :], in1=vb[:, ts_], op=ALU.mult)
            k.dma("sp", N.bbonT[cs, :], bon[:])
        stg = o16[0:2]
        cn2 = [0]
        for c in range(4):
            proj_F_to_dram(k, P, Ref(W.t[j, :, 5056 + c * 128:5056 + (c + 1) * 128], W.buf), N.qmT, c * 128, stg, cn2)


class RwTmp:
    def __init__(self, k):
        f = lambda dt, n=128: k.sb([128, n], dt)
        self.KiP = f(BF16)
        self.PiP = f(BF16)
        self.AbT = f(F32)
        self.Ab = f(F32)
        self.AkT = f(BF16)
        self.ArT = f(BF16, 256)
        self.PTb = f(BF16)
        self.Zb = f(BF16, 64)
        self.Un = f(BF16, 64)
        self.y = f(F32, 64)
        self.junk = f(F32, 64)
        self.s1 = k.sb([128, 1], F32)
        self.s2 = k.sb([128, 1], F32)
        self.mean = k.sb([128, 1], F32)
        self.var = k.sb([128, 1], F32)
        self.rs = k.sb([128, 1], F32)
        self.tmpH = f(F32, 64)
        self.ws = TriWS(k)


class RwPair:
    def __init__(self, k):
        f = lambda dt, n=128: k.sb([128, n], dt)
        self.g = f(F32)
        self.gx = f(F32)
        self.Ei, self.En, self.Ex, self.Ed = f(F32), f(F32), f(F32), f(F32)
        self.Rd, self.Ki, self.Pi, self.KKd, self.Kdc, self.Pdc = (f(BF16) for _ in range(6))
        self.Kdt, self.Pdt, self.Vt = f(BF16), f(BF16), f(BF16)
        self.yn = f(BF16)
        self.yf = f(F32)
        self.yo = f(BF16)


def phase_scan_b(k, C, N, j):
    CREF[0] = C
    with phase(k):
        strictT = C.mask(-1, 1, ALU.is_gt)
        inclT = C.mask(-1, 1, ALU.is_ge)
        msk2i = k.sb([128, 2, 128], F32)
        for i in range(2):
            k.op("dve", "tensor_copy", out=msk2i[:, i, :], in_=inclT[:])
        hm = k.sb([128, 2], F32)
        k.op("pool", "memset", ap=hm[:], constant=0.0)
        k.op("pool", "memset", ap=hm[0:64, 0:1], constant=1.0)
        k.op("pool", "memset", ap=hm[64:128, 1:2], constant=1.0)
        V = lambda name: Ref(getattr(N, name).t[j, :], getattr(N, name).buf)
        gng, gnb = colvec(k, V("b_gn_g")), colvec(k, V("b_gn_b"))
        names = ("brT", "bkT", "bkkT", "bbT", "bvT", "bbonT", "bgT")
        inb = [{nm: k.sb([128, S], BF16) for nm in names} for _ in range(2)]
        lwb = [k.sb([128, S], F32) for _ in range(2)]
        prs = [RwPair(k) for _ in range(2)]
        tms = [[RwTmp(k) for _ in range(2)] for _ in range(2)]
        Hf = [k.sb([128, 64], F32) for _ in range(2)]
        Hb = [k.sb([128, 64], BF16) for _ in range(2)]
        un = 0
        pu = 0
        for c in range(12):
            cs = slice(c * 128, (c + 1) * 128)
            I = inb[c % 2]
            lw = lwb[c % 2]
            for nm in names:
                k.dma("sp", I[nm][:], getattr(N, nm)[cs, :])
            k.dma("sp", lw[:], N.blwT[cs, :])
            for i in range(2):
                k.op("pool", "memset", ap=Hf[i][:], constant=0.0)
                k.op("pool", "memset", ap=Hb[i][:], constant=0.0)
            def pair_prep(n):
                tc = slice(n * 128, (n + 1) * 128)
                Pp = prs[n % 2]
                k.op("dve", "tensor_tensor_scan", out=Pp.g[:], data0=C.onesf[:], data1=lw[:, tc], initial=0.0,
                     op0=ALU.mult, op1=ALU.add)
                k.op("dve", "tensor_tensor", out=Pp.gx[:], in0=Pp.g[:], in1=lw[:, tc], op=ALU.subtract)
                k.act(out=Pp.Ei[:], in_=Pp.g[:], func=AF.Exp)
                k.act(out=Pp.En[:], in_=Pp.g[:], func=AF.Exp, scale=-1.0)
                k.act(out=Pp.Ex[:], in_=Pp.gx[:], func=AF.Exp)
                k.act(out=Pp.Ed[:], in_=Pp.g[:], func=AF.Exp, scale=-1.0, bias=Pp.g[:, 127:128])
                k.op("dve", "tensor_tensor", out=Pp.Rd[:], in0=I["brT"][:, tc], in1=Pp.Ei[:], op=ALU.mult)
                k.op("dve", "tensor_tensor", out=Pp.Ki[:], in0=I["bkT"][:, tc], in1=Pp.En[:], op=ALU.mult)
                k.op("dve", "tensor_tensor", out=Pp.Pi[:], in0=I["bbT"][:, tc], in1=Pp.En[:], op=ALU.mult)
                k.op("dve", "tensor_tensor", out=Pp.KKd[:], in0=I["bkkT"][:, tc], in1=Pp.Ex[:], op=ALU.mult)
                k.op("pool", "tensor_tensor", out=Pp.Kdc[:], in0=I["bkT"][:, tc], in1=Pp.Ed[:], op=ALU.mult)
                k.op("pool", "tensor_tensor", out=Pp.Pdc[:], in0=I["bbT"][:, tc], in1=Pp.Ed[:], op=ALU.mult)
                pst = psbf(nextps(C))
                k.tr(pst[:, 0:128], Pp.Kdc[:], C.identb[:], sig=False)
                k.tr(pst[:, 128:256], Pp.Pdc[:], C.identb[:], sig=False)
                k.tr(pst[:, 256:384], I["bvT"][:, tc], C.identb[:])
                k.op("act", "copy", out=Pp.Kdt[:], in_=pst[:, 0:128])
                k.op("act", "copy", out=Pp.Pdt[:], in_=pst[:, 128:256])
                k.op("act", "copy", out=Pp.Vt[:], in_=pst[:, 256:384])

            def prep(n, i):
                tc = slice(n * 128, (n + 1) * 128)
                Pp = prs[n % 2]
                T = tms[i][n % 2]
                k.op("dve", "tensor_scalar", out=T.KiP[:], in0=Pp.Ki[:], scalar1=hm[:, i:i + 1], scalar2=None,
                     op0=ALU.mult)
                k.op("dve", "tensor_scalar", out=T.PiP[:], in0=Pp.Pi[:], scalar1=hm[:, i:i + 1], scalar2=None,
                     op0=ALU.mult)
                psA = nextps(C)
                k.mm(psA[:, 0:128], lhsT=T.PiP[:], rhs=Pp.KKd[:], sig=False)
                k.mm(psA[:, 128:256], lhsT=T.KiP[:], rhs=Pp.KKd[:], sig=False)
                k.mm(psA[:, 256:384], lhsT=T.KiP[:], rhs=Pp.Rd[:], sig=False)
                k.mm(psA[:, 384:512], lhsT=T.PiP[:], rhs=Pp.Rd[:])
                k.op("dve", "tensor_tensor", out=T.AbT[:], in0=psA[:, 0:128], in1=strictT[:], op=ALU.mult)
                k.op("dve", "tensor_tensor", out=T.AkT[:], in0=psA[:, 128:256], in1=strictT[:], op=ALU.mult)
                k.op("dve", "tensor_tensor", out=T.ArT.view(T.ArT.t[:, :].rearrange("p (a b) -> p a b", a=2)),
                     in0=psA.view(psA.t[:, 256:512].rearrange("p (a b) -> p a b", a=2)), in1=msk2i[:],
                     op=ALU.mult)
                yield
                psl = nextps(C)
                k.op("pe", "transpose", out=psl[:, 0:128], in_=T.AbT[:], identity=C.identf[:])
                k.op("act", "copy", out=T.Ab[:], in_=psl[:, 0:128])
                yield
                res = [None]
                for _ in tri_inv_gen(k, C, T.Ab, T.AbT, T.ws, res):
                    yield
                k.op("act", "copy", out=T.PTb[:], in_=res[0][:])


            def seq(n, i):
                tc = slice(n * 128, (n + 1) * 128)
                Pp = prs[n % 2]
                T = tms[i][n % 2]
                vs = slice(i * 64, (i + 1) * 64)
                psz = nextps(C)
                k.mm(psz[:, 0:64], lhsT=Pp.KKd[:], rhs=Hb[i][:], start=True, stop=False)
                k.mm(psz[:, 0:64], lhsT=T.AkT[:], rhs=Pp.Vt[:, vs], start=False, stop=True)
                k.op("act", "copy", out=T.Zb[:], in_=psz[:, 0:64])
                yield
                psu = nextps(C)
                k.mm(psu[:, 0:64], lhsT=T.PTb[:], rhs=T.Zb[:])
                k.op("dve", "tensor_scalar", out=T.Un[:], in0=psu[:, 0:64], scalar1=-1.0, scalar2=None,
                     op0=ALU.mult)
                yield
                psy = nextps(C)
                k.mm(psy[:, 0:64], lhsT=Pp.Rd[:], rhs=Hb[i][:], start=True, stop=False)
                k.mm(psy[:, 0:64], lhsT=T.ArT[:, 0:128], rhs=Pp.Vt[:, vs], start=False, stop=False)
                k.mm(psy[:, 0:64], lhsT=T.ArT[:, 128:256], rhs=T.Un[:], start=False, stop=True)
                psh = nextps(C)
                k.mm(psh[:, 0:64], lhsT=Pp.Kdt[:], rhs=Pp.Vt[:, vs], start=True, stop=False)
                k.mm(psh[:, 0:64], lhsT=Pp.Pdt[:], rhs=T.Un[:], start=False, stop=True)
                k.op("dve", "tensor_scalar", out=T.tmpH[:], in0=Hf[i][:], scalar1=Pp.Ei[:, 127:128], scalar2=None,
                     op0=ALU.mult)
                k.op("dve", "scalar_tensor_tensor", out=Hf[i][:], in0=psh[:, 0:64], scalar=hm[:, i:i + 1],
                     in1=T.tmpH[:], op0=ALU.mult, op1=ALU.add)
                k.op("act", "copy", out=Hb[i][:], in_=Hf[i][:])
                yield
                k.op("pool", "memset", ap=T.s1[:], constant=0.0)
                k.op("pool", "memset", ap=T.s2[:], constant=0.0)
                k.act(out=T.y[:], in_=psy[:, 0:64], func=AF.Identity, accum_out=T.s1[:])
                k.act(out=T.junk[:], in_=T.y[:], func=AF.Square, accum_out=T.s2[:])
                k.op("dve", "tensor_scalar", out=T.mean[:], in0=T.s1[:], scalar1=1.0 / 64.0, scalar2=None,
                     op0=ALU.mult)
                k.op("dve", "tensor_tensor", out=T.var[:], in0=T.mean[:], in1=T.mean[:], op=ALU.mult)
                k.op("dve", "scalar_tensor_tensor", out=T.var[:], in0=T.s2[:], scalar=1.0 / 64.0, in1=T.var[:],
                     op0=ALU.mult, op1=ALU.subtract)
                k.op("dve", "tensor_scalar", out=T.var[:], in0=T.var[:], scalar1=64e-5, scalar2=None, op0=ALU.add)
                k.act(out=T.var[:], in_=T.var[:], func=AF.Sqrt)
                k.op("dve", "reciprocal", out=T.rs[:], in_=T.var[:])
                k.op("dve", "tensor_scalar", out=Pp.yn[:, vs], in0=T.y[:], scalar1=T.mean[:, 0:1],
                     scalar2=T.rs[:, 0:1], op0=ALU.subtract, op1=ALU.mult)

            def assemble(n):
                tc = slice(n * 128, (n + 1) * 128)
                Pp = prs[n % 2]
                pst = psbf(nextps(C))
                k.tr(pst[:, 0:128], Pp.yn[:], C.identb[:])
                k.op("dve", "tensor_scalar", out=Pp.yf[:], in0=pst[:, 0:128], scalar1=gng[:, c:c + 1],
                     scalar2=gnb[:, c:c + 1], op0=ALU.mult, op1=ALU.add)
                k.op("pool", "tensor_tensor", out=Pp.yf[:], in0=Pp.yf[:], in1=I["bbonT"][:, tc], op=ALU.add)
                k.op("pool", "tensor_tensor", out=Pp.yo[:], in0=Pp.yf[:], in1=I["bgT"][:, tc], op=ALU.mult)
                k.dma("sp", TB(N.yT.t)[cs, tc], Pp.yo[:])

            for step in range(NT + 1):
                gens = []
                if step < NT:
                    pair_prep(step)
                    gens += [prep(step, 0), prep(step, 1)]
                if step >= 1:
                    gens += [seq(step - 1, 0), seq(step - 1, 1)]
                interleave(gens)
                if step >= 1:
                    assemble(step - 1)
        mem_attention(k, C, N)


def phase_mixer_b(k, C, N, j):
    phase_proj_b(k, C, N, j)
    phase_scan_b(k, C, N, j)


class TriWS:
    def __init__(self, k):
        self.x = [k.sb([128, 128], F32) for _ in range(2)]
        self.xt = [k.sb([128, 128], F32) for _ in range(2)]
        self.pt = [k.sb([128, 128], F32) for _ in range(2)]


def tri_inv_gen(k, C, L, LT, ws, res):
    X, XT = L, LT
    PT = ws.pt[0]
    k.op("dve", "tensor_tensor", out=PT[:], in0=C.identf[:], in1=LT[:], op=ALU.subtract)
    for lvl in range(6):
        ps = nextps(C)
        k.mm(ps[:, 0:128], lhsT=XT[:], rhs=X[:])
        if lvl < 5:
            k.mm(ps[:, 128:256], lhsT=X[:], rhs=XT[:])
        X2 = ws.x[lvl % 2]
        k.op("act", "copy", out=X2[:], in_=ps[:, 0:128])
        X2T = ws.xt[lvl % 2]
        if lvl < 5:
            k.op("act", "copy", out=X2T[:], in_=ps[:, 128:256])
        yield
        ps3 = nextps(C)
        k.mm(ps3[:, 0:128], lhsT=X2[:], rhs=PT[:])
        PTn = ws.pt[(lvl + 1) % 2]
        k.op("dve", "tensor_tensor", out=PTn[:], in0=PT[:], in1=ps3[:, 0:128], op=ALU.add)
        X, XT, PT = X2, X2T, PTn
        yield
    res[0] = PT


def interleave(gens):
    alive = list(gens)
    while alive:
        for g in list(alive):
            try:
                next(g)
            except StopIteration:
                alive.remove(g)


EXTRA_SCRATCH = [
    ("gq", [768, S], BF16), ("gk", [768, S], BF16), ("gv", [1536, S], BF16), ("gz", [S, 1536], BF16),
    ("gbg", [S, 24], F32),
]


def phase_proj_c(k, C, N, j):
    with phase(k):
        h = load_hT(k, N.hT, S)
        P = ProjCtx(k, C, h, S, nwt=3, wmax=512)
        W = N.c_w_in
        P.plan([(Ref(W.t[j, :, t * 128:(t + 1) * 128], W.buf), 128) for t in range(24)]
               + [(Ref(W.t[j, :, 3072 + zb * 512:3072 + (zb + 1) * 512], W.buf), 512) for zb in range(3)]
               + [(Ref(W.t[j, :, 4608:4632], W.buf), 24)]
               + [(Ref(W.t[j, :, 4632 + c * 128:4632 + (c + 1) * 128], W.buf), 128) for c in range(4)])
        cwj = k.sb([24, 4, 128], F32)
        k.dma("sp", cwj[:], Ref(N.c_conv.t[j].rearrange("k (t p) -> t k p", p=128), N.c_conv.buf))
        cw = k.sb([128, 4, 24], F32)
        for kk in range(4):
            ps = nextps(C)
            k.op("pe", "transpose", out=ps[:, 0:24], in_=cwj[:, kk, :], identity=C.identf[0:24, 0:24])
            k.op("dve", "tensor_copy", out=cw[:, kk, :], in_=ps[:, 0:24])
        ub = [k.sb([128, 3 + S], F32) for _ in range(2)]
        for t_ in ub:
            k.op("pool", "memset", ap=t_[:, 0:3], constant=0.0)
        cv = [k.sb([128, S], F32) for _ in range(2)]
        sq = k.sb([128, S], BF16)
        rn = k.sb([128, S], F32)
        ob = [k.sb([128, S], BF16) for _ in range(2)]
        for t in range(24):
            u = ub[t % 2]

            def evac(tb, ps, u=u):
                alt_copy(k, tb, u[:, 3 + tb * 512:3 + (tb + 1) * 512], ps[:, :])
            P.F(Ref(W.t[j, :, t * 128:(t + 1) * 128], W.buf), evac)
            c = cv[t % 2]
            k.op("dve", "tensor_scalar", out=c[:], in0=u[:, 3:3 + S], scalar1=cw[:, 3, t:t + 1], scalar2=None,
                 op0=ALU.mult)
            for kk in range(3):
                k.op("dve", "scalar_tensor_tensor", out=c[:], in0=u[:, kk:kk + S], scalar=cw[:, kk, t:t + 1],
                     in1=c[:], op0=ALU.mult, op1=ALU.add)
            k.act(out=c[:], in_=c[:], func=AF.Silu)
            o = ob[t % 2]
            if t < 12:
                k.act(out=sq[:], in_=c[:], func=AF.Square)
                for tb in range(4):
                    ps = nextps(C)
                    k.mm(ps[:, :], lhsT=C.onesb[:], rhs=sq[:, tb * 512:(tb + 1) * 512])
                    k.op("dve", "tensor_scalar", out=rn[:, tb * 512:(tb + 1) * 512], in0=ps[:, :], scalar1=1e-6,
                         scalar2=None, op0=ALU.add)
                k.act(out=rn[:], in_=rn[:], func=AF.Sqrt)
                k.op("dve", "reciprocal", out=rn[:], in_=rn[:])
                k.op("dve", "tensor_tensor", out=o[:], in0=c[:], in1=rn[:], op=ALU.mult)
            else:
                k.op("pool", "tensor_copy", out=o[:], in_=c[:])
            if t < 6:
                dst = N.gq[t * 128:(t + 1) * 128, :]
            elif t < 12:
                dst = N.gk[(t - 6) * 128:(t - 5) * 128, :]
            else:
                dst = N.gv[(t - 12) * 128:(t - 11) * 128, :]
            k.dma("sp", dst, o[:])
        zs = [k.sb([128, 512], BF16) for _ in range(2)]
        for zb in range(3):
            def evz(tt, ps, zb=zb):
                s_ = zs[tt % 2]
                k.act(out=s_[:], in_=ps[:, :], func=AF.Silu)
                k.dma("sp", N.gz[tt * 128:(tt + 1) * 128, zb * 512:(zb + 1) * 512], s_[:])
            P.T(Ref(W.t[j, :, 3072 + zb * 512:3072 + (zb + 1) * 512], W.buf), 512, evz)
        al = k.sb([128, 12], F32)
        k.dma("sp", al[:], Ref(bcast_rows(N.c_a_log.t[j, :], 12), N.c_a_log.buf))
        dtb = k.sb([128, 12], F32)
        k.dma("sp", dtb[:], Ref(bcast_rows(N.c_dt_bias.t[j, :], 12), N.c_dt_bias.buf))
        nea = k.sb([128, 12], F32)
        k.act(out=nea[:], in_=al[:], func=AF.Exp)
        k.op("dve", "tensor_scalar", out=nea[:], in0=nea[:], scalar1=-1.0, scalar2=None, op0=ALU.mult)
        bg = [k.sb([128, 24], F32) for _ in range(2)]

        def evbg(tt, ps):
            s_ = bg[tt % 2]
            k.act(out=s_[:, 0:12], in_=ps[:, 0:12], func=AF.Sigmoid)
            k.op("dve", "tensor_tensor", out=s_[:, 12:24], in0=ps[:, 12:24], in1=dtb[:], op=ALU.add)
            k.act(out=s_[:, 12:24], in_=s_[:, 12:24], func=AF.Exp)
            k.act(out=s_[:, 12:24], in_=s_[:, 12:24], func=AF.Ln, bias=C.onesf[:, 0:1])
            k.op("dve", "tensor_tensor", out=s_[:, 12:24], in0=s_[:, 12:24], in1=nea[:], op=ALU.mult)
            k.dma("sp", N.gbg[tt * 128:(tt + 1) * 128, :], s_[:])
        P.T(Ref(W.t[j, :, 4608:4632], W.buf), 24, evbg)
        stg = [k.sb([128, S], BF16) for _ in range(2)]
        cnt = [0]
        for c in range(4):
            proj_F_to_dram(k, P, Ref(W.t[j, :, 4632 + c * 128:4632 + (c + 1) * 128], W.buf), N.qmT, c * 128, stg, cnt)


class GdnTmp:
    def __init__(self, k):
        f = lambda dt: k.sb([128, 128], dt)
        self.vtok = k.sb([128, 256], BF16)
        self.kdec = f(BF16)
        self.gbc = f(F32)
        self.tmp = f(F32)
        self.DT = f(F32)
        self.L2T = f(F32)
        self.L2 = f(F32)
        self.AT = f(BF16)
        self.PTb = f(BF16)
        self.u = f(F32)
        self.wtok = f(BF16)
        self.wT = f(BF16)
        self.vnew = f(BF16)
        self.ob = f(F32)
        self.o = f(F32)
        self.junk = f(F32)
        self.ss = k.sb([128, 1], F32)
        self.sd = k.sb([128, 1], F32)
        self.rs = k.sb([128, 1], F32)
        self.y = f(F32)
        self.y2 = f(BF16)
        self.yT = f(BF16)
        self.zt = f(BF16)
        self.ws = TriWS(k)


def phase_scan_c(k, C, N, j):
    with phase(k):
        M1 = C.mask(-1, 1, ALU.is_ge)
        strictT = C.mask(-1, 1, ALU.is_gt)
        bgall = k.sb([128, NT, 24], F32)
        k.dma("sp", bgall[:], N.gbg.view(N.gbg.t.rearrange("(t p) c -> p t c", p=128)))
        gc = k.sb([128, NT, 12], F32)
        gl = k.sb([128, NT, 12], F32)
        for n in range(NT):
            ps = nextps(C)
            k.mm(ps[:, 0:12], lhsT=M1[:], rhs=bgall[:, n, 12:24])
            k.mm(ps[:, 16:28], lhsT=C.onesf[:], rhs=bgall[:, n, 12:24])
            k.op("act", "copy", out=gc[:, n, :], in_=ps[:, 0:12])
            k.op("act", "copy", out=gl[:, n, :], in_=ps[:, 16:28])
        egc = k.sb([128, NT, 12], F32)
        egl = k.sb([128, NT, 12], F32)
        edec = k.sb([128, NT, 12], F32)
        qsc = k.sb([128, NT, 12], F32)
        k.act(out=egc[:], in_=gc[:], func=AF.Exp)
        k.act(out=egl[:], in_=gl[:], func=AF.Exp)
        k.op("dve", "tensor_tensor", out=edec[:], in0=gl[:], in1=gc[:], op=ALU.subtract)
        k.act(out=edec[:], in_=edec[:], func=AF.Exp)
        k.op("dve", "tensor_scalar", out=qsc[:], in0=egc[:], scalar1=float(128.0 ** -0.5), scalar2=None, op0=ALU.mult)
        normg = k.sb([128, 128], F32)
        k.dma("sp", normg[:], Ref(bcast_rows(N.c_norm_g.t[j, :], 128), N.c_norm_g.buf))
        kTs = [k.sb([128, S], BF16) for _ in range(2)]
        qTs = [k.sb([128, S], BF16) for _ in range(2)]
        vTs = [k.sb([128, S], BF16) for _ in range(4)]
        KKs = [k.sb([128, 128], F32) for _ in range(2)]
        QKs = [k.sb([128, 128], F32) for _ in range(2)]
        ktoks = [k.sb([128, 128], BF16) for _ in range(2)]
        tmps = [[GdnTmp(k) for _ in range(2)] for _ in range(2)]
        H = [k.sb([128, 128], F32) for _ in range(2)]
        Hb = [k.sb([128, 128], BF16) for _ in range(2)]
        un = 0
        sh = 0
        for hq in range(6):
            kT, qT = kTs[hq % 2], qTs[hq % 2]
            k.dma("sp", kT[:], N.gk[hq * 128:(hq + 1) * 128, :])
            k.dma("sp", qT[:], N.gq[hq * 128:(hq + 1) * 128, :])
            vT2 = []
            for i in range(2):
                hv = 2 * hq + i
                vT = vTs[hv % 4]
                k.dma("sp", vT[:], N.gv[hv * 128:(hv + 1) * 128, :])
                vT2.append(vT)
                k.op("pool", "memset", ap=H[i][:], constant=0.0)
                k.op("pool", "memset", ap=Hb[i][:], constant=0.0)
            def shared_prep(n):
                tc = slice(n * 128, (n + 1) * 128)
                KK, QK, ktok = KKs[n % 2], QKs[n % 2], ktoks[n % 2]
                ps = nextps(C)
                k.mm(ps[:, 0:128], lhsT=kT[:, tc], rhs=kT[:, tc])
                k.mm(ps[:, 128:256], lhsT=kT[:, tc], rhs=qT[:, tc])
                k.op("dve", "tensor_tensor", out=KK[:], in0=ps[:, 0:128], in1=strictT[:], op=ALU.mult)
                k.op("dve", "scalar_tensor_tensor", out=QK[:], in0=ps[:, 128:256], scalar=float(128.0 ** -0.5),
                     in1=M1[:], op0=ALU.mult, op1=ALU.mult)
                pst = psbf(nextps(C))
                k.tr(pst[:, 0:128], kT[:, tc], C.identb[:])
                k.op("act", "copy", out=ktok[:], in_=pst[:, 0:128])

            def prep(n, i):
                tc = slice(n * 128, (n + 1) * 128)
                KK, QK, ktok = KKs[n % 2], QKs[n % 2], ktoks[n % 2]
                hv = 2 * hq + i
                T = tmps[i][n % 2]
                bcol = bgall[:, n, hv:hv + 1]
                gcol = bgall[:, n, 12 + hv:13 + hv]
                pst = psbf(nextps(C))
                k.tr(pst[:, 0:128], vT2[i][:, tc], C.identb[:])
                k.op("act", "copy", out=T.vtok[:, 0:128], in_=pst[:, 0:128])
                k.op("dve", "tensor_scalar", out=T.vtok[:, 128:256], in0=ktok[:], scalar1=egc[:, n, hv:hv + 1],
                     scalar2=None, op0=ALU.mult)
                k.act(out=T.kdec[:], in_=ktok[:], func=AF.Copy, scale=edec[:, n, hv:hv + 1])
                k.op("dve", "tensor_scalar", out=T.gbc[:], in0=C.onesf[:], scalar1=gcol, scalar2=None, op0=ALU.mult)
                psg = nextps(C)
                k.mm(psg[:, 0:128], lhsT=T.gbc[:], rhs=M1[:])
                k.op("dve", "tensor_scalar", out=T.tmp[:], in0=psg[:, 0:128], scalar1=gc[:, n, hv:hv + 1],
                     scalar2=0.0, op0=ALU.subtract, op1=ALU.min)
                k.act(out=T.DT[:], in_=T.tmp[:], func=AF.Exp)
                k.op("dve", "scalar_tensor_tensor", out=T.L2T[:], in0=KK[:], scalar=bcol, in1=T.DT[:],
                     op0=ALU.mult, op1=ALU.mult)
                k.op("dve", "tensor_tensor", out=T.AT[:], in0=QK[:], in1=T.DT[:], op=ALU.mult)
                yield
                psl = nextps(C)
                k.op("pe", "transpose", out=psl[:, 0:128], in_=T.L2T[:], identity=C.identf[:])
                k.op("act", "copy", out=T.L2[:], in_=psl[:, 0:128])
                yield
                res = [None]
                for _ in tri_inv_gen(k, C, T.L2, T.L2T, T.ws, res):
                    yield
                PT = res[0]
                k.op("act", "copy", out=T.PTb[:], in_=PT[:])
                psu = nextps(C)
                k.mm(psu[:, 0:256], lhsT=T.PTb[:], rhs=T.vtok[:])
                k.op("dve", "tensor_scalar", out=T.u[:], in0=psu[:, 0:128], scalar1=bcol, scalar2=None, op0=ALU.mult)
                k.op("dve", "tensor_scalar", out=T.wtok[:], in0=psu[:, 128:256], scalar1=bcol, scalar2=None,
                     op0=ALU.mult)
                pst = psbf(nextps(C))
                k.tr(pst[:, 0:128], T.wtok[:], C.identb[:])
                k.op("act", "copy", out=T.wT[:], in_=pst[:, 0:128])


            def seq(n, i):
                tc = slice(n * 128, (n + 1) * 128)
                hv = 2 * hq + i
                T = tmps[i][n % 2]
                ps1 = nextps(C)
                k.mm(ps1[:, 0:128], lhsT=T.wT[:], rhs=Hb[i][:])
                k.op("dve", "tensor_tensor", out=T.vnew[:], in0=T.u[:], in1=ps1[:, 0:128], op=ALU.subtract)
                yield
                pso = nextps(C)
                k.mm(pso[:, 0:128], lhsT=qT[:, tc], rhs=Hb[i][:])
                k.mm(pso[:, 128:256], lhsT=T.AT[:], rhs=T.vnew[:])
                k.op("act", "copy", out=T.ob[:], in_=pso[:, 128:256])
                k.op("dve", "scalar_tensor_tensor", out=T.o[:], in0=pso[:, 0:128], scalar=qsc[:, n, hv:hv + 1],
                     in1=T.ob[:], op0=ALU.mult, op1=ALU.add)
                psh = nextps(C)
                k.mm(psh[:, 0:128], lhsT=T.kdec[:], rhs=T.vnew[:])
                k.op("dve", "scalar_tensor_tensor", out=H[i][:], in0=H[i][:], scalar=egl[:, n, hv:hv + 1],
                     in1=psh[:, 0:128], op0=ALU.mult, op1=ALU.add)
                k.op("act", "copy", out=Hb[i][:], in_=H[i][:])
                yield
                k.op("pool", "memset", ap=T.ss[:], constant=0.0)
                k.act(out=T.junk[:], in_=T.o[:], func=AF.Square, accum_out=T.ss[:])
                k.op("dve", "tensor_scalar", out=T.sd[:], in0=T.ss[:], scalar1=1.0 / 128.0, scalar2=EPS,
                     op0=ALU.mult, op1=ALU.add)
                k.act(out=T.sd[:], in_=T.sd[:], func=AF.Sqrt)
                k.op("dve", "reciprocal", out=T.rs[:], in_=T.sd[:])
                k.op("dve", "scalar_tensor_tensor", out=T.y[:], in0=T.o[:], scalar=T.rs[:, 0:1], in1=normg[:],
                     op0=ALU.mult, op1=ALU.mult)
                k.dma("sp", T.zt[:], N.gz[n * 128:(n + 1) * 128, hv * 128:(hv + 1) * 128])
                k.op("pool", "tensor_tensor", out=T.y2[:], in0=T.y[:], in1=T.zt[:], op=ALU.mult)
                pst = psbf(nextps(C))
                k.tr(pst[:, 0:128], T.y2[:], C.identb[:])
                k.op("act", "copy", out=T.yT[:], in_=pst[:, 0:128])
                k.dma("sp", TB(N.yT.t)[hv * 128:(hv + 1) * 128, tc], T.yT[:])

            for step in range(NT + 1):
                gens = []
                if step < NT:
                    shared_prep(step)
                    gens += [prep(step, 0), prep(step, 1)]
                if step >= 1:
                    gens += [seq(step - 1, 0), seq(step - 1, 1)]
                interleave(gens)
        mem_attention(k, C, N)


def phase_mixer_c(k, C, N, j):
    phase_proj_c(k, C, N, j)
    phase_scan_c(k, C, N, j)


WEIGHT_SHAPES = [
    ("attn_norm", [4, 2048]), ("mem_norm", [4, 2048]), ("w_mem_kv", [4, 2048, 1024]), ("w_out", [4, 2048, 2048]),
    ("ffn_norm", [4, 2048]), ("w_ffn_up", [4, 2048, 11264]), ("ffn_conv", [4, 3, 11264]),
    ("w_ffn_down", [4, 5632, 2048]), ("final_norm", [2048]), ("a_w_in", [2, 2048, 2560]), ("a_sinks", [2, 24]),
    ("b_w_in", [1, 2048, 5568]), ("b_mu", [1, 5056]), ("b_w0", [1, 1536]), ("b_w_decay_up", [1, 96, 1536]),
    ("b_a0", [1, 1536]), ("b_w_iclr_up", [1, 96, 1536]), ("b_w_gate_up", [1, 256, 1536]), ("b_k_k", [1, 1536]),
    ("b_k_a", [1, 1536]), ("b_r_k", [1, 24, 64]), ("b_gn_g", [1, 1536]), ("b_gn_b", [1, 1536]),
    ("c_w_in", [1, 2048, 5144]), ("c_conv", [1, 4, 3072]), ("c_a_log", [1, 12]), ("c_dt_bias", [1, 12]),
    ("c_norm_g", [1, 128]),
]

SCRATCH = [
    ("xs", [S, D], F32), ("hT", [D, S], BF16), ("memhT", [D, 256], BF16), ("memkT", [512, 256], BF16),
    ("memv", [256, 512], BF16), ("qT", [1536, S], BF16), ("kT2", [512, S], BF16), ("v2", [S, 512], BF16),
    ("qmT", [512, S], BF16), ("yT", [D, S], BF16), ("aT", [DFF, S], BF16),
]


def fresh_patch():
    TB.f = lambda self: TB(self.t)


def emit_layer(k, C, N, li, cfg):
    kind, j = li % 3, li // 3
    only = cfg.get("only")

    def ph(name, fn, *a):
        if only is None or name in only:
            fn(*a)
    x_in = N.x if li == list(cfg.get('layers', range(4)))[0] else N.xs
    ph("norm1", phase_norm, k, C, x_in, Ref(N.attn_norm.t[li, :], N.attn_norm.buf), N.hT, S)
    ph("normm", phase_norm, k, C, N.mem, Ref(N.mem_norm.t[li, :], N.mem_norm.buf), N.memhT, 256)
    ph("memkv", phase_mem_kv, k, C, N, li)
    if kind == 0:
        ph("proj", phase_proj_a, k, C, N, j)
        ph("mix", phase_attn_a, k, C, N, j)
    elif kind == 1:
        ph("mix", phase_mixer_b, k, C, N, j)
    else:
        ph("mix", phase_mixer_c, k, C, N, j)
    ph("outproj", phase_outproj, k, C, N, li, x_in, N.xs)
    ph("ffnup", phase_ffn_up, k, C, N, li)
    ph("ffndown", phase_ffn_down, k, C, N, li, N.xs)
    return True


def build(cfg):
    nc = bass.Bass("TRN2", target_bir_lowering=False)
    dump = cfg.get("dump", ())
    with ExitStack() as st:
        k = K(nc, st)
        N = Net()
        N.x = k.dram("x", [S, D], F32, kind="ExternalInput")
        N.mem = k.dram("mem", [256, D], F32, kind="ExternalInput")
        used = cfg.get("weights")
        for name, shape in WEIGHT_SHAPES:
            if used is None or name in used:
                setattr(N, name, k.dram(name, shape, F32, kind="ExternalInput"))
        N.out = k.dram("out", [S, D], F32, kind="ExternalOutput")
        for name, shape, dt in SCRATCH + EXTRA_SCRATCH + B_SCRATCH:
            setattr(N, name, k.dram(name, shape, dt, kind=("ExternalOutput" if name in dump else "Internal")))
        C = setup_consts(k)
        layers = cfg.get("layers", range(4))
        CFG.clear()
        CFG.update(cfg)
        ok = True
        PH["n"] = 0
        PH["max"] = cfg.get("max_phases", 10 ** 9)
        try:
            for li in layers:
                ok = emit_layer(k, C, N, li, cfg)
                if not ok:
                    break
            if ok and cfg.get("final", True):
                phase_final_norm(k, C, N.xs, Ref(N.final_norm.t[:], N.final_norm.buf), N.out)
        except StopBuild:
            pass
        k_barrier(k)
        k.finish()
        k.stats = {n: (e.nins, e.count) for n, e in k.eng.items()}
        print("instr stats", k.stats)
    return nc


_CACHE = {}


def run(inputs, cfg, cores=8):
    key = repr(sorted((a, repr(b)) for a, b in cfg.items()))
    if key not in _CACHE:
        _CACHE[key] = build(cfg)
    nc = _CACHE[key]
    used = cfg.get("weights")
    wts = {n: np.ascontiguousarray(inputs[n], dtype=np.float32) for n, _ in WEIGHT_SHAPES
           if used is None or n in used}
    in_maps = []
    for b in range(cores):
        m = dict(wts)
        m["x"] = np.ascontiguousarray(inputs["x"][b], dtype=np.float32)
        m["mem"] = np.ascontiguousarray(inputs["mem"][b], dtype=np.float32)
        in_maps.append(m)
    return run_bass_kernel_spmd(nc, in_maps, core_ids=list(range(cores)))


def kernel(**inputs):
    res = run(inputs, {"layers": (0, 1, 2, 3)}, cores=8)
    return np.stack([np.asarray(r["out"], dtype=np.float32) for r in res.results], axis=0)
```

```python
import numpy as np
import concourse.bass as bass
import concourse.mybir as mybir
from concourse.bass_utils import run_bass_kernel_spmd

F32 = mybir.dt.float32
BF16 = mybir.dt.bfloat16
I32 = mybir.dt.int32
AF = mybir.ActivationFunctionType
ALU = mybir.AluOpType
AX = mybir.AxisListType


class Buf:
    __slots__ = ("w", "rs")

    def __init__(self):
        self.w = None
        self.rs = {}


class Ref:
    __slots__ = ("ap", "buf")

    def __init__(self, ap, buf):
        self.ap = ap
        self.buf = buf


class TB:
    def __init__(self, t, buf=None):
        self.t = t
        self.buf = buf or Buf()

    def __getitem__(self, idx):
        return Ref(self.t[idx], self.buf)

    def view(self, ap):
        return Ref(ap, self.buf)

    def part(self):
        return TB(self.t, Buf())


class Eng:
    def __init__(self, name, obj, sem):
        self.name = name
        self.obj = obj
        self.sem = sem
        self.count = 0
        self.waited = {}
        self.dma_sems = []
        self.dma_uses = []
        self.rr = 0
        self.nins = 0


WRITE_KW = ("out", "accum_out")


class K:
    def __init__(self, nc, stack, ndma=8):
        self.nc = nc
        self.stack = stack
        self.eng = {}
        for name, obj in (("pe", nc.tensor), ("act", nc.scalar), ("dve", nc.vector),
                          ("pool", nc.gpsimd), ("sp", nc.sync)):
            sem = stack.enter_context(nc.semaphore("s_" + name))
            self.eng[name] = Eng(name, obj, sem)
        for q in ("sp", "act", "pool"):
            E = self.eng[q]
            for i in range(ndma):
                E.dma_sems.append(stack.enter_context(nc.semaphore("d_%s%d" % (q, i))))
                E.dma_uses.append(0)
        self.uid = 0

    def sb(self, shape, dtype, name=None):
        self.uid += 1
        t = self.stack.enter_context(self.nc.sbuf_tensor(name or ("sb%d" % self.uid), list(shape), dtype))
        return TB(t)

    def ps(self, shape, dtype, name=None):
        self.uid += 1
        t = self.stack.enter_context(self.nc.psum_tensor(name or ("ps%d" % self.uid), list(shape), dtype))
        return TB(t)

    def dram(self, name, shape, dtype, kind="Internal"):
        t = self.nc.dram_tensor(name, list(shape), dtype, kind=kind)
        return TB(t.ap())

    def _wait(self, E, evs):
        for sem, val, owner in evs:
            if owner == "pe" and E.name == "pe":
                continue
            key = id(sem)
            if E.waited.get(key, 0) >= val:
                continue
            E.obj.wait_ge(sem, val)
            E.waited[key] = val
            E.nins += 1

    def _deps(self, reads, writes):
        evs = []
        for b in reads:
            if b.w is not None:
                evs.append(b.w)
        for b in writes:
            if b.w is not None:
                evs.append(b.w)
            evs.extend(b.rs.values())
        return evs

    def _record(self, ev, reads, writes):
        key = id(ev[0])
        for b in reads:
            old = b.rs.get(key)
            if old is None or old[1] < ev[1]:
                b.rs[key] = ev
        for b in writes:
            b.w = ev
            b.rs = {}

    def op(self, en, meth, *args, sig=True, R=(), W=(), **kw):
        E = self.eng[en]
        reads = [r.buf if isinstance(r, Ref) else r for r in R]
        writes = [w.buf if isinstance(w, Ref) else w for w in W]
        a2 = []
        for a in args:
            if isinstance(a, Ref):
                reads.append(a.buf)
                a = a.ap
            a2.append(a)
        k2 = {}
        for n, v in kw.items():
            if isinstance(v, Ref):
                (writes if n in WRITE_KW else reads).append(v.buf)
                v = v.ap
            k2[n] = v
        self._wait(E, self._deps(reads, writes))
        ins = getattr(E.obj, meth)(*a2, **k2)
        E.nins += 1
        if sig:
            E.count += 1
            ins.then_inc(E.sem, 1)
            ev = (E.sem, E.count, en)
        else:
            ev = (E.sem, E.count + 1, en)
        self._record(ev, reads, writes)
        return ins

    def dma(self, q, out, in_, **kw):
        E = self.eng[q]
        self._wait(E, self._deps([in_.buf], [out.buf]))
        k = E.rr
        sem = E.dma_sems[k]
        if E.dma_uses[k] > 0:
            self._wait(E, [(sem, 16 * E.dma_uses[k], "dma")])
        E.obj.dma_start(out=out.ap, in_=in_.ap, **kw).then_inc(sem, 16)
        E.nins += 1
        E.dma_uses[k] += 1
        ev = (sem, 16 * E.dma_uses[k], "dma")
        self._record(ev, [in_.buf], [out.buf])
        E.rr = (k + 1) % len(E.dma_sems)

    def finish(self):
        for q in ("sp", "act", "pool"):
            E = self.eng[q]
            for sem, uses in zip(E.dma_sems, E.dma_uses):
                if uses:
                    self._wait(E, [(sem, 16 * uses, "dma")])

    def mm(self, out, lhsT, rhs, start=True, stop=True, sig=None, **kw):
        if sig is None:
            sig = stop
        return self.op("pe", "matmul", out=out, lhsT=lhsT, rhs=rhs, start=start, stop=stop, sig=sig, **kw)

    def tr(self, out, in_, ident, sig=True):
        return self.op("pe", "transpose", out=out, in_=in_, identity=ident, sig=sig)

    def act(self, out, in_, func, **kw):
        return self.op("act", "activation", out=out, in_=in_, func=func, **kw)


from contextlib import ExitStack, contextmanager

S = 2048
D = 2048
NT = 16
NCH = 16
DFF = 5632
NFT = 44
EPS = 1e-6
NEG = -30000.0
WRITE_KW = ("out", "accum_out", "ap")


class Net:
    pass


def k_barrier(k):
    evs = []
    for n, E in k.eng.items():
        if E.count:
            evs.append((E.sem, E.count, n))
        for sem, uses in zip(E.dma_sems, E.dma_uses):
            if uses:
                evs.append((sem, 16 * uses, "dma"))
    for n, E in k.eng.items():
        k._wait(E, evs)


class StopBuild(Exception):
    pass


CFG = {}
PH = {"n": 0, "max": 10 ** 9}


@contextmanager
def phase(k):
    if PH["n"] >= PH["max"]:
        raise StopBuild()
    PH["n"] += 1
    k_barrier(k)
    saved = k.stack
    with ExitStack() as st:
        k.stack = st
        yield
        k_barrier(k)
    k.stack = saved


def psbf(ps):
    return TB(ps.t[:].bitcast(BF16), ps.buf)


def setup_consts(k):
    C = Net()
    C.onesf = k.sb([128, 128], F32)
    k.op("pool", "memset", ap=C.onesf[:], constant=1.0)
    C.onesb = k.sb([128, 128], BF16)
    k.op("dve", "tensor_copy", out=C.onesb[:], in_=C.onesf[:])

    def mask(cm, step, cmp):
        m = k.sb([128, 128], F32)
        k.op("pool", "affine_select", out=m[:], in_=C.onesf[:], pattern=[[step, 128]],
             compare_op=cmp, fill=0.0, base=0, channel_multiplier=cm)
        return m
    C.mask = mask
    C.identf = mask(1, -1, ALU.is_equal)
    C.identb = k.sb([128, 128], BF16)
    k.op("dve", "tensor_copy", out=C.identb[:], in_=C.identf[:])
    C.ps = [k.ps([128, 512], F32) for _ in range(8)]
    C.psi = 0
    return C


def nextps(C):
    p = C.ps[C.psi % 8]
    C.psi += 1
    return p


def bcast_rows(ap1d, n):
    return ap1d.partition_broadcast(128)


class NormBufs:
    def __init__(self, k, with_x=True):
        self.xt = [k.sb([128, D], F32) for _ in range(2)] if with_x else None
        self.junk = k.sb([128, D], BF16)
        self.ss = [k.sb([128, 1], F32) for _ in range(2)]
        self.rstd = [k.sb([128, 1], F32) for _ in range(2)]
        self.sd = [k.sb([128, 1], F32) for _ in range(2)]
        self.xn = [k.sb([128, D], BF16) for _ in range(2)]
        self.hts = [k.sb([128, NCH, 128], BF16) for _ in range(2)]


def rstd_from_ss(k, nb, b):
    k.op("dve", "tensor_scalar", out=nb.sd[b][:], in0=nb.ss[b][:], scalar1=1.0 / D, scalar2=EPS,
         op0=ALU.mult, op1=ALU.add)
    k.act(out=nb.sd[b][:], in_=nb.sd[b][:], func=AF.Sqrt)
    k.op("dve", "reciprocal", out=nb.rstd[b][:], in_=nb.sd[b][:])


def norm_tile(k, C, nb, xt, g_rep, out_tb, t, b):
    k.op("dve", "memset", ap=nb.ss[b][:], constant=0.0)
    k.act(out=nb.junk[:], in_=xt[:], func=AF.Square, accum_out=nb.ss[b][:])
    rstd_from_ss(k, nb, b)
    k.op("dve", "scalar_tensor_tensor", out=nb.xn[b][:], in0=xt[:], scalar=nb.rstd[b][:, 0:1],
         in1=g_rep[:], op0=ALU.mult, op1=ALU.mult)
    for half in range(2):
        ps = psbf(nextps(C))
        for c8 in range(8):
            c = half * 8 + c8
            k.tr(ps[:, c8 * 128:(c8 + 1) * 128], nb.xn[b][:, c * 128:(c + 1) * 128], C.identb[:], sig=(c8 == 7))
        src = ps.view(ps.t[:, :].rearrange("p (c n) -> p c n", c=8))
        if half == 0:
            k.op("act", "copy", out=nb.hts[b][:, 0:8, :], in_=src)
        else:
            k.op("dve", "tensor_copy", out=nb.hts[b][:, 8:16, :], in_=src)
    dst = out_tb.t.rearrange("(c p) n -> p c n", p=128)[:, :, t * 128:(t + 1) * 128]
    k.dma("sp", TB(out_tb.t).view(dst), nb.hts[b][:])


def load_grep(k, gvec_ref):
    g_rep = k.sb([128, D], F32)
    k.dma("sp", g_rep[:], Ref(bcast_rows(gvec_ref.ap, D), gvec_ref.buf))
    return g_rep


def phase_norm(k, C, x_tb, gvec_ref, out_tb, ntok):
    with phase(k):
        g_rep = load_grep(k, gvec_ref)
        nb = NormBufs(k)
        nt_ = ntok // 128
        k.dma("sp", nb.xt[0][:], x_tb[0:128, :])
        for t in range(nt_):
            b = t % 2
            if t + 1 < nt_:
                k.dma("sp", nb.xt[1 - b][:], x_tb[(t + 1) * 128:(t + 2) * 128, :])
            norm_tile(k, C, nb, nb.xt[b], g_rep, out_tb, t, b)


def phase_final_norm(k, C, x_tb, gvec_ref, out_tb):
    with phase(k):
        g_rep = load_grep(k, gvec_ref)
        nb = NormBufs(k)
        ot = [k.sb([128, D], F32) for _ in range(2)]
        k.dma("sp", nb.xt[0][:], x_tb[0:128, :])
        for t in range(NT):
            b = t % 2
            if t + 1 < NT:
                k.dma("sp", nb.xt[1 - b][:], x_tb[(t + 1) * 128:(t + 2) * 128, :])
            k.op("dve", "memset", ap=nb.ss[b][:], constant=0.0)
            k.act(out=nb.junk[:], in_=nb.xt[b][:], func=AF.Square, accum_out=nb.ss[b][:])
            rstd_from_ss(k, nb, b)
            k.op("dve", "scalar_tensor_tensor", out=ot[b][:], in0=nb.xt[b][:], scalar=nb.rstd[b][:, 0:1],
                 in1=g_rep[:], op0=ALU.mult, op1=ALU.mult)
            k.dma("sp", out_tb[t * 128:(t + 1) * 128, :], ot[b][:])


def load_hT(k, hT_tb, ntok):
    h = k.sb([128, NCH, ntok], BF16)
    v = hT_tb.t.rearrange("(c p) n -> p c n", p=128)
    for c0 in range(0, NCH, 4):
        k.dma("sp", h[:, c0:c0 + 4, :], hT_tb.view(v[:, c0:c0 + 4, :]))
    return h


class Stager:
    def __init__(self, k, nbuf=3, elems=2048, engines=("pool",)):
        self.k = k
        self.bufs = [k.sb([128, elems], F32) for _ in range(nbuf)]
        self.elems = elems
        self.i = 0
        self.engines = engines
        self.e = 0

    def load(self, dst, src, shape):
        k = self.k
        a, b = shape
        assert a * b <= self.elems
        st = self.bufs[self.i % len(self.bufs)]
        self.i += 1
        sv = st.view(st.t[:, 0:a * b].rearrange("p (a b) -> p a b", a=a))
        k.dma("sp", sv, src)
        eng = self.engines[self.e % len(self.engines)]
        self.e += 1
        if eng == "act":
            k.op("act", "copy", out=dst, in_=sv)
        else:
            k.op(eng, "tensor_copy", out=dst, in_=sv)


def load_w(k, wt, n, wref, stager):
    v = wref.ap.rearrange("(c p) n -> p c n", p=128)
    for n0 in range(0, n, 128):
        w = min(128, n - n0)
        stager.load(wt[:, :, n0:n0 + w], Ref(v[:, :, n0:n0 + w], wref.buf), (NCH, w))


class ProjCtx:
    def __init__(self, k, C, h_sb, ntok, nwt=3, wmax=128):
        self.k, self.C, self.h, self.ntok = k, C, h_sb, ntok
        self.wts = [k.sb([128, NCH, wmax], BF16) for _ in range(nwt)]
        self.i = 0
        self.stager = Stager(k, nbuf=2)
        self.q = []
        self.qi = 0
        self.loaded = {}

    def plan(self, lst):
        self.q = list(lst)
        self.qi = 0
        self.loaded = {}

    def _load(self, wref, n):
        wt = self.wts[self.i % len(self.wts)]
        self.i += 1
        load_w(self.k, wt, n, wref, self.stager)
        return wt

    def _take(self, wref, n):
        key = repr(wref.ap)
        if self.qi < len(self.q) and repr(self.q[self.qi][0].ap) == key:
            if self.qi not in self.loaded:
                self.loaded[self.qi] = self._load(wref, n)
            wt = self.loaded.pop(self.qi)
            self.qi += 1
            if self.qi < len(self.q):
                nr, nn = self.q[self.qi]
                self.loaded[self.qi] = self._load(nr, nn)
            return wt
        return self._load(wref, n)

    def F(self, wref, evac, n=128):
        k = self.k
        wt = self._take(wref, n)
        for tb in range(self.ntok // 512 if self.ntok >= 512 else 1):
            w = min(512, self.ntok)
            ps = nextps(self.C)
            for c in range(NCH):
                k.mm(ps[0:n, 0:w], lhsT=wt[:, c, 0:n], rhs=self.h[:, c, tb * 512:tb * 512 + w],
                     start=(c == 0), stop=(c == NCH - 1))
            evac(tb, ps)

    def T(self, wref, n, evac):
        k = self.k
        wt = self._take(wref, n)
        for tt in range(self.ntok // 128):
            ps = nextps(self.C)
            for c in range(NCH):
                k.mm(ps[:, 0:n], lhsT=self.h[:, c, tt * 128:(tt + 1) * 128], rhs=wt[:, c, 0:n],
                     start=(c == 0), stop=(c == NCH - 1))
            evac(tt, ps)


def alt_copy(k, i, out, in_):
    if i % 2 == 0:
        k.op("act", "copy", out=out, in_=in_)
    else:
        k.op("dve", "tensor_copy", out=out, in_=in_)


def proj_F_to_dram(k, P, wref, dst_tb, row0, stg, cnt, n=128):
    st = stg[cnt[0] % len(stg)]
    cnt[0] += 1

    def evac(tb, ps):
        w = min(512, P.ntok)
        alt_copy(k, tb, st[0:n, tb * 512:tb * 512 + w], ps[0:n, 0:w])
    P.F(wref, evac, n)
    k.dma("sp", dst_tb[row0:row0 + n, :], st[0:n, 0:P.ntok])


def phase_mem_kv(k, C, N, li):
    with phase(k):
        h = load_hT(k, N.memhT, 256)
        P = ProjCtx(k, C, h, 256, nwt=3, wmax=512)
        stg = [k.sb([128, 512], BF16) for _ in range(2)]
        cnt = [0]
        W = N.w_mem_kv
        P.plan([(Ref(W.t[li, :, j * 128:(j + 1) * 128], W.buf), 128) for j in range(4)]
               + [(Ref(W.t[li, :, 512:1024], W.buf), 512)])
        for j in range(4):
            proj_F_to_dram(k, P, Ref(W.t[li, :, j * 128:(j + 1) * 128], W.buf), N.memkT, j * 128, stg, cnt)
        st2 = [k.sb([128, 512], BF16) for _ in range(2)]

        def evac(tt, ps):
            s = st2[tt % 2]
            alt_copy(k, tt, s[:, :], ps[:, 0:512])
            k.dma("sp", N.memv[tt * 128:(tt + 1) * 128, :], s[:, :])
        P.T(Ref(W.t[li, :, 512:1024], W.buf), 512, evac)


def mem_attention(k, C, N):
    kT = k.sb([128, 4, 256], BF16)
    k.dma("sp", kT[:], N.memkT.view(N.memkT.t.rearrange("(h p) m -> p h m", p=128)))
    mv = k.sb([128, 2, 512], BF16)
    k.dma("sp", mv[:], N.memv.view(N.memv.t.rearrange("(t p) n -> p t n", p=128)))
    qm = [k.sb([128, 4, 512], BF16) for _ in range(2)]
    pt = [k.sb([128, 2, 512], BF16) for _ in range(2)]
    rec = [k.sb([128, 512], F32) for _ in range(2)]
    ym = [k.sb([128, 4, 512], BF16) for _ in range(2)]
    sc = 1.0 / np.sqrt(128.0)
    u = 0
    for tb in range(4):
        q = qm[tb % 2]
        k.dma("sp", q[:], N.qmT.view(N.qmT.t.rearrange("(h p) s -> p h s", p=128)[:, :, tb * 512:(tb + 1) * 512]))
        y = ym[tb % 2]
        for hm in range(4):
            p = pt[u % 2]
            r = rec[u % 2]
            u += 1
            for mt in range(2):
                ps = nextps(C)
                k.mm(ps[:, :], lhsT=kT[:, hm, mt * 128:(mt + 1) * 128], rhs=q[:, hm, :])
                k.act(out=p[:, mt, :], in_=ps[:, :], func=AF.Exp, scale=float(sc))
            pso = nextps(C)
            psd = nextps(C)
            for mt in range(2):
                k.mm(pso[:, :], lhsT=mv[:, mt, hm * 128:(hm + 1) * 128], rhs=p[:, mt, :], start=(mt == 0), stop=(mt == 1))
            for mt in range(2):
                k.mm(psd[:, :], lhsT=C.onesb[:], rhs=p[:, mt, :], start=(mt == 0), stop=(mt == 1))
            k.op("dve", "reciprocal", out=r[:], in_=psd[:, :])
            k.op("dve", "tensor_tensor", out=y[:, hm, :], in0=pso[:, :], in1=r[:], op=ALU.mult)
        dst = N.yT.t[1536:2048, :].rearrange("(h p) s -> p h s", p=128)[:, :, tb * 512:(tb + 1) * 512]
        k.dma("sp", N.yT.view(dst), y[:])


def alibi_slope(h):
    return float(2.0 ** (-8.0 * (h + 1.0) / 24.0))


def phase_proj_a(k, C, N, j):
    with phase(k):
        h = load_hT(k, N.hT, S)
        P = ProjCtx(k, C, h, S, nwt=3, wmax=512)
        stg = [k.sb([128, S], BF16) for _ in range(2)]
        cnt = [0]
        W = N.a_w_in
        P.plan([(Ref(W.t[j, :, c * 128:(c + 1) * 128], W.buf), 128) for c in range(12)]
               + [(Ref(W.t[j, :, 2048 + c * 128:2048 + (c + 1) * 128], W.buf), 128) for c in range(4)]
               + [(Ref(W.t[j, :, 1536 + c * 128:1536 + (c + 1) * 128], W.buf), 128) for c in range(2)]
               + [(Ref(W.t[j, :, 1792:2048], W.buf), 256)])
        for c in range(12):
            proj_F_to_dram(k, P, Ref(W.t[j, :, c * 128:(c + 1) * 128], W.buf), N.qT, c * 128, stg, cnt)
        for c in range(4):
            proj_F_to_dram(k, P, Ref(W.t[j, :, 2048 + c * 128:2048 + (c + 1) * 128], W.buf), N.qmT, c * 128, stg, cnt)
        for c in range(2):
            st = stg[cnt[0] % 2]
            cnt[0] += 1

            def evac(tb, ps, st=st):
                alt_copy(k, tb, st[:, tb * 512:(tb + 1) * 512], ps[:, :])
            P.F(Ref(W.t[j, :, 1536 + c * 128:1536 + (c + 1) * 128], W.buf), evac)
            for gg in range(2):
                g = 2 * c + gg
                for dup in range(2):
                    k.dma("sp", N.kT2[g * 128 + dup * 64:g * 128 + dup * 64 + 64, :], st[gg * 64:(gg + 1) * 64, :])
        st2 = [k.sb([128, 4, 128], BF16) for _ in range(2)]

        def evacv(tt, ps):
            s = st2[tt % 2]
            src = ps.view(ps.t[:, 0:256].rearrange("p (g d) -> p g d", g=4))
            k.op("act", "copy", out=s[:, :, 0:64], in_=src)
            k.op("dve", "tensor_copy", out=s[:, :, 64:128], in_=src)
            k.dma("sp", N.v2.view(N.v2.t[tt * 128:(tt + 1) * 128, :].rearrange("p (g d) -> p g d", g=4)), s[:])
        P.T(Ref(W.t[j, :, 1792:2048], W.buf), 256, evacv)


def phase_attn_a(k, C, N, j):
    with phase(k):
        dist = k.sb([128, 128], F32)
        k.op("pool", "iota", dist[:], pattern=[[1, 128]], base=0, channel_multiplier=-1,
             allow_small_or_imprecise_dtypes=True, W=[dist[:]])
        mbc = k.sb([128, 24, 128], F32)
        mbp = k.sb([128, 24, 128], F32)
        for h in range(24):
            sl = alibi_slope(h)
            k.op("dve", "tensor_scalar", out=mbc[:, h, :], in0=dist[:], scalar1=-sl, scalar2=None, op0=ALU.mult)
            k.op("dve", "tensor_scalar", out=mbp[:, h, :], in0=dist[:], scalar1=-sl, scalar2=-128.0 * sl,
                 op0=ALU.mult, op1=ALU.add)
        k.op("pool", "affine_select", out=mbc[:], in_=mbc[:], pattern=[[0, 24], [1, 128]],
             compare_op=ALU.is_ge, fill=NEG, base=0, channel_multiplier=-1)
        k.op("pool", "affine_select", out=mbp[:], in_=mbp[:], pattern=[[0, 24], [-1, 128]],
             compare_op=ALU.is_gt, fill=NEG, base=0, channel_multiplier=1)
        sk = k.sb([128, 24], F32)
        k.dma("sp", sk[:], Ref(bcast_rows(N.a_sinks.t[j, :], 24), N.a_sinks.buf))
        sinkexp = k.sb([128, 24], F32)
        k.act(out=sinkexp[:], in_=sk[:], func=AF.Exp)
        kTz = k.sb([128, 8, S], BF16)
        k.op("pool", "memset", ap=kTz[:], constant=0.0)
        for g in range(4):
            for hf in range(2):
                k.dma("sp", kTz[hf * 64:hf * 64 + 64, 2 * g + hf, :],
                      N.kT2[g * 128 + hf * 64:g * 128 + hf * 64 + 64, :])
        v2 = k.sb([128, NT, 512], BF16)
        vv = N.v2.t.rearrange("(t p) n -> p t n", p=128)
        for t0 in range(0, NT, 4):
            k.dma("sp", v2[:, t0:t0 + 4, :], N.v2.view(vv[:, t0:t0 + 4, :]))
        qs = [k.sb([128, 12, 512], BF16) for _ in range(2)]
        ys = [k.sb([128, 12, 512], BF16) for _ in range(2)]
        scb = [k.sb([128, 2, 384], F32) for _ in range(2)]
        ptb = [k.sb([128, 2, 384], BF16) for _ in range(2)]
        rcb = [k.sb([128, 3, 128], F32) for _ in range(2)]
        qv = N.qT.t.rearrange("(c p) s -> p c s", p=128)
        yv = N.yT.t[0:1536, :].rearrange("(c p) s -> p c s", p=128)
        u = 0
        for n4 in range(CFG.get("attn_n4", 4)):
            q = qs[n4 % 2]
            y = ys[n4 % 2]
            k.dma("sp", q[:], N.qT.view(qv[:, :, n4 * 512:(n4 + 1) * 512]))
            for nn in range(4):
                n = n4 * 4 + nn
                qc = slice(nn * 128, (nn + 1) * 128)
                kbs = [n] if n == 0 else [n - 1, n]
                for g in range(4):
                    for h3 in range(2):
                        sc_, pt_, rc_ = scb[u % 2], ptb[u % 2], rcb[u % 2]
                        u += 1
                        heads = [g * 6 + h3 * 3 + i for i in range(3)]
                        pss = []
                        for bi, kb in enumerate(kbs):
                            ps = nextps(C)
                            pss.append(ps)
                            for i, hh in enumerate(heads):
                                c, hf = hh // 2, hh % 2
                                k.mm(ps[:, i * 128:(i + 1) * 128], lhsT=kTz[:, 2 * g + hf, kb * 128:(kb + 1) * 128],
                                     rhs=q[:, c, qc], start=True, stop=True, sig=(i == 2))
                        for bi, kb in enumerate(kbs):
                            mb = mbc if kb == n else mbp
                            k.op("dve", "scalar_tensor_tensor",
                                 out=sc_.view(sc_.t[:, bi, :].rearrange("p (h q) -> p h q", h=3)),
                                 in0=pss[bi].view(pss[bi].t[:, 0:384].rearrange("p (h q) -> p h q", h=3)),
                                 scalar=0.125, in1=mb[:, heads[0]:heads[0] + 3, :], op0=ALU.mult, op1=ALU.add)
                            k.act(out=pt_[:, bi, :], in_=sc_[:, bi, :], func=AF.Exp)
                        if CFG.get("attn_stage", 3) < 2:
                            continue
                        pso = nextps(C)
                        psd = nextps(C)
                        nk = len(kbs)
                        for bi, kb in enumerate(kbs):
                            k.mm(pso[:, 0:384], lhsT=v2[:, kb, g * 128:(g + 1) * 128], rhs=pt_[:, bi, :],
                                 start=(bi == 0), stop=(bi == nk - 1))
                        for bi, kb in enumerate(kbs):
                            k.mm(psd[:, 0:384], lhsT=C.onesb[:], rhs=pt_[:, bi, :],
                                 start=(bi == 0), stop=(bi == nk - 1))
                        if CFG.get("attn_stage", 3) < 3:
                            continue
                        for i, hh in enumerate(heads):
                            k.op("dve", "tensor_scalar", out=rc_[:, i, :], in0=psd[:, i * 128:(i + 1) * 128],
                                 scalar1=sinkexp[:, hh:hh + 1], scalar2=None, op0=ALU.add)
                        k.op("dve", "reciprocal", out=rc_[:], in_=rc_[:])
                        for i, hh in enumerate(heads):
                            c, hf = hh // 2, hh % 2
                            rows = slice(hf * 64, hf * 64 + 64)
                            k.op("dve", "tensor_tensor", out=y[rows, c, qc], in0=pso[rows, i * 128:(i + 1) * 128],
                                 in1=rc_[rows, i, :], op=ALU.mult)
            k.dma("sp", N.yT.view(yv[:, :, n4 * 512:(n4 + 1) * 512]), y[:])
        if CFG.get("memattn", True):
            mem_attention(k, C, N)


def phase_outproj(k, C, N, li, x_in, x_out):
    with phase(k):
        g_rep = load_grep(k, Ref(N.ffn_norm.t[li, :], N.ffn_norm.buf))
        nb = NormBufs(k)
        wo = k.sb([128, NCH, D], BF16)
        wv = N.w_out.t[li].rearrange("(c p) n -> p c n", p=128)
        stg = Stager(k, nbuf=3, elems=2048, engines=("pool", "act", "pool", "dve"))
        for c0 in range(NCH):
            stg.load(wo[:, c0:c0 + 1, :], Ref(wv[:, c0:c0 + 1, :], N.w_out.buf), (1, D))
        yts = [k.sb([128, NCH, 128], BF16) for _ in range(2)]
        yv = N.yT.t.rearrange("(c p) s -> p c s", p=128)
        def ld(t):
            k.dma("sp", yts[t % 2][:], N.yT.view(yv[:, :, t * 128:(t + 1) * 128]))
            k.dma("sp", nb.xt[t % 2][:], TB(x_in.t)[t * 128:(t + 1) * 128, :])
        ld(0)
        for t in range(NT):
            b = t % 2
            yt = yts[b]
            xt = nb.xt[b]
            if t + 1 < NT:
                ld(t + 1)
            for nbk in range(4):
                ps = nextps(C)
                for c in range(NCH):
                    k.mm(ps[:, :], lhsT=yt[:, c, :], rhs=wo[:, c, nbk * 512:(nbk + 1) * 512],
                         start=(c == 0), stop=(c == NCH - 1))
                k.op("dve", "tensor_tensor", out=xt[:, nbk * 512:(nbk + 1) * 512],
                     in0=xt[:, nbk * 512:(nbk + 1) * 512], in1=ps[:, :], op=ALU.add)
            k.dma("sp", TB(x_out.t)[t * 128:(t + 1) * 128, :], xt[:])
            norm_tile(k, C, nb, xt, g_rep, N.hT, t, b)


def phase_ffn_up(k, C, N, li):
    with phase(k):
        h = load_hT(k, N.hT, S)
        cwj = k.sb([88, 3, 128], F32)
        k.dma("sp", cwj[:], Ref(N.ffn_conv.t[li].rearrange("k (j p) -> j k p", p=128), N.ffn_conv.buf))
        cw = k.sb([128, 3, 88], F32)
        for kk in range(3):
            ps = nextps(C)
            k.op("pe", "transpose", out=ps[:, 0:88], in_=cwj[:, kk, :], identity=C.identf[0:88, 0:88])
            k.op("dve", "tensor_copy", out=cw[:, kk, :], in_=ps[:, 0:88])
        wg = [k.sb([128, NCH, 128], BF16) for _ in range(3)]
        wvv = [k.sb([128, NCH, 128], BF16) for _ in range(3)]
        ug = [k.sb([128, 2 + S], F32) for _ in range(2)]
        uv = [k.sb([128, 2 + S], F32) for _ in range(2)]
        for t_ in ug + uv:
            k.op("pool", "memset", ap=t_[:, 0:2], constant=0.0)
        cg = [k.sb([128, 1024], F32) for _ in range(2)]
        cv = [k.sb([128, 1024], F32) for _ in range(2)]
        sg = [k.sb([128, 1024], F32) for _ in range(2)]
        ao = [k.sb([128, 1024], BF16) for _ in range(2)]
        stg = Stager(k, nbuf=4)
        W = N.w_ffn_up
        u = 0
        def ldw(j):
            load_w(k, wg[j % 3], 128, Ref(W.t[li, :, j * 128:(j + 1) * 128], W.buf), stg)
            load_w(k, wvv[j % 3], 128, Ref(W.t[li, :, DFF + j * 128:DFF + (j + 1) * 128], W.buf), stg)
        ldw(0)
        for j in range(NFT):
            a, b_ = wg[j % 3], wvv[j % 3]
            if j + 1 < NFT:
                ldw(j + 1)
            ugj, uvj = ug[j % 2], uv[j % 2]
            for half in range(2):
                o = half * 1024
                for which, wt, ub in ((0, a, ugj), (1, b_, uvj)):
                    for tb in range(2):
                        ps = nextps(C)
                        for c in range(NCH):
                            k.mm(ps[:, :], lhsT=wt[:, c, :], rhs=h[:, c, o + tb * 512:o + (tb + 1) * 512],
                                 start=(c == 0), stop=(c == NCH - 1))
                        k.op("act", "copy", out=ub[:, 2 + o + tb * 512:2 + o + (tb + 1) * 512], in_=ps[:, :])
                cgu, cvu, sgu, aou = cg[u % 2], cv[u % 2], sg[u % 2], ao[u % 2]
                u += 1
                for eng, ub, co, jj in (("dve", ugj, cgu, j), ("dve", uvj, cvu, NFT + j)):
                    k.op(eng, "tensor_scalar", out=co[:], in0=ub[:, 2 + o:2 + o + 1024],
                         scalar1=cw[:, 2, jj:jj + 1], scalar2=None, op0=ALU.mult)
                    k.op(eng, "scalar_tensor_tensor", out=co[:], in0=ub[:, 1 + o:1 + o + 1024],
                         scalar=cw[:, 1, jj:jj + 1], in1=co[:], op0=ALU.mult, op1=ALU.add)
                    k.op(eng, "scalar_tensor_tensor", out=co[:], in0=ub[:, o:o + 1024],
                         scalar=cw[:, 0, jj:jj + 1], in1=co[:], op0=ALU.mult, op1=ALU.add)
                k.act(out=sgu[:], in_=cgu[:], func=AF.Silu)
                k.op("pool", "tensor_tensor", out=aou[:], in0=sgu[:], in1=cvu[:], op=ALU.mult)
                k.dma("sp", N.aT[j * 128:(j + 1) * 128, o:o + 1024], aou[:])


def phase_ffn_down(k, C, N, li, x_tb):
    with phase(k):
        wd = k.sb([128, NFT, 1024], BF16)
        ats = [k.sb([128, NFT, 256], BF16) for _ in range(2)]
        xts = [k.sb([128, 2, 1024], F32) for _ in range(2)]
        stg = Stager(k, nbuf=3, elems=2048, engines=("pool", "act", "dve"))
        av = N.aT.t.rearrange("(c p) s -> p c s", p=128)
        W = N.w_ffn_down
        u = 0
        for nh in range(2):
            wv = W.t[li, :, nh * 1024:(nh + 1) * 1024].rearrange("(c p) n -> p c n", p=128)
            for c0 in range(0, NFT, 2):
                stg.load(wd[:, c0:c0 + 2, :], Ref(wv[:, c0:c0 + 2, :], W.buf), (2, 1024))
            def ld(uu, nh_, t2_):
                at_, xt_ = ats[uu % 2], xts[uu % 2]
                for c0 in range(0, NFT, 11):
                    k.dma("sp", at_[:, c0:c0 + 11, :], N.aT.view(av[:, c0:c0 + 11, t2_ * 256:(t2_ + 1) * 256]))
                xv_ = x_tb.t[t2_ * 256:(t2_ + 1) * 256, nh_ * 1024:(nh_ + 1) * 1024].rearrange(
                    "(t p) n -> p t n", p=128)
                k.dma("sp", xt_[:], TB(x_tb.t).view(xv_))
            if nh == 0:
                ld(0, 0, 0)
            for t2 in range(NT // 2):
                at, xt = ats[u % 2], xts[u % 2]
                u += 1
                if t2 + 1 < NT // 2:
                    ld(u, nh, t2 + 1)
                elif nh == 0:
                    ld(u, 1, 0)
                xv = x_tb.t[t2 * 256:(t2 + 1) * 256, nh * 1024:(nh + 1) * 1024].rearrange("(t p) n -> p t n", p=128)
                for ts in range(2):
                    for nbk in range(2):
                        ps = nextps(C)
                        for c in range(NFT):
                            k.mm(ps[:, :], lhsT=at[:, c, ts * 128:(ts + 1) * 128],
                                 rhs=wd[:, c, nbk * 512:(nbk + 1) * 512], start=(c == 0), stop=(c == NFT - 1))
                        k.op("dve", "tensor_tensor", out=xt[:, ts, nbk * 512:(nbk + 1) * 512],
                             in0=xt[:, ts, nbk * 512:(nbk + 1) * 512], in1=ps[:, :], op=ALU.add)
                k.dma("sp", TB(x_tb.t).view(xv), xt[:])


B_SCRATCH = [
    ("brT", [1536, S], BF16), ("bkT", [1536, S], BF16), ("bkkT", [1536, S], BF16), ("bbT", [1536, S], BF16),
    ("bvT", [1536, S], BF16), ("blwT", [1536, S], F32), ("bbonT", [1536, S], BF16), ("bgT", [1536, S], BF16),
]
DECAY_C = 0.6065306597126334


def colvec(k, ref1d, n=1536):
    nc_ = n // 128
    rows = k.sb([nc_, 128], F32)
    k.dma("sp", rows[:], Ref(ref1d.ap.rearrange("(c p) -> c p", p=128), ref1d.buf))
    t = k.sb([128, nc_], F32)
    ps = nextps(CREF[0])
    k.op("pe", "transpose", out=ps[:, 0:nc_], in_=rows[:], identity=CREF[0].identf[0:nc_, 0:nc_])
    k.op("dve", "tensor_copy", out=t[:], in_=ps[:, 0:nc_])
    return t


CREF = [None]


def phase_proj_b(k, C, N, j):
    CREF[0] = C
    with phase(k):
        h = load_hT(k, N.hT, S)
        P = ProjCtx(k, C, h, S, nwt=3, wmax=128)
        W = N.b_w_in
        pl = [(4608, 96), (4704, 96), (4800, 128), (4928, 128)]
        for c in range(12):
            pl += [(3072 + c * 128, 128), (c * 128, 128), (1536 + c * 128, 128)]
        pl += [(5056 + c * 128, 128) for c in range(4)]
        P.plan([(Ref(W.t[j, :, c0:c0 + n], W.buf), n) for c0, n in pl])
        V = lambda name: Ref(getattr(N, name).t[j, :], getattr(N, name).buf)
        w0c, a0c, kkc, kac, gngc = colvec(k, V("b_w0")), colvec(k, V("b_a0")), colvec(k, V("b_k_k")), \
            colvec(k, V("b_k_a")), None
        rkc = colvec(k, Ref(N.b_r_k.t[j].rearrange("h d -> (h d)"), N.b_r_k.buf))
        omka = k.sb([128, 12], F32)
        k.op("dve", "tensor_scalar", out=omka[:], in0=kac[:], scalar1=-1.0, scalar2=1.0, op0=ALU.mult, op1=ALU.add)
        blk = k.sb([128, 128], BF16)
        k.op("pool", "memset", ap=blk[:], constant=0.0)
        k.op("pool", "memset", ap=blk[0:64, 0:64], constant=1.0)
        k.op("pool", "memset", ap=blk[64:128, 64:128], constant=1.0)
        wdec = k.sb([96, 1536], BF16)
        wicl = k.sb([96, 1536], BF16)
        wgt = k.sb([128, 2, 1536], BF16)
        aT = k.sb([128, S], F32)
        t1 = k.sb([128, S], F32)
        rn = k.sb([128, S], F32)
        t2 = rn
        k.dma("sp", aT[0:96, 0:1536], Ref(N.b_w_decay_up.t[j], N.b_w_decay_up.buf))
        k.op("pool", "tensor_copy", out=wdec[:], in_=aT[0:96, 0:1536])
        k.dma("sp", t1[0:96, 0:1536], Ref(N.b_w_iclr_up.t[j], N.b_w_iclr_up.buf))
        k.op("pool", "tensor_copy", out=wicl[:], in_=t1[0:96, 0:1536])
        for kc in range(2):
            k.dma("sp", rn[:, 0:1536], Ref(N.b_w_gate_up.t[j, kc * 128:(kc + 1) * 128, :], N.b_w_gate_up.buf))
            k.op("pool", "tensor_copy", out=wgt[:, kc, :], in_=rn[:, 0:1536])
        ub = [k.sb([128, 1 + S], F32) for _ in range(1)]
        for t_ in ub:
            k.op("pool", "memset", ap=t_[:, 0:1], constant=0.0)
        mus = [k.sb([128, 2], F32) for _ in range(2)]
        mx = [k.sb([128, S], F32) for _ in range(2)]
        cnt = [0]

        def mixed(c0, n):
            i = cnt[0]
            cnt[0] += 1
            u, mu, m = ub[0], mus[i % 2], mx[i % 2]

            def evac(tb, ps):
                alt_copy(k, tb, u[0:n, 1 + tb * 512:1 + (tb + 1) * 512], ps[0:n, :])
            P.F(Ref(W.t[j, :, c0:c0 + n], W.buf), evac, n)
            k.dma("sp", mu[0:n, 0:1], Ref(N.b_mu.t[j, c0:c0 + n].rearrange("(p o) -> p o", o=1), N.b_mu.buf))
            k.op("dve", "tensor_scalar", out=mu[0:n, 1:2], in0=mu[0:n, 0:1], scalar1=-1.0, scalar2=1.0,
                 op0=ALU.mult, op1=ALU.add)
            k.op("dve", "tensor_scalar", out=m[0:n, :], in0=u[0:n, 0:S], scalar1=mu[0:n, 0:1], scalar2=None,
                 op0=ALU.mult)
            k.op("dve", "scalar_tensor_tensor", out=m[0:n, :], in0=u[0:n, 1:1 + S], scalar=mu[0:n, 1:2],
                 in1=m[0:n, :], op0=ALU.mult, op1=ALU.add)
            return m
        twT = k.sb([96, S], BF16)
        adT = k.sb([96, S], BF16)
        sgT = k.sb([128, 2, S], BF16)
        m = mixed(4608, 96)
        k.act(out=twT[:], in_=m[0:96, :], func=AF.Tanh)
        m = mixed(4704, 96)
        k.op("dve", "tensor_copy", out=adT[:], in_=m[0:96, :])
        for i in range(2):
            m = mixed(4800 + i * 128, 128)
            k.act(out=sgT[:, i, :], in_=m[:], func=AF.Sigmoid)
        sq = k.sb([128, S], BF16)
        o16 = [k.sb([128, S], BF16) for _ in range(3)]
        o32 = [k.sb([128, S], F32) for _ in range(1)]
        vb = k.sb([128, S], BF16)
        rb = k.sb([128, S], BF16)
        oc = [0]

        def out16():
            oc[0] += 1
            return o16[oc[0] % 3]
        for c in range(12):
            cs = slice(c * 128, (c + 1) * 128)
            lw = o32[0]
            go = out16()
            for tb in range(4):
                ts_ = slice(tb * 512, (tb + 1) * 512)
                ps = nextps(C)
                k.mm(ps[:, :], lhsT=wicl[:, cs], rhs=adT[:, ts_])
                k.act(out=aT[:, ts_], in_=ps[:, :], func=AF.Sigmoid, bias=a0c[:, c:c + 1])
                ps = nextps(C)
                k.mm(ps[:, :], lhsT=wdec[:, cs], rhs=twT[:, ts_])
                k.act(out=lw[:, ts_], in_=ps[:, :], func=AF.Sigmoid, bias=w0c[:, c:c + 1])
                ps = nextps(C)
                for kc in range(2):
                    k.mm(ps[:, :], lhsT=wgt[:, kc, cs], rhs=sgT[:, kc, ts_], start=(kc == 0), stop=(kc == 1))
                k.op("dve", "tensor_copy", out=go[:, ts_], in_=ps[:, :])
            k.op("dve", "tensor_scalar", out=lw[:], in0=lw[:], scalar1=-DECAY_C, scalar2=None, op0=ALU.mult)
            k.dma("sp", N.blwT[cs, :], lw[:])
            k.dma("sp", N.bgT[cs, :], go[:])
            m = mixed(3072 + c * 128, 128)
            k.op("pool", "tensor_copy", out=vb[:], in_=m[:])
            k.dma("sp", N.bvT[cs, :], vb[:])
            m = mixed(c * 128, 128)
            k.op("pool", "tensor_copy", out=rb[:], in_=m[:])
            k.dma("sp", N.brT[cs, :], rb[:])
            m = mixed(1536 + c * 128, 128)
            k.op("dve", "tensor_scalar", out=t1[:], in0=m[:], scalar1=kkc[:, c:c + 1], scalar2=None, op0=ALU.mult)
            k.act(out=sq[:], in_=t1[:], func=AF.Square)
            for tb in range(4):
                ts_ = slice(tb * 512, (tb + 1) * 512)
                ps = nextps(C)
                k.mm(ps[:, :], lhsT=blk[:], rhs=sq[:, ts_])
                k.op("dve", "tensor_scalar", out=rn[:, ts_], in0=ps[:, :], scalar1=1e-6, scalar2=None, op0=ALU.add)
            k.act(out=rn[:], in_=rn[:], func=AF.Sqrt)
            k.op("dve", "reciprocal", out=rn[:], in_=rn[:])
            kko = out16()
            k.op("dve", "tensor_tensor", out=t1[:], in0=t1[:], in1=rn[:], op=ALU.mult)
            k.op("pool", "tensor_copy", out=kko[:], in_=t1[:])
            k.dma("sp", N.bkkT[cs, :], kko[:])
            bo = out16()
            k.op("dve", "tensor_tensor", out=bo[:], in0=t1[:], in1=aT[:], op=ALU.mult)
            k.dma("sp", N.bbT[cs, :], bo[:])
            k.op("dve", "tensor_scalar", out=t2[:], in0=aT[:], scalar1=kac[:, c:c + 1], scalar2=omka[:, c:c + 1],
                 op0=ALU.mult, op1=ALU.add)
            k.op("dve", "tensor_tensor", out=t2[:], in0=t2[:], in1=m[:], op=ALU.mult)
            ko = out16()
            k.op("pool", "tensor_copy", out=ko[:], in_=t2[:])
            k.dma("sp", N.bkT[cs, :], ko[:])
            k.op("dve", "scalar_tensor_tensor", out=sq[:], in0=t2[:], scalar=rkc[:, c:c + 1], in1=rb[:],
                 op0=ALU.mult, op1=ALU.mult)
            bon = out16()
            for tb in range(4):
                ts_ = slice(tb * 512, (tb + 1) * 512)
                ps = nextps(C)
                k.mm(ps[:, :], lhsT=blk[:], rhs=sq[:, ts_])
                k.op("dve", "tensor_tensor", out=bon[:, ts_], in0=ps[:, :], in1=vb[:, ts_], op=ALU.mult)
            k.dma("sp", N.bbonT[cs, :], bon[:])
        stg = o16[0:2]
        cn2 = [0]
        for c in range(4):
            proj_F_to_dram(k, P, Ref(W.t[j, :, 5056 + c * 128:5056 + (c + 1) * 128], W.buf), N.qmT, c * 128, stg, cn2)


class RwTmp:
    def __init__(self, k):
        f = lambda dt, n=128: k.sb([128, n], dt)
        self.KiP = f(BF16)
        self.PiP = f(BF16)
        self.AbT = f(F32)
        self.Ab = f(F32)
        self.AkT = f(BF16)
        self.ArT = f(BF16, 256)
        self.PTb = f(BF16)
        self.Zb = f(BF16, 64)
        self.Un = f(BF16, 64)
        self.y = f(F32, 64)
        self.junk = f(F32, 64)
        self.s1 = k.sb([128, 1], F32)
        self.s2 = k.sb([128, 1], F32)
        self.mean = k.sb([128, 1], F32)
        self.var = k.sb([128, 1], F32)
        self.rs = k.sb([128, 1], F32)
        self.tmpH = f(F32, 64)
        self.ws = TriWS(k)


class RwPair:
    def __init__(self, k):
        f = lambda dt, n=128: k.sb([128, n], dt)
        self.g = f(F32)
        self.gx = f(F32)
        self.Ei, self.En, self.Ex, self.Ed = f(F32), f(F32), f(F32), f(F32)
        self.Rd, self.Ki, self.Pi, self.KKd, self.Kdc, self.Pdc = (f(BF16) for _ in range(6))
        self.Kdt, self.Pdt, self.Vt = f(BF16), f(BF16), f(BF16)
        self.yn = f(BF16)
        self.yf = f(F32)
        self.yo = f(BF16)


def phase_scan_b(k, C, N, j):
    CREF[0] = C
    with phase(k):
        strictT = C.mask(-1, 1, ALU.is_gt)
        inclT = C.mask(-1, 1, ALU.is_ge)
        msk2i = k.sb([128, 2, 128], F32)
        for i in range(2):
            k.op("dve", "tensor_copy", out=msk2i[:, i, :], in_=inclT[:])
        hm = k.sb([128, 2], F32)
        k.op("pool", "memset", ap=hm[:], constant=0.0)
        k.op("pool", "memset", ap=hm[0:64, 0:1], constant=1.0)
        k.op("pool", "memset", ap=hm[64:128, 1:2], constant=1.0)
        V = lambda name: Ref(getattr(N, name).t[j, :], getattr(N, name).buf)
        gng, gnb = colvec(k, V("b_gn_g")), colvec(k, V("b_gn_b"))
        names = ("brT", "bkT", "bkkT", "bbT", "bvT", "bbonT", "bgT")
        inb = [{nm: k.sb([128, S], BF16) for nm in names} for _ in range(2)]
        lwb = [k.sb([128, S], F32) for _ in range(2)]
        prs = [[RwPair(k) for _ in range(2)] for _ in range(2)]
        tms = [[[RwTmp(k) for _ in range(2)] for _ in range(2)] for _ in range(2)]
        Hf = [[k.sb([128, 64], F32) for _ in range(2)] for _ in range(2)]
        Hb = [[k.sb([128, 64], BF16) for _ in range(2)] for _ in range(2)]
        un = 0
        pu = 0
        def make(c, slot):
            cs = slice(c * 128, (c + 1) * 128)
            I = inb[slot]
            lw = lwb[slot]
            for nm in names:
                k.dma("sp", I[nm][:], getattr(N, nm)[cs, :])
            k.dma("sp", lw[:], N.blwT[cs, :])
            for i in range(2):
                k.op("pool", "memset", ap=Hf[slot][i][:], constant=0.0)
                k.op("pool", "memset", ap=Hb[slot][i][:], constant=0.0)
            def pair_prep(n):
                tc = slice(n * 128, (n + 1) * 128)
                Pp = prs[slot][n % 2]
                k.op("dve", "tensor_tensor_scan", out=Pp.g[:], data0=C.onesf[:], data1=lw[:, tc], initial=0.0,
                     op0=ALU.mult, op1=ALU.add)
                k.op("dve", "tensor_tensor", out=Pp.gx[:], in0=Pp.g[:], in1=lw[:, tc], op=ALU.subtract)
                k.act(out=Pp.Ei[:], in_=Pp.g[:], func=AF.Exp)
                k.act(out=Pp.En[:], in_=Pp.g[:], func=AF.Exp, scale=-1.0)
                k.act(out=Pp.Ex[:], in_=Pp.gx[:], func=AF.Exp)
                k.act(out=Pp.Ed[:], in_=Pp.g[:], func=AF.Exp, scale=-1.0, bias=Pp.g[:, 127:128])
                k.op("dve", "tensor_tensor", out=Pp.Rd[:], in0=I["brT"][:, tc], in1=Pp.Ei[:], op=ALU.mult)
                k.op("dve", "tensor_tensor", out=Pp.Ki[:], in0=I["bkT"][:, tc], in1=Pp.En[:], op=ALU.mult)
                k.op("dve", "tensor_tensor", out=Pp.Pi[:], in0=I["bbT"][:, tc], in1=Pp.En[:], op=ALU.mult)
                k.op("dve", "tensor_tensor", out=Pp.KKd[:], in0=I["bkkT"][:, tc], in1=Pp.Ex[:], op=ALU.mult)
                k.op("pool", "tensor_tensor", out=Pp.Kdc[:], in0=I["bkT"][:, tc], in1=Pp.Ed[:], op=ALU.mult)
                k.op("pool", "tensor_tensor", out=Pp.Pdc[:], in0=I["bbT"][:, tc], in1=Pp.Ed[:], op=ALU.mult)
                pst = psbf(nextps(C))
                k.tr(pst[:, 0:128], Pp.Kdc[:], C.identb[:], sig=False)
                k.tr(pst[:, 128:256], Pp.Pdc[:], C.identb[:], sig=False)
                k.tr(pst[:, 256:384], I["bvT"][:, tc], C.identb[:])
                k.op("act", "copy", out=Pp.Kdt[:], in_=pst[:, 0:128])
                k.op("act", "copy", out=Pp.Pdt[:], in_=pst[:, 128:256])
                k.op("act", "copy", out=Pp.Vt[:], in_=pst[:, 256:384])

            def prep(n, i):
                tc = slice(n * 128, (n + 1) * 128)
                Pp = prs[slot][n % 2]
                T = tms[slot][i][n % 2]
                k.op("dve", "tensor_scalar", out=T.KiP[:], in0=Pp.Ki[:], scalar1=hm[:, i:i + 1], scalar2=None,
                     op0=ALU.mult)
                k.op("dve", "tensor_scalar", out=T.PiP[:], in0=Pp.Pi[:], scalar1=hm[:, i:i + 1], scalar2=None,
                     op0=ALU.mult)
                psA = nextps(C)
                k.mm(psA[:, 0:128], lhsT=T.PiP[:], rhs=Pp.KKd[:], sig=False)
                k.mm(psA[:, 128:256], lhsT=T.KiP[:], rhs=Pp.KKd[:], sig=False)
                k.mm(psA[:, 256:384], lhsT=T.KiP[:], rhs=Pp.Rd[:], sig=False)
                k.mm(psA[:, 384:512], lhsT=T.PiP[:], rhs=Pp.Rd[:])
                k.op("dve", "tensor_tensor", out=T.AbT[:], in0=psA[:, 0:128], in1=strictT[:], op=ALU.mult)
                k.op("dve", "tensor_tensor", out=T.AkT[:], in0=psA[:, 128:256], in1=strictT[:], op=ALU.mult)
                k.op("dve", "tensor_tensor", out=T.ArT.view(T.ArT.t[:, :].rearrange("p (a b) -> p a b", a=2)),
                     in0=psA.view(psA.t[:, 256:512].rearrange("p (a b) -> p a b", a=2)), in1=msk2i[:],
                     op=ALU.mult)
                yield
                psl = nextps(C)
                k.op("pe", "transpose", out=psl[:, 0:128], in_=T.AbT[:], identity=C.identf[:])
                k.op("act", "copy", out=T.Ab[:], in_=psl[:, 0:128])
                yield
                res = [None]
                for _ in tri_inv_gen(k, C, T.Ab, T.AbT, T.ws, res):
                    yield
                k.op("act", "copy", out=T.PTb[:], in_=res[0][:])


            def seq(n, i):
                tc = slice(n * 128, (n + 1) * 128)
                Pp = prs[slot][n % 2]
                T = tms[slot][i][n % 2]
                vs = slice(i * 64, (i + 1) * 64)
                psz = nextps(C)
                k.mm(psz[:, 0:64], lhsT=Pp.KKd[:], rhs=Hb[slot][i][:], start=True, stop=False)
                k.mm(psz[:, 0:64], lhsT=T.AkT[:], rhs=Pp.Vt[:, vs], start=False, stop=True)
                k.op("act", "copy", out=T.Zb[:], in_=psz[:, 0:64])
                yield
                psu = nextps(C)
                k.mm(psu[:, 0:64], lhsT=T.PTb[:], rhs=T.Zb[:])
                k.op("dve", "tensor_scalar", out=T.Un[:], in0=psu[:, 0:64], scalar1=-1.0, scalar2=None,
                     op0=ALU.mult)
                yield
                psy = nextps(C)
                k.mm(psy[:, 0:64], lhsT=Pp.Rd[:], rhs=Hb[slot][i][:], start=True, stop=False)
                k.mm(psy[:, 0:64], lhsT=T.ArT[:, 0:128], rhs=Pp.Vt[:, vs], start=False, stop=False)
                k.mm(psy[:, 0:64], lhsT=T.ArT[:, 128:256], rhs=T.Un[:], start=False, stop=True)
                psh = nextps(C)
                k.mm(psh[:, 0:64], lhsT=Pp.Kdt[:], rhs=Pp.Vt[:, vs], start=True, stop=False)
                k.mm(psh[:, 0:64], lhsT=Pp.Pdt[:], rhs=T.Un[:], start=False, stop=True)
                k.op("dve", "tensor_scalar", out=T.tmpH[:], in0=Hf[slot][i][:], scalar1=Pp.Ei[:, 127:128], scalar2=None,
                     op0=ALU.mult)
                k.op("dve", "scalar_tensor_tensor", out=Hf[slot][i][:], in0=psh[:, 0:64], scalar=hm[:, i:i + 1],
                     in1=T.tmpH[:], op0=ALU.mult, op1=ALU.add)
                k.op("act", "copy", out=Hb[slot][i][:], in_=Hf[slot][i][:])
                k.op("pool", "memset", ap=T.s1[:], constant=0.0)
                k.op("pool", "memset", ap=T.s2[:], constant=0.0)
                k.act(out=T.y[:], in_=psy[:, 0:64], func=AF.Identity, accum_out=T.s1[:])
                yield
                k.act(out=T.junk[:], in_=T.y[:], func=AF.Square, accum_out=T.s2[:])
                k.op("dve", "tensor_scalar", out=T.mean[:], in0=T.s1[:], scalar1=1.0 / 64.0, scalar2=None,
                     op0=ALU.mult)
                k.op("dve", "tensor_tensor", out=T.var[:], in0=T.mean[:], in1=T.mean[:], op=ALU.mult)
                k.op("dve", "scalar_tensor_tensor", out=T.var[:], in0=T.s2[:], scalar=1.0 / 64.0, in1=T.var[:],
                     op0=ALU.mult, op1=ALU.subtract)
                k.op("dve", "tensor_scalar", out=T.var[:], in0=T.var[:], scalar1=64e-5, scalar2=None, op0=ALU.add)
                k.act(out=T.var[:], in_=T.var[:], func=AF.Sqrt)
                k.op("dve", "reciprocal", out=T.rs[:], in_=T.var[:])
                k.op("dve", "tensor_scalar", out=Pp.yn[:, vs], in0=T.y[:], scalar1=T.mean[:, 0:1],
                     scalar2=T.rs[:, 0:1], op0=ALU.subtract, op1=ALU.mult)

            def assemble(n):
                tc = slice(n * 128, (n + 1) * 128)
                Pp = prs[slot][n % 2]
                pst = psbf(nextps(C))
                k.tr(pst[:, 0:128], Pp.yn[:], C.identb[:])
                k.op("dve", "tensor_scalar", out=Pp.yf[:], in0=pst[:, 0:128], scalar1=gng[:, c:c + 1],
                     scalar2=gnb[:, c:c + 1], op0=ALU.mult, op1=ALU.add)
                k.op("pool", "tensor_tensor", out=Pp.yf[:], in0=Pp.yf[:], in1=I["bbonT"][:, tc], op=ALU.add)
                k.op("pool", "tensor_tensor", out=Pp.yo[:], in0=Pp.yf[:], in1=I["bgT"][:, tc], op=ALU.mult)
                k.dma("sp", TB(N.yT.t)[cs, tc], Pp.yo[:])

            return pair_prep, prep, seq, assemble

        for c0 in range(0, 12, 2):
            fs = [make(c0 + sl, sl) for sl in range(2)]
            for step in range(NT + 1):
                gens = []
                if step < NT:
                    for f in fs:
                        f[0](step)
                    for f in fs:
                        gens += [f[1](step, 0), f[1](step, 1)]
                if step >= 1:
                    for f in fs:
                        gens += [f[2](step - 1, 0), f[2](step - 1, 1)]
                interleave(gens)
                if step >= 1:
                    for f in fs:
                        f[3](step - 1)
        mem_attention(k, C, N)


def phase_mixer_b(k, C, N, j):
    phase_proj_b(k, C, N, j)
    phase_scan_b(k, C, N, j)


class TriWS:
    def __init__(self, k):
        self.x = [k.sb([128, 128], F32) for _ in range(2)]
        self.xt = [k.sb([128, 128], F32) for _ in range(2)]
        self.pt = [k.sb([128, 128], F32) for _ in range(2)]


def tri_inv_gen(k, C, L, LT, ws, res):
    X, XT = L, LT
    PT = ws.pt[0]
    k.op("dve", "tensor_tensor", out=PT[:], in0=C.identf[:], in1=LT[:], op=ALU.subtract)
    for lvl in range(6):
        ps = nextps(C)
        k.mm(ps[:, 0:128], lhsT=XT[:], rhs=X[:])
        if lvl < 5:
            k.mm(ps[:, 128:256], lhsT=X[:], rhs=XT[:])
        X2 = ws.x[lvl % 2]
        k.op("act", "copy", out=X2[:], in_=ps[:, 0:128])
        X2T = ws.xt[lvl % 2]
        if lvl < 5:
            k.op("act", "copy", out=X2T[:], in_=ps[:, 128:256])
        yield
        ps3 = nextps(C)
        k.mm(ps3[:, 0:128], lhsT=X2[:], rhs=PT[:])
        PTn = ws.pt[(lvl + 1) % 2]
        k.op("dve", "tensor_tensor", out=PTn[:], in0=PT[:], in1=ps3[:, 0:128], op=ALU.add)
        X, XT, PT = X2, X2T, PTn
        yield
    res[0] = PT


def interleave(gens):
    alive = list(gens)
    while alive:
        for g in list(alive):
            try:
                next(g)
            except StopIteration:
                alive.remove(g)


EXTRA_SCRATCH = [
    ("gq", [768, S], BF16), ("gk", [768, S], BF16), ("gv", [1536, S], BF16), ("gz", [S, 1536], BF16),
    ("gbg", [S, 24], F32),
]


def phase_proj_c(k, C, N, j):
    with phase(k):
        h = load_hT(k, N.hT, S)
        P = ProjCtx(k, C, h, S, nwt=3, wmax=512)
        W = N.c_w_in
        P.plan([(Ref(W.t[j, :, t * 128:(t + 1) * 128], W.buf), 128) for t in range(24)]
               + [(Ref(W.t[j, :, 3072 + zb * 512:3072 + (zb + 1) * 512], W.buf), 512) for zb in range(3)]
               + [(Ref(W.t[j, :, 4608:4632], W.buf), 24)]
               + [(Ref(W.t[j, :, 4632 + c * 128:4632 + (c + 1) * 128], W.buf), 128) for c in range(4)])
        cwj = k.sb([24, 4, 128], F32)
        k.dma("sp", cwj[:], Ref(N.c_conv.t[j].rearrange("k (t p) -> t k p", p=128), N.c_conv.buf))
        cw = k.sb([128, 4, 24], F32)
        for kk in range(4):
            ps = nextps(C)
            k.op("pe", "transpose", out=ps[:, 0:24], in_=cwj[:, kk, :], identity=C.identf[0:24, 0:24])
            k.op("dve", "tensor_copy", out=cw[:, kk, :], in_=ps[:, 0:24])
        ub = [k.sb([128, 3 + S], F32) for _ in range(2)]
        for t_ in ub:
            k.op("pool", "memset", ap=t_[:, 0:3], constant=0.0)
        cv = [k.sb([128, S], F32) for _ in range(2)]
        sq = k.sb([128, S], BF16)
        rn = k.sb([128, S], F32)
        ob = [k.sb([128, S], BF16) for _ in range(2)]
        for t in range(24):
            u = ub[t % 2]

            def evac(tb, ps, u=u):
                alt_copy(k, tb, u[:, 3 + tb * 512:3 + (tb + 1) * 512], ps[:, :])
            P.F(Ref(W.t[j, :, t * 128:(t + 1) * 128], W.buf), evac)
            c = cv[t % 2]
            k.op("dve", "tensor_scalar", out=c[:], in0=u[:, 3:3 + S], scalar1=cw[:, 3, t:t + 1], scalar2=None,
                 op0=ALU.mult)
            for kk in range(3):
                k.op("dve", "scalar_tensor_tensor", out=c[:], in0=u[:, kk:kk + S], scalar=cw[:, kk, t:t + 1],
                     in1=c[:], op0=ALU.mult, op1=ALU.add)
            k.act(out=c[:], in_=c[:], func=AF.Silu)
            o = ob[t % 2]
            if t < 12:
                k.act(out=sq[:], in_=c[:], func=AF.Square)
                for tb in range(4):
                    ps = nextps(C)
                    k.mm(ps[:, :], lhsT=C.onesb[:], rhs=sq[:, tb * 512:(tb + 1) * 512])
                    k.op("dve", "tensor_scalar", out=rn[:, tb * 512:(tb + 1) * 512], in0=ps[:, :], scalar1=1e-6,
                         scalar2=None, op0=ALU.add)
                k.act(out=rn[:], in_=rn[:], func=AF.Sqrt)
                k.op("dve", "reciprocal", out=rn[:], in_=rn[:])
                k.op("dve", "tensor_tensor", out=o[:], in0=c[:], in1=rn[:], op=ALU.mult)
            else:
                k.op("pool", "tensor_copy", out=o[:], in_=c[:])
            if t < 6:
                dst = N.gq[t * 128:(t + 1) * 128, :]
            elif t < 12:
                dst = N.gk[(t - 6) * 128:(t - 5) * 128, :]
            else:
                dst = N.gv[(t - 12) * 128:(t - 11) * 128, :]
            k.dma("sp", dst, o[:])
        zs = [k.sb([128, 512], BF16) for _ in range(2)]
        for zb in range(3):
            def evz(tt, ps, zb=zb):
                s_ = zs[tt % 2]
                k.act(out=s_[:], in_=ps[:, :], func=AF.Silu)
                k.dma("sp", N.gz[tt * 128:(tt + 1) * 128, zb * 512:(zb + 1) * 512], s_[:])
            P.T(Ref(W.t[j, :, 3072 + zb * 512:3072 + (zb + 1) * 512], W.buf), 512, evz)
        al = k.sb([128, 12], F32)
        k.dma("sp", al[:], Ref(bcast_rows(N.c_a_log.t[j, :], 12), N.c_a_log.buf))
        dtb = k.sb([128, 12], F32)
        k.dma("sp", dtb[:], Ref(bcast_rows(N.c_dt_bias.t[j, :], 12), N.c_dt_bias.buf))
        nea = k.sb([128, 12], F32)
        k.act(out=nea[:], in_=al[:], func=AF.Exp)
        k.op("dve", "tensor_scalar", out=nea[:], in0=nea[:], scalar1=-1.0, scalar2=None, op0=ALU.mult)
        bg = [k.sb([128, 24], F32) for _ in range(2)]

        def evbg(tt, ps):
            s_ = bg[tt % 2]
            k.act(out=s_[:, 0:12], in_=ps[:, 0:12], func=AF.Sigmoid)
            k.op("dve", "tensor_tensor", out=s_[:, 12:24], in0=ps[:, 12:24], in1=dtb[:], op=ALU.add)
            k.act(out=s_[:, 12:24], in_=s_[:, 12:24], func=AF.Exp)
            k.act(out=s_[:, 12:24], in_=s_[:, 12:24], func=AF.Ln, bias=C.onesf[:, 0:1])
            k.op("dve", "tensor_tensor", out=s_[:, 12:24], in0=s_[:, 12:24], in1=nea[:], op=ALU.mult)
            k.dma("sp", N.gbg[tt * 128:(tt + 1) * 128, :], s_[:])
        P.T(Ref(W.t[j, :, 4608:4632], W.buf), 24, evbg)
        stg = [k.sb([128, S], BF16) for _ in range(2)]
        cnt = [0]
        for c in range(4):
            proj_F_to_dram(k, P, Ref(W.t[j, :, 4632 + c * 128:4632 + (c + 1) * 128], W.buf), N.qmT, c * 128, stg, cnt)


class GdnTmp:
    def __init__(self, k):
        f = lambda dt: k.sb([128, 128], dt)
        self.vtok = k.sb([128, 256], BF16)
        self.kdec = f(BF16)
        self.gbc = f(F32)
        self.tmp = f(F32)
        self.DT = f(F32)
        self.L2T = f(F32)
        self.L2 = f(F32)
        self.AT = f(BF16)
        self.PTb = f(BF16)
        self.u = f(F32)
        self.wtok = f(BF16)
        self.wT = f(BF16)
        self.vnew = f(BF16)
        self.ob = f(F32)
        self.o = f(F32)
        self.junk = f(F32)
        self.ss = k.sb([128, 1], F32)
        self.sd = k.sb([128, 1], F32)
        self.rs = k.sb([128, 1], F32)
        self.y = f(F32)
        self.y2 = f(BF16)
        self.yT = f(BF16)
        self.zt = f(BF16)
        self.ws = TriWS(k)


def phase_scan_c(k, C, N, j):
    with phase(k):
        M1 = C.mask(-1, 1, ALU.is_ge)
        strictT = C.mask(-1, 1, ALU.is_gt)
        bgall = k.sb([128, NT, 24], F32)
        k.dma("sp", bgall[:], N.gbg.view(N.gbg.t.rearrange("(t p) c -> p t c", p=128)))
        gc = k.sb([128, NT, 12], F32)
        gl = k.sb([128, NT, 12], F32)
        for n in range(NT):
            ps = nextps(C)
            k.mm(ps[:, 0:12], lhsT=M1[:], rhs=bgall[:, n, 12:24])
            k.mm(ps[:, 16:28], lhsT=C.onesf[:], rhs=bgall[:, n, 12:24])
            k.op("act", "copy", out=gc[:, n, :], in_=ps[:, 0:12])
            k.op("act", "copy", out=gl[:, n, :], in_=ps[:, 16:28])
        egc = k.sb([128, NT, 12], F32)
        egl = k.sb([128, NT, 12], F32)
        edec = k.sb([128, NT, 12], F32)
        qsc = k.sb([128, NT, 12], F32)
        k.act(out=egc[:], in_=gc[:], func=AF.Exp)
        k.act(out=egl[:], in_=gl[:], func=AF.Exp)
        k.op("dve", "tensor_tensor", out=edec[:], in0=gl[:], in1=gc[:], op=ALU.subtract)
        k.act(out=edec[:], in_=edec[:], func=AF.Exp)
        k.op("dve", "tensor_scalar", out=qsc[:], in0=egc[:], scalar1=float(128.0 ** -0.5), scalar2=None, op0=ALU.mult)
        normg = k.sb([128, 128], F32)
        k.dma("sp", normg[:], Ref(bcast_rows(N.c_norm_g.t[j, :], 128), N.c_norm_g.buf))
        kTs = [k.sb([128, S], BF16) for _ in range(2)]
        qTs = [k.sb([128, S], BF16) for _ in range(2)]
        vTs = [k.sb([128, S], BF16) for _ in range(4)]
        KKs = [k.sb([128, 128], F32) for _ in range(2)]
        QKs = [k.sb([128, 128], F32) for _ in range(2)]
        ktoks = [k.sb([128, 128], BF16) for _ in range(2)]
        tmps = [[GdnTmp(k) for _ in range(2)] for _ in range(2)]
        H = [k.sb([128, 128], F32) for _ in range(2)]
        Hb = [k.sb([128, 128], BF16) for _ in range(2)]
        un = 0
        sh = 0
        for hq in range(6):
            kT, qT = kTs[hq % 2], qTs[hq % 2]
            k.dma("sp", kT[:], N.gk[hq * 128:(hq + 1) * 128, :])
            k.dma("sp", qT[:], N.gq[hq * 128:(hq + 1) * 128, :])
            vT2 = []
            for i in range(2):
                hv = 2 * hq + i
                vT = vTs[hv % 4]
                k.dma("sp", vT[:], N.gv[hv * 128:(hv + 1) * 128, :])
                vT2.append(vT)
                k.op("pool", "memset", ap=H[i][:], constant=0.0)
                k.op("pool", "memset", ap=Hb[i][:], constant=0.0)
            def shared_prep(n):
                tc = slice(n * 128, (n + 1) * 128)
                KK, QK, ktok = KKs[n % 2], QKs[n % 2], ktoks[n % 2]
                ps = nextps(C)
                k.mm(ps[:, 0:128], lhsT=kT[:, tc], rhs=kT[:, tc])
                k.mm(ps[:, 128:256], lhsT=kT[:, tc], rhs=qT[:, tc])
                k.op("dve", "tensor_tensor", out=KK[:], in0=ps[:, 0:128], in1=strictT[:], op=ALU.mult)
                k.op("dve", "scalar_tensor_tensor", out=QK[:], in0=ps[:, 128:256], scalar=float(128.0 ** -0.5),
                     in1=M1[:], op0=ALU.mult, op1=ALU.mult)
                pst = psbf(nextps(C))
                k.tr(pst[:, 0:128], kT[:, tc], C.identb[:])
                k.op("act", "copy", out=ktok[:], in_=pst[:, 0:128])

            def prep(n, i):
                tc = slice(n * 128, (n + 1) * 128)
                KK, QK, ktok = KKs[n % 2], QKs[n % 2], ktoks[n % 2]
                hv = 2 * hq + i
                T = tmps[i][n % 2]
                bcol = bgall[:, n, hv:hv + 1]
                gcol = bgall[:, n, 12 + hv:13 + hv]
                pst = psbf(nextps(C))
                k.tr(pst[:, 0:128], vT2[i][:, tc], C.identb[:])
                k.op("act", "copy", out=T.vtok[:, 0:128], in_=pst[:, 0:128])
                k.op("dve", "tensor_scalar", out=T.vtok[:, 128:256], in0=ktok[:], scalar1=egc[:, n, hv:hv + 1],
                     scalar2=None, op0=ALU.mult)
                k.act(out=T.kdec[:], in_=ktok[:], func=AF.Copy, scale=edec[:, n, hv:hv + 1])
                k.op("dve", "tensor_scalar", out=T.gbc[:], in0=C.onesf[:], scalar1=gcol, scalar2=None, op0=ALU.mult)
                psg = nextps(C)
                k.mm(psg[:, 0:128], lhsT=T.gbc[:], rhs=M1[:])
                k.op("dve", "tensor_scalar", out=T.tmp[:], in0=psg[:, 0:128], scalar1=gc[:, n, hv:hv + 1],
                     scalar2=0.0, op0=ALU.subtract, op1=ALU.min)
                k.act(out=T.DT[:], in_=T.tmp[:], func=AF.Exp)
                k.op("dve", "scalar_tensor_tensor", out=T.L2T[:], in0=KK[:], scalar=bcol, in1=T.DT[:],
                     op0=ALU.mult, op1=ALU.mult)
                k.op("dve", "tensor_tensor", out=T.AT[:], in0=QK[:], in1=T.DT[:], op=ALU.mult)
                yield
                psl = nextps(C)
                k.op("pe", "transpose", out=psl[:, 0:128], in_=T.L2T[:], identity=C.identf[:])
                k.op("act", "copy", out=T.L2[:], in_=psl[:, 0:128])
                yield
                res = [None]
                for _ in tri_inv_gen(k, C, T.L2, T.L2T, T.ws, res):
                    yield
                PT = res[0]
                k.op("act", "copy", out=T.PTb[:], in_=PT[:])
                psu = nextps(C)
                k.mm(psu[:, 0:256], lhsT=T.PTb[:], rhs=T.vtok[:])
                k.op("dve", "tensor_scalar", out=T.u[:], in0=psu[:, 0:128], scalar1=bcol, scalar2=None, op0=ALU.mult)
                k.op("dve", "tensor_scalar", out=T.wtok[:], in0=psu[:, 128:256], scalar1=bcol, scalar2=None,
                     op0=ALU.mult)
                pst = psbf(nextps(C))
                k.tr(pst[:, 0:128], T.wtok[:], C.identb[:])
                k.op("act", "copy", out=T.wT[:], in_=pst[:, 0:128])


            def seq(n, i):
                tc = slice(n * 128, (n + 1) * 128)
                hv = 2 * hq + i
                T = tmps[i][n % 2]
                ps1 = nextps(C)
                k.mm(ps1[:, 0:128], lhsT=T.wT[:], rhs=Hb[i][:])
                k.op("dve", "tensor_tensor", out=T.vnew[:], in0=T.u[:], in1=ps1[:, 0:128], op=ALU.subtract)
                yield
                pso = nextps(C)
                k.mm(pso[:, 0:128], lhsT=qT[:, tc], rhs=Hb[i][:])
                k.mm(pso[:, 128:256], lhsT=T.AT[:], rhs=T.vnew[:])
                k.op("act", "copy", out=T.ob[:], in_=pso[:, 128:256])
                k.op("dve", "scalar_tensor_tensor", out=T.o[:], in0=pso[:, 0:128], scalar=qsc[:, n, hv:hv + 1],
                     in1=T.ob[:], op0=ALU.mult, op1=ALU.add)
                psh = nextps(C)
                k.mm(psh[:, 0:128], lhsT=T.kdec[:], rhs=T.vnew[:])
                k.op("dve", "scalar_tensor_tensor", out=H[i][:], in0=H[i][:], scalar=egl[:, n, hv:hv + 1],
                     in1=psh[:, 0:128], op0=ALU.mult, op1=ALU.add)
                k.op("act", "copy", out=Hb[i][:], in_=H[i][:])
                yield
                k.op("pool", "memset", ap=T.ss[:], constant=0.0)
                k.act(out=T.junk[:], in_=T.o[:], func=AF.Square, accum_out=T.ss[:])
                k.op("dve", "tensor_scalar", out=T.sd[:], in0=T.ss[:], scalar1=1.0 / 128.0, scalar2=EPS,
                     op0=ALU.mult, op1=ALU.add)
                k.act(out=T.sd[:], in_=T.sd[:], func=AF.Sqrt)
                k.op("dve", "reciprocal", out=T.rs[:], in_=T.sd[:])
                k.op("dve", "scalar_tensor_tensor", out=T.y[:], in0=T.o[:], scalar=T.rs[:, 0:1], in1=normg[:],
                     op0=ALU.mult, op1=ALU.mult)
                k.dma("sp", T.zt[:], N.gz[n * 128:(n + 1) * 128, hv * 128:(hv + 1) * 128])
                k.op("pool", "tensor_tensor", out=T.y2[:], in0=T.y[:], in1=T.zt[:], op=ALU.mult)
                pst = psbf(nextps(C))
                k.tr(pst[:, 0:128], T.y2[:], C.identb[:])
                k.op("act", "copy", out=T.yT[:], in_=pst[:, 0:128])
                k.dma("sp", TB(N.yT.t)[hv * 128:(hv + 1) * 128, tc], T.yT[:])

            for step in range(NT + 1):
                gens = []
                if step < NT:
                    shared_prep(step)
                    gens += [prep(step, 0), prep(step, 1)]
                if step >= 1:
                    gens += [seq(step - 1, 0), seq(step - 1, 1)]
                interleave(gens)
        mem_attention(k, C, N)


def phase_mixer_c(k, C, N, j):
    phase_proj_c(k, C, N, j)
    phase_scan_c(k, C, N, j)


WEIGHT_SHAPES = [
    ("attn_norm", [4, 2048]), ("mem_norm", [4, 2048]), ("w_mem_kv", [4, 2048, 1024]), ("w_out", [4, 2048, 2048]),
    ("ffn_norm", [4, 2048]), ("w_ffn_up", [4, 2048, 11264]), ("ffn_conv", [4, 3, 11264]),
    ("w_ffn_down", [4, 5632, 2048]), ("final_norm", [2048]), ("a_w_in", [2, 2048, 2560]), ("a_sinks", [2, 24]),
    ("b_w_in", [1, 2048, 5568]), ("b_mu", [1, 5056]), ("b_w0", [1, 1536]), ("b_w_decay_up", [1, 96, 1536]),
    ("b_a0", [1, 1536]), ("b_w_iclr_up", [1, 96, 1536]), ("b_w_gate_up", [1, 256, 1536]), ("b_k_k", [1, 1536]),
    ("b_k_a", [1, 1536]), ("b_r_k", [1, 24, 64]), ("b_gn_g", [1, 1536]), ("b_gn_b", [1, 1536]),
    ("c_w_in", [1, 2048, 5144]), ("c_conv", [1, 4, 3072]), ("c_a_log", [1, 12]), ("c_dt_bias", [1, 12]),
    ("c_norm_g", [1, 128]),
]

SCRATCH = [
    ("xs", [S, D], F32), ("hT", [D, S], BF16), ("memhT", [D, 256], BF16), ("memkT", [512, 256], BF16),
    ("memv", [256, 512], BF16), ("qT", [1536, S], BF16), ("kT2", [512, S], BF16), ("v2", [S, 512], BF16),
    ("qmT", [512, S], BF16), ("yT", [D, S], BF16), ("aT", [DFF, S], BF16),
]


def fresh_patch():
    TB.f = lambda self: TB(self.t)


def emit_layer(k, C, N, li, cfg):
    kind, j = li % 3, li // 3
    only = cfg.get("only")

    def ph(name, fn, *a):
        if only is None or name in only:
            fn(*a)
    x_in = N.x if li == list(cfg.get('layers', range(4)))[0] else N.xs
    ph("norm1", phase_norm, k, C, x_in, Ref(N.attn_norm.t[li, :], N.attn_norm.buf), N.hT, S)
    ph("normm", phase_norm, k, C, N.mem, Ref(N.mem_norm.t[li, :], N.mem_norm.buf), N.memhT, 256)
    ph("memkv", phase_mem_kv, k, C, N, li)
    if kind == 0:
        ph("proj", phase_proj_a, k, C, N, j)
        ph("mix", phase_attn_a, k, C, N, j)
    elif kind == 1:
        ph("mix", phase_mixer_b, k, C, N, j)
    else:
        ph("mix", phase_mixer_c, k, C, N, j)
    ph("outproj", phase_outproj, k, C, N, li, x_in, N.xs)
    ph("ffnup", phase_ffn_up, k, C, N, li)
    ph("ffndown", phase_ffn_down, k, C, N, li, N.xs)
    return True


def build(cfg):
    nc = bass.Bass("TRN2", target_bir_lowering=False)
    dump = cfg.get("dump", ())
    with ExitStack() as st:
        k = K(nc, st)
        N = Net()
        N.x = k.dram("x", [S, D], F32, kind="ExternalInput")
        N.mem = k.dram("mem", [256, D], F32, kind="ExternalInput")
        used = cfg.get("weights")
        for name, shape in WEIGHT_SHAPES:
            if used is None or name in used:
                setattr(N, name, k.dram(name, shape, F32, kind="ExternalInput"))
        N.out = k.dram("out", [S, D], F32, kind="ExternalOutput")
        for name, shape, dt in SCRATCH + EXTRA_SCRATCH + B_SCRATCH:
            setattr(N, name, k.dram(name, shape, dt, kind=("ExternalOutput" if name in dump else "Internal")))
        C = setup_consts(k)
        layers = cfg.get("layers", range(4))
        CFG.clear()
        CFG.update(cfg)
        ok = True
        PH["n"] = 0
        PH["max"] = cfg.get("max_phases", 10 ** 9)
        try:
            for li in layers:
                ok = emit_layer(k, C, N, li, cfg)
                if not ok:
                    break
            if ok and cfg.get("final", True):
                phase_final_norm(k, C, N.xs, Ref(N.final_norm.t[:], N.final_norm.buf), N.out)
        except StopBuild:
            pass
        k_barrier(k)
        k.finish()
        k.stats = {n: (e.nins, e.count) for n, e in k.eng.items()}
        print("instr stats", k.stats)
    return nc


_CACHE = {}


def run(inputs, cfg, cores=8):
    key = repr(sorted((a, repr(b)) for a, b in cfg.items()))
    if key not in _CACHE:
        _CACHE[key] = build(cfg)
    nc = _CACHE[key]
    used = cfg.get("weights")
    wts = {n: np.ascontiguousarray(inputs[n], dtype=np.float32) for n, _ in WEIGHT_SHAPES
           if used is None or n in used}
    in_maps = []
    for b in range(cores):
        m = dict(wts)
        m["x"] = np.ascontiguousarray(inputs["x"][b], dtype=np.float32)
        m["mem"] = np.ascontiguousarray(inputs["mem"][b], dtype=np.float32)
        in_maps.append(m)
    return run_bass_kernel_spmd(nc, in_maps, core_ids=list(range(cores)))


def kernel(**inputs):
    res = run(inputs, {"layers": (0, 1, 2, 3)}, cores=8)
    return np.stack([np.asarray(r["out"], dtype=np.float32) for r in res.results], axis=0)
```

```python
import numpy as np
import concourse.bass as bass
import concourse.mybir as mybir
from concourse.bass_utils import run_bass_kernel_spmd

F32 = mybir.dt.float32
BF16 = mybir.dt.bfloat16
I32 = mybir.dt.int32
AF = mybir.ActivationFunctionType
ALU = mybir.AluOpType
AX = mybir.AxisListType


class Buf:
    __slots__ = ("w", "rs")

    def __init__(self):
        self.w = None
        self.rs = {}


class Ref:
    __slots__ = ("ap", "buf")

    def __init__(self, ap, buf):
        self.ap = ap
        self.buf = buf


class TB:
    def __init__(self, t, buf=None):
        self.t = t
        self.buf = buf or Buf()

    def __getitem__(self, idx):
        return Ref(self.t[idx], self.buf)

    def view(self, ap):
        return Ref(ap, self.buf)

    def part(self):
        return TB(self.t, Buf())


class Eng:
    def __init__(self, name, obj, sem):
        self.name = name
        self.obj = obj
        self.sem = sem
        self.count = 0
        self.waited = {}
        self.dma_sems = []
        self.dma_uses = []
        self.rr = 0
        self.nins = 0


WRITE_KW = ("out", "accum_out")


class K:
    def __init__(self, nc, stack, ndma=8):
        self.nc = nc
        self.stack = stack
        self.eng = {}
        for name, obj in (("pe", nc.tensor), ("act", nc.scalar), ("dve", nc.vector),
                          ("pool", nc.gpsimd), ("sp", nc.sync)):
            sem = stack.enter_context(nc.semaphore("s_" + name))
            self.eng[name] = Eng(name, obj, sem)
        for q in ("sp", "act", "pool"):
            E = self.eng[q]
            for i in range(ndma):
                E.dma_sems.append(stack.enter_context(nc.semaphore("d_%s%d" % (q, i))))
                E.dma_uses.append(0)
        self.uid = 0

    def sb(self, shape, dtype, name=None):
        self.uid += 1
        t = self.stack.enter_context(self.nc.sbuf_tensor(name or ("sb%d" % self.uid), list(shape), dtype))
        return TB(t)

    def ps(self, shape, dtype, name=None):
        self.uid += 1
        t = self.stack.enter_context(self.nc.psum_tensor(name or ("ps%d" % self.uid), list(shape), dtype))
        return TB(t)

    def dram(self, name, shape, dtype, kind="Internal"):
        t = self.nc.dram_tensor(name, list(shape), dtype, kind=kind)
        return TB(t.ap())

    def _wait(self, E, evs):
        for sem, val, owner in evs:
            if owner == "pe" and E.name == "pe":
                continue
            key = id(sem)
            if E.waited.get(key, 0) >= val:
                continue
            E.obj.wait_ge(sem, val)
            E.waited[key] = val
            E.nins += 1

    def _deps(self, reads, writes):
        evs = []
        for b in reads:
            if b.w is not None:
                evs.append(b.w)
        for b in writes:
            if b.w is not None:
                evs.append(b.w)
            evs.extend(b.rs.values())
        return evs

    def _record(self, ev, reads, writes):
        key = id(ev[0])
        for b in reads:
            old = b.rs.get(key)
            if old is None or old[1] < ev[1]:
                b.rs[key] = ev
        for b in writes:
            b.w = ev
            b.rs = {}

    def op(self, en, meth, *args, sig=True, R=(), W=(), **kw):
        E = self.eng[en]
        reads = [r.buf if isinstance(r, Ref) else r for r in R]
        writes = [w.buf if isinstance(w, Ref) else w for w in W]
        a2 = []
        for a in args:
            if isinstance(a, Ref):
                reads.append(a.buf)
                a = a.ap
            a2.append(a)
        k2 = {}
        for n, v in kw.items():
            if isinstance(v, Ref):
                (writes if n in WRITE_KW else reads).append(v.buf)
                v = v.ap
            k2[n] = v
        self._wait(E, self._deps(reads, writes))
        ins = getattr(E.obj, meth)(*a2, **k2)
        E.nins += 1
        if sig:
            E.count += 1
            ins.then_inc(E.sem, 1)
            ev = (E.sem, E.count, en)
        else:
            ev = (E.sem, E.count + 1, en)
        self._record(ev, reads, writes)
        return ins

    def dma(self, q, out, in_, **kw):
        E = self.eng[q]
        self._wait(E, self._deps([in_.buf], [out.buf]))
        k = E.rr
        sem = E.dma_sems[k]
        if E.dma_uses[k] > 0:
            self._wait(E, [(sem, 16 * E.dma_uses[k], "dma")])
        E.obj.dma_start(out=out.ap, in_=in_.ap, **kw).then_inc(sem, 16)
        E.nins += 1
        E.dma_uses[k] += 1
        ev = (sem, 16 * E.dma_uses[k], "dma")
        self._record(ev, [in_.buf], [out.buf])
        E.rr = (k + 1) % len(E.dma_sems)

    def finish(self):
        for q in ("sp", "act", "pool"):
            E = self.eng[q]
            for sem, uses in zip(E.dma_sems, E.dma_uses):
                if uses:
                    self._wait(E, [(sem, 16 * uses, "dma")])

    def mm(self, out, lhsT, rhs, start=True, stop=True, sig=None, **kw):
        if sig is None:
            sig = stop
        return self.op("pe", "matmul", out=out, lhsT=lhsT, rhs=rhs, start=start, stop=stop, sig=sig, **kw)

    def tr(self, out, in_, ident, sig=True):
        return self.op("pe", "transpose", out=out, in_=in_, identity=ident, sig=sig)

    def act(self, out, in_, func, **kw):
        return self.op("act", "activation", out=out, in_=in_, func=func, **kw)


from contextlib import ExitStack, contextmanager

S = 2048
D = 2048
NT = 16
NCH = 16
DFF = 5632
NFT = 44
EPS = 1e-6
NEG = -30000.0
WRITE_KW = ("out", "accum_out", "ap")


class Net:
    pass


def k_barrier(k):
    evs = []
    for n, E in k.eng.items():
        if E.count:
            evs.append((E.sem, E.count, n))
        for sem, uses in zip(E.dma_sems, E.dma_uses):
            if uses:
                evs.append((sem, 16 * uses, "dma"))
    for n, E in k.eng.items():
        k._wait(E, evs)


class StopBuild(Exception):
    pass


CFG = {}
PH = {"n": 0, "max": 10 ** 9}


@contextmanager
def phase(k):
    if PH["n"] >= PH["max"]:
        raise StopBuild()
    PH["n"] += 1
    k_barrier(k)
    saved = k.stack
    with ExitStack() as st:
        k.stack = st
        yield
        k_barrier(k)
    k.stack = saved


def psbf(ps):
    return TB(ps.t[:].bitcast(BF16), ps.buf)


def setup_consts(k):
    C = Net()
    C.onesf = k.sb([128, 128], F32)
    k.op("pool", "memset", ap=C.onesf[:], constant=1.0)
    C.onesb = k.sb([128, 128], BF16)
    k.op("dve", "tensor_copy", out=C.onesb[:], in_=C.onesf[:])

    def mask(cm, step, cmp):
        m = k.sb([128, 128], F32)
        k.op("pool", "affine_select", out=m[:], in_=C.onesf[:], pattern=[[step, 128]],
             compare_op=cmp, fill=0.0, base=0, channel_multiplier=cm)
        return m
    C.mask = mask
    C.identf = mask(1, -1, ALU.is_equal)
    C.identb = k.sb([128, 128], BF16)
    k.op("dve", "tensor_copy", out=C.identb[:], in_=C.identf[:])
    C.ps = [k.ps([128, 512], F32) for _ in range(8)]
    C.psi = 0
    return C


def nextps(C):
    p = C.ps[C.psi % 8]
    C.psi += 1
    return p


def bcast_rows(ap1d, n):
    return ap1d.partition_broadcast(128)


class NormBufs:
    def __init__(self, k, with_x=True):
        self.xt = [k.sb([128, D], F32) for _ in range(2)] if with_x else None
        self.junk = k.sb([128, D], BF16)
        self.ss = [k.sb([128, 1], F32) for _ in range(2)]
        self.rstd = [k.sb([128, 1], F32) for _ in range(2)]
        self.sd = [k.sb([128, 1], F32) for _ in range(2)]
        self.xn = [k.sb([128, D], BF16) for _ in range(2)]
        self.hts = [k.sb([128, NCH, 128], BF16) for _ in range(2)]


def rstd_from_ss(k, nb, b):
    k.op("dve", "tensor_scalar", out=nb.sd[b][:], in0=nb.ss[b][:], scalar1=1.0 / D, scalar2=EPS,
         op0=ALU.mult, op1=ALU.add)
    k.act(out=nb.sd[b][:], in_=nb.sd[b][:], func=AF.Sqrt)
    k.op("dve", "reciprocal", out=nb.rstd[b][:], in_=nb.sd[b][:])


def norm_tile(k, C, nb, xt, g_rep, out_tb, t, b):
    k.op("dve", "memset", ap=nb.ss[b][:], constant=0.0)
    k.act(out=nb.junk[:], in_=xt[:], func=AF.Square, accum_out=nb.ss[b][:])
    rstd_from_ss(k, nb, b)
    k.op("dve", "scalar_tensor_tensor", out=nb.xn[b][:], in0=xt[:], scalar=nb.rstd[b][:, 0:1],
         in1=g_rep[:], op0=ALU.mult, op1=ALU.mult)
    for half in range(2):
        ps = psbf(nextps(C))
        for c8 in range(8):
            c = half * 8 + c8
            k.tr(ps[:, c8 * 128:(c8 + 1) * 128], nb.xn[b][:, c * 128:(c + 1) * 128], C.identb[:], sig=(c8 == 7))
        src = ps.view(ps.t[:, :].rearrange("p (c n) -> p c n", c=8))
        if half == 0:
            k.op("act", "copy", out=nb.hts[b][:, 0:8, :], in_=src)
        else:
            k.op("dve", "tensor_copy", out=nb.hts[b][:, 8:16, :], in_=src)
    dst = out_tb.t.rearrange("(c p) n -> p c n", p=128)[:, :, t * 128:(t + 1) * 128]
    k.dma("sp", TB(out_tb.t).view(dst), nb.hts[b][:])


def load_grep(k, gvec_ref):
    g_rep = k.sb([128, D], F32)
    k.dma("sp", g_rep[:], Ref(bcast_rows(gvec_ref.ap, D), gvec_ref.buf))
    return g_rep


def phase_norm(k, C, x_tb, gvec_ref, out_tb, ntok):
    with phase(k):
        g_rep = load_grep(k, gvec_ref)
        nb = NormBufs(k)
        nt_ = ntok // 128
        k.dma("sp", nb.xt[0][:], x_tb[0:128, :])
        for t in range(nt_):
            b = t % 2
            if t + 1 < nt_:
                k.dma("sp", nb.xt[1 - b][:], x_tb[(t + 1) * 128:(t + 2) * 128, :])
            norm_tile(k, C, nb, nb.xt[b], g_rep, out_tb, t, b)


def phase_final_norm(k, C, x_tb, gvec_ref, out_tb):
    with phase(k):
        g_rep = load_grep(k, gvec_ref)
        nb = NormBufs(k)
        ot = [k.sb([128, D], F32) for _ in range(2)]
        k.dma("sp", nb.xt[0][:], x_tb[0:128, :])
        for t in range(NT):
            b = t % 2
            if t + 1 < NT:
                k.dma("sp", nb.xt[1 - b][:], x_tb[(t + 1) * 128:(t + 2) * 128, :])
            k.op("dve", "memset", ap=nb.ss[b][:], constant=0.0)
            k.act(out=nb.junk[:], in_=nb.xt[b][:], func=AF.Square, accum_out=nb.ss[b][:])
            rstd_from_ss(k, nb, b)
            k.op("dve", "scalar_tensor_tensor", out=ot[b][:], in0=nb.xt[b][:], scalar=nb.rstd[b][:, 0:1],
                 in1=g_rep[:], op0=ALU.mult, op1=ALU.mult)
            k.dma("sp", out_tb[t * 128:(t + 1) * 128, :], ot[b][:])


def load_hT(k, hT_tb, ntok):
    h = k.sb([128, NCH, ntok], BF16)
    v = hT_tb.t.rearrange("(c p) n -> p c n", p=128)
    for c0 in range(0, NCH, 4):
        k.dma("sp", h[:, c0:c0 + 4, :], hT_tb.view(v[:, c0:c0 + 4, :]))
    return h


class Stager:
    def __init__(self, k, nbuf=3, elems=2048, engines=("pool",)):
        self.k = k
        self.bufs = [k.sb([128, elems], F32) for _ in range(nbuf)]
        self.elems = elems
        self.i = 0
        self.engines = engines
        self.e = 0

    def load(self, dst, src, shape):
        k = self.k
        a, b = shape
        assert a * b <= self.elems
        st = self.bufs[self.i % len(self.bufs)]
        self.i += 1
        sv = st.view(st.t[:, 0:a * b].rearrange("p (a b) -> p a b", a=a))
        k.dma("sp", sv, src)
        eng = self.engines[self.e % len(self.engines)]
        self.e += 1
        if eng == "act":
            k.op("act", "copy", out=dst, in_=sv)
        else:
            k.op(eng, "tensor_copy", out=dst, in_=sv)


def load_w(k, wt, n, wref, stager):
    v = wref.ap.rearrange("(c p) n -> p c n", p=128)
    for n0 in range(0, n, 128):
        w = min(128, n - n0)
        stager.load(wt[:, :, n0:n0 + w], Ref(v[:, :, n0:n0 + w], wref.buf), (NCH, w))


class ProjCtx:
    def __init__(self, k, C, h_sb, ntok, nwt=3, wmax=128):
        self.k, self.C, self.h, self.ntok = k, C, h_sb, ntok
        self.wts = [k.sb([128, NCH, wmax], BF16) for _ in range(nwt)]
        self.i = 0
        self.stager = Stager(k, nbuf=2)
        self.q = []
        self.qi = 0
        self.loaded = {}

    def plan(self, lst):
        self.q = list(lst)
        self.qi = 0
        self.loaded = {}

    def _load(self, wref, n):
        wt = self.wts[self.i % len(self.wts)]
        self.i += 1
        load_w(self.k, wt, n, wref, self.stager)
        return wt

    def _take(self, wref, n):
        key = repr(wref.ap)
        if self.qi < len(self.q) and repr(self.q[self.qi][0].ap) == key:
            if self.qi not in self.loaded:
                self.loaded[self.qi] = self._load(wref, n)
            wt = self.loaded.pop(self.qi)
            self.qi += 1
            if self.qi < len(self.q):
                nr, nn = self.q[self.qi]
                self.loaded[self.qi] = self._load(nr, nn)
            return wt
        return self._load(wref, n)

    def F(self, wref, evac, n=128):
        k = self.k
        wt = self._take(wref, n)
        for tb in range(self.ntok // 512 if self.ntok >= 512 else 1):
            w = min(512, self.ntok)
            ps = nextps(self.C)
            for c in range(NCH):
                k.mm(ps[0:n, 0:w], lhsT=wt[:, c, 0:n], rhs=self.h[:, c, tb * 512:tb * 512 + w],
                     start=(c == 0), stop=(c == NCH - 1))
            evac(tb, ps)

    def T(self, wref, n, evac):
        k = self.k
        wt = self._take(wref, n)
        for tt in range(self.ntok // 128):
            ps = nextps(self.C)
            for c in range(NCH):
                k.mm(ps[:, 0:n], lhsT=self.h[:, c, tt * 128:(tt + 1) * 128], rhs=wt[:, c, 0:n],
                     start=(c == 0), stop=(c == NCH - 1))
            evac(tt, ps)


def alt_copy(k, i, out, in_):
    if i % 2 == 0:
        k.op("act", "copy", out=out, in_=in_)
    else:
        k.op("dve", "tensor_copy", out=out, in_=in_)


def proj_F_to_dram(k, P, wref, dst_tb, row0, stg, cnt, n=128):
    st = stg[cnt[0] % len(stg)]
    cnt[0] += 1

    def evac(tb, ps):
        w = min(512, P.ntok)
        alt_copy(k, tb, st[0:n, tb * 512:tb * 512 + w], ps[0:n, 0:w])
    P.F(wref, evac, n)
    k.dma("sp", dst_tb[row0:row0 + n, :], st[0:n, 0:P.ntok])


def phase_mem_kv(k, C, N, li):
    with phase(k):
        h = load_hT(k, N.memhT, 256)
        P = ProjCtx(k, C, h, 256, nwt=3, wmax=512)
        stg = [k.sb([128, 512], BF16) for _ in range(2)]
        cnt = [0]
        W = N.w_mem_kv
        P.plan([(Ref(W.t[li, :, j * 128:(j + 1) * 128], W.buf), 128) for j in range(4)]
               + [(Ref(W.t[li, :, 512:1024], W.buf), 512)])
        for j in range(4):
            proj_F_to_dram(k, P, Ref(W.t[li, :, j * 128:(j + 1) * 128], W.buf), N.memkT, j * 128, stg, cnt)
        st2 = [k.sb([128, 512], BF16) for _ in range(2)]

        def evac(tt, ps):
            s = st2[tt % 2]
            alt_copy(k, tt, s[:, :], ps[:, 0:512])
            k.dma("sp", N.memv[tt * 128:(tt + 1) * 128, :], s[:, :])
        P.T(Ref(W.t[li, :, 512:1024], W.buf), 512, evac)


def mem_attention(k, C, N):
    kT = k.sb([128, 4, 256], BF16)
    k.dma("sp", kT[:], N.memkT.view(N.memkT.t.rearrange("(h p) m -> p h m", p=128)))
    mv = k.sb([128, 2, 512], BF16)
    k.dma("sp", mv[:], N.memv.view(N.memv.t.rearrange("(t p) n -> p t n", p=128)))
    qm = [k.sb([128, 4, 512], BF16) for _ in range(2)]
    pt = [k.sb([128, 2, 512], BF16) for _ in range(2)]
    rec = [k.sb([128, 512], F32) for _ in range(2)]
    ym = [k.sb([128, 4, 512], BF16) for _ in range(2)]
    sc = 1.0 / np.sqrt(128.0)
    u = 0
    for tb in range(4):
        q = qm[tb % 2]
        k.dma("sp", q[:], N.qmT.view(N.qmT.t.rearrange("(h p) s -> p h s", p=128)[:, :, tb * 512:(tb + 1) * 512]))
        y = ym[tb % 2]
        for hm in range(4):
            p = pt[u % 2]
            r = rec[u % 2]
            u += 1
            for mt in range(2):
                ps = nextps(C)
                k.mm(ps[:, :], lhsT=kT[:, hm, mt * 128:(mt + 1) * 128], rhs=q[:, hm, :])
                k.act(out=p[:, mt, :], in_=ps[:, :], func=AF.Exp, scale=float(sc))
            pso = nextps(C)
            psd = nextps(C)
            for mt in range(2):
                k.mm(pso[:, :], lhsT=mv[:, mt, hm * 128:(hm + 1) * 128], rhs=p[:, mt, :], start=(mt == 0), stop=(mt == 1))
            for mt in range(2):
                k.mm(psd[:, :], lhsT=C.onesb[:], rhs=p[:, mt, :], start=(mt == 0), stop=(mt == 1))
            k.op("dve", "reciprocal", out=r[:], in_=psd[:, :])
            k.op("dve", "tensor_tensor", out=y[:, hm, :], in0=pso[:, :], in1=r[:], op=ALU.mult)
        dst = N.yT.t[1536:2048, :].rearrange("(h p) s -> p h s", p=128)[:, :, tb * 512:(tb + 1) * 512]
        k.dma("sp", N.yT.view(dst), y[:])


def alibi_slope(h):
    return float(2.0 ** (-8.0 * (h + 1.0) / 24.0))


def phase_proj_a(k, C, N, j):
    with phase(k):
        h = load_hT(k, N.hT, S)
        P = ProjCtx(k, C, h, S, nwt=3, wmax=512)
        stg = [k.sb([128, S], BF16) for _ in range(2)]
        cnt = [0]
        W = N.a_w_in
        P.plan([(Ref(W.t[j, :, c * 128:(c + 1) * 128], W.buf), 128) for c in range(12)]
               + [(Ref(W.t[j, :, 2048 + c * 128:2048 + (c + 1) * 128], W.buf), 128) for c in range(4)]
               + [(Ref(W.t[j, :, 1536 + c * 128:1536 + (c + 1) * 128], W.buf), 128) for c in range(2)]
               + [(Ref(W.t[j, :, 1792:2048], W.buf), 256)])
        for c in range(12):
            proj_F_to_dram(k, P, Ref(W.t[j, :, c * 128:(c + 1) * 128], W.buf), N.qT, c * 128, stg, cnt)
        for c in range(4):
            proj_F_to_dram(k, P, Ref(W.t[j, :, 2048 + c * 128:2048 + (c + 1) * 128], W.buf), N.qmT, c * 128, stg, cnt)
        for c in range(2):
            st = stg[cnt[0] % 2]
            cnt[0] += 1

            def evac(tb, ps, st=st):
                alt_copy(k, tb, st[:, tb * 512:(tb + 1) * 512], ps[:, :])
            P.F(Ref(W.t[j, :, 1536 + c * 128:1536 + (c + 1) * 128], W.buf), evac)
            for gg in range(2):
                g = 2 * c + gg
                for dup in range(2):
                    k.dma("sp", N.kT2[g * 128 + dup * 64:g * 128 + dup * 64 + 64, :], st[gg * 64:(gg + 1) * 64, :])
        st2 = [k.sb([128, 4, 128], BF16) for _ in range(2)]

        def evacv(tt, ps):
            s = st2[tt % 2]
            src = ps.view(ps.t[:, 0:256].rearrange("p (g d) -> p g d", g=4))
            k.op("act", "copy", out=s[:, :, 0:64], in_=src)
            k.op("dve", "tensor_copy", out=s[:, :, 64:128], in_=src)
            k.dma("sp", N.v2.view(N.v2.t[tt * 128:(tt + 1) * 128, :].rearrange("p (g d) -> p g d", g=4)), s[:])
        P.T(Ref(W.t[j, :, 1792:2048], W.buf), 256, evacv)


def phase_attn_a(k, C, N, j):
    with phase(k):
        dist = k.sb([128, 128], F32)
        k.op("pool", "iota", dist[:], pattern=[[1, 128]], base=0, channel_multiplier=-1,
             allow_small_or_imprecise_dtypes=True, W=[dist[:]])
        mbc = k.sb([128, 24, 128], F32)
        mbp = k.sb([128, 24, 128], F32)
        for h in range(24):
            sl = alibi_slope(h)
            k.op("dve", "tensor_scalar", out=mbc[:, h, :], in0=dist[:], scalar1=-sl, scalar2=None, op0=ALU.mult)
            k.op("dve", "tensor_scalar", out=mbp[:, h, :], in0=dist[:], scalar1=-sl, scalar2=-128.0 * sl,
                 op0=ALU.mult, op1=ALU.add)
        k.op("pool", "affine_select", out=mbc[:], in_=mbc[:], pattern=[[0, 24], [1, 128]],
             compare_op=ALU.is_ge, fill=NEG, base=0, channel_multiplier=-1)
        k.op("pool", "affine_select", out=mbp[:], in_=mbp[:], pattern=[[0, 24], [-1, 128]],
             compare_op=ALU.is_gt, fill=NEG, base=0, channel_multiplier=1)
        sk = k.sb([128, 24], F32)
        k.dma("sp", sk[:], Ref(bcast_rows(N.a_sinks.t[j, :], 24), N.a_sinks.buf))
        sinkexp = k.sb([128, 24], F32)
        k.act(out=sinkexp[:], in_=sk[:], func=AF.Exp)
        kTz = k.sb([128, 8, S], BF16)
        k.op("pool", "memset", ap=kTz[:], constant=0.0)
        for g in range(4):
            for hf in range(2):
                k.dma("sp", kTz[hf * 64:hf * 64 + 64, 2 * g + hf, :],
                      N.kT2[g * 128 + hf * 64:g * 128 + hf * 64 + 64, :])
        v2 = k.sb([128, NT, 512], BF16)
        vv = N.v2.t.rearrange("(t p) n -> p t n", p=128)
        for t0 in range(0, NT, 4):
            k.dma("sp", v2[:, t0:t0 + 4, :], N.v2.view(vv[:, t0:t0 + 4, :]))
        qs = [k.sb([128, 12, 512], BF16) for _ in range(2)]
        ys = [k.sb([128, 12, 512], BF16) for _ in range(2)]
        scb = [k.sb([128, 2, 384], F32) for _ in range(2)]
        ptb = [k.sb([128, 2, 384], BF16) for _ in range(2)]
        rcb = [k.sb([128, 3, 128], F32) for _ in range(2)]
        qv = N.qT.t.rearrange("(c p) s -> p c s", p=128)
        yv = N.yT.t[0:1536, :].rearrange("(c p) s -> p c s", p=128)
        u = 0
        for n4 in range(CFG.get("attn_n4", 4)):
            q = qs[n4 % 2]
            y = ys[n4 % 2]
            k.dma("sp", q[:], N.qT.view(qv[:, :, n4 * 512:(n4 + 1) * 512]))
            for nn in range(4):
                n = n4 * 4 + nn
                qc = slice(nn * 128, (nn + 1) * 128)
                kbs = [n] if n == 0 else [n - 1, n]
                for g in range(4):
                    for h3 in range(2):
                        sc_, pt_, rc_ = scb[u % 2], ptb[u % 2], rcb[u % 2]
                        u += 1
                        heads = [g * 6 + h3 * 3 + i for i in range(3)]
                        pss = []
                        for bi, kb in enumerate(kbs):
                            ps = nextps(C)
                            pss.append(ps)
                            for i, hh in enumerate(heads):
                                c, hf = hh // 2, hh % 2
                                k.mm(ps[:, i * 128:(i + 1) * 128], lhsT=kTz[:, 2 * g + hf, kb * 128:(kb + 1) * 128],
                                     rhs=q[:, c, qc], start=True, stop=True, sig=(i == 2))
                        for bi, kb in enumerate(kbs):
                            mb = mbc if kb == n else mbp
                            k.op("dve", "scalar_tensor_tensor",
                                 out=sc_.view(sc_.t[:, bi, :].rearrange("p (h q) -> p h q", h=3)),
                                 in0=pss[bi].view(pss[bi].t[:, 0:384].rearrange("p (h q) -> p h q", h=3)),
                                 scalar=0.125, in1=mb[:, heads[0]:heads[0] + 3, :], op0=ALU.mult, op1=ALU.add)
                            k.act(out=pt_[:, bi, :], in_=sc_[:, bi, :], func=AF.Exp)
                        if CFG.get("attn_stage", 3) < 2:
                            continue
                        pso = nextps(C)
                        psd = nextps(C)
                        nk = len(kbs)
                        for bi, kb in enumerate(kbs):
                            k.mm(pso[:, 0:384], lhsT=v2[:, kb, g * 128:(g + 1) * 128], rhs=pt_[:, bi, :],
                                 start=(bi == 0), stop=(bi == nk - 1))
                        for bi, kb in enumerate(kbs):
                            k.mm(psd[:, 0:384], lhsT=C.onesb[:], rhs=pt_[:, bi, :],
                                 start=(bi == 0), stop=(bi == nk - 1))
                        if CFG.get("attn_stage", 3) < 3:
                            continue
                        for i, hh in enumerate(heads):
                            k.op("dve", "tensor_scalar", out=rc_[:, i, :], in0=psd[:, i * 128:(i + 1) * 128],
                                 scalar1=sinkexp[:, hh:hh + 1], scalar2=None, op0=ALU.add)
                        k.op("dve", "reciprocal", out=rc_[:], in_=rc_[:])
                        for i, hh in enumerate(heads):
                            c, hf = hh // 2, hh % 2
                            rows = slice(hf * 64, hf * 64 + 64)
                            k.op("dve", "tensor_tensor", out=y[rows, c, qc], in0=pso[rows, i * 128:(i + 1) * 128],
                                 in1=rc_[rows, i, :], op=ALU.mult)
            k.dma("sp", N.yT.view(yv[:, :, n4 * 512:(n4 + 1) * 512]), y[:])
        if CFG.get("memattn", True):
            mem_attention(k, C, N)


def phase_outproj(k, C, N, li, x_in, x_out):
    with phase(k):
        g_rep = load_grep(k, Ref(N.ffn_norm.t[li, :], N.ffn_norm.buf))
        nb = NormBufs(k)
        wo = k.sb([128, NCH, D], BF16)
        wv = N.w_out.t[li].rearrange("(c p) n -> p c n", p=128)
        stg = Stager(k, nbuf=3, elems=2048, engines=("pool", "act", "pool", "dve"))
        for c0 in range(NCH):
            stg.load(wo[:, c0:c0 + 1, :], Ref(wv[:, c0:c0 + 1, :], N.w_out.buf), (1, D))
        yts = [k.sb([128, NCH, 128], BF16) for _ in range(2)]
        yv = N.yT.t.rearrange("(c p) s -> p c s", p=128)
        def ld(t):
            k.dma("sp", yts[t % 2][:], N.yT.view(yv[:, :, t * 128:(t + 1) * 128]))
            k.dma("sp", nb.xt[t % 2][:], TB(x_in.t)[t * 128:(t + 1) * 128, :])
        ld(0)
        for t in range(NT):
            b = t % 2
            yt = yts[b]
            xt = nb.xt[b]
            if t + 1 < NT:
                ld(t + 1)
            for nbk in range(4):
                ps = nextps(C)
                for c in range(NCH):
                    k.mm(ps[:, :], lhsT=yt[:, c, :], rhs=wo[:, c, nbk * 512:(nbk + 1) * 512],
                         start=(c == 0), stop=(c == NCH - 1))
                k.op("dve", "tensor_tensor", out=xt[:, nbk * 512:(nbk + 1) * 512],
                     in0=xt[:, nbk * 512:(nbk + 1) * 512], in1=ps[:, :], op=ALU.add)
            k.dma("sp", TB(x_out.t)[t * 128:(t + 1) * 128, :], xt[:])
            norm_tile(k, C, nb, xt, g_rep, N.hT, t, b)


def phase_ffn_up(k, C, N, li):
    with phase(k):
        h = load_hT(k, N.hT, S)
        cwj = k.sb([88, 3, 128], F32)
        k.dma("sp", cwj[:], Ref(N.ffn_conv.t[li].rearrange("k (j p) -> j k p", p=128), N.ffn_conv.buf))
        cw = k.sb([128, 3, 88], F32)
        for kk in range(3):
            ps = nextps(C)
            k.op("pe", "transpose", out=ps[:, 0:88], in_=cwj[:, kk, :], identity=C.identf[0:88, 0:88])
            k.op("dve", "tensor_copy", out=cw[:, kk, :], in_=ps[:, 0:88])
        wg = [k.sb([128, NCH, 128], BF16) for _ in range(3)]
        wvv = [k.sb([128, NCH, 128], BF16) for _ in range(3)]
        ug = [k.sb([128, 2 + S], F32) for _ in range(2)]
        uv = [k.sb([128, 2 + S], F32) for _ in range(2)]
        for t_ in ug + uv:
            k.op("pool", "memset", ap=t_[:, 0:2], constant=0.0)
        cg = [k.sb([128, 1024], F32) for _ in range(2)]
        cv = [k.sb([128, 1024], F32) for _ in range(2)]
        sg = [k.sb([128, 1024], F32) for _ in range(2)]
        ao = [k.sb([128, 1024], BF16) for _ in range(2)]
        stg = Stager(k, nbuf=4)
        W = N.w_ffn_up
        u = 0
        def ldw(j):
            load_w(k, wg[j % 3], 128, Ref(W.t[li, :, j * 128:(j + 1) * 128], W.buf), stg)
            load_w(k, wvv[j % 3], 128, Ref(W.t[li, :, DFF + j * 128:DFF + (j + 1) * 128], W.buf), stg)
        ldw(0)
        for j in range(NFT):
            a, b_ = wg[j % 3], wvv[j % 3]
            if j + 1 < NFT:
                ldw(j + 1)
            ugj, uvj = ug[j % 2], uv[j % 2]
            for half in range(2):
                o = half * 1024
                for which, wt, ub in ((0, a, ugj), (1, b_, uvj)):
                    for tb in range(2):
                        ps = nextps(C)
                        for c in range(NCH):
                            k.mm(ps[:, :], lhsT=wt[:, c, :], rhs=h[:, c, o + tb * 512:o + (tb + 1) * 512],
                                 start=(c == 0), stop=(c == NCH - 1))
                        k.op("act", "copy", out=ub[:, 2 + o + tb * 512:2 + o + (tb + 1) * 512], in_=ps[:, :])
                cgu, cvu, sgu, aou = cg[u % 2], cv[u % 2], sg[u % 2], ao[u % 2]
                u += 1
                for eng, ub, co, jj in (("dve", ugj, cgu, j), ("dve", uvj, cvu, NFT + j)):
                    k.op(eng, "tensor_scalar", out=co[:], in0=ub[:, 2 + o:2 + o + 1024],
                         scalar1=cw[:, 2, jj:jj + 1], scalar2=None, op0=ALU.mult)
                    k.op(eng, "scalar_tensor_tensor", out=co[:], in0=ub[:, 1 + o:1 + o + 1024],
                         scalar=cw[:, 1, jj:jj + 1], in1=co[:], op0=ALU.mult, op1=ALU.add)
                    k.op(eng, "scalar_tensor_tensor", out=co[:], in0=ub[:, o:o + 1024],
                         scalar=cw[:, 0, jj:jj + 1], in1=co[:], op0=ALU.mult, op1=ALU.add)
                k.act(out=sgu[:], in_=cgu[:], func=AF.Silu)
                k.op("pool", "tensor_tensor", out=aou[:], in0=sgu[:], in1=cvu[:], op=ALU.mult)
                k.dma("sp", N.aT[j * 128:(j + 1) * 128, o:o + 1024], aou[:])


def phase_ffn_down(k, C, N, li, x_tb):
    with phase(k):
        wd = k.sb([128, NFT, 1024], BF16)
        ats = [k.sb([128, NFT, 256], BF16) for _ in range(2)]
        xts = [k.sb([128, 2, 1024], F32) for _ in range(2)]
        stg = Stager(k, nbuf=3, elems=2048, engines=("pool", "act", "dve"))
        av = N.aT.t.rearrange("(c p) s -> p c s", p=128)
        W = N.w_ffn_down
        u = 0
        for nh in range(2):
            wv = W.t[li, :, nh * 1024:(nh + 1) * 1024].rearrange("(c p) n -> p c n", p=128)
            for c0 in range(0, NFT, 2):
                stg.load(wd[:, c0:c0 + 2, :], Ref(wv[:, c0:c0 + 2, :], W.buf), (2, 1024))
            def ld(uu, nh_, t2_):
                at_, xt_ = ats[uu % 2], xts[uu % 2]
                for c0 in range(0, NFT, 11):
                    k.dma("sp", at_[:, c0:c0 + 11, :], N.aT.view(av[:, c0:c0 + 11, t2_ * 256:(t2_ + 1) * 256]))
                xv_ = x_tb.t[t2_ * 256:(t2_ + 1) * 256, nh_ * 1024:(nh_ + 1) * 1024].rearrange(
                    "(t p) n -> p t n", p=128)
                k.dma("sp", xt_[:], TB(x_tb.t).view(xv_))
            if nh == 0:
                ld(0, 0, 0)
            for t2 in range(NT // 2):
                at, xt = ats[u % 2], xts[u % 2]
                u += 1
                if t2 + 1 < NT // 2:
                    ld(u, nh, t2 + 1)
                elif nh == 0:
                    ld(u, 1, 0)
                xv = x_tb.t[t2 * 256:(t2 + 1) * 256, nh * 1024:(nh + 1) * 1024].rearrange("(t p) n -> p t n", p=128)
                for ts in range(2):
                    for nbk in range(2):
                        ps = nextps(C)
                        for c in range(NFT):
                            k.mm(ps[:, :], lhsT=at[:, c, ts * 128:(ts + 1) * 128],
                                 rhs=wd[:, c, nbk * 512:(nbk + 1) * 512], start=(c == 0), stop=(c == NFT - 1))
                        k.op("dve", "tensor_tensor", out=xt[:, ts, nbk * 512:(nbk + 1) * 512],
                             in0=xt[:, ts, nbk * 512:(nbk + 1) * 512], in1=ps[:, :], op=ALU.add)
                k.dma("sp", TB(x_tb.t).view(xv), xt[:])


B_SCRATCH = [
    ("brT", [1536, S], BF16), ("bkT", [1536, S], BF16), ("bkkT", [1536, S], BF16), ("bbT", [1536, S], BF16),
    ("bvT", [1536, S], BF16), ("blwT", [1536, S], F32), ("bbonT", [1536, S], BF16), ("bgT", [1536, S], BF16),
]
DECAY_C = 0.6065306597126334


def colvec(k, ref1d, n=1536):
    nc_ = n // 128
    rows = k.sb([nc_, 128], F32)
    k.dma("sp", rows[:], Ref(ref1d.ap.rearrange("(c p) -> c p", p=128), ref1d.buf))
    t = k.sb([128, nc_], F32)
    ps = nextps(CREF[0])
    k.op("pe", "transpose", out=ps[:, 0:nc_], in_=rows[:], identity=CREF[0].identf[0:nc_, 0:nc_])
    k.op("dve", "tensor_copy", out=t[:], in_=ps[:, 0:nc_])
    return t


CREF = [None]


def phase_proj_b(k, C, N, j):
    CREF[0] = C
    with phase(k):
        h = load_hT(k, N.hT, S)
        P = ProjCtx(k, C, h, S, nwt=3, wmax=128)
        W = N.b_w_in
        pl = [(4608, 96), (4704, 96), (4800, 128), (4928, 128)]
        for c in range(12):
            pl += [(3072 + c * 128, 128), (c * 128, 128), (1536 + c * 128, 128)]
        pl += [(5056 + c * 128, 128) for c in range(4)]
        P.plan([(Ref(W.t[j, :, c0:c0 + n], W.buf), n) for c0, n in pl])
        V = lambda name: Ref(getattr(N, name).t[j, :], getattr(N, name).buf)
        w0c, a0c, kkc, kac, gngc = colvec(k, V("b_w0")), colvec(k, V("b_a0")), colvec(k, V("b_k_k")), \
            colvec(k, V("b_k_a")), None
        rkc = colvec(k, Ref(N.b_r_k.t[j].rearrange("h d -> (h d)"), N.b_r_k.buf))
        omka = k.sb([128, 12], F32)
        k.op("dve", "tensor_scalar", out=omka[:], in0=kac[:], scalar1=-1.0, scalar2=1.0, op0=ALU.mult, op1=ALU.add)
        blk = k.sb([128, 128], BF16)
        k.op("pool", "memset", ap=blk[:], constant=0.0)
        k.op("pool", "memset", ap=blk[0:64, 0:64], constant=1.0)
        k.op("pool", "memset", ap=blk[64:128, 64:128], constant=1.0)
        wdec = k.sb([96, 1536], BF16)
        wicl = k.sb([96, 1536], BF16)
        wgt = k.sb([128, 2, 1536], BF16)
        aT = k.sb([128, S], F32)
        t1 = k.sb([128, S], F32)
        rn = k.sb([128, S], F32)
        t2 = rn
        k.dma("sp", aT[0:96, 0:1536], Ref(N.b_w_decay_up.t[j], N.b_w_decay_up.buf))
        k.op("pool", "tensor_copy", out=wdec[:], in_=aT[0:96, 0:1536])
        k.dma("sp", t1[0:96, 0:1536], Ref(N.b_w_iclr_up.t[j], N.b_w_iclr_up.buf))
        k.op("pool", "tensor_copy", out=wicl[:], in_=t1[0:96, 0:1536])
        for kc in range(2):
            k.dma("sp", rn[:, 0:1536], Ref(N.b_w_gate_up.t[j, kc * 128:(kc + 1) * 128, :], N.b_w_gate_up.buf))
            k.op("pool", "tensor_copy", out=wgt[:, kc, :], in_=rn[:, 0:1536])
        ub = [k.sb([128, 1 + S], F32) for _ in range(1)]
        for t_ in ub:
            k.op("pool", "memset", ap=t_[:, 0:1], constant=0.0)
        mus = [k.sb([128, 2], F32) for _ in range(2)]
        mx = [k.sb([128, S], F32) for _ in range(2)]
        cnt = [0]

        def mixed(c0, n):
            i = cnt[0]
            cnt[0] += 1
            u, mu, m = ub[0], mus[i % 2], mx[i % 2]

            def evac(tb, ps):
                alt_copy(k, tb, u[0:n, 1 + tb * 512:1 + (tb + 1) * 512], ps[0:n, :])
            P.F(Ref(W.t[j, :, c0:c0 + n], W.buf), evac, n)
            k.dma("sp", mu[0:n, 0:1], Ref(N.b_mu.t[j, c0:c0 + n].rearrange("(p o) -> p o", o=1), N.b_mu.buf))
            k.op("dve", "tensor_scalar", out=mu[0:n, 1:2], in0=mu[0:n, 0:1], scalar1=-1.0, scalar2=1.0,
                 op0=ALU.mult, op1=ALU.add)
            k.op("dve", "tensor_scalar", out=m[0:n, :], in0=u[0:n, 0:S], scalar1=mu[0:n, 0:1], scalar2=None,
                 op0=ALU.mult)
            k.op("dve", "scalar_tensor_tensor", out=m[0:n, :], in0=u[0:n, 1:1 + S], scalar=mu[0:n, 1:2],
                 in1=m[0:n, :], op0=ALU.mult, op1=ALU.add)
            return m
        twT = k.sb([96, S], BF16)
        adT = k.sb([96, S], BF16)
        sgT = k.sb([128, 2, S], BF16)
        m = mixed(4608, 96)
        k.act(out=twT[:], in_=m[0:96, :], func=AF.Tanh)
        m = mixed(4704, 96)
        k.op("dve", "tensor_copy", out=adT[:], in_=m[0:96, :])
        for i in range(2):
            m = mixed(4800 + i * 128, 128)
            k.act(out=sgT[:, i, :], in_=m[:], func=AF.Sigmoid)
        sq = k.sb([128, S], BF16)
        o16 = [k.sb([128, S], BF16) for _ in range(3)]
        o32 = [k.sb([128, S], F32) for _ in range(1)]
        vb = k.sb([128, S], BF16)
        rb = k.sb([128, S], BF16)
        oc = [0]

        def out16():
            oc[0] += 1
            return o16[oc[0] % 3]
        for c in range(12):
            cs = slice(c * 128, (c + 1) * 128)
            lw = o32[0]
            go = out16()
            for tb in range(4):
                ts_ = slice(tb * 512, (tb + 1) * 512)
                ps = nextps(C)
                k.mm(ps[:, :], lhsT=wicl[:, cs], rhs=adT[:, ts_])
                k.act(out=aT[:, ts_], in_=ps[:, :], func=AF.Sigmoid, bias=a0c[:, c:c + 1])
                ps = nextps(C)
                k.mm(ps[:, :], lhsT=wdec[:, cs], rhs=twT[:, ts_])
                k.act(out=lw[:, ts_], in_=ps[:, :], func=AF.Sigmoid, bias=w0c[:, c:c + 1])
                ps = nextps(C)
                for kc in range(2):
                    k.mm(ps[:, :], lhsT=wgt[:, kc, cs], rhs=sgT[:, kc, ts_], start=(kc == 0), stop=(kc == 1))
                k.op("dve", "tensor_copy", out=go[:, ts_], in_=ps[:, :])
            k.op("dve", "tensor_scalar", out=lw[:], in0=lw[:], scalar1=-DECAY_C, scalar2=None, op0=ALU.mult)
            k.dma("sp", N.blwT[cs, :], lw[:])
            k.dma("sp", N.bgT[cs, :], go[:])
            m = mixed(3072 + c * 128, 128)
            k.op("pool", "tensor_copy", out=vb[:], in_=m[:])
            k.dma("sp", N.bvT[cs, :], vb[:])
            m = mixed(c * 128, 128)
            k.op("pool", "tensor_copy", out=rb[:], in_=m[:])
            k.dma("sp", N.brT[cs, :], rb[:])
            m = mixed(1536 + c * 128, 128)
            k.op("dve", "tensor_scalar", out=t1[:], in0=m[:], scalar1=kkc[:, c:c + 1], scalar2=None, op0=ALU.mult)
            k.act(out=sq[:], in_=t1[:], func=AF.Square)
            for tb in range(4):
                ts_ = slice(tb * 512, (tb + 1) * 512)
                ps = nextps(C)
                k.mm(ps[:, :], lhsT=blk[:], rhs=sq[:, ts_])
                k.op("dve", "tensor_scalar", out=rn[:, ts_], in0=ps[:, :], scalar1=1e-6, scalar2=None, op0=ALU.add)
            k.act(out=rn[:], in_=rn[:], func=AF.Sqrt)
            k.op("dve", "reciprocal", out=rn[:], in_=rn[:])
            kko = out16()
            k.op("dve", "tensor_tensor", out=t1[:], in0=t1[:], in1=rn[:], op=ALU.mult)
            k.op("pool", "tensor_copy", out=kko[:], in_=t1[:])
            k.dma("sp", N.bkkT[cs, :], kko[:])
            bo = out16()
            k.op("dve", "tensor_tensor", out=bo[:], in0=t1[:], in1=aT[:], op=ALU.mult)
            k.dma("sp", N.bbT[cs, :], bo[:])
            k.op("dve", "tensor_scalar", out=t2[:], in0=aT[:], scalar1=kac[:, c:c + 1], scalar2=omka[:, c:c + 1],
                 op0=ALU.mult, op1=ALU.add)
            k.op("dve", "tensor_tensor", out=t2[:], in0=t2[:], in1=m[:], op=ALU.mult)
            ko = out16()
            k.op("pool", "tensor_copy", out=ko[:], in_=t2[:])
            k.dma("sp", N.bkT[cs, :], ko[:])
            k.op("dve", "scalar_tensor_tensor", out=sq[:], in0=t2[:], scalar=rkc[:, c:c + 1], in1=rb[:],
                 op0=ALU.mult, op1=ALU.mult)
            bon = out16()
            for tb in range(4):
                ts_ = slice(tb * 512, (tb + 1) * 512)
                ps = nextps(C)
                k.mm(ps[:, :], lhsT=blk[:], rhs=sq[:, ts_])
                k.op("dve", "tensor_tensor", out=bon[:, ts_], in0=ps[:, :], in1=vb[:, ts_], op=ALU.mult)
            k.dma("sp", N.bbonT[cs, :], bon[:])
        stg = o16[0:2]
        cn2 = [0]
        for c in range(4):
            proj_F_to_dram(k, P, Ref(W.t[j, :, 5056 + c * 128:5056 + (c + 1) * 128], W.buf), N.qmT, c * 128, stg, cn2)


class RwTmp:
    def __init__(self, k):
        f = lambda dt, n=128: k.sb([128, n], dt)
        self.KiP = f(BF16)
        self.PiP = f(BF16)
        self.AbT = f(F32)
        self.Ab = f(F32)
        self.AkT = f(BF16)
        self.ArT = f(BF16, 256)
        self.PTb = f(BF16)
        self.Zb = f(BF16, 64)
        self.Un = f(BF16, 64)
        self.y = f(F32, 64)
        self.junk = f(F32, 64)
        self.s1 = k.sb([128, 1], F32)
        self.s2 = k.sb([128, 1], F32)
        self.mean = k.sb([128, 1], F32)
        self.var = k.sb([128, 1], F32)
        self.rs = k.sb([128, 1], F32)
        self.tmpH = f(F32, 64)
        self.ws = TriWS(k)


class RwPair:
    def __init__(self, k):
        f = lambda dt, n=128: k.sb([128, n], dt)
        self.g = f(F32)
        self.gx = f(F32)
        self.Ei, self.En, self.Ex, self.Ed = f(F32), f(F32), f(F32), f(F32)
        self.Rd, self.Ki, self.Pi, self.KKd, self.Kdc, self.Pdc = (f(BF16) for _ in range(6))
        self.Kdt, self.Pdt, self.Vt = f(BF16), f(BF16), f(BF16)
        self.yn = f(BF16)
        self.yf = f(F32)
        self.yo = f(BF16)


def phase_scan_b(k, C, N, j):
    CREF[0] = C
    with phase(k):
        strictT = C.mask(-1, 1, ALU.is_gt)
        inclT = C.mask(-1, 1, ALU.is_ge)
        msk2i = k.sb([128, 2, 128], F32)
        for i in range(2):
            k.op("dve", "tensor_copy", out=msk2i[:, i, :], in_=inclT[:])
        hm = k.sb([128, 2], F32)
        k.op("pool", "memset", ap=hm[:], constant=0.0)
        k.op("pool", "memset", ap=hm[0:64, 0:1], constant=1.0)
        k.op("pool", "memset", ap=hm[64:128, 1:2], constant=1.0)
        V = lambda name: Ref(getattr(N, name).t[j, :], getattr(N, name).buf)
        gng, gnb = colvec(k, V("b_gn_g")), colvec(k, V("b_gn_b"))
        names = ("brT", "bkT", "bkkT", "bbT", "bvT", "bbonT", "bgT")
        inb = [{nm: k.sb([128, S], BF16) for nm in names} for _ in range(2)]
        lwb = [k.sb([128, S], F32) for _ in range(2)]
        prs = [[RwPair(k) for _ in range(2)] for _ in range(2)]
        tms = [[[RwTmp(k) for _ in range(2)] for _ in range(2)] for _ in range(2)]
        Hf = [[k.sb([128, 64], F32) for _ in range(2)] for _ in range(2)]
        Hb = [[k.sb([128, 64], BF16) for _ in range(2)] for _ in range(2)]
        un = 0
        pu = 0
        def make(c, slot):
            cs = slice(c * 128, (c + 1) * 128)
            I = inb[slot]
            lw = lwb[slot]
            for nm in names:
                k.dma("sp", I[nm][:], getattr(N, nm)[cs, :])
            k.dma("sp", lw[:], N.blwT[cs, :])
            for i in range(2):
                k.op("pool", "memset", ap=Hf[slot][i][:], constant=0.0)
                k.op("pool", "memset", ap=Hb[slot][i][:], constant=0.0)
            def pair_prep(n):
                tc = slice(n * 128, (n + 1) * 128)
                Pp = prs[slot][n % 2]
                k.op("dve", "tensor_tensor_scan", out=Pp.g[:], data0=C.onesf[:], data1=lw[:, tc], initial=0.0,
                     op0=ALU.mult, op1=ALU.add)
                k.op("dve", "tensor_tensor", out=Pp.gx[:], in0=Pp.g[:], in1=lw[:, tc], op=ALU.subtract)
                k.act(out=Pp.Ei[:], in_=Pp.g[:], func=AF.Exp)
                k.act(out=Pp.En[:], in_=Pp.g[:], func=AF.Exp, scale=-1.0)
                k.act(out=Pp.Ex[:], in_=Pp.gx[:], func=AF.Exp)
                k.act(out=Pp.Ed[:], in_=Pp.g[:], func=AF.Exp, scale=-1.0, bias=Pp.g[:, 127:128])
                k.op("dve", "tensor_tensor", out=Pp.Rd[:], in0=I["brT"][:, tc], in1=Pp.Ei[:], op=ALU.mult)
                k.op("dve", "tensor_tensor", out=Pp.Ki[:], in0=I["bkT"][:, tc], in1=Pp.En[:], op=ALU.mult)
                k.op("dve", "tensor_tensor", out=Pp.Pi[:], in0=I["bbT"][:, tc], in1=Pp.En[:], op=ALU.mult)
                k.op("dve", "tensor_tensor", out=Pp.KKd[:], in0=I["bkkT"][:, tc], in1=Pp.Ex[:], op=ALU.mult)
                k.op("pool", "tensor_tensor", out=Pp.Kdc[:], in0=I["bkT"][:, tc], in1=Pp.Ed[:], op=ALU.mult)
                k.op("pool", "tensor_tensor", out=Pp.Pdc[:], in0=I["bbT"][:, tc], in1=Pp.Ed[:], op=ALU.mult)
                pst = psbf(nextps(C))
                k.tr(pst[:, 0:128], Pp.Kdc[:], C.identb[:], sig=False)
                k.tr(pst[:, 128:256], Pp.Pdc[:], C.identb[:], sig=False)
                k.tr(pst[:, 256:384], I["bvT"][:, tc], C.identb[:])
                k.op("act", "copy", out=Pp.Kdt[:], in_=pst[:, 0:128])
                k.op("act", "copy", out=Pp.Pdt[:], in_=pst[:, 128:256])
                k.op("act", "copy", out=Pp.Vt[:], in_=pst[:, 256:384])

            def prep(n, i):
                tc = slice(n * 128, (n + 1) * 128)
                Pp = prs[slot][n % 2]
                T = tms[slot][i][n % 2]
                k.op("dve", "tensor_scalar", out=T.KiP[:], in0=Pp.Ki[:], scalar1=hm[:, i:i + 1], scalar2=None,
                     op0=ALU.mult)
                k.op("dve", "tensor_scalar", out=T.PiP[:], in0=Pp.Pi[:], scalar1=hm[:, i:i + 1], scalar2=None,
                     op0=ALU.mult)
                psA = nextps(C)
                k.mm(psA[:, 0:128], lhsT=T.PiP[:], rhs=Pp.KKd[:], sig=False)
                k.mm(psA[:, 128:256], lhsT=T.KiP[:], rhs=Pp.KKd[:], sig=False)
                k.mm(psA[:, 256:384], lhsT=T.KiP[:], rhs=Pp.Rd[:], sig=False)
                k.mm(psA[:, 384:512], lhsT=T.PiP[:], rhs=Pp.Rd[:])
                k.op("dve", "tensor_tensor", out=T.AbT[:], in0=psA[:, 0:128], in1=strictT[:], op=ALU.mult)
                k.op("dve", "tensor_tensor", out=T.AkT[:], in0=psA[:, 128:256], in1=strictT[:], op=ALU.mult)
                k.op("dve", "tensor_tensor", out=T.ArT.view(T.ArT.t[:, :].rearrange("p (a b) -> p a b", a=2)),
                     in0=psA.view(psA.t[:, 256:512].rearrange("p (a b) -> p a b", a=2)), in1=msk2i[:],
                     op=ALU.mult)
                yield
                psl = nextps(C)
                k.op("pe", "transpose", out=psl[:, 0:128], in_=T.AbT[:], identity=C.identf[:])
                k.op("act", "copy", out=T.Ab[:], in_=psl[:, 0:128])
                yield
                res = [None]
                for _ in tri_inv_gen(k, C, T.Ab, T.AbT, T.ws, res):
                    yield
                k.op("act", "copy", out=T.PTb[:], in_=res[0][:])


            def seq(n, i):
                tc = slice(n * 128, (n + 1) * 128)
                Pp = prs[slot][n % 2]
                T = tms[slot][i][n % 2]
                vs = slice(i * 64, (i + 1) * 64)
                psz = nextps(C)
                k.mm(psz[:, 0:64], lhsT=Pp.KKd[:], rhs=Hb[slot][i][:], start=True, stop=False)
                k.mm(psz[:, 0:64], lhsT=T.AkT[:], rhs=Pp.Vt[:, vs], start=False, stop=True)
                k.op("act", "copy", out=T.Zb[:], in_=psz[:, 0:64])
                yield
                psu = nextps(C)
                k.mm(psu[:, 0:64], lhsT=T.PTb[:], rhs=T.Zb[:])
                k.op("dve", "tensor_scalar", out=T.Un[:], in0=psu[:, 0:64], scalar1=-1.0, scalar2=None,
                     op0=ALU.mult)
                yield
                psy = nextps(C)
                k.mm(psy[:, 0:64], lhsT=Pp.Rd[:], rhs=Hb[slot][i][:], start=True, stop=False)
                k.mm(psy[:, 0:64], lhsT=T.ArT[:, 0:128], rhs=Pp.Vt[:, vs], start=False, stop=False)
                k.mm(psy[:, 0:64], lhsT=T.ArT[:, 128:256], rhs=T.Un[:], start=False, stop=True)
                psh = nextps(C)
                k.mm(psh[:, 0:64], lhsT=Pp.Kdt[:], rhs=Pp.Vt[:, vs], start=True, stop=False)
                k.mm(psh[:, 0:64], lhsT=Pp.Pdt[:], rhs=T.Un[:], start=False, stop=True)
                k.op("dve", "tensor_scalar", out=T.tmpH[:], in0=Hf[slot][i][:], scalar1=Pp.Ei[:, 127:128], scalar2=None,
                     op0=ALU.mult)
                k.op("dve", "scalar_tensor_tensor", out=Hf[slot][i][:], in0=psh[:, 0:64], scalar=hm[:, i:i + 1],
                     in1=T.tmpH[:], op0=ALU.mult, op1=ALU.add)
                k.op("act", "copy", out=Hb[slot][i][:], in_=Hf[slot][i][:])
                k.op("pool", "memset", ap=T.s1[:], constant=0.0)
                k.op("pool", "memset", ap=T.s2[:], constant=0.0)
                k.act(out=T.y[:], in_=psy[:, 0:64], func=AF.Identity, accum_out=T.s1[:])
                yield
                k.act(out=T.junk[:], in_=T.y[:], func=AF.Square, accum_out=T.s2[:])
                k.op("dve", "tensor_scalar", out=T.mean[:], in0=T.s1[:], scalar1=1.0 / 64.0, scalar2=None,
                     op0=ALU.mult)
                k.op("dve", "tensor_tensor", out=T.var[:], in0=T.mean[:], in1=T.mean[:], op=ALU.mult)
                k.op("dve", "scalar_tensor_tensor", out=T.var[:], in0=T.s2[:], scalar=1.0 / 64.0, in1=T.var[:],
                     op0=ALU.mult, op1=ALU.subtract)
                k.op("dve", "tensor_scalar", out=T.var[:], in0=T.var[:], scalar1=64e-5, scalar2=None, op0=ALU.add)
                k.act(out=T.var[:], in_=T.var[:], func=AF.Sqrt)
                k.op("dve", "reciprocal", out=T.rs[:], in_=T.var[:])
                k.op("dve", "tensor_scalar", out=Pp.yn[:, vs], in0=T.y[:], scalar1=T.mean[:, 0:1],
                     scalar2=T.rs[:, 0:1], op0=ALU.subtract, op1=ALU.mult)

            def assemble(n):
                tc = slice(n * 128, (n + 1) * 128)
                Pp = prs[slot][n % 2]
                pst = psbf(nextps(C))
                k.tr(pst[:, 0:128], Pp.yn[:], C.identb[:])
                k.op("dve", "tensor_scalar", out=Pp.yf[:], in0=pst[:, 0:128], scalar1=gng[:, c:c + 1],
                     scalar2=gnb[:, c:c + 1], op0=ALU.mult, op1=ALU.add)
                k.op("pool", "tensor_tensor", out=Pp.yf[:], in0=Pp.yf[:], in1=I["bbonT"][:, tc], op=ALU.add)
                k.op("pool", "tensor_tensor", out=Pp.yo[:], in0=Pp.yf[:], in1=I["bgT"][:, tc], op=ALU.mult)
                k.dma("sp", TB(N.yT.t)[cs, tc], Pp.yo[:])

            return pair_prep, prep, seq, assemble

        for c0 in range(0, 12, 2):
            fs = [make(c0 + sl, sl) for sl in range(2)]
            for step in range(NT + 1):
                gens = []
                if step < NT:
                    for f in fs:
                        f[0](step)
                    for f in fs:
                        gens += [f[1](step, 0), f[1](step, 1)]
                if step >= 1:
                    for f in fs:
                        gens += [f[2](step - 1, 0), f[2](step - 1, 1)]
                interleave(gens)
                if step >= 1:
                    for f in fs:
                        f[3](step - 1)
        mem_attention(k, C, N)


def phase_mixer_b(k, C, N, j):
    phase_proj_b(k, C, N, j)
    phase_scan_b(k, C, N, j)


class TriWS:
    def __init__(self, k):
        self.x = [k.sb([128, 128], F32) for _ in range(2)]
        self.xt = [k.sb([128, 128], F32) for _ in range(2)]
        self.pt = [k.sb([128, 128], F32) for _ in range(2)]


def tri_inv_gen(k, C, L, LT, ws, res):
    X, XT = L, LT
    PT = ws.pt[0]
    k.op("dve", "tensor_tensor", out=PT[:], in0=C.identf[:], in1=LT[:], op=ALU.subtract)
    for lvl in range(6):
        ps = nextps(C)
        k.mm(ps[:, 0:128], lhsT=XT[:], rhs=X[:])
        if lvl < 5:
            k.mm(ps[:, 128:256], lhsT=X[:], rhs=XT[:])
        X2 = ws.x[lvl % 2]
        k.op("act", "copy", out=X2[:], in_=ps[:, 0:128])
        X2T = ws.xt[lvl % 2]
        if lvl < 5:
            k.op("act", "copy", out=X2T[:], in_=ps[:, 128:256])
        yield
        ps3 = nextps(C)
        k.mm(ps3[:, 0:128], lhsT=X2[:], rhs=PT[:])
        PTn = ws.pt[(lvl + 1) % 2]
        k.op("dve", "tensor_tensor", out=PTn[:], in0=PT[:], in1=ps3[:, 0:128], op=ALU.add)
        X, XT, PT = X2, X2T, PTn
        yield
    res[0] = PT


def interleave(gens):
    alive = list(gens)
    while alive:
        for g in list(alive):
            try:
                next(g)
            except StopIteration:
                alive.remove(g)


EXTRA_SCRATCH = [
    ("gq", [768, S], BF16), ("gk", [768, S], BF16), ("gv", [1536, S], BF16), ("gz", [S, 1536], BF16),
    ("gbg", [S, 24], F32),
]


def phase_proj_c(k, C, N, j):
    with phase(k):
        h = load_hT(k, N.hT, S)
        P = ProjCtx(k, C, h, S, nwt=3, wmax=512)
        W = N.c_w_in
        P.plan([(Ref(W.t[j, :, t * 128:(t + 1) * 128], W.buf), 128) for t in range(24)]
               + [(Ref(W.t[j, :, 3072 + zb * 512:3072 + (zb + 1) * 512], W.buf), 512) for zb in range(3)]
               + [(Ref(W.t[j, :, 4608:4632], W.buf), 24)]
               + [(Ref(W.t[j, :, 4632 + c * 128:4632 + (c + 1) * 128], W.buf), 128) for c in range(4)])
        cwj = k.sb([24, 4, 128], F32)
        k.dma("sp", cwj[:], Ref(N.c_conv.t[j].rearrange("k (t p) -> t k p", p=128), N.c_conv.buf))
        cw = k.sb([128, 4, 24], F32)
        for kk in range(4):
            ps = nextps(C)
            k.op("pe", "transpose", out=ps[:, 0:24], in_=cwj[:, kk, :], identity=C.identf[0:24, 0:24])
            k.op("dve", "tensor_copy", out=cw[:, kk, :], in_=ps[:, 0:24])
        ub = [k.sb([128, 3 + S], F32) for _ in range(2)]
        for t_ in ub:
            k.op("pool", "memset", ap=t_[:, 0:3], constant=0.0)
        cv = [k.sb([128, S], F32) for _ in range(2)]
        sq = k.sb([128, S], BF16)
        rn = k.sb([128, S], F32)
        ob = [k.sb([128, S], BF16) for _ in range(2)]
        for t in range(24):
            u = ub[t % 2]

            def evac(tb, ps, u=u):
                alt_copy(k, tb, u[:, 3 + tb * 512:3 + (tb + 1) * 512], ps[:, :])
            P.F(Ref(W.t[j, :, t * 128:(t + 1) * 128], W.buf), evac)
            c = cv[t % 2]
            k.op("dve", "tensor_scalar", out=c[:], in0=u[:, 3:3 + S], scalar1=cw[:, 3, t:t + 1], scalar2=None,
                 op0=ALU.mult)
            for kk in range(3):
                k.op("dve", "scalar_tensor_tensor", out=c[:], in0=u[:, kk:kk + S], scalar=cw[:, kk, t:t + 1],
                     in1=c[:], op0=ALU.mult, op1=ALU.add)
            k.act(out=c[:], in_=c[:], func=AF.Silu)
            o = ob[t % 2]
            if t < 12:
                k.act(out=sq[:], in_=c[:], func=AF.Square)
                for tb in range(4):
                    ps = nextps(C)
                    k.mm(ps[:, :], lhsT=C.onesb[:], rhs=sq[:, tb * 512:(tb + 1) * 512])
                    k.op("dve", "tensor_scalar", out=rn[:, tb * 512:(tb + 1) * 512], in0=ps[:, :], scalar1=1e-6,
                         scalar2=None, op0=ALU.add)
                k.act(out=rn[:], in_=rn[:], func=AF.Sqrt)
                k.op("dve", "reciprocal", out=rn[:], in_=rn[:])
                k.op("dve", "tensor_tensor", out=o[:], in0=c[:], in1=rn[:], op=ALU.mult)
            else:
                k.op("pool", "tensor_copy", out=o[:], in_=c[:])
            if t < 6:
                dst = N.gq[t * 128:(t + 1) * 128, :]
            elif t < 12:
                dst = N.gk[(t - 6) * 128:(t - 5) * 128, :]
            else:
                dst = N.gv[(t - 12) * 128:(t - 11) * 128, :]
            k.dma("sp", dst, o[:])
        zs = [k.sb([128, 512], BF16) for _ in range(2)]
        for zb in range(3):
            def evz(tt, ps, zb=zb):
                s_ = zs[tt % 2]
                k.act(out=s_[:], in_=ps[:, :], func=AF.Silu)
                k.dma("sp", N.gz[tt * 128:(tt + 1) * 128, zb * 512:(zb + 1) * 512], s_[:])
            P.T(Ref(W.t[j, :, 3072 + zb * 512:3072 + (zb + 1) * 512], W.buf), 512, evz)
        al = k.sb([128, 12], F32)
        k.dma("sp", al[:], Ref(bcast_rows(N.c_a_log.t[j, :], 12), N.c_a_log.buf))
        dtb = k.sb([128, 12], F32)
        k.dma("sp", dtb[:], Ref(bcast_rows(N.c_dt_bias.t[j, :], 12), N.c_dt_bias.buf))
        nea = k.sb([128, 12], F32)
        k.act(out=nea[:], in_=al[:], func=AF.Exp)
        k.op("dve", "tensor_scalar", out=nea[:], in0=nea[:], scalar1=-1.0, scalar2=None, op0=ALU.mult)
        bg = [k.sb([128, 24], F32) for _ in range(2)]

        def evbg(tt, ps):
            s_ = bg[tt % 2]
            k.act(out=s_[:, 0:12], in_=ps[:, 0:12], func=AF.Sigmoid)
            k.op("dve", "tensor_tensor", out=s_[:, 12:24], in0=ps[:, 12:24], in1=dtb[:], op=ALU.add)
            k.act(out=s_[:, 12:24], in_=s_[:, 12:24], func=AF.Exp)
            k.act(out=s_[:, 12:24], in_=s_[:, 12:24], func=AF.Ln, bias=C.onesf[:, 0:1])
            k.op("dve", "tensor_tensor", out=s_[:, 12:24], in0=s_[:, 12:24], in1=nea[:], op=ALU.mult)
            k.dma("sp", N.gbg[tt * 128:(tt + 1) * 128, :], s_[:])
        P.T(Ref(W.t[j, :, 4608:4632], W.buf), 24, evbg)
        stg = [k.sb([128, S], BF16) for _ in range(2)]
        cnt = [0]
        for c in range(4):
            proj_F_to_dram(k, P, Ref(W.t[j, :, 4632 + c * 128:4632 + (c + 1) * 128], W.buf), N.qmT, c * 128, stg, cnt)


class GdnTmp:
    def __init__(self, k):
        f = lambda dt: k.sb([128, 128], dt)
        self.vtok = k.sb([128, 256], BF16)
        self.kdec = f(BF16)
        self.gbc = f(F32)
        self.tmp = f(F32)
        self.DT = f(F32)
        self.L2T = f(F32)
        self.L2 = f(F32)
        self.AT = f(BF16)
        self.PTb = f(BF16)
        self.u = f(F32)
        self.wtok = f(BF16)
        self.wT = f(BF16)
        self.vnew = f(BF16)
        self.ob = f(F32)
        self.o = f(F32)
        self.junk = f(F32)
        self.ss = k.sb([128, 1], F32)
        self.sd = k.sb([128, 1], F32)
        self.rs = k.sb([128, 1], F32)
        self.y = f(F32)
        self.y2 = f(BF16)
        self.yT = f(BF16)
        self.zt = f(BF16)
        self.ws = TriWS(k)


def phase_scan_c(k, C, N, j):
    with phase(k):
        M1 = C.mask(-1, 1, ALU.is_ge)
        strictT = C.mask(-1, 1, ALU.is_gt)
        bgall = k.sb([128, NT, 24], F32)
        k.dma("sp", bgall[:], N.gbg.view(N.gbg.t.rearrange("(t p) c -> p t c", p=128)))
        gc = k.sb([128, NT, 12], F32)
        gl = k.sb([128, NT, 12], F32)
        for n in range(NT):
            ps = nextps(C)
            k.mm(ps[:, 0:12], lhsT=M1[:], rhs=bgall[:, n, 12:24])
            k.mm(ps[:, 16:28], lhsT=C.onesf[:], rhs=bgall[:, n, 12:24])
            k.op("act", "copy", out=gc[:, n, :], in_=ps[:, 0:12])
            k.op("act", "copy", out=gl[:, n, :], in_=ps[:, 16:28])
        egc = k.sb([128, NT, 12], F32)
        egl = k.sb([128, NT, 12], F32)
        edec = k.sb([128, NT, 12], F32)
        qsc = k.sb([128, NT, 12], F32)
        k.act(out=egc[:], in_=gc[:], func=AF.Exp)
        k.act(out=egl[:], in_=gl[:], func=AF.Exp)
        k.op("dve", "tensor_tensor", out=edec[:], in0=gl[:], in1=gc[:], op=ALU.subtract)
        k.act(out=edec[:], in_=edec[:], func=AF.Exp)
        k.op("dve", "tensor_scalar", out=qsc[:], in0=egc[:], scalar1=float(128.0 ** -0.5), scalar2=None, op0=ALU.mult)
        normg = k.sb([128, 128], F32)
        k.dma("sp", normg[:], Ref(bcast_rows(N.c_norm_g.t[j, :], 128), N.c_norm_g.buf))
        kTs = [k.sb([128, S], BF16) for _ in range(2)]
        qTs = [k.sb([128, S], BF16) for _ in range(2)]
        vTs = [k.sb([128, S], BF16) for _ in range(4)]
        KKs = [[k.sb([128, 128], F32) for _ in range(2)] for _ in range(2)]
        QKs = [[k.sb([128, 128], F32) for _ in range(2)] for _ in range(2)]
        ktoks = [[k.sb([128, 128], BF16) for _ in range(2)] for _ in range(2)]
        tmps = [[[GdnTmp(k) for _ in range(2)] for _ in range(2)] for _ in range(2)]
        H = [[k.sb([128, 128], F32) for _ in range(2)] for _ in range(2)]
        Hb = [[k.sb([128, 128], BF16) for _ in range(2)] for _ in range(2)]
        un = 0
        sh = 0
        def make(hq, slot):
            kT, qT = kTs[slot], qTs[slot]
            k.dma("sp", kT[:], N.gk[hq * 128:(hq + 1) * 128, :])
            k.dma("sp", qT[:], N.gq[hq * 128:(hq + 1) * 128, :])
            vT2 = []
            for i in range(2):
                hv = 2 * hq + i
                vT = vTs[slot * 2 + i]
                k.dma("sp", vT[:], N.gv[hv * 128:(hv + 1) * 128, :])
                vT2.append(vT)
                k.op("pool", "memset", ap=H[slot][i][:], constant=0.0)
                k.op("pool", "memset", ap=Hb[slot][i][:], constant=0.0)
            def shared_prep(n):
                tc = slice(n * 128, (n + 1) * 128)
                KK, QK, ktok = KKs[slot][n % 2], QKs[slot][n % 2], ktoks[slot][n % 2]
                ps = nextps(C)
                k.mm(ps[:, 0:128], lhsT=kT[:, tc], rhs=kT[:, tc])
                k.mm(ps[:, 128:256], lhsT=kT[:, tc], rhs=qT[:, tc])
                k.op("dve", "tensor_tensor", out=KK[:], in0=ps[:, 0:128], in1=strictT[:], op=ALU.mult)
                k.op("dve", "scalar_tensor_tensor", out=QK[:], in0=ps[:, 128:256], scalar=float(128.0 ** -0.5),
                     in1=M1[:], op0=ALU.mult, op1=ALU.mult)
                pst = psbf(nextps(C))
                k.tr(pst[:, 0:128], kT[:, tc], C.identb[:])
                k.op("act", "copy", out=ktok[:], in_=pst[:, 0:128])

            def prep(n, i):
                tc = slice(n * 128, (n + 1) * 128)
                KK, QK, ktok = KKs[slot][n % 2], QKs[slot][n % 2], ktoks[slot][n % 2]
                hv = 2 * hq + i
                T = tmps[slot][i][n % 2]
                bcol = bgall[:, n, hv:hv + 1]
                gcol = bgall[:, n, 12 + hv:13 + hv]
                pst = psbf(nextps(C))
                k.tr(pst[:, 0:128], vT2[i][:, tc], C.identb[:])
                k.op("act", "copy", out=T.vtok[:, 0:128], in_=pst[:, 0:128])
                k.op("dve", "tensor_scalar", out=T.vtok[:, 128:256], in0=ktok[:], scalar1=egc[:, n, hv:hv + 1],
                     scalar2=None, op0=ALU.mult)
                k.act(out=T.kdec[:], in_=ktok[:], func=AF.Copy, scale=edec[:, n, hv:hv + 1])
                k.op("dve", "tensor_scalar", out=T.gbc[:], in0=C.onesf[:], scalar1=gcol, scalar2=None, op0=ALU.mult)
                psg = nextps(C)
                k.mm(psg[:, 0:128], lhsT=T.gbc[:], rhs=M1[:])
                k.op("dve", "tensor_scalar", out=T.tmp[:], in0=psg[:, 0:128], scalar1=gc[:, n, hv:hv + 1],
                     scalar2=0.0, op0=ALU.subtract, op1=ALU.min)
                k.act(out=T.DT[:], in_=T.tmp[:], func=AF.Exp)
                k.op("dve", "scalar_tensor_tensor", out=T.L2T[:], in0=KK[:], scalar=bcol, in1=T.DT[:],
                     op0=ALU.mult, op1=ALU.mult)
                k.op("dve", "tensor_tensor", out=T.AT[:], in0=QK[:], in1=T.DT[:], op=ALU.mult)
                yield
                psl = nextps(C)
                k.op("pe", "transpose", out=psl[:, 0:128], in_=T.L2T[:], identity=C.identf[:])
                k.op("act", "copy", out=T.L2[:], in_=psl[:, 0:128])
                yield
                res = [None]
                for _ in tri_inv_gen(k, C, T.L2, T.L2T, T.ws, res):
                    yield
                PT = res[0]
                k.op("act", "copy", out=T.PTb[:], in_=PT[:])
                psu = nextps(C)
                k.mm(psu[:, 0:256], lhsT=T.PTb[:], rhs=T.vtok[:])
                k.op("dve", "tensor_scalar", out=T.u[:], in0=psu[:, 0:128], scalar1=bcol, scalar2=None, op0=ALU.mult)
                k.op("dve", "tensor_scalar", out=T.wtok[:], in0=psu[:, 128:256], scalar1=bcol, scalar2=None,
                     op0=ALU.mult)
                pst = psbf(nextps(C))
                k.tr(pst[:, 0:128], T.wtok[:], C.identb[:])
                k.op("act", "copy", out=T.wT[:], in_=pst[:, 0:128])


            def seq(n, i):
                tc = slice(n * 128, (n + 1) * 128)
                hv = 2 * hq + i
                T = tmps[slot][i][n % 2]
                ps1 = nextps(C)
                k.mm(ps1[:, 0:128], lhsT=T.wT[:], rhs=Hb[slot][i][:])
                k.op("dve", "tensor_tensor", out=T.vnew[:], in0=T.u[:], in1=ps1[:, 0:128], op=ALU.subtract)
                yield
                pso = nextps(C)
                k.mm(pso[:, 0:128], lhsT=qT[:, tc], rhs=Hb[slot][i][:])
                k.mm(pso[:, 128:256], lhsT=T.AT[:], rhs=T.vnew[:])
                k.op("act", "copy", out=T.ob[:], in_=pso[:, 128:256])
                k.op("dve", "scalar_tensor_tensor", out=T.o[:], in0=pso[:, 0:128], scalar=qsc[:, n, hv:hv + 1],
                     in1=T.ob[:], op0=ALU.mult, op1=ALU.add)
                psh = nextps(C)
                k.mm(psh[:, 0:128], lhsT=T.kdec[:], rhs=T.vnew[:])
                k.op("dve", "scalar_tensor_tensor", out=H[slot][i][:], in0=H[slot][i][:], scalar=egl[:, n, hv:hv + 1],
                     in1=psh[:, 0:128], op0=ALU.mult, op1=ALU.add)
                k.op("act", "copy", out=Hb[slot][i][:], in_=H[slot][i][:])
                yield
                k.op("pool", "memset", ap=T.ss[:], constant=0.0)
                k.act(out=T.junk[:], in_=T.o[:], func=AF.Square, accum_out=T.ss[:])
                k.op("dve", "tensor_scalar", out=T.sd[:], in0=T.ss[:], scalar1=1.0 / 128.0, scalar2=EPS,
                     op0=ALU.mult, op1=ALU.add)
                k.act(out=T.sd[:], in_=T.sd[:], func=AF.Sqrt)
                k.op("dve", "reciprocal", out=T.rs[:], in_=T.sd[:])
                k.op("dve", "scalar_tensor_tensor", out=T.y[:], in0=T.o[:], scalar=T.rs[:, 0:1], in1=normg[:],
                     op0=ALU.mult, op1=ALU.mult)
                k.dma("sp", T.zt[:], N.gz[n * 128:(n + 1) * 128, hv * 128:(hv + 1) * 128])
                k.op("pool", "tensor_tensor", out=T.y2[:], in0=T.y[:], in1=T.zt[:], op=ALU.mult)
                pst = psbf(nextps(C))
                k.tr(pst[:, 0:128], T.y2[:], C.identb[:])
                k.op("act", "copy", out=T.yT[:], in_=pst[:, 0:128])
                k.dma("sp", TB(N.yT.t)[hv * 128:(hv + 1) * 128, tc], T.yT[:])

            return shared_prep, prep, seq

        for hq0 in range(0, 6, 2):
            fs = [make(hq0 + sl, sl) for sl in range(2)]
            for step in range(NT + 1):
                gens = []
                if step < NT:
                    for f in fs:
                        f[0](step)
                    for f in fs:
                        gens += [f[1](step, 0), f[1](step, 1)]
                if step >= 1:
                    for f in fs:
                        gens += [f[2](step - 1, 0), f[2](step - 1, 1)]
                interleave(gens)
        mem_attention(k, C, N)


def phase_mixer_c(k, C, N, j):
    phase_proj_c(k, C, N, j)
    phase_scan_c(k, C, N, j)


WEIGHT_SHAPES = [
    ("attn_norm", [4, 2048]), ("mem_norm", [4, 2048]), ("w_mem_kv", [4, 2048, 1024]), ("w_out", [4, 2048, 2048]),
    ("ffn_norm", [4, 2048]), ("w_ffn_up", [4, 2048, 11264]), ("ffn_conv", [4, 3, 11264]),
    ("w_ffn_down", [4, 5632, 2048]), ("final_norm", [2048]), ("a_w_in", [2, 2048, 2560]), ("a_sinks", [2, 24]),
    ("b_w_in", [1, 2048, 5568]), ("b_mu", [1, 5056]), ("b_w0", [1, 1536]), ("b_w_decay_up", [1, 96, 1536]),
    ("b_a0", [1, 1536]), ("b_w_iclr_up", [1, 96, 1536]), ("b_w_gate_up", [1, 256, 1536]), ("b_k_k", [1, 1536]),
    ("b_k_a", [1, 1536]), ("b_r_k", [1, 24, 64]), ("b_gn_g", [1, 1536]), ("b_gn_b", [1, 1536]),
    ("c_w_in", [1, 2048, 5144]), ("c_conv", [1, 4, 3072]), ("c_a_log", [1, 12]), ("c_dt_bias", [1, 12]),
    ("c_norm_g", [1, 128]),
]

SCRATCH = [
    ("xs", [S, D], F32), ("hT", [D, S], BF16), ("memhT", [D, 256], BF16), ("memkT", [512, 256], BF16),
    ("memv", [256, 512], BF16), ("qT", [1536, S], BF16), ("kT2", [512, S], BF16), ("v2", [S, 512], BF16),
    ("qmT", [512, S], BF16), ("yT", [D, S], BF16), ("aT", [DFF, S], BF16),
]


def fresh_patch():
    TB.f = lambda self: TB(self.t)


def emit_layer(k, C, N, li, cfg):
    kind, j = li % 3, li // 3
    only = cfg.get("only")

    def ph(name, fn, *a):
        if only is None or name in only:
            fn(*a)
    x_in = N.x if li == list(cfg.get('layers', range(4)))[0] else N.xs
    ph("norm1", phase_norm, k, C, x_in, Ref(N.attn_norm.t[li, :], N.attn_norm.buf), N.hT, S)
    ph("normm", phase_norm, k, C, N.mem, Ref(N.mem_norm.t[li, :], N.mem_norm.buf), N.memhT, 256)
    ph("memkv", phase_mem_kv, k, C, N, li)
    if kind == 0:
        ph("proj", phase_proj_a, k, C, N, j)
        ph("mix", phase_attn_a, k, C, N, j)
    elif kind == 1:
        ph("mix", phase_mixer_b, k, C, N, j)
    else:
        ph("mix", phase_mixer_c, k, C, N, j)
    ph("outproj", phase_outproj, k, C, N, li, x_in, N.xs)
    ph("ffnup", phase_ffn_up, k, C, N, li)
    ph("ffndown", phase_ffn_down, k, C, N, li, N.xs)
    return True


def build(cfg):
    nc = bass.Bass("TRN2", target_bir_lowering=False)
    dump = cfg.get("dump", ())
    with ExitStack() as st:
        k = K(nc, st)
        N = Net()
        N.x = k.dram("x", [S, D], F32, kind="ExternalInput")
        N.mem = k.dram("mem", [256, D], F32, kind="ExternalInput")
        used = cfg.get("weights")
        for name, shape in WEIGHT_SHAPES:
            if used is None or name in used:
                setattr(N, name, k.dram(name, shape, F32, kind="ExternalInput"))
        N.out = k.dram("out", [S, D], F32, kind="ExternalOutput")
        for name, shape, dt in SCRATCH + EXTRA_SCRATCH + B_SCRATCH:
            setattr(N, name, k.dram(name, shape, dt, kind=("ExternalOutput" if name in dump else "Internal")))
        C = setup_consts(k)
        layers = cfg.get("layers", range(4))
        CFG.clear()
        CFG.update(cfg)
        ok = True
        PH["n"] = 0
        PH["max"] = cfg.get("max_phases", 10 ** 9)
        try:
            for li in layers:
                ok = emit_layer(k, C, N, li, cfg)
                if not ok:
                    break
            if ok and cfg.get("final", True):
                phase_final_norm(k, C, N.xs, Ref(N.final_norm.t[:], N.final_norm.buf), N.out)
        except StopBuild:
            pass
        k_barrier(k)
        k.finish()
        k.stats = {n: (e.nins, e.count) for n, e in k.eng.items()}
        print("instr stats", k.stats)
    return nc


_CACHE = {}


def run(inputs, cfg, cores=8):
    key = repr(sorted((a, repr(b)) for a, b in cfg.items()))
    if key not in _CACHE:
        _CACHE[key] = build(cfg)
    nc = _CACHE[key]
    used = cfg.get("weights")
    wts = {n: np.ascontiguousarray(inputs[n], dtype=np.float32) for n, _ in WEIGHT_SHAPES
           if used is None or n in used}
    in_maps = []
    for b in range(cores):
        m = dict(wts)
        m["x"] = np.ascontiguousarray(inputs["x"][b], dtype=np.float32)
        m["mem"] = np.ascontiguousarray(inputs["mem"][b], dtype=np.float32)
        in_maps.append(m)
    return run_bass_kernel_spmd(nc, in_maps, core_ids=list(range(cores)))


def kernel(**inputs):
    res = run(inputs, {"layers": (0, 1, 2, 3)}, cores=8)
    return np.stack([np.asarray(r["out"], dtype=np.float32) for r in res.results], axis=0)
```
